# Optimizing a Trainium2 kernel written in Bass

```python
import math
import jax, jax.numpy as jnp
from jax import lax
import numpy as np

D_MODEL = 1024
BATCH = 8
SEQ = 2048
DEPTH = 4
DEC_BATCH = 128
DEC_SEQ = 1
PAST_LEN = 16384
PAGE_SIZE = 128

N_MIXERS = 3
EPS = 1e-6
CONV_K = 4
SSD_INNER = 2 * D_MODEL
SSD_HEAD_DIM = 64
SSD_HEADS = SSD_INNER // SSD_HEAD_DIM
SSD_GROUPS = 8
SSD_STATE = 128
SSD_REP = SSD_HEADS // SSD_GROUPS
SSD_CONV_DIM = SSD_INNER + 2 * SSD_GROUPS * SSD_STATE
SSD_IN = SSD_INNER + SSD_CONV_DIM + SSD_HEADS
SSD_CHUNK = 128
GM_INNER = 2 * D_MODEL
GM_GROUPS = 8
GM_GROUP_DIM = GM_INNER // GM_GROUPS
GM_CHUNK = 128
ML_INNER = 2 * D_MODEL
ML_HEADS = 4
ML_HEAD_DIM = ML_INNER // ML_HEADS
ML_BLOCK = 4
ML_NBLK = ML_INNER // ML_BLOCK
ML_CHUNK = 128

kernel_name = "hybrid_ssd_gmlp_mlstm_step"

F32 = jnp.float32


def _rmsnorm(x, w):
    xf = x.astype(F32)
    y = xf * lax.rsqrt(jnp.mean(xf * xf, axis=-1, keepdims=True) + EPS)
    return (y * w.astype(F32)).astype(x.dtype)


def _causal_conv(xs, buf, w, b):
    T = xs.shape[1]
    full = jnp.concatenate([buf.astype(xs.dtype), xs], axis=1)
    y = b.astype(xs.dtype)
    for k in range(CONV_K):
        y = y + full[:, k:k + T] * w[k].astype(xs.dtype)
    return y, full[:, T:]


def _to_chunks(a, L):
    B, T = a.shape[:2]
    return jnp.moveaxis(a.reshape(B, T // L, L, *a.shape[2:]), 1, 0)


def _from_chunks(a):
    nC, B, L = a.shape[:3]
    return jnp.moveaxis(a, 0, 1).reshape(B, nC * L, *a.shape[3:])


def _ssd_scan(x, dt, A, Bm, Cm, h0):
    T = x.shape[1]
    L = math.gcd(T, SSD_CHUNK)
    mask = jnp.tril(jnp.ones((L, L), bool))[None, :, :, None, None]

    def step(h, inp):
        xc, dtc, Bc, Cc = inp
        cum = jnp.cumsum(dtc * A, axis=1)
        seg = cum[:, :, None] - cum[:, None, :]
        decay = jnp.exp(jnp.where(mask, seg, -jnp.inf))
        cb = jnp.einsum('blgn,bsgn->blsg', Cc, Bc)
        mix = cb[..., None] * decay * dtc[:, None]
        y = jnp.einsum('blsgr,bsgrp->blgrp', mix, xc)
        y = y + jnp.einsum('blgn,bgrpn->blgrp', Cc, h) * jnp.exp(cum)[..., None]
        w_end = jnp.exp(cum[:, -1:] - cum) * dtc
        h = jnp.exp(cum[:, -1])[..., None, None] * h + jnp.einsum('bsgr,bsgn,bsgrp->bgrpn', w_end, Bc, xc)
        return h, y

    h, ys = lax.scan(step, h0, (_to_chunks(x, L), _to_chunks(dt, L), _to_chunks(Bm, L), _to_chunks(Cm, L)))
    return _from_chunks(ys), h


def _mamba_layer(x, ssm0, conv0, norm_w, in_proj, conv_w, conv_b, dt_bias, A_log, D_skip, gnorm_w, out_proj):
    Bsz, T, _ = x.shape
    xn = _rmsnorm(x, norm_w)
    z, xbc, dt = jnp.split(xn @ in_proj, [SSD_INNER, SSD_INNER + SSD_CONV_DIM], axis=-1)
    xbc, conv_new = _causal_conv(xbc, conv0, conv_w, conv_b)
    xbc = jax.nn.silu(xbc)
    xs, Bm, Cm = jnp.split(xbc, [SSD_INNER, SSD_INNER + SSD_GROUPS * SSD_STATE], axis=-1)
    xs = xs.reshape(Bsz, T, SSD_GROUPS, SSD_REP, SSD_HEAD_DIM).astype(F32)
    Bm = Bm.reshape(Bsz, T, SSD_GROUPS, SSD_STATE).astype(F32)
    Cm = Cm.reshape(Bsz, T, SSD_GROUPS, SSD_STATE).astype(F32)
    dt = jax.nn.softplus(dt.astype(F32) + dt_bias.astype(F32)).reshape(Bsz, T, SSD_GROUPS, SSD_REP)
    A = -jnp.exp(A_log.astype(F32)).reshape(SSD_GROUPS, SSD_REP)
    h0 = ssm0.astype(F32).reshape(Bsz, SSD_GROUPS, SSD_REP, SSD_HEAD_DIM, SSD_STATE)
    y, h = _ssd_scan(xs, dt, A, Bm, Cm, h0)
    y = y + D_skip.astype(F32).reshape(SSD_GROUPS, SSD_REP)[..., None] * xs
    y = y.reshape(Bsz, T, SSD_GROUPS, SSD_INNER // SSD_GROUPS) * jax.nn.silu(
        z.astype(F32)).reshape(Bsz, T, SSD_GROUPS, SSD_INNER // SSD_GROUPS)
    y = y * lax.rsqrt(jnp.mean(y * y, axis=-1, keepdims=True) + EPS)
    y = y.reshape(Bsz, T, SSD_INNER) * gnorm_w.astype(F32)
    out = y.astype(x.dtype) @ out_proj
    return x + out, h.reshape(Bsz, SSD_HEADS, SSD_HEAD_DIM, SSD_STATE), conv_new


def _gmlp_layer(x, norm_w, in_proj, v_ln_w, v_ln_b, spatial_w, spatial_b, out_proj):
    Bsz, T, _ = x.shape
    xn = _rmsnorm(x, norm_w)
    u, v, z = jnp.split(xn @ in_proj, 3, axis=-1)
    vf = v.astype(F32)
    mu = jnp.mean(vf, axis=-1, keepdims=True)
    vc = vf - mu
    vn = vc * lax.rsqrt(jnp.mean(vc * vc, axis=-1, keepdims=True) + EPS) * v_ln_w.astype(F32) + v_ln_b.astype(F32)
    n_chunks = -(-T // GM_CHUNK)
    Tp = n_chunks * GM_CHUNK
    vp = jnp.pad(vn, ((0, 0), (0, Tp - T), (0, 0))).reshape(Bsz, n_chunks, GM_CHUNK, GM_GROUPS, GM_GROUP_DIM)
    ws = jnp.where(jnp.tril(jnp.ones((GM_CHUNK, GM_CHUNK), bool))[None], spatial_w.astype(F32), 0.0)
    mixed = jnp.einsum('gts,bcsgd->bctgd', ws, vp) + spatial_b.astype(F32).T[None, None, :, :, None]
    mixed = mixed.reshape(Bsz, Tp, GM_INNER)[:, :T]
    g = u.astype(F32) * mixed * jax.nn.silu(z.astype(F32))
    out = g.astype(x.dtype) @ out_proj
    return x + out, vn.astype(x.dtype)


def _blockdiag(x, w):
    xb = x.reshape(*x.shape[:-1], ML_NBLK, ML_BLOCK)
    return jnp.einsum('...nj,nji->...ni', xb, w.astype(x.dtype)).reshape(x.shape)


def _mlstm_scan(q, k, v, ig, lf, C0, n0, m0):
    T = q.shape[1]
    L = math.gcd(T, ML_CHUNK)
    mask = jnp.tril(jnp.ones((L, L), bool))

    def step(carry, inp):
        C, n, m = carry
        qc, kc, vc, igc, lfc = inp
        b = jnp.cumsum(lfc, axis=1).transpose(0, 2, 1)
        igt = igc.transpose(0, 2, 1)
        d = jnp.where(mask, b[:, :, :, None] - b[:, :, None, :] + igt[:, :, None, :], -jnp.inf)
        inter = b + m[:, :, None]
        mt = jnp.maximum(inter, jnp.max(d, axis=-1))
        w = jnp.exp(d - mt[..., None]) * jnp.einsum('blhd,bshd->bhls', qc, kc)
        wi = jnp.exp(inter - mt)
        num = wi[..., None] * jnp.einsum('bhvk,blhk->bhlv', C, qc) + jnp.einsum('bhls,bshv->bhlv', w, vc)
        den = wi * jnp.einsum('bhk,blhk->bhl', n, qc) + jnp.sum(w, axis=-1)
        den = jnp.maximum(jnp.abs(den), jnp.exp(-mt))
        h = (num / den[..., None]).transpose(0, 2, 1, 3)
        m_new = mt[:, :, -1]
        we = jnp.exp(b[:, :, -1:] - b + igt - m_new[..., None])
        dp = jnp.exp(inter[:, :, -1] - m_new)
        C = dp[..., None, None] * C + jnp.einsum('bhs,bshv,bshk->bhvk', we, vc, kc)
        n = dp[..., None] * n + jnp.einsum('bhs,bshk->bhk', we, kc)
        return (C, n, m_new), h

    xs = tuple(_to_chunks(a, L) for a in (q, k, v, ig, lf))
    (C, n, m), hs = lax.scan(step, (C0, n0, m0), xs)
    return _from_chunks(hs), C, n, m


def _mlstm_layer(x, C0, n0, m0, conv0, norm_w, in_proj, conv_w, conv_b, w_q, w_k, w_v, w_o, b_o,
                 w_if, b_if, mh_norm_w, skip, out_proj):
    Bsz, T, _ = x.shape
    xn = _rmsnorm(x, norm_w)
    xm, z = jnp.split(xn @ in_proj, 2, axis=-1)
    xc, conv_new = _causal_conv(xm, conv0, conv_w, conv_b)
    xc = jax.nn.silu(xc)
    q = _blockdiag(xc, w_q)
    k = _blockdiag(xc, w_k)
    v = _blockdiag(xm, w_v)
    gates = (jnp.concatenate([q, k, v], axis=-1) @ w_if + b_if).astype(F32)
    ig, fg = jnp.split(gates, 2, axis=-1)
    lf = jax.nn.log_sigmoid(fg)
    o = jax.nn.sigmoid((_blockdiag(xm, w_o) + b_o).astype(F32)).reshape(Bsz, T, ML_HEADS, ML_HEAD_DIM)
    shp = (Bsz, T, ML_HEADS, ML_HEAD_DIM)
    qh = q.astype(F32).reshape(shp)
    kh = k.astype(F32).reshape(shp) * (ML_HEAD_DIM ** -0.5)
    vh = v.astype(F32).reshape(shp)
    h, C, n, m = _mlstm_scan(qh, kh, vh, ig, lf, C0.astype(F32), n0.astype(F32), m0.astype(F32))
    h = o * h
    mu = jnp.mean(h, axis=-1, keepdims=True)
    hc = h - mu
    hn = (hc * lax.rsqrt(jnp.mean(hc * hc, axis=-1, keepdims=True) + EPS)).reshape(Bsz, T, ML_INNER)
    hn = hn * mh_norm_w.astype(F32) + skip.astype(F32) * xc.astype(F32)
    g = hn * jax.nn.silu(z.astype(F32))
    out = g.astype(x.dtype) @ out_proj
    return x + out, C, n, m, conv_new


def _normal(key, shape, scale):
    return jax.random.normal(key, shape, F32) * scale


def _gain(key, n):
    return 1.0 + 0.02 * jax.random.normal(key, (n,), F32)


def _mamba_params(key, p):
    ks = jax.random.split(key, 9)
    dt = jnp.exp(jax.random.uniform(ks[4], (SSD_HEADS,), F32, math.log(1e-3), math.log(1e-1)))
    return {
        p + "norm_w": _gain(ks[0], D_MODEL),
        p + "in_proj": _normal(ks[1], (D_MODEL, SSD_IN), D_MODEL ** -0.5),
        p + "conv_w": _normal(ks[2], (CONV_K, SSD_CONV_DIM), CONV_K ** -0.5),
        p + "conv_b": _normal(ks[3], (SSD_CONV_DIM,), 0.02),
        p + "dt_bias": dt + jnp.log(-jnp.expm1(-dt)),
        p + "A_log": jnp.log(jax.random.uniform(ks[5], (SSD_HEADS,), F32, 1.0, 16.0)),
        p + "D_skip": _gain(ks[6], SSD_HEADS),
        p + "gnorm_w": _gain(ks[7], SSD_INNER),
        p + "out_proj": _normal(ks[8], (SSD_INNER, D_MODEL), SSD_INNER ** -0.5),
    }


def _gmlp_params(key, p):
    ks = jax.random.split(key, 7)
    return {
        p + "norm_w": _gain(ks[0], D_MODEL),
        p + "in_proj": _normal(ks[1], (D_MODEL, 3 * GM_INNER), D_MODEL ** -0.5),
        p + "v_ln_w": _gain(ks[2], GM_INNER),
        p + "v_ln_b": _normal(ks[3], (GM_INNER,), 0.02),
        p + "spatial_w": _normal(ks[4], (GM_GROUPS, GM_CHUNK, GM_CHUNK), GM_CHUNK ** -0.5),
        p + "spatial_b": 1.0 + _normal(ks[5], (GM_GROUPS, GM_CHUNK), 0.02),
        p + "out_proj": _normal(ks[6], (GM_INNER, D_MODEL), GM_INNER ** -0.5),
    }


def _mlstm_params(key, p):
    ks = jax.random.split(key, 15)
    b_i = _normal(ks[10], (ML_HEADS,), 0.1)
    b_f = jnp.linspace(3.0, 6.0, ML_HEADS, dtype=F32) + _normal(ks[11], (ML_HEADS,), 0.1)
    return {
        p + "norm_w": _gain(ks[0], D_MODEL),
        p + "in_proj": _normal(ks[1], (D_MODEL, 2 * ML_INNER), D_MODEL ** -0.5),
        p + "conv_w": _normal(ks[2], (CONV_K, ML_INNER), CONV_K ** -0.5),
        p + "conv_b": _normal(ks[3], (ML_INNER,), 0.02),
        p + "w_q": _normal(ks[4], (ML_NBLK, ML_BLOCK, ML_BLOCK), ML_BLOCK ** -0.5),
        p + "w_k": _normal(ks[5], (ML_NBLK, ML_BLOCK, ML_BLOCK), ML_BLOCK ** -0.5),
        p + "w_v": _normal(ks[6], (ML_NBLK, ML_BLOCK, ML_BLOCK), ML_BLOCK ** -0.5),
        p + "w_o": _normal(ks[7], (ML_NBLK, ML_BLOCK, ML_BLOCK), ML_BLOCK ** -0.5),
        p + "b_o": _normal(ks[8], (ML_INNER,), 0.1),
        p + "w_if": _normal(ks[9], (3 * ML_INNER, 2 * ML_HEADS), (3 * ML_INNER) ** -0.5),
        p + "b_if": jnp.concatenate([b_i, b_f]),
        p + "mh_norm_w": _gain(ks[12], ML_INNER),
        p + "skip": _gain(ks[13], ML_INNER),
        p + "out_proj": _normal(ks[14], (ML_INNER, D_MODEL), ML_INNER ** -0.5),
    }


def setup_inputs(seed: int = 0) -> dict:
    key = jax.random.key(seed)
    ks = jax.random.split(key, 16)
    d = {
        "x_prompt": _normal(ks[0], (BATCH, SEQ, D_MODEL), 1.0),
        "x_sample": _normal(ks[1], (DEC_BATCH, DEC_SEQ, D_MODEL), 1.0),
        "state_l0_ssm": _normal(ks[2], (DEC_BATCH, SSD_HEADS, SSD_HEAD_DIM, SSD_STATE), 0.1),
        "state_l0_conv": _normal(ks[3], (DEC_BATCH, CONV_K - 1, SSD_CONV_DIM), 1.0),
        "state_l2_C": _normal(ks[4], (DEC_BATCH, ML_HEADS, ML_HEAD_DIM, ML_HEAD_DIM), 0.05),
        "state_l2_n": _normal(ks[5], (DEC_BATCH, ML_HEADS, ML_HEAD_DIM), 0.05),
        "state_l2_m": _normal(ks[6], (DEC_BATCH, ML_HEADS), 1.0),
        "state_l2_conv": _normal(ks[7], (DEC_BATCH, CONV_K - 1, ML_INNER), 1.0),
        "state_l3_ssm": _normal(ks[8], (DEC_BATCH, SSD_HEADS, SSD_HEAD_DIM, SSD_STATE), 0.1),
        "state_l3_conv": _normal(ks[9], (DEC_BATCH, CONV_K - 1, SSD_CONV_DIM), 1.0),
    }
    d.update(_mamba_params(ks[10], "l0_"))
    d.update(_gmlp_params(ks[11], "l1_"))
    d.update(_mlstm_params(ks[12], "l2_"))
    d.update(_mamba_params(ks[13], "l3_"))
    d["final_norm_w"] = _gain(ks[14], D_MODEL)
    return d


def reference(x_prompt, x_sample,
              state_l0_ssm, state_l0_conv,
              state_l2_C, state_l2_n, state_l2_m, state_l2_conv,
              state_l3_ssm, state_l3_conv,
              l0_norm_w, l0_in_proj, l0_conv_w, l0_conv_b, l0_dt_bias, l0_A_log, l0_D_skip, l0_gnorm_w, l0_out_proj,
              l1_norm_w, l1_in_proj, l1_v_ln_w, l1_v_ln_b, l1_spatial_w, l1_spatial_b, l1_out_proj,
              l2_norm_w, l2_in_proj, l2_conv_w, l2_conv_b, l2_w_q, l2_w_k, l2_w_v, l2_w_o, l2_b_o,
              l2_w_if, l2_b_if, l2_mh_norm_w, l2_skip, l2_out_proj,
              l3_norm_w, l3_in_proj, l3_conv_w, l3_conv_b, l3_dt_bias, l3_A_log, l3_D_skip, l3_gnorm_w, l3_out_proj,
              final_norm_w):
    Bp = x_prompt.shape[0]
    layer_params = [
        (l0_norm_w, l0_in_proj, l0_conv_w, l0_conv_b, l0_dt_bias, l0_A_log, l0_D_skip, l0_gnorm_w, l0_out_proj),
        (l1_norm_w, l1_in_proj, l1_v_ln_w, l1_v_ln_b, l1_spatial_w, l1_spatial_b, l1_out_proj),
        (l2_norm_w, l2_in_proj, l2_conv_w, l2_conv_b, l2_w_q, l2_w_k, l2_w_v, l2_w_o, l2_b_o,
         l2_w_if, l2_b_if, l2_mh_norm_w, l2_skip, l2_out_proj),
        (l3_norm_w, l3_in_proj, l3_conv_w, l3_conv_b, l3_dt_bias, l3_A_log, l3_D_skip, l3_gnorm_w, l3_out_proj),
    ]
    layer_states = [
        (state_l0_ssm, state_l0_conv),
        (),
        (state_l2_C, state_l2_n, state_l2_m, state_l2_conv),
        (state_l3_ssm, state_l3_conv),
    ]
    hp, hs = x_prompt, x_sample
    outs = []
    for i in range(DEPTH):
        kind = i % N_MIXERS
        prm = layer_params[i]
        st = layer_states[i]
        if kind == 0:
            ssm_zero = jnp.zeros((Bp, SSD_HEADS, SSD_HEAD_DIM, SSD_STATE), F32)
            conv_zero = jnp.zeros((Bp, CONV_K - 1, SSD_CONV_DIM), hp.dtype)
            hp, p_ssm, p_conv = _mamba_layer(hp, ssm_zero, conv_zero, *prm)
            hs, s_ssm, s_conv = _mamba_layer(hs, st[0], st[1], *prm)
            outs.append((p_ssm, p_conv, s_ssm, s_conv))
        elif kind == 1:
            hp, _ = _gmlp_layer(hp, *prm)
            hs, s_v = _gmlp_layer(hs, *prm)
            outs.append((s_v,))
        else:
            C_zero = jnp.zeros((Bp, ML_HEADS, ML_HEAD_DIM, ML_HEAD_DIM), F32)
            n_zero = jnp.zeros((Bp, ML_HEADS, ML_HEAD_DIM), F32)
            m_zero = jnp.zeros((Bp, ML_HEADS), F32)
            conv_zero = jnp.zeros((Bp, CONV_K - 1, ML_INNER), hp.dtype)
            hp, p_C, p_n, p_m, p_conv = _mlstm_layer(hp, C_zero, n_zero, m_zero, conv_zero, *prm)
            hs, s_C, s_n, s_m, s_conv = _mlstm_layer(hs, st[0], st[1], st[2], st[3], *prm)
            outs.append((p_C, p_n, p_m, p_conv, s_C, s_n, s_m, s_conv))
    y_prompt = _rmsnorm(hp, final_norm_w)
    y_sample = _rmsnorm(hs, final_norm_w)
    p0_ssm, p0_conv, s0_ssm, s0_conv = outs[0]
    (s1_v,) = outs[1]
    p2_C, p2_n, p2_m, p2_conv, s2_C, s2_n, s2_m, s2_conv = outs[2]
    p3_ssm, p3_conv, s3_ssm, s3_conv = outs[3]
    return (y_prompt, y_sample,
            p0_ssm, p0_conv, s0_ssm, s0_conv,
            s1_v,
            p2_C, p2_n, p2_m, p2_conv, s2_C, s2_n, s2_m, s2_conv,
            p3_ssm, p3_conv, s3_ssm, s3_conv)
```

```python
import sys
import math
from contextlib import ExitStack
import numpy as np
import concourse.bass as bass
import concourse.mybir as mybir
from concourse.bass_utils import run_bass_kernel_spmd

F32 = mybir.dt.float32
BF16 = mybir.dt.bfloat16
AF = mybir.ActivationFunctionType
ALU = mybir.AluOpType
AX = mybir.AxisListType

NCORES = 8
D = 1024
SEQ = 2048
NS = 16
NT = SEQ + NS
EPS = 1e-6
TT = [(0, 512), (512, 512), (1024, 512), (1536, 512), (2048, 16)]
NEG = -30000.0


class Op:
    __slots__ = ("eng", "fn", "deps", "dkey", "token", "hasdep", "idx", "where", "seg")


class Sched:
    ENGS = ("pe", "act", "dve", "pool", "sp")
    EPOCH = 12000

    def __init__(self, nc):
        self.nc = nc
        self.ops = []
        self.lw = {}
        self.rd = {}
        self.last_dma = {}
        self.last_on = {}
        self.psum_rd = {}
        self.seg = 0

    def add(self, eng, fn, reads=(), writes=(), dkey=None):
        op = Op()
        op.eng, op.fn, op.dkey, op.hasdep, op.token = eng, fn, dkey, False, None
        op.idx = len(self.ops)
        op.seg = self.seg
        f = sys._getframe(1)
        wl = []
        while f is not None and len(wl) < 4:
            wl.append(f.f_lineno)
            f = f.f_back
        op.where = wl
        deps = {}
        for k in reads:
            w = self.lw.get(k)
            if w is not None:
                deps[w.idx] = w
            if k.startswith("pb") and eng in ("act", "dve"):
                bank = k.split(".")[0]
                other = self.psum_rd.get((bank, "dve" if eng == "act" else "act"))
                if other is not None:
                    deps[other.idx] = other
                self.psum_rd[(bank, eng)] = op
        for k in writes:
            w = self.lw.get(k)
            if w is not None:
                deps[w.idx] = w
            for r in self.rd.get(k, ()):
                deps[r.idx] = r
        op.deps = list(deps.values())
        for d in op.deps:
            d.hasdep = True
        for k in reads:
            lst = self.rd.setdefault(k, [])
            if dkey is None:
                lst[:] = [r for r in lst if r.dkey is not None or r.eng != eng]
            lst.append(op)
        for k in writes:
            self.lw[k] = op
            self.rd[k] = []
        self.ops.append(op)
        if dkey is not None:
            self.last_dma[dkey] = op
        else:
            self.last_on[eng] = op
        return op

    def barrier(self):
        prev = list(self.last_on.values()) + list(self.last_dma.values())
        for e in self.ENGS:
            op = self.add(e, None)
            op.deps = list(prev)
            for d in prev:
                d.hasdep = True
        self.lw.clear()
        self.rd.clear()
        self.seg += 1

    def emit(self, stack):
        nc = self.nc
        cnt = {}
        sems = {}

        def sem_for(key):
            if key not in sems:
                sems[key] = stack.enter_context(nc.semaphore("s%d" % len(sems)))
            return sems[key]

        slot_of = {}
        cur_seg = -1
        for op in self.ops:
            if op.seg != cur_seg:
                cur_seg = op.seg
                slot_of = {}
            if op.dkey is not None:
                if op.dkey not in slot_of:
                    slot_of[op.dkey] = len(slot_of)
                k = ("dma", slot_of[op.dkey])
                cnt[k] = cnt.get(k, 0) + 16
                op.token = (k, cnt[k])
            elif op.hasdep:
                c = cnt.get(op.eng, 0) + 1
                cnt[op.eng] = c
                ep = (c - 1) // self.EPOCH
                op.token = ((op.eng, ep), c - ep * self.EPOCH)
        for op in self.ops:
            if op.token is not None:
                sem_for(op.token[0])
        final = [v.token for k, v in self.last_dma.items()]
        block = stack.enter_context(nc.Block())
        handles = {"pe": nc.tensor, "act": nc.scalar, "dve": nc.vector, "pool": nc.gpsimd, "sp": nc.sync}
        deco = {"pe": block.tensor, "act": block.scalar, "dve": block.vector, "pool": block.gpsimd, "sp": block.sync}
        for eng in self.ENGS:
            myops = [o for o in self.ops if o.eng == eng]

            def body(e, myops=myops, eng=eng):
                waited = {}
                for op in myops:
                    need = {}
                    for d in op.deps:
                        if d.token is None:
                            continue
                        if d.dkey is None and d.eng == eng == "pe":
                            continue
                        sk, val = d.token
                        if val > need.get(sk, 0):
                            need[sk] = val
                    for sk, val in need.items():
                        if waited.get(sk, 0) >= val:
                            continue
                        waited[sk] = val
                        e.wait_ge(sems[sk], val)
                    if op.fn is None:
                        if op.token is not None:
                            e.nop().then_inc(sems[op.token[0]], 1)
                        continue
                    try:
                        inst = op.fn(e)
                    except Exception:
                        print('EMIT FAILED for op added at lines', op.where, 'eng', op.eng)
                        raise
                    if op.token is not None:
                        inst.then_inc(sems[op.token[0]], 16 if op.dkey is not None else 1)
                if eng == "sp":
                    for sk, val in final:
                        e.wait_ge(sems[sk], val)

            deco[eng](body)
        self.nsems = len(sems)


class Prog:
    def __init__(self, cfg):
        self.cfg = cfg
        self.nc = bass.Bass("TRN2", target_bir_lowering=False)
        self.S = Sched(self.nc)
        self.dram = {}
        self.uid = 0

    def din(self, name, shape):
        t = self.nc.dram_tensor(name, list(shape), F32, kind="ExternalInput")
        self.dram[name] = t
        return t.ap()

    def dout(self, name, shape):
        t = self.nc.dram_tensor(name, list(shape), F32, kind="ExternalOutput")
        self.dram[name] = t
        return t.ap()

    def sb(self, stack, name, shape, dt):
        self.uid += 1
        return stack.enter_context(self.nc.sbuf_tensor("%s_%d" % (name, self.uid), list(shape), dt))

    def key(self, base):
        self.uid += 1
        return "%s#%d" % (base, self.uid)

    def op(self, eng, fn, r=(), w=()):
        return self.S.add(eng, fn, r, w)

    def dma(self, q, out, in_, r=(), w=(), dkey=None, nonc=False):
        if nonc:
            fn = lambda e: e.dma_start(out=out, in_=in_, allow_slow_non_contiguous=True)
        else:
            fn = lambda e: e.dma_start(out=out, in_=in_)
        return self.S.add(q, fn, r, w, dkey=dkey)

    def mm(self, out, lhsT, rhs, start, stop, r=(), w=()):
        return self.S.add("pe", lambda e: e.matmul(out, lhsT=lhsT, rhs=rhs, start=start, stop=stop), r, w)

    def tr(self, out, in_, ident, r=(), w=()):
        return self.S.add("pe", lambda e: e.transpose(out, in_, ident), r, w)

    def act(self, out, in_, func, r=(), w=(), bias=None, scale=None, accum=None):
        kw = {}
        if bias is not None:
            kw["bias"] = bias
        if scale is not None:
            kw["scale"] = scale
        if accum is not None:
            kw["accum_out"] = accum
        return self.S.add("act", lambda e: e.activation(out=out, in_=in_, func=func, **kw), r, w)

    def tt(self, eng, out, in0, in1, op, r=(), w=()):
        return self.S.add(eng, lambda e: e.tensor_tensor(out=out, in0=in0, in1=in1, op=op), r, w)

    def ts(self, eng, out, in0, s1, s2, op0, op1=None, r=(), w=()):
        if op1 is None:
            return self.S.add(eng, lambda e: e.tensor_scalar(out=out, in0=in0, scalar1=s1, scalar2=None, op0=op0), r, w)
        return self.S.add(eng, lambda e: e.tensor_scalar(out=out, in0=in0, scalar1=s1, scalar2=s2, op0=op0, op1=op1), r, w)

    def stt(self, out, in0, scalar, in1, op0, op1, r=(), w=()):
        return self.S.add("dve", lambda e: e.scalar_tensor_tensor(out=out, in0=in0, scalar=scalar, in1=in1, op0=op0, op1=op1), r, w)

    def copy(self, eng, out, in_, r=(), w=()):
        if eng == "act":
            return self.S.add("act", lambda e: e.activation(out=out, in_=in_, func=AF.Copy), r, w)
        return self.S.add(eng, lambda e: e.tensor_copy(out=out, in_=in_), r, w)

    def recip(self, out, in_, r=(), w=()):
        return self.S.add("dve", lambda e: e.reciprocal(out=out, in_=in_), r, w)

    def memset(self, eng, ap, val, r=(), w=()):
        return self.S.add(eng, lambda e: e.memset(ap, val), r, w)


def bc(ap, shape):
    return ap.to_broadcast(list(shape))


def build(cfg):
    P = Prog(cfg)
    nc = P.nc
    layers = cfg.get("layers", [0, 1, 2, 3])
    final_norm = cfg.get("final_norm", True)

    x_p = P.din("x_prompt", [SEQ, D])
    x_s = P.din("x_sample", [NS, D])
    consts_d = P.din("consts", [128, 7 * 128])
    W = {}
    wshapes = {
        "final_norm_w": [D],
        "l1_norm_w": [D], "l1_in_proj": [D, 6144], "l1_v_ln_w": [2048], "l1_v_ln_b": [2048],
        "l1_spatial_w": [8, 128, 128], "l1_spatial_b": [8, 128], "l1_out_proj": [2048, D],
    }
    for li_ in (0, 3):
        p_ = "l%d_" % li_
        wshapes.update({p_ + "norm_w": [D], p_ + "in_proj": [D, 6176], p_ + "conv_w": [4, 4096], p_ + "conv_b": [4096],
                        p_ + "dt_bias": [32], p_ + "A_log": [32], p_ + "D_skip": [32], p_ + "gnorm_w": [2048], p_ + "out_proj": [2048, D]})
    for k, s in wshapes.items():
        if k == "final_norm_w" or int(k[1]) in layers:
            W[k] = P.din(k, s)
    DR = {}
    if 2 in layers:
        for k_, s_ in {"l2_norm_w": [D], "l2_in_proj": [D, 4096], "l2_conv_w": [4, 2048], "l2_conv_b": [2048], "l2_w_q": [512, 4, 4],
                       "l2_w_k": [512, 4, 4], "l2_w_v": [512, 4, 4], "l2_w_o": [512, 4, 4], "l2_b_o": [2048], "l2_w_if": [6144, 8],
                       "l2_b_if": [8], "l2_mh_norm_w": [2048], "l2_skip": [2048], "l2_out_proj": [2048, D]}.items():
            W[k_] = P.din(k_, s_)
        for k_, s_ in {"state_l2_C": [NS, 4, 512, 512], "state_l2_n": [NS, 4, 512], "state_l2_m": [NS, 4], "state_l2_conv": [NS, 3, 2048]}.items():
            DR[k_] = P.din(k_, s_)
        for k_, s_ in {"p2_C": [4, 512, 512], "p2_n": [4, 512], "p2_m": [4], "p2_conv": [3, 2048],
                       "s2_C": [NS, 4, 512, 512], "s2_n": [NS, 4, 512], "s2_m": [NS, 4], "s2_conv": [NS, 3, 2048]}.items():
            DR[k_] = P.dout(k_, s_)
    consts2_d = P.din("consts2", [32, 2048])
    for li_ in (0, 3):
        if li_ in layers:
            DR["state_l%d_ssm" % li_] = P.din("state_l%d_ssm" % li_, [NS, 32, 64, 128])
            DR["state_l%d_conv" % li_] = P.din("state_l%d_conv" % li_, [NS, 3, 4096])
            DR["p%d_ssm" % li_] = P.dout("p%d_ssm" % li_, [32, 64, 128])
            DR["p%d_conv" % li_] = P.dout("p%d_conv" % li_, [3, 4096])
            DR["s%d_ssm" % li_] = P.dout("s%d_ssm" % li_, [NS, 32, 64, 128])
            DR["s%d_conv" % li_] = P.dout("s%d_conv" % li_, [NS, 3, 4096])
    y_p = P.dout("y_prompt", [SEQ, D])
    y_s = P.dout("y_sample", [NS, D])
    s1_v = P.dout("s1_v", [NS, 2048]) if 1 in layers else None

    root = ExitStack()
    with root:
        G = root
        xres = P.sb(G, "xres", [128, 8, NT], F32)
        xn = P.sb(G, "xn", [128, 8, NT], BF16)
        cst = P.sb(G, "cst", [128, 7 * 128], F32)
        ident_b = P.sb(G, "ident_b", [128, 128], BF16)
        ones_b = P.sb(G, "ones_b", [128, 128], BF16)
        nw = P.sb(G, "nw", [128, 5, 8], F32)
        rs_s = P.sb(G, "rs_s", [128, 512], F32)
        rs_r = P.sb(G, "rs_r", [128, 512], F32)
        sq = [P.sb(G, "sq%d" % i, [128, 512], BF16) for i in range(2)]
        pb = [G.enter_context(nc.psum_tensor("pb%d" % i, [128, 512], F32)) for i in range(8)]
        ident_f = cst[:, 0:128]

        def xk(kc, ti):
            return "xres.%d.%d" % (kc, ti)

        def xnk(ti):
            return "xn.%d" % ti

        P.dma("sp", cst[:, :], consts_d[:, :], w=["cst"], dkey="cst")
        P.copy("dve", ident_b[:, :], cst[:, 0:128], r=["cst"], w=["ident_b"])
        P.copy("dve", ones_b[:, :], cst[:, 128:256], r=["cst"], w=["ones_b"])
        nwn = {0: "l0_norm_w", 1: "l1_norm_w", 2: "l2_norm_w", 3: "l3_norm_w", 4: "final_norm_w"}
        for li in list(layers) + [4]:
            if nwn[li] in W:
                P.dma("sp", nw[:, li, :], W[nwn[li]].rearrange("(k p) -> p k", p=128), w=["nw%d" % li],
                      dkey="nw%d" % li, nonc=True)

        with ExitStack() as L:
            xin = [P.sb(L, "xin%d" % i, [128, 4, D], F32) for i in range(2)]
            xsin = P.sb(L, "xsin", [NS, D], F32)
            for ti in range(4):
                buf = xin[ti % 2]
                bk = "xin%d" % (ti % 2)
                P.dma("sp", buf[:, :, :], x_p[ti * 512:(ti + 1) * 512, :].rearrange("(c p) d -> p c d", p=128),
                      w=[bk], dkey=bk)
                for kc in range(8):
                    ps = pb[kc % 8]
                    pk = "pb%d" % (kc % 8)
                    for c in range(4):
                        P.tr(ps[:, c * 128:(c + 1) * 128], buf[:, c, kc * 128:(kc + 1) * 128], ident_f,
                             r=[bk, "cst"], w=[pk])
                    P.copy("act" if kc % 2 else "dve", xres[:, kc, ti * 512:(ti + 1) * 512], ps[:, :], r=[pk], w=[xk(kc, ti)])
            P.dma("sp", xsin[:, :], x_s[:, :], w=["xsin"], dkey="xsin")
            for kc in range(8):
                P.tr(pb[0][:, kc * 16:(kc + 1) * 16], xsin[:, kc * 128:(kc + 1) * 128], ident_f[0:NS, 0:NS],
                     r=["xsin", "cst"], w=["pb0"])
            P.copy("dve", xres[:, :, SEQ:NT], pb[0][:, 0:128].rearrange("p (k t) -> p k t", k=8),
                   r=["pb0"], w=[xk(kc, 4) for kc in range(8)])
            P.S.barrier()

        def emit_norm(li):
            for ti, (t0, tn) in enumerate(TT):
                for kc in range(8):
                    P.act(sq[kc % 2][:, :tn], xres[:, kc, t0:t0 + tn], AF.Square, r=[xk(kc, ti)], w=["sq%d" % (kc % 2)])
                    P.mm(pb[7][:, :tn], ones_b[:, :], sq[kc % 2][:, :tn], kc == 0, kc == 7,
                         r=["sq%d" % (kc % 2), "ones_b"], w=["pb7"])
                P.act(rs_s[:, :tn], pb[7][:, :tn], AF.Sqrt, r=["pb7", "epsc"], w=["rs_s"], bias=epsc[:, 0:1], scale=1.0 / D)
                P.recip(rs_r[:, :tn], rs_s[:, :tn], r=["rs_s"], w=["rs_r"])
                for kc in range(8):
                    P.stt(xn[:, kc, t0:t0 + tn], xres[:, kc, t0:t0 + tn], nw[:, li, kc:kc + 1], rs_r[:, :tn],
                          ALU.mult, ALU.mult, r=[xk(kc, ti), "rs_r", "nw%d" % li], w=[xnk(ti)])

        epsc = P.sb(G, "epsc", [128, 1], F32)
        P.memset("dve", epsc[:, :], EPS, w=["epsc"])

        def layer_gmlp():
            li = 1
            emit_norm(li)
            Win = W["l1_in_proj"].rearrange("(k p) j -> p k j", p=128)
            Wout = W["l1_out_proj"].rearrange("(k p) j -> p k j", p=128)
            with ExitStack() as L:
                vtok = P.sb(L, "vtok", [128, 9, 2048], BF16)
                wv = [P.sb(L, "wv%d" % i, [128, 8, 512], BF16) for i in range(2)]
                wuz = wv
                wo = [P.sb(L, "wo%d" % i, [128, 2, D], BF16) for i in range(2)]
                stats = P.sb(L, "stats", [128, 9, 4, 6], F32)
                mv = P.sb(L, "mv", [128, 9, 2], F32)
                rstd = P.sb(L, "rstd", [128, 9], F32)
                nmr = P.sb(L, "nmr", [128, 9], F32)
                wsT = P.sb(L, "wsT", [128, 8, 128], BF16)
                lnw = P.sb(L, "lnw", [128, 16], F32)
                lnb = P.sb(L, "lnb", [128, 16], F32)
                Ec = P.sb(L, "Ec", [128, 2, 128], F32)
                rsb = P.sb(L, "rsb", [128, 8, 128], F32)
                sbb = P.sb(L, "sbb", [128, 8, 128], F32)
                lnr = P.sb(L, "lnr", [NS, 2048], F32)
                ws00 = P.sb(L, "ws00", [NS, 8], F32)
                sb00 = P.sb(L, "sb00", [NS, 8], F32)
                vns = P.sb(L, "vns", [NS, 2048], F32)
                mxsT = P.sb(L, "mxsT", [128, 16, NS], F32)
                sz = [P.sb(L, "sz%d" % i, [128, 512], BF16) for i in range(2)]
                mx = [P.sb(L, "mx%d" % i, [128, 512], F32) for i in range(2)]
                gt = [P.sb(L, "gt%d" % i, [128, 2, 512], BF16) for i in range(2)]
                onesw = P.sb(L, "onesw", [128, 128], BF16)

                P.dma("sp", lnw[:, :], W["l1_v_ln_w"].rearrange("(k p) -> p k", p=128), w=["lnw"], dkey="lnw", nonc=True)
                P.dma("sp", lnb[:, :], W["l1_v_ln_b"].rearrange("(k p) -> p k", p=128), w=["lnb"], dkey="lnb", nonc=True)
                P.dma("sp", ws00[:, :], bc(W["l1_spatial_w"][:, 0:1, 0:1].rearrange("g a b -> (a b) g"), [NS, 8]),
                      w=["ws00"], dkey="ws00", nonc=True)
                P.dma("sp", sb00[:, :], bc(W["l1_spatial_b"][:, 0:1].rearrange("g a -> a g"), [NS, 8]),
                      w=["sb00"], dkey="sb00", nonc=True)
                for h2 in range(2):
                    P.dma("sp", mx[h2][:, :].rearrange("p (g s) -> p g s", g=4),
                          W["l1_spatial_w"][h2 * 4:(h2 + 1) * 4].rearrange("g t s -> t g s"), w=["mx%d" % h2], dkey="wsl%d" % h2)
                P.dma("sp", sbb[:, :, :], bc(W["l1_spatial_b"].rearrange("(o g) t -> o g t", o=1), [128, 8, 128]),
                      w=["sbb"], dkey="sbb")
                causal = cst[:, 256:384]
                for g in range(8):
                    ps = pb[g % 2]
                    pk = "pb%d" % (g % 2)
                    P.tr(ps[:, 0:128], mx[g // 4][:, (g % 4) * 128:(g % 4 + 1) * 128], ident_f, r=["mx%d" % (g // 4), "cst"], w=[pk])
                    P.tt("dve", wsT[:, g, :], ps[:, 0:128], causal, ALU.mult, r=[pk, "cst"], w=["wsT"])
                P.copy("dve", onesw[:, :], cst[:, 128:256], r=["cst"], w=["onesw"])
                for g2 in range(2):
                    P.mm(pb[2 + g2][:, :], onesw[:, :], wsT[:, g2 * 4:(g2 + 1) * 4, :], True, True,
                         r=["onesw", "wsT"], w=["pb%d" % (2 + g2)])
                    P.copy("dve", rsb[:, g2 * 4:(g2 + 1) * 4, :], pb[2 + g2][:, :].rearrange("p (g t) -> p g t", g=4),
                           r=["pb%d" % (2 + g2)], w=["rsb"])

                wq = 0
                stg = cfg.get('gm_stage', 9)
                for half in range(cfg.get('gm_halves', 2)):
                    chunks = list(range(8 * half, 8 * half + 8))
                    ttiles = [2 * half, 2 * half + 1] + ([4] if half == 1 else [])
                    nslot = 8 + (1 if half == 1 else 0)
                    bank = 0
                    for ct in range(cfg.get('v_nct', 4) if stg >= 1 else 0):
                        wb = wv[ct % 2]
                        wk = "wv%d" % (ct % 2)
                        P.dma("pool", wb[:, :, :], Win[:, :, 2048 + ct * 512: 2048 + (ct + 1) * 512], w=[wk + "u", wk + "z"], dkey=wk + "u")
                        for sl in range(cfg.get("v_nsl", nslot)):
                            if sl < 8:
                                c = chunks[sl]
                                t0, M, ti = c * 128, 128, c // 4
                            else:
                                t0, M, ti = SEQ, NS, 4
                            ps = pb[bank % 7]
                            pk = "pb%d" % (bank % 7)
                            bank += 1
                            for kc in range(8):
                                P.mm(ps[:M, :], xn[:, kc, t0:t0 + M], wb[:, kc, :], kc == 0, kc == 7,
                                     r=[xnk(ti), wk + "u", wk + "z"], w=[pk])
                            if cfg.get("v_statcopy"):
                                P.copy("dve", mx[0][:M, :], ps[:M, :], r=[pk], w=["mx0"])
                            elif not cfg.get("v_nostats"):
                                P.S.add("dve", (lambda e, o=stats[:M, sl, ct, :], i=ps[:M, :]: e.bn_stats(out=o, in_=i)),
                                        [pk], ["stats.%d" % sl])
                            if not cfg.get("v_nocopy"):
                                P.copy(cfg.get("v_copyeng", "act"), vtok[:M, sl, ct * 512:(ct + 1) * 512], ps[:M, :], r=[pk] + (["stats.%d" % sl] if cfg.get("v_serial") else []), w=["vtok.%d" % sl])
                    for sl in range(nslot if stg >= 2 else 0):
                        M = 128 if sl < 8 else NS
                        P.S.add("dve", (lambda e, o=mv[:M, sl, :], i=stats[:M, sl, :, :]: e.bn_aggr(out=o, in_=i)),
                                ["stats.%d" % sl], ["mv.%d" % sl])
                        P.act(rstd[:M, sl:sl + 1], mv[:M, sl, 1:2], AF.Sqrt, r=["mv.%d" % sl, "epsc"], w=["rstd.%d" % sl],
                              bias=epsc[:M, 0:1], scale=1.0)
                        P.recip(rstd[:M, sl:sl + 1], rstd[:M, sl:sl + 1], r=["rstd.%d" % sl], w=["rstd.%d" % sl])
                        P.stt(nmr[:M, sl:sl + 1], mv[:M, sl, 0:1], -1.0, rstd[:M, sl:sl + 1], ALU.mult, ALU.mult,
                              r=["mv.%d" % sl, "rstd.%d" % sl], w=["nmr.%d" % sl])
                        if sl < 8:
                            P.ts("dve", vtok[:M, sl, :], vtok[:M, sl, :], rstd[:M, sl:sl + 1], nmr[:M, sl:sl + 1],
                                 ALU.mult, ALU.add, r=["vtok.%d" % sl, "rstd.%d" % sl, "nmr.%d" % sl], w=["vtok.%d" % sl])
                        else:
                            P.ts("dve", vns[:, :], vtok[:M, sl, :], rstd[:M, sl:sl + 1], nmr[:M, sl:sl + 1],
                                 ALU.mult, ALU.add, r=["vtok.%d" % sl, "rstd.%d" % sl, "nmr.%d" % sl], w=["vns"])
                            P.dma("sp", lnr[:, :], bc(W["l1_v_ln_w"].rearrange("(o n) -> o n", o=1), [NS, 2048]), w=["lnr"], dkey="lnr")
                            P.tt("dve", vns[:, :], vns[:, :], lnr[:, :], ALU.mult, r=["vns", "lnr"], w=["vns"])
                            P.dma("sp", lnr[:, :], bc(W["l1_v_ln_b"].rearrange("(o n) -> o n", o=1), [NS, 2048]), w=["lnr"], dkey="lnr")
                            P.tt("dve", vns[:, :], vns[:, :], lnr[:, :], ALU.add, r=["vns", "lnr"], w=["vns"])
                            P.dma("sp", s1_v[:, :], vns[:, :], r=["vns"], dkey="o_s1v")
                            mxs = lnr
                            P.tt("dve", mxs[:, :].rearrange("p (g d) -> p g d", g=8), vns[:, :].rearrange("p (g d) -> p g d", g=8),
                                 bc(ws00[:, :].rearrange("p (g o) -> p g o", o=1), [NS, 8, 256]), ALU.mult,
                                 r=["vns", "ws00"], w=["lnr"])
                            P.tt("dve", mxs[:, :].rearrange("p (g d) -> p g d", g=8), mxs[:, :].rearrange("p (g d) -> p g d", g=8),
                                 bc(sb00[:, :].rearrange("p (g o) -> p g o", o=1), [NS, 8, 256]), ALU.add,
                                 r=["lnr", "sb00"], w=["lnr"])
                            for jt in range(16):
                                P.tr(pb[7][:, jt * NS:(jt + 1) * NS], mxs[:, jt * 128:(jt + 1) * 128], ident_f[0:NS, 0:NS],
                                     r=["lnr", "cst"], w=["pb7"])
                            P.copy("dve", mxsT[:, :, :], pb[7][:, 0:16 * NS].rearrange("p (j t) -> p j t", j=16),
                                   r=["pb7"], w=["mxsT"])
                    def load_g(g_, slot_):
                        wb_, wk_ = wuz[slot_], "wv%d" % slot_
                        ob_, ok_ = wo[slot_], "wo%d" % slot_
                        P.dma("pool", wb_[:, :, 0:256], Win[:, :, g_ * 256:(g_ + 1) * 256], w=[wk_ + "u"], dkey=wk_ + "u")
                        P.dma("pool", wb_[:, :, 256:512], Win[:, :, 4096 + g_ * 256: 4096 + (g_ + 1) * 256], w=[wk_ + "z"], dkey=wk_ + "z")
                        P.dma("pool", ob_[:, :, :], Wout[:, 2 * g_:2 * g_ + 2, :], w=[ok_], dkey=ok_)
                    if stg >= 3:
                        load_g(0, wq % 2)
                    for g in range(8 if stg >= 3 else 0):
                        slot = wq % 2
                        wq += 1
                        wb, wk = wuz[slot], "wv%d" % slot
                        ob, ok = wo[slot], "wo%d" % slot
                        if g < 7:
                            load_g(g + 1, wq % 2)
                        for ti in ttiles:
                            t0, tn = TT[ti]
                            gb = gt[ti % 2]
                            gk = "gt%d" % (ti % 2)
                            for jl in range(2):
                                jt = 2 * g + jl
                                o = 3 * jl
                                pu, pz, pm = pb[o], pb[o + 1], pb[o + 2]
                                ku, kz, km = "pb%d" % o, "pb%d" % (o + 1), "pb%d" % (o + 2)
                                for kc in range(8):
                                    P.mm(pu[:, :tn], wb[:, kc, jl * 128:(jl + 1) * 128], xn[:, kc, t0:t0 + tn], kc == 0, kc == 7,
                                         r=[wk + "u", xnk(ti)], w=[ku])
                                for kc in range(8):
                                    P.mm(pz[:, :tn], wb[:, kc, 256 + jl * 128:256 + (jl + 1) * 128], xn[:, kc, t0:t0 + tn],
                                         kc == 0, kc == 7, r=[wk + "z", xnk(ti)], w=[kz])
                                s_ = sz[jl]
                                sk_ = "sz%d" % jl
                                m_ = mx[jl]
                                mk_ = "mx%d" % jl
                                P.act(s_[:, :tn], pz[:, :tn], AF.Silu, r=[kz], w=[sk_])
                                if ti < 4:
                                    for c4 in range(4):
                                        c = ti * 4 + c4
                                        sl = c - 8 * half
                                        P.mm(pm[:, c4 * 128:(c4 + 1) * 128], vtok[:, sl, jt * 128:(jt + 1) * 128], wsT[:, g, :],
                                             True, True, r=["vtok.%d" % sl, "wsT"], w=[km])
                                    P.stt(Ec[:, jl, :], rsb[:, g, :], lnb[:, jt:jt + 1], sbb[:, g, :], ALU.mult, ALU.add,
                                          r=["rsb", "lnb", "sbb"], w=["Ec%d" % jl])
                                    P.stt(m_[:, :].rearrange("p (c t) -> p c t", c=4), pm[:, :].rearrange("p (c t) -> p c t", c=4),
                                          lnw[:, jt:jt + 1], bc(Ec[:, jl:jl + 1, :], [128, 4, 128]), ALU.mult, ALU.add,
                                          r=[km, "lnw", "Ec%d" % jl], w=[mk_])
                                    min_ = m_[:, :tn]
                                    rk = [mk_]
                                else:
                                    min_ = mxsT[:, jt, :]
                                    rk = ["mxsT"]
                                P.tt("dve", m_[:, :tn], pu[:, :tn], min_, ALU.mult, r=[ku] + rk, w=[mk_])
                                P.tt("pool", gb[:, jl, :tn], m_[:, :tn], s_[:, :tn], ALU.mult, r=[mk_, sk_], w=[gk])
                            for dt_ in range(8):
                                po = pb[6 + dt_ % 2]
                                pk = "pb%d" % (6 + dt_ % 2)
                                for jl in range(2):
                                    P.mm(po[:, :tn], ob[:, jl, dt_ * 128:(dt_ + 1) * 128], gb[:, jl, :tn], jl == 0, jl == 1,
                                         r=[ok, gk], w=[pk])
                                P.tt("dve", xres[:, dt_, t0:t0 + tn], xres[:, dt_, t0:t0 + tn], po[:, :tn], ALU.add,
                                     r=[pk, xk(dt_, ti)], w=[xk(dt_, ti)])
                P.S.barrier()

        def layer_ssd(li):
            p_ = "l%d_" % li
            emit_norm(li)
            Win = W[p_ + "in_proj"].rearrange("(k p) j -> p k j", p=128)
            Wout = W[p_ + "out_proj"].rearrange("(k p) j -> p k j", p=128)
            st_ssm = DR["state_l%d_ssm" % li]
            st_conv = DR["state_l%d_conv" % li]
            o_pssm, o_pconv, o_sssm, o_sconv = DR["p%d_ssm" % li], DR["p%d_conv" % li], DR["s%d_ssm" % li], DR["s%d_conv" % li]
            triu_f, sgt_f, ltm_f = cst[:, 256:384], cst[:, 384:512], cst[:, 512:640]
            ones_f = cst[:, 128:256]
            with ExitStack() as L:
                cw = P.sb(L, "cw", [128, 32, 5], F32)
                dtb = P.sb(L, "dtb", [32, 4], F32)
                Dcol = P.sb(L, "Dcol", [128, 16], F32)
                gnw = P.sb(L, "gnw", [128, 16], F32)
                wdt = P.sb(L, "wdt", [128, 8, 32], BF16)
                ddt = P.sb(L, "ddt", [128, 17, 64], F32)
                ecum = P.sb(L, "ecum", [128, 16, 32], F32)
                wend = P.sb(L, "wend", [128, 16, 32], F32)
                eL = P.sb(L, "eL", [128, 16, 32], F32)
                dtmp = P.sb(L, "dtmp", [32, 2, 512], F32)
                dsT = P.sb(L, "dsT", [32, 2, NS], F32)
                csb = P.sb(L, "csb", [128, 32], F32)
                negI = P.sb(L, "negI", [128, 128], F32)
                wblk = [P.sb(L, "wblk%d" % i, [128, 8, 768], BF16) for i in range(2)]
                wo = [P.sb(L, "wo0", [128, 2, D], BF16)] * 2
                dg = P.sb(L, "dg", [128, 16, 128], BF16)
                dgD = P.sb(L, "dgD", [128, 2, 128], BF16)
                cwg = P.sb(L, "cwg", [128, 4, 5], F32)
                xr = [[P.sb(L, "xr%d_0" % j, [128, 515], BF16)] * 2 for j in range(4)]
                xbc = [P.sb(L, "xbc%d" % i, [128, 4, 512], BF16) for i in range(2)]
                ctail = P.sb(L, "ctail", [128, 4, 3], F32)
                xBt = P.sb(L, "xBt", [128, 384], BF16)
                szt = P.sb(L, "szt", [128, 256], BF16)
                Ub = [P.sb(L, "Ub%d" % i, [128, 128], F32) for i in range(2)]
                dec = P.sb(L, "dec", [128, 512], BF16)
                mixT = P.sb(L, "mixT", [128, 4, 128], BF16)
                t1 = P.sb(L, "t1", [128, 256], F32)
                t2 = P.sb(L, "t2", [128, 256], F32)
                ssq = P.sb(L, "ssq", [128, 2], F32)
                yn = P.sb(L, "yn", [128, 256], BF16)
                gT = [P.sb(L, "gT%d" % i, [128, 2, 512], BF16) for i in range(2)]
                xw = P.sb(L, "xw", [128, 256], BF16)
                hT = P.sb(L, "hT", [128, 256], F32)
                hTb = P.sb(L, "hTb", [128, 256], BF16)
                htmp = P.sb(L, "htmp", [128, 256], F32)
                hout = P.sb(L, "hout", [128, 2, 128], F32)
                ctout = P.sb(L, "ctout", [3, 512], F32)
                xrs = P.sb(L, "xrs", [128, 4, NS], F32)
                cstk = P.sb(L, "cstk", [48, 512], F32)
                cs = P.sb(L, "cs", [128, 4, 48], F32)
                acc = P.sb(L, "acc", [128, 4, NS], F32)
                atmp = P.sb(L, "atmp", [128, 4, NS], F32)
                xbs = P.sb(L, "xbs", [128, 4, NS], F32)
                ncs = P.sb(L, "ncs", [128, 4, 48], F32)
                szs = P.sb(L, "szs", [128, 2, NS], F32)
                Eg = P.sb(L, "Eg", [32, 256], F32)
                dtc = P.sb(L, "dtc", [128, 2, 2, NS], F32)
                xdtc = P.sb(L, "xdtc", [128, 2, NS], F32)
                rhsB = P.sb(L, "rhsB", [128, NS, 128], BF16)
                h0t = [P.sb(L, "h0t%d" % i, [128, 4, 128], F32) for i in range(2)]
                hnt = [P.sb(L, "hnt%d" % i, [128, 4, 128], F32) for i in range(2)]
                s1 = P.sb(L, "s1", [128, 4, 128], F32)
                ys = P.sb(L, "ys", [128, 2, NS], F32)
                gsm = P.sb(L, "gsm", [128, 2, NS], F32)
                gsq = P.sb(L, "gsq", [128, 2, NS], BF16)
                gTs = P.sb(L, "gTs", [128, 2, NS], BF16)
                rsd = P.sb(L, "rsd", [128, NS], F32)
                pbT2 = pb[2].bitcast(BF16)
                pbT7 = pb[7].bitcast(BF16)

                for k in range(4):
                    P.dma("sp", cw[:, :, k], W[p_ + "conv_w"][k].rearrange("(j p) -> p j", p=128), w=["cw"], dkey="cw%d" % k, nonc=True)
                P.dma("sp", cw[:, :, 4], W[p_ + "conv_b"].rearrange("(j p) -> p j", p=128), w=["cw"], dkey="cw4", nonc=True)
                P.dma("sp", dtb[:, 0:1], W[p_ + "dt_bias"].rearrange("(h o) -> h o", o=1), w=["dtb"], dkey="dtb0", nonc=True)
                P.dma("sp", dtb[:, 1:2], W[p_ + "A_log"].rearrange("(h o) -> h o", o=1), w=["dtb"], dkey="dtb1", nonc=True)
                P.act(dtb[:, 2:3], dtb[:, 1:2], AF.Exp, r=["dtb"], w=["dtb"])
                P.ts("dve", dtb[:, 2:3], dtb[:, 2:3], -1.0, None, ALU.mult, r=["dtb"], w=["dtb"])
                P.memset("dve", dtb[:, 3:4], 1.0, w=["dtb"])
                Dsk = W[p_ + "D_skip"].rearrange("(j two o) -> two o j", two=2, o=1)
                P.dma("sp", Dcol[0:64, :], bc(Dsk[0], [64, 16]), w=["Dcol"], dkey="Dcol0", nonc=True)
                P.dma("sp", Dcol[64:128, :], bc(Dsk[1], [64, 16]), w=["Dcol"], dkey="Dcol1", nonc=True)
                P.dma("sp", gnw[:, :], W[p_ + "gnorm_w"].rearrange("(k p) -> p k", p=128), w=["gnw"], dkey="gnw", nonc=True)
                P.dma("pool", wdt[:, :, :], Win[:, :, 6144:6176], w=["wdt"], dkey="wdt")
                P.ts("dve", negI[:, :], ident_f, NEG, None, ALU.mult, r=["cst"], w=["negI"])

                for ti, (t0, tn) in enumerate(TT):
                    for kc in range(8):
                        P.mm(pb[0][:32, :tn], wdt[:, kc, :], xn[:, kc, t0:t0 + tn], kc == 0, kc == 7, r=["wdt", xnk(ti)], w=["pb0"])
                    P.act(dtmp[:, 0, :tn], pb[0][:32, :tn], AF.Exp, r=["pb0", "dtb"], w=["dtmp0"], bias=dtb[:, 0:1])
                    P.act(dtmp[:, 0, :tn], dtmp[:, 0, :tn], AF.Ln, r=["dtmp0", "dtb"], w=["dtmp0"], bias=dtb[:, 3:4])
                    P.ts("dve", dtmp[:, 1, :tn], dtmp[:, 0, :tn], dtb[:, 2:3], None, ALU.mult, r=["dtmp0", "dtb"], w=["dtmp1"])
                    if ti == 4:
                        P.copy("dve", dsT[:, :, :], dtmp[:, :, 0:NS], r=["dtmp0", "dtmp1"], w=["dsT"])
                        continue
                    for c4 in range(4):
                        c = ti * 4 + c4
                        P.tr(pb[1][:, 0:32], dtmp[:, 0, c4 * 128:(c4 + 1) * 128], ident_f[0:32, 0:32], r=["dtmp0", "cst"], w=["pb1"])
                        P.tr(pb[1][:, 32:64], dtmp[:, 1, c4 * 128:(c4 + 1) * 128], ident_f[0:32, 0:32], r=["dtmp1", "cst"], w=["pb1"])
                        P.copy("dve", ddt[:, c, :], pb[1][:, 0:64], r=["pb1"], w=["ddt.%d" % c])
                        P.mm(pb[2][:, 0:32], triu_f, ddt[:, c, 32:64], True, True, r=["cst", "ddt.%d" % c], w=["pb2"])
                        P.mm(pb[2][:, 32:64], ones_f, ddt[:, c, 32:64], True, True, r=["cst", "ddt.%d" % c], w=["pb2"])
                        P.act(ecum[:, c, :], pb[2][:, 0:32], AF.Exp, r=["pb2"], w=["ecum.%d" % c])
                        P.act(eL[:, c, :], pb[2][:, 32:64], AF.Exp, r=["pb2"], w=["eL.%d" % c])
                        P.copy("dve", csb[:, :], pb[2][:, 0:32], r=["pb2"], w=["csb"])
                        P.tt("dve", csb[:, :], pb[2][:, 32:64], csb[:, :], ALU.subtract, r=["pb2", "csb"], w=["csb"])
                        P.act(wend[:, c, :], csb[:, :], AF.Exp, r=["csb"], w=["wend.%d" % c])
                        P.tt("dve", wend[:, c, :], wend[:, c, :], ddt[:, c, 0:32], ALU.mult, r=["wend.%d" % c, "ddt.%d" % c], w=["wend.%d" % c])

                ngroups = cfg.get("ssd_groups", 8)

                def load_g(g, slot):
                    wb, wk = wblk[slot], "wblk%d" % slot
                    P.dma("pool", wb[:, :, 0:256], Win[:, :, 2048 + 256 * g: 2048 + 256 * (g + 1)], w=[wk + "x"], dkey=wk + "x")
                    P.dma("pool", wb[:, :, 256:384], Win[:, :, 4096 + 128 * g: 4096 + 128 * (g + 1)], w=[wk + "B"], dkey=wk + "B")
                    P.dma("pool", wb[:, :, 384:512], Win[:, :, 5120 + 128 * g: 5120 + 128 * (g + 1)], w=[wk + "C"], dkey=wk + "C")
                    P.dma("pool", wb[:, :, 512:768], Win[:, :, 256 * g: 256 * (g + 1)], w=[wk + "z"], dkey=wk + "z")

                load_g(0, 0)
                for g in range(ngroups):
                    slot = g % 2
                    wb, wk = wblk[slot], "wblk%d" % slot
                    wkeys = [wk + "x", wk + "x", wk + "B", wk + "C"]
                    ob, ok = wo[slot], "wo0"
                    P.dma("pool", ob[:, :, :], Wout[:, 2 * g:2 * g + 2, :], w=["wo0"], dkey="wo0")
                    if g + 1 < ngroups:
                        load_g(g + 1, 1 - slot)
                    jts = [2 * g, 2 * g + 1, 16 + g, 24 + g]
                    for j in range(4):
                        P.copy("dve", cwg[:, j, :], cw[:, jts[j], :], r=["cw"], w=["cwg"])
                        for k in range(4):
                            P.ts("dve", dg[:, j * 4 + k, :], ident_f, cw[:, jts[j], k:k + 1], None, ALU.mult, r=["cst", "cw"], w=["dg"])
                    for jl in range(2):
                        P.ts("dve", dgD[:, jl, :], ident_f, Dcol[:, 2 * g + jl:2 * g + jl + 1], None, ALU.mult, r=["cst", "Dcol"], w=["dgD"])
                    P.memset("dve", hT[:, :], 0.0, w=["hT"])
                    P.memset("dve", hTb[:, :], 0.0, w=["hTb"])
                    for j in range(4):
                        P.memset("dve", xr[j][0][:, 0:3], 0.0, w=["xr%d_0" % j])

                    for ti in range(4):
                        t0 = ti * 512
                        xb = xbc[ti % 2]
                        xbk = "xbc%d" % (ti % 2)
                        for j in range(4):
                            ps, pk = pb[j % 2], "pb%d" % (j % 2)
                            xrb, xrk = xr[j][0], "xr%d_0" % j
                            for kc in range(8):
                                P.mm(ps[:, :], wb[:, kc, j * 128:(j + 1) * 128], xn[:, kc, t0:t0 + 512], kc == 0, kc == 7,
                                     r=[wkeys[j], xnk(ti)], w=[pk])
                            P.copy("act", xrb[:, 3:515], ps[:, :], r=[pk], w=[xrk])
                            if ti == 3:
                                P.copy("dve", ctail[:, j, :], ps[:, 509:512], r=[pk], w=["ctail"])
                            pc, pck = pb[6], "pb6"
                            for k in range(4):
                                P.mm(pc[:, :], dg[:, j * 4 + k, :], xrb[:, k:k + 512], k == 0, k == 3, r=["dg", xrk], w=[pck])
                            if ti < 3:
                                P.copy("pool", xrb[:, 0:3], xrb[:, 512:515], r=[xrk], w=[xrk])
                            P.act(xb[:, j, :], pc[:, :], AF.Silu, r=[pck, "cw"], w=[xbk + ".%d" % j], bias=cw[:, jts[j], 4:5])
                        gb, gk = gT[ti % 2], "gT%d" % (ti % 2)
                        for c4 in range(4):
                            c = ti * 4 + c4
                            cs_ = slice(c4 * 128, (c4 + 1) * 128)
                            tok = slice(c * 128, (c + 1) * 128)
                            for j in range(3):
                                P.tr(pbT2[:, j * 128:(j + 1) * 128], xb[:, j, cs_], ident_b[:, :], r=[xbk + ".%d" % j, "ident_b"], w=["pb2"])
                            P.copy("dve", xBt[:, :], pbT2[:, 0:384], r=["pb2"], w=["xBt"])
                            for kc in range(8):
                                P.mm(pb[6][:, 0:256], xn[:, kc, tok], wb[:, kc, 512:768], kc == 0, kc == 7, r=[xnk(ti), wk + "z"], w=["pb6"])
                            P.act(szt[:, :], pb[6][:, 0:256], AF.Silu, r=["pb6"], w=["szt"])
                            P.mm(pb[3][:, 0:128], xb[:, 2, cs_], xb[:, 3, cs_], True, True, r=[xbk + ".2", xbk + ".3"], w=["pb3"])
                            for hh in range(4):
                                h = 4 * g + hh
                                U, Uk = Ub[hh % 2], "Ub%d" % (hh % 2)
                                P.ts("dve", U[:, :], sgt_f, ddt[:, c, 32 + h:33 + h], None, ALU.mult, r=["cst", "ddt.%d" % c], w=[Uk])
                                P.mm(pb[4][:, hh * 128:(hh + 1) * 128], U[:, :], triu_f, True, False, r=[Uk, "cst"], w=["pb4"])
                                P.mm(pb[4][:, hh * 128:(hh + 1) * 128], negI[:, :], ltm_f, False, True, r=["negI", "cst"], w=["pb4"])
                            P.act(dec[:, :], pb[4][:, :], AF.Exp, r=["pb4"], w=["dec"])
                            for hh in range(4):
                                h = 4 * g + hh
                                P.stt(mixT[:, hh, :], dec[:, hh * 128:(hh + 1) * 128], ddt[:, c, h:h + 1], pb[3][:, 0:128],
                                      ALU.mult, ALU.mult, r=["dec", "ddt.%d" % c, "pb3"], w=["mixT"])
                            for jl in range(2):
                                P.mm(pb[5][:, jl * 128:(jl + 1) * 128], xb[:, jl, cs_], dgD[:, jl, :], True, False, r=[xbk + ".%d" % jl, "dgD"], w=["pb5"])
                                for hh in (2 * jl, 2 * jl + 1):
                                    P.mm(pb[5][:, hh * 64:(hh + 1) * 64], mixT[:, hh, :], xBt[:, hh * 64:(hh + 1) * 64], False, hh == 2 * jl + 1,
                                         r=["mixT", "xBt"], w=["pb5"])
                            P.mm(pb[3][:, 128:384], xb[:, 3, cs_], hTb[:, :], True, True, r=[xbk + ".3", "hTb"], w=["pb3"])
                            P.tt("dve", t1[:, :].rearrange("p (h q) -> p h q", h=4), pb[3][:, 128:384].rearrange("p (h q) -> p h q", h=4),
                                 bc(ecum[:, c, 4 * g:4 * g + 4].rearrange("p (h o) -> p h o", o=1), [128, 4, 64]), ALU.mult,
                                 r=["pb3", "ecum.%d" % c], w=["t1"])
                            P.tt("dve", t2[:, :], pb[5][:, 0:256], t1[:, :], ALU.add, r=["pb5", "t1"], w=["t2"])
                            P.tt("pool", t2[:, :], t2[:, :], szt[:, :], ALU.mult, r=["t2", "szt"], w=["t2"])
                            P.act(t1[:, :], t2[:, :], AF.Square, r=["t2"], w=["t1", "ssq"], accum=ssq[:, 0:1])
                            P.act(ssq[:, 1:2], ssq[:, 0:1], AF.Sqrt, r=["ssq", "epsc"], w=["ssq"], bias=epsc[:, 0:1], scale=1.0 / 256)
                            P.recip(ssq[:, 1:2], ssq[:, 1:2], r=["ssq"], w=["ssq"])
                            P.ts("dve", yn[:, :], t2[:, :], ssq[:, 1:2], None, ALU.mult, r=["t2", "ssq"], w=["yn"])
                            for jl in range(2):
                                P.tr(pbT7[:, jl * 128:(jl + 1) * 128], yn[:, jl * 128:(jl + 1) * 128], ident_b[:, :], r=["yn", "ident_b"], w=["pb7"])
                            for jl in range(2):
                                P.ts("dve", gb[:, jl, cs_], pbT7[:, jl * 128:(jl + 1) * 128], gnw[:, 2 * g + jl:2 * g + jl + 1], None, ALU.mult,
                                     r=["pb7", "gnw"], w=[gk])
                            P.tt("pool", xw[:, :].rearrange("p (h q) -> p h q", h=4), xBt[:, 0:256].rearrange("p (h q) -> p h q", h=4),
                                 bc(wend[:, c, 4 * g:4 * g + 4].rearrange("p (h o) -> p h o", o=1), [128, 4, 64]), ALU.mult,
                                 r=["xBt", "wend.%d" % c], w=["xw"])
                            P.mm(pb[7][:, 256:512], xBt[:, 256:384], xw[:, :], True, True, r=["xBt", "xw"], w=["pb7"])
                            P.tt("pool", htmp[:, :].rearrange("p (h q) -> p h q", h=4), hT[:, :].rearrange("p (h q) -> p h q", h=4),
                                 bc(eL[:, c, 4 * g:4 * g + 4].rearrange("p (h o) -> p h o", o=1), [128, 4, 64]), ALU.mult,
                                 r=["hT", "eL.%d" % c], w=["htmp"])
                            P.tt("dve", hT[:, :], htmp[:, :], pb[7][:, 256:512], ALU.add, r=["htmp", "pb7"], w=["hT"])
                            P.copy("act", hTb[:, :], hT[:, :], r=["hT"], w=["hTb"])
                        for dt_ in range(8):
                            po, pk = pb[dt_ % 2], "pb%d" % (dt_ % 2)
                            for jl in range(2):
                                P.mm(po[:, :], ob[:, jl, dt_ * 128:(dt_ + 1) * 128], gb[:, jl, :], jl == 0, jl == 1, r=[ok, gk], w=[pk])
                            P.tt("dve", xres[:, dt_, t0:t0 + 512], xres[:, dt_, t0:t0 + 512], po[:, :], ALU.add, r=[pk, xk(dt_, ti)], w=[xk(dt_, ti)])
                    for jl in range(2):
                        P.tr(pb[2][:, jl * 128:(jl + 1) * 128], hT[:, jl * 128:(jl + 1) * 128], ident_f, r=["hT", "cst"], w=["pb2"])
                    P.copy("dve", hout[:, :, :], pb[2][:, 0:256].rearrange("p (j n) -> p j n", j=2), r=["pb2"], w=["hout"])
                    P.dma("sp", o_pssm[4 * g:4 * g + 4].rearrange("(j hh) q n -> (hh q) j n", j=2), hout[:, :, :], r=["hout"], dkey="o_hout")
                    for j in range(4):
                        P.tr(pb[3][0:3, j * 128:(j + 1) * 128], ctail[:, j, :], ident_f, r=["ctail", "cst"], w=["pb3"])
                    P.copy("dve", ctout[:, :], pb[3][0:3, :], r=["pb3"], w=["ctout"])
                    P.dma("sp", o_pconv[:, 256 * g:256 * (g + 1)], ctout[:, 0:256], r=["ctout"], dkey="o_ctx")
                    P.dma("sp", o_pconv[:, 2048 + 128 * g:2048 + 128 * (g + 1)], ctout[:, 256:384], r=["ctout"], dkey="o_ctB")
                    P.dma("sp", o_pconv[:, 3072 + 128 * g:3072 + 128 * (g + 1)], ctout[:, 384:512], r=["ctout"], dkey="o_ctC")

                    ti = 4
                    for j in range(4):
                        for kc in range(8):
                            P.mm(pb[0][:, j * NS:(j + 1) * NS], wb[:, kc, j * 128:(j + 1) * 128], xn[:, kc, SEQ:NT], kc == 0, kc == 7,
                                 r=[wkeys[j], xnk(4)], w=["pb0"])
                    P.copy("dve", xrs[:, :, :], pb[0][:, 0:4 * NS].rearrange("p (j b) -> p j b", j=4), r=["pb0"], w=["xrs"])
                    cranges = [(256 * g, 256), (2048 + 128 * g, 128), (3072 + 128 * g, 128)]
                    off = 0
                    for (c0, cn) in cranges:
                        P.dma("sp", cstk[:, off:off + cn], st_conv[:, :, c0:c0 + cn].rearrange("b k c -> (b k) c"), w=["cstk"], dkey="cstk%d" % off)
                        off += cn
                    for j in range(4):
                        P.tr(pb[1][:, j * 48:(j + 1) * 48], cstk[:, j * 128:(j + 1) * 128], ident_f[0:48, 0:48], r=["cstk", "cst"], w=["pb1"])
                    P.copy("dve", cs[:, :, :], pb[1][:, 0:192].rearrange("p (j q) -> p j q", j=4), r=["pb1"], w=["cs"])
                    cs4 = cs[:, :, :].rearrange("p j (b k) -> p j b k", k=3)
                    ncs4 = ncs[:, :, :].rearrange("p j (b k) -> p j b k", k=3)
                    P.tt("dve", acc[:, :, :], xrs[:, :, :], bc(cwg[:, :, 3:4], [128, 4, NS]), ALU.mult, r=["xrs", "cwg"], w=["acc"])
                    P.tt("dve", acc[:, :, :], acc[:, :, :], bc(cwg[:, :, 4:5], [128, 4, NS]), ALU.add, r=["acc", "cwg"], w=["acc"])
                    for k in range(3):
                        P.tt("dve", atmp[:, :, :], cs4[:, :, :, k], bc(cwg[:, :, k:k + 1], [128, 4, NS]), ALU.mult, r=["cs", "cwg"], w=["atmp"])
                        P.tt("dve", acc[:, :, :], acc[:, :, :], atmp[:, :, :], ALU.add, r=["acc", "atmp"], w=["acc"])
                    P.act(xbs[:, :, :], acc[:, :, :], AF.Silu, r=["acc"], w=["xbs"])
                    P.copy("dve", ncs4[:, :, :, 0:2], cs4[:, :, :, 1:3], r=["cs"], w=["ncs"])
                    P.copy("dve", ncs4[:, :, :, 2], xrs[:, :, :], r=["xrs", "ncs"], w=["ncs"])
                    for j in range(4):
                        P.tr(pb[1][0:48, j * 128:(j + 1) * 128], ncs[:, j, :], ident_f, r=["ncs", "cst"], w=["pb1"])
                    P.copy("dve", cstk[:, :], pb[1][0:48, :], r=["pb1"], w=["cstk"])
                    off = 0
                    for (c0, cn) in cranges:
                        P.dma("sp", o_sconv[:, :, c0:c0 + cn].rearrange("b k c -> (b k) c"), cstk[:, off:off + cn], r=["cstk"], dkey="o_cstk%d" % off)
                        off += cn
                    for jl in range(2):
                        for kc in range(8):
                            P.mm(pb[2][:, jl * NS:(jl + 1) * NS], wb[:, kc, 512 + jl * 128:512 + (jl + 1) * 128], xn[:, kc, SEQ:NT], kc == 0, kc == 7,
                                 r=[wk + "z", xnk(4)], w=["pb2"])
                    P.act(szs[:, :, :], pb[2][:, 0:2 * NS].rearrange("p (j b) -> p j b", j=2), AF.Silu, r=["pb2"], w=["szs"])
                    P.dma("sp", Eg[:, :], consts2_d[:, 256 * g:256 * (g + 1)], w=["Eg"], dkey="Eg")
                    for jl in range(2):
                        for q in range(2):
                            P.mm(pb[2][:, 64 + (jl * 2 + q) * NS: 64 + (jl * 2 + q + 1) * NS], Eg[:, jl * 128:(jl + 1) * 128], dsT[:, q, :], True, True,
                                 r=["Eg", "dsT"], w=["pb2"])
                    dview = pb[2][:, 64:64 + 4 * NS].rearrange("p (j q b) -> p j q b", j=2, q=2)
                    P.copy("dve", dtc[:, :, 0, :], dview[:, :, 0, :], r=["pb2"], w=["dtc"])
                    P.act(dtc[:, :, 1, :], dview[:, :, 1, :], AF.Exp, r=["pb2"], w=["dtc"])
                    P.tt("dve", xdtc[:, :, :], xbs[:, 0:2, :], dtc[:, :, 0, :], ALU.mult, r=["xbs", "dtc"], w=["xdtc"])
                    for which, j, base in (("B", 2, 0), ("C", 3, 4)):
                        P.tt("dve", rhsB[:, :, :], bc(ident_b[:, :].rearrange("p (o n) -> p o n", o=1), [128, NS, 128]),
                             bc(xbs[:, j, :].rearrange("p (b o) -> p b o", o=1), [128, NS, 128]), ALU.mult, r=["ident_b", "xbs"], w=["rhsB"])
                        for q in range(4):
                            P.mm(pb[base + q][:, :], ones_b[:, :], rhsB[:, 4 * q:4 * q + 4, :], True, True, r=["ones_b", "rhsB"],
                                 w=["pb%d" % (base + q)])
                    for jl in range(2):
                        for q in range(4):
                            hb, hk = h0t[q % 2], "h0t%d" % (q % 2)
                            nb, nk = hnt[q % 2], "hnt%d" % (q % 2)
                            bs = slice(4 * q, 4 * q + 4)
                            bk = ["pb%d" % q]
                            ck = ["pb%d" % (4 + q)]
                            P.dma("sp", hb[:, :, :], st_ssm[4 * q:4 * q + 4, 4 * g + 2 * jl:4 * g + 2 * jl + 2].rearrange("b h q n -> (h q) b n"),
                                  w=[hk], dkey=hk)
                            P.tt("dve", s1[:, :, :], pb[q][:, :].rearrange("p (b n) -> p b n", b=4),
                                 bc(xdtc[:, jl, bs].rearrange("p (b o) -> p b o", o=1), [128, 4, 128]), ALU.mult, r=bk + ["xdtc"], w=["s1"])
                            P.tt("pool", hb[:, :, :], hb[:, :, :],
                                 bc(dtc[:, jl, 1, bs].rearrange("p (b o) -> p b o", o=1), [128, 4, 128]), ALU.mult, r=[hk, "dtc"], w=[hk])
                            P.tt("pool", nb[:, :, :], hb[:, :, :], s1[:, :, :], ALU.add, r=["s1", hk], w=[nk])
                            P.tt("dve", s1[:, :, :], nb[:, :, :], pb[4 + q][:, :].rearrange("p (b n) -> p b n", b=4), ALU.mult,
                                 r=[nk] + ck, w=["s1"])
                            P.S.add("dve", (lambda e, o=ys[:, jl, bs], i=s1[:, :, :]: e.tensor_reduce(out=o, in_=i, axis=AX.X, op=ALU.add)),
                                    ["s1"], ["ys"])
                            P.dma("sp", o_sssm[4 * q:4 * q + 4, 4 * g + 2 * jl:4 * g + 2 * jl + 2].rearrange("b h q n -> (h q) b n"), nb[:, :, :],
                                  r=[nk], dkey="o_" + nk)
                    P.tt("dve", gsm[:, :, :], xbs[:, 0:2, :], bc(Dcol[:, 2 * g:2 * g + 2].rearrange("p (j o) -> p j o", o=1), [128, 2, NS]), ALU.mult,
                         r=["xbs", "Dcol"], w=["gsm"])
                    P.tt("dve", gsm[:, :, :], gsm[:, :, :], ys[:, :, :], ALU.add, r=["gsm", "ys"], w=["gsm"])
                    P.tt("dve", gsm[:, :, :], gsm[:, :, :], szs[:, :, :], ALU.mult, r=["gsm", "szs"], w=["gsm"])
                    P.tt("dve", gsq[:, :, :], gsm[:, :, :], gsm[:, :, :], ALU.mult, r=["gsm"], w=["gsq"])
                    for jl in range(2):
                        P.mm(pb[0][:, 0:NS], ones_b[:, :], gsq[:, jl, :], jl == 0, jl == 1, r=["ones_b", "gsq"], w=["pb0"])
                    P.act(rsd[:, :], pb[0][:, 0:NS], AF.Sqrt, r=["pb0", "epsc"], w=["rsd"], bias=epsc[:, 0:1], scale=1.0 / 256)
                    P.recip(rsd[:, :], rsd[:, :], r=["rsd"], w=["rsd"])
                    for jl in range(2):
                        P.stt(gTs[:, jl, :], gsm[:, jl, :], gnw[:, 2 * g + jl:2 * g + jl + 1], rsd[:, :], ALU.mult, ALU.mult,
                              r=["gsm", "gnw", "rsd"], w=["gTs"])
                    for dt_ in range(8):
                        po, pk = pb[dt_ % 2], "pb%d" % (dt_ % 2)
                        for jl in range(2):
                            P.mm(po[:, 0:NS], ob[:, jl, dt_ * 128:(dt_ + 1) * 128], gTs[:, jl, :], jl == 0, jl == 1, r=[ok, "gTs"], w=[pk])
                        P.tt("dve", xres[:, dt_, SEQ:NT], xres[:, dt_, SEQ:NT], po[:, 0:NS], ALU.add, r=[pk, xk(dt_, 4)], w=[xk(dt_, 4)])
                P.S.barrier()

        def layer_mlstm():
            li = 2
            emit_norm(li)
            Win = W["l2_in_proj"].rearrange("(k p) j -> p k j", p=128)
            Wout = W["l2_out_proj"].rearrange("(k p) j -> p k j", p=128)
            st_C, st_n, st_m, st_conv = DR["state_l2_C"], DR["state_l2_n"], DR["state_l2_m"], DR["state_l2_conv"]
            o_pC, o_pn, o_pm, o_pconv = DR["p2_C"], DR["p2_n"], DR["p2_m"], DR["p2_conv"]
            o_sC, o_sn, o_sm, o_sconv = DR["s2_C"], DR["s2_n"], DR["s2_m"], DR["s2_conv"]
            bmask = cst[:, 640:768]
            gemask = cst[:, 768:896]
            ones_f = cst[:, 128:256]
            KS = 512.0 ** -0.5
            with ExitStack() as L:
                cw = P.sb(L, "cw", [128, 16, 5], F32)
                wq4 = P.sb(L, "wq4", [128, 16, 4, 4], F32)
                wif = P.sb(L, "wif", [128, 48, 8], BF16)
                mhw = P.sb(L, "mhw", [128, 16], F32)
                skp = P.sb(L, "skp", [128, 16], F32)
                bocol = P.sb(L, "bocol", [128, 16], F32)
                gacc = P.sb(L, "gacc", [128, 17, 8], F32)
                xcs_all = P.sb(L, "xcs_all", [128, 16, NS], F32)
                xms_all = P.sb(L, "xms_all", [128, 16, NS], F32)
                dg = P.sb(L, "dg", [128, 16, 128], BF16)
                Dm = P.sb(L, "Dm", [128, 16, 128], BF16)
                xr = [P.sb(L, "xr%d" % j, [128, 515], BF16) for j in range(4)]
                xc = P.sb(L, "xc", [128, 4, 512], BF16)
                qT = P.sb(L, "qT", [128, 4, 512], BF16)
                khT = P.sb(L, "khT", [128, 4, 512], BF16)
                szT = P.sb(L, "szT", [128, 4, 512], BF16)
                gT = P.sb(L, "gT", [128, 4, 512], BF16)
                wbuf = P.sb(L, "wbuf", [128, 8, 1024], BF16)
                wo = P.sb(L, "wo", [128, 4, D], BF16)
                ctail = P.sb(L, "ctail", [128, 3], F32)
                ctout = P.sb(L, "ctout", [3, 128], F32)
                cstk = P.sb(L, "cstk", [48, 128], F32)
                cs = P.sb(L, "cs", [128, 48], F32)
                ncs = P.sb(L, "ncs", [128, 48], F32)
                sA = P.sb(L, "sA", [128, 4, NS], F32)
                sB = P.sb(L, "sB", [128, 4, NS], BF16)
                gcol = P.sb(L, "gcol", [128, 128], F32)
                ga = P.sb(L, "ga", [64, 128], F32)
                gs1 = P.sb(L, "gs1", [64, 8], F32)
                ends4 = P.sb(L, "ends4", [4, 16, 2], F32)
                m4 = P.sb(L, "m4", [4, 2, 16], F32)
                tokq = P.sb(L, "tokq", [128, 5, 64], F32)
                vtok = P.sb(L, "vtok", [128, 512], BF16)
                kwt = P.sb(L, "kwt", [128, 512], BF16)
                otok = P.sb(L, "otok", [128, 512], BF16)
                bobc = P.sb(L, "bobc", [128, 512], F32)
                Sm = P.sb(L, "Sm", [128, 128], F32)
                Eg_ = P.sb(L, "Eg_", [128, 128], F32)
                wls = P.sb(L, "wls", [128, 128], BF16)
                wT = P.sb(L, "wT", [128, 128], BF16)
                Asb = P.sb(L, "Asb", [128, 512], F32)
                num = P.sb(L, "num", [128, 512], F32)
                sm = P.sb(L, "sm", [128, 16], F32)
                st6 = P.sb(L, "st6", [128, 8], F32)
                hn = P.sb(L, "hn", [128, 512], BF16)
                g1 = P.sb(L, "g1", [128, 128], F32)
                sxc = P.sb(L, "sxc", [128, 128], F32)
                CT = P.sb(L, "CT", [128, 4, 512], F32)
                CTb = P.sb(L, "CTb", [128, 4, 512], BF16)
                nst = P.sb(L, "nst", [128, 4], F32)
                nstb = P.sb(L, "nstb", [128, 4], BF16)
                Cout = P.sb(L, "Cout", [128, 512], F32)
                sg = P.sb(L, "sg", [NS, 64], F32)
                n0t = P.sb(L, "n0t", [NS, 512], F32)
                qtok = P.sb(L, "qtok", [NS, 512], F32)
                wid = P.sb(L, "wid", [NS, NS, 8], F32)
                wibc = P.sb(L, "wibc", [128, NS, 8], F32)
                Cnt1 = P.sb(L, "Cnt1", [128, 512], F32)
                numT = P.sb(L, "numT", [128, 4, NS], F32)
                hsT = P.sb(L, "hsT", [128, 4, NS], F32)
                hs2 = P.sb(L, "hs2", [128, 4, NS], F32)
                mrs = P.sb(L, "mrs", [128, 4, NS], F32)
                gTs = P.sb(L, "gTs", [128, 4, NS], BF16)
                pbT1 = pb[1].bitcast(BF16)
                scr = P.nc.dram_tensor("ml_scr", [64, 4], F32, kind="Internal").ap()

                for k in range(4):
                    P.dma("sp", cw[:, :, k], W["l2_conv_w"][k].rearrange("(j p) -> p j", p=128), w=["cw"], dkey="cw%d" % k, nonc=True)
                P.dma("sp", cw[:, :, 4], W["l2_conv_b"].rearrange("(j p) -> p j", p=128), w=["cw"], dkey="cw4", nonc=True)
                for pi, nm in enumerate(("w_q", "w_k", "w_v", "w_o")):
                    P.dma("sp", wq4[:, :, pi, :], W["l2_" + nm].rearrange("n j i -> (n j) i").rearrange("(t p) i -> p t i", p=128),
                          w=["wq4"], dkey="wq4%d" % pi, nonc=True)
                P.dma("pool", wif[:, :, :], W["l2_w_if"].rearrange("(t p) g -> p t g", p=128), w=["wif"], dkey="wif")
                P.dma("sp", mhw[:, :], W["l2_mh_norm_w"].rearrange("(k p) -> p k", p=128), w=["mhw"], dkey="mhw", nonc=True)
                P.dma("sp", skp[:, :], W["l2_skip"].rearrange("(k p) -> p k", p=128), w=["skp"], dkey="skp", nonc=True)
                P.dma("sp", bocol[:, :], W["l2_b_o"].rearrange("(k p) -> p k", p=128), w=["bocol"], dkey="bocol", nonc=True)
                bif2 = W["l2_b_if"].rearrange("(g o) -> g o", o=1)
                for h in range(4):
                    P.dma("sp", gs1[h * 16:(h + 1) * 16, 0:1], bc(bif2[h:h + 1, :], [16, 1]), w=["gs1"], dkey="gs1a%d" % h, nonc=True)
                    P.dma("sp", gs1[h * 16:(h + 1) * 16, 1:2], bc(bif2[4 + h:5 + h, :], [16, 1]), w=["gs1"], dkey="gs1b%d" % h, nonc=True)
                P.ts("dve", gs1[:, 2:3], gs1[:, 1:2], -1.0, None, ALU.mult, r=["gs1"], w=["gs1"])
                P.memset("dve", gs1[:, 3:4], 1.0, w=["gs1"])
                P.memset("dve", wid[:, :, :], 0.0, w=["wid"])

                def build_D(dst_idx, jt, pi):
                    P.tt("dve", Dm[:, dst_idx, :].rearrange("p (n i) -> p n i", i=4),
                         bc(wq4[:, jt, pi:pi + 1, :], [128, 32, 4]), bmask.rearrange("p (n i) -> p n i", i=4), ALU.mult,
                         r=["wq4", "cst"], w=["Dm"])

                def build_dg(dst_base, jt):
                    for k in range(4):
                        P.ts("dve", dg[:, dst_base + k, :], ident_f, cw[:, jt, k:k + 1], None, ALU.mult, r=["cst", "cw"], w=["dg"])

                def proj_conv(jt, jl, ti, wcols, wkey):
                    t0 = ti * 512
                    for kc in range(8):
                        P.mm(pb[0][:, :], wbuf[:, kc, wcols:wcols + 128], xn[:, kc, t0:t0 + 512], kc == 0, kc == 7, r=[wkey, xnk(ti)], w=["pb0"])
                    P.copy("act", xr[jl][:, 3:515], pb[0][:, :], r=["pb0"], w=["xr%d" % jl])
                    for k in range(4):
                        P.mm(pb[1][:, :], dg[:, jl * 4 + k, :], xr[jl][:, k:k + 512], k == 0, k == 3, r=["dg", "xr%d" % jl], w=["pb1"])
                    P.act(xc[:, jl, :], pb[1][:, :], AF.Silu, r=["pb1", "cw"], w=["xc.%d" % jl], bias=cw[:, jt, 4:5])

                stgA = cfg.get("ml_stage", 9)
                for jt in range(16 if stgA >= 1 else 0):
                    slot = jt % 4
                    wk = "wA%d" % slot
                    P.dma("pool", wbuf[:, :, slot * 128:(slot + 1) * 128], Win[:, :, jt * 128:(jt + 1) * 128], w=[wk], dkey=wk)
                    build_dg(0, jt)
                    for pi in range(3):
                        build_D(pi * 4, jt, pi)
                    P.memset("dve", xr[0][:, 0:3], 0.0, w=["xr0"])
                    for ti in range(4):
                        proj_conv(jt, 0, ti, slot * 128, wk)
                        if ti == 3:
                            P.copy("dve", ctail[:, :], pb[0][:, 509:512], r=["pb0"], w=["ctail"])
                        else:
                            P.copy("pool", xr[0][:, 0:3], xr[0][:, 512:515], r=["xr0"], w=["xr0"])
                        P.mm(pb[2][:, :], Dm[:, 0, :], xc[:, 0, :], True, True, r=["Dm", "xc.0"], w=["pb2"])
                        P.mm(pb[3][:, :], Dm[:, 4, :], xc[:, 0, :], True, True, r=["Dm", "xc.0"], w=["pb3"])
                        P.mm(pb[4][:, :], Dm[:, 8, :], xr[0][:, 3:515], True, True, r=["Dm", "xr0"], w=["pb4"])
                        P.copy("act", qT[:, 0, :], pb[2][:, :], r=["pb2"], w=["qT"])
                        P.copy("dve", khT[:, 0, :], pb[3][:, :], r=["pb3"], w=["khT"])
                        P.copy("act", szT[:, 0, :], pb[4][:, :], r=["pb4"], w=["szT"])
                        for c4 in range(4):
                            cs_ = slice(c4 * 128, (c4 + 1) * 128)
                            o_ = pb[5][:, c4 * 8:(c4 + 1) * 8]
                            P.mm(o_, qT[:, 0, cs_], wif[:, jt, :], True, False, r=["qT", "wif"], w=["pb5"])
                            P.mm(o_, khT[:, 0, cs_], wif[:, 16 + jt, :], False, False, r=["khT", "wif"], w=["pb5"])
                            P.mm(o_, szT[:, 0, cs_], wif[:, 32 + jt, :], False, True, r=["szT", "wif"], w=["pb5"])
                        gv = gacc[:, 4 * ti:4 * ti + 4, :]
                        pv = pb[5][:, 0:32].rearrange("p (c g) -> p c g", g=8)
                        if jt == 0:
                            P.copy("dve", gv, pv, r=["pb5"], w=["gacc.%d" % ti])
                        else:
                            P.tt("dve", gv, gv, pv, ALU.add, r=["pb5", "gacc.%d" % ti], w=["gacc.%d" % ti])
                    P.tr(pb[6][0:3, 0:128], ctail[:, :], ident_f, r=["ctail", "cst"], w=["pb6"])
                    P.copy("dve", ctout[:, :], pb[6][0:3, 0:128], r=["pb6"], w=["ctout"])
                    P.dma("sp", o_pconv[:, jt * 128:(jt + 1) * 128], ctout[:, :], r=["ctout"], dkey="o_ctout")
                    for kc in range(8):
                        P.mm(pb[6][:, 128:128 + NS], wbuf[:, kc, slot * 128:(slot + 1) * 128], xn[:, kc, SEQ:NT], kc == 0, kc == 7, r=[wk, xnk(4)], w=["pb6"])
                    P.copy("dve", xms_all[:, jt, :], pb[6][:, 128:128 + NS], r=["pb6"], w=["xms_all"])
                    P.dma("sp", cstk[:, :], st_conv[:, :, jt * 128:(jt + 1) * 128].rearrange("b k c -> (b k) c"), w=["cstk"], dkey="cstk")
                    P.tr(pb[6][:, 256:304], cstk[:, :], ident_f[0:48, 0:48], r=["cstk", "cst"], w=["pb6"])
                    P.copy("dve", cs[:, :], pb[6][:, 256:304], r=["pb6"], w=["cs"])
                    cs3 = cs[:, :].rearrange("p (b k) -> p b k", k=3)
                    ncs3 = ncs[:, :].rearrange("p (b k) -> p b k", k=3)
                    a0 = sA[:, 0, :]
                    a1 = sA[:, 1, :]
                    P.ts("dve", a0, xms_all[:, jt, :], cw[:, jt, 3:4], cw[:, jt, 4:5], ALU.mult, ALU.add, r=["xms_all", "cw"], w=["sA"])
                    for k in range(3):
                        P.stt(a0, cs3[:, :, k], cw[:, jt, k:k + 1], a0, ALU.mult, ALU.add, r=["cs", "cw", "sA"], w=["sA"])
                    P.act(xcs_all[:, jt, :], a0, AF.Silu, r=["sA"], w=["xcs_all"])
                    P.copy("dve", ncs3[:, :, 0:2], cs3[:, :, 1:3], r=["cs"], w=["ncs"])
                    P.copy("dve", ncs3[:, :, 2], xms_all[:, jt, :], r=["xms_all", "ncs"], w=["ncs"])
                    P.tr(pb[6][0:48, 320:448], ncs[:, :], ident_f, r=["ncs", "cst"], w=["pb6"])
                    P.copy("dve", cstk[:, :], pb[6][0:48, 320:448], r=["pb6"], w=["cstk"])
                    P.dma("sp", o_sconv[:, :, jt * 128:(jt + 1) * 128].rearrange("b k c -> (b k) c"), cstk[:, :], r=["cstk"], dkey="o_cstk")
                    P.copy("dve", sB[:, 0, :], xcs_all[:, jt, :], r=["xcs_all"], w=["sB"])
                    P.copy("dve", sB[:, 1, :], xms_all[:, jt, :], r=["xms_all"], w=["sB"])
                    for pi in range(3):
                        P.mm(pb[7][:, pi * NS:(pi + 1) * NS], Dm[:, pi * 4, :], sB[:, 0 if pi < 2 else 1, :], True, True, r=["Dm", "sB"], w=["pb7"])
                    P.copy("dve", sB[:, 2:4, :].rearrange("p a b -> p (a b)")[:, 0:2 * NS], pb[7][:, 0:2 * NS], r=["pb7"], w=["sB2"])
                    P.copy("dve", sA[:, 2, :].bitcast(BF16)[:, 0:NS], pb[7][:, 2 * NS:3 * NS], r=["pb7"], w=["sA2"])
                    sv = sA[:, 2, :].bitcast(BF16)[:, 0:NS]
                    P.mm(pb[7][0:NS, 64:72], sB[:, 2, :], wif[:, jt, :], True, False, r=["sB2", "wif"], w=["pb7"])
                    P.mm(pb[7][0:NS, 64:72], sB[:, 3, :], wif[:, 16 + jt, :], False, False, r=["sB2", "wif"], w=["pb7"])
                    P.mm(pb[7][0:NS, 64:72], sv, wif[:, 32 + jt, :], False, True, r=["sA2", "wif"], w=["pb7"])
                    if jt == 0:
                        P.copy("dve", gacc[0:NS, 16, :], pb[7][0:NS, 64:72], r=["pb7"], w=["gacc.4"])
                    else:
                        P.tt("dve", gacc[0:NS, 16, :], gacc[0:NS, 16, :], pb[7][0:NS, 64:72], ALU.add, r=["pb7", "gacc.4"], w=["gacc.4"])

                if stgA >= 2:
                    gk = ["gacc.%d" % i for i in range(4)]
                    P.copy("dve", gcol[:, :].rearrange("p (g c) -> p c g", c=16), gacc[:, 0:16, :], r=gk, w=["gcol"])
                    P.memset("dve", num[0:64, 256:384], 1.0, w=["gq7"])
                    P.tr(pb[0][0:64, 0:128], gcol[:, 0:64], ident_f, r=["gcol", "cst"], w=["pb0"])
                    P.tr(pb[0][0:64, 128:256], gcol[:, 64:128], ident_f, r=["gcol", "cst"], w=["pb0"])
                    ig = ga[:, :]
                    lf, cm, wi_, em_ = [Asb[0:64, i * 128:(i + 1) * 128] for i in range(4)]
                    we_, dpb, one_ = [num[0:64, i * 128:(i + 1) * 128] for i in range(3)]
                    P.ts("dve", ig, pb[0][0:64, 0:128], gs1[:, 0:1], None, ALU.add, r=["pb0", "gs1"], w=["gq0"])
                    P.act(lf, pb[0][0:64, 128:256], AF.Exp, r=["pb0", "gs1"], w=["gq1"], bias=gs1[:, 2:3], scale=-1.0)
                    P.act(lf, lf, AF.Ln, r=["gq1", "gs1"], w=["gq1"], bias=gs1[:, 3:4])
                    P.ts("dve", lf, lf, -1.0, None, ALU.mult, r=["gq1"], w=["gq1"])
                    P.S.add("dve", (lambda e: e.tensor_tensor_scan(out=lf, data0=one_, data1=lf, initial=0.0, op0=ALU.mult, op1=ALU.add)),
                            ["gq1", "gq7"], ["gq1"])
                    P.tt("dve", ig, ig, lf, ALU.subtract, r=["gq0", "gq1"], w=["gq0"])
                    P.S.add("dve", (lambda e: e.tensor_tensor_scan(out=cm, data0=ig, data1=ig, initial=-1e30, op0=ALU.max, op1=ALU.max)),
                            ["gq0"], ["gq2"])
                    P.copy("dve", gs1[:, 4:5], cm[:, 127:128], r=["gq2"], w=["gs1e"])
                    P.copy("dve", gs1[:, 5:6], lf[:, 127:128], r=["gq1"], w=["gs1e"])
                    P.dma("sp", scr[:, 0:2], gs1[:, 4:6], r=["gs1e"], w=["scr01"], dkey="scr_a", nonc=True)
                    P.dma("sp", ends4[:, :, :], scr[:, 0:2].rearrange("(h c) t -> h c t", c=16), r=["scr01"], w=["ends4"], dkey="scr_b", nonc=True)
                    P.S.add("dve", (lambda e: e.tensor_tensor_scan(out=m4[:, 0, :], data0=ends4[:, :, 0], data1=ends4[:, :, 1], initial=0.0,
                                                                   op0=ALU.max, op1=ALU.add)), ["ends4"], ["m4"])
                    P.memset("dve", m4[:, 1, 0:1], 0.0, w=["m4b"])
                    P.copy("dve", m4[:, 1, 1:16], m4[:, 0, 0:15], r=["m4"], w=["m4b"])
                    P.dma("sp", o_pm.rearrange("(h o) -> h o", o=1), m4[:, 0, 15:16], r=["m4"], dkey="o_pm", nonc=True)
                    P.dma("sp", scr[:, 2:3].rearrange("(h c) t -> h c t", c=16), m4[:, 1, :].rearrange("h (c o) -> h c o", o=1), r=["m4b"], w=["scr2"],
                          dkey="scr_c", nonc=True)
                    P.dma("sp", gs1[:, 6:7], scr[:, 2:3], r=["scr2"], w=["gs1m"], dkey="scr_d", nonc=True)
                    mprev = gs1[:, 6:7]
                    P.ts("dve", cm, cm, mprev, None, ALU.max, r=["gq2", "gs1m"], w=["gq2"])
                    P.act(wi_, cm, AF.Exp, r=["gq2", "gs1m"], w=["gq3"], bias=mprev, scale=-1.0)
                    P.tt("dve", em_, lf, cm, ALU.add, r=["gq1", "gq2"], w=["gq4"])
                    P.act(em_, em_, AF.Exp, r=["gq4"], w=["gq4"], scale=-1.0)
                    P.ts("dve", gs1[:, 7:8], cm[:, 127:128], -1.0, math.log(KS), ALU.mult, ALU.add, r=["gq2"], w=["gs1n"])
                    P.act(we_, ig, AF.Exp, r=["gq0", "gs1n"], w=["gq5"], bias=gs1[:, 7:8])
                    P.tt("dve", gs1[:, 4:5], mprev, cm[:, 127:128], ALU.subtract, r=["gs1m", "gq2", "gs1e"], w=["gs1e"])
                    P.act(gs1[:, 4:5], gs1[:, 4:5], AF.Exp, r=["gs1e"], w=["gs1e"])
                    P.ts("dve", dpb, one_, gs1[:, 4:5], None, ALU.mult, r=["gq7", "gs1e"], w=["gq6"])
                    for qi, src in enumerate((cm, wi_, em_, we_, dpb)):
                        P.tr(pb[2][:, qi * 64:(qi + 1) * 64], src, ident_f[0:64, 0:64], r=["gq%d" % (2 + qi), "cst"], w=["pb2"])
                    P.copy("dve", tokq[:, :, :], pb[2][:, 0:320].rearrange("p (q x) -> p q x", q=5), r=["pb2"], w=["tokq"])

                    P.dma("sp", sg[:, 28:36], bc(W["l2_b_if"].rearrange("(o g) -> o g", o=1), [NS, 8]), w=["sg"], dkey="sg_b")
                    P.dma("sp", sg[:, 8:12], st_m[:, :], w=["sg"], dkey="sg_m")
                    P.tt("dve", sg[:, 0:8], gacc[0:NS, 16, :], sg[:, 28:36], ALU.add, r=["gacc.4", "sg"], w=["sg"])
                    P.act(sg[:, 4:8], sg[:, 4:8], AF.Exp, r=["sg"], w=["sg"], scale=-1.0)
                    P.act(sg[:, 4:8], sg[:, 4:8], AF.Ln, r=["sg", "gs1"], w=["sg"], bias=gs1[0:NS, 3:4])
                    P.ts("dve", sg[:, 4:8], sg[:, 4:8], -1.0, None, ALU.mult, r=["sg"], w=["sg"])
                    P.tt("dve", sg[:, 8:12], sg[:, 8:12], sg[:, 4:8], ALU.add, r=["sg"], w=["sg"])
                    P.tt("dve", sg[:, 12:16], sg[:, 8:12], sg[:, 0:4], ALU.max, r=["sg"], w=["sg"])
                    P.dma("sp", o_sm[:, :], sg[:, 12:16], r=["sg"], dkey="o_sm")
                    P.tt("dve", sg[:, 16:20], sg[:, 0:4], sg[:, 12:16], ALU.subtract, r=["sg"], w=["sg"])
                    P.act(sg[:, 16:20], sg[:, 16:20], AF.Exp, r=["sg"], w=["sg"])
                    P.tt("dve", sg[:, 20:24], sg[:, 8:12], sg[:, 12:16], ALU.subtract, r=["sg"], w=["sg"])
                    P.act(sg[:, 20:24], sg[:, 20:24], AF.Exp, r=["sg"], w=["sg"])
                    P.act(sg[:, 24:28], sg[:, 12:16], AF.Exp, r=["sg"], w=["sg"], scale=-1.0)
                    P.tt("dve", wid[:, :, 0:4], bc(sg[:, 20:24].rearrange("p (o h) -> p o h", o=1), [NS, NS, 4]),
                         bc(ident_f[0:NS, 0:NS].rearrange("p (b o) -> p b o", o=1), [NS, NS, 4]), ALU.mult, r=["sg", "cst"], w=["wid"])

                nheads = cfg.get("ml_heads", 4) if stgA >= 3 else 0
                for h in range(nheads):
                    P.dma("pool", wbuf[:, :, 0:512], Win[:, :, 512 * h:512 * (h + 1)], w=["wA0", "wA1", "wA2", "wA3"], dkey="wBx")
                    P.dma("pool", wbuf[:, :, 512:1024], Win[:, :, 2048 + 512 * h:2048 + 512 * (h + 1)], w=["wBz"], dkey="wBz")
                    P.dma("pool", wo[:, :, :], Wout[:, 4 * h:4 * h + 4, :], w=["wo"], dkey="wo")
                    P.dma("sp", bobc[:, :], bc(W["l2_b_o"][512 * h:512 * (h + 1)].rearrange("(o n) -> o n", o=1), [128, 512]), w=["bobc"], dkey="bobc")
                    wxk = ["wA0", "wA1", "wA2", "wA3"]
                    for jl in range(4):
                        build_dg(jl * 4, 4 * h + jl)
                        for pi in range(4):
                            build_D(pi * 4 + jl, 4 * h + jl, pi)
                        P.memset("dve", xr[jl][:, 0:3], 0.0, w=["xr%d" % jl])
                    P.memset("dve", CT[:, :, :], 0.0, w=["CT"])
                    P.memset("pool", CTb[:, :, :], 0.0, w=["CTb"])
                    P.memset("dve", nst[:, :], 0.0, w=["nst"])
                    P.memset("dve", nstb[:, :], 0.0, w=["nstb"])
                    for ti in range(4):
                        t0 = ti * 512
                        for jl in range(4):
                            jt = 4 * h + jl
                            proj_conv(jt, jl, ti, jl * 128, wxk[jl])
                            if ti < 3:
                                P.copy("pool", xr[jl][:, 0:3], xr[jl][:, 512:515], r=["xr%d" % jl], w=["xr%d" % jl])
                            P.mm(pb[2][:, :], Dm[:, 0 + jl, :], xc[:, jl, :], True, True, r=["Dm", "xc.%d" % jl], w=["pb2"])
                            P.mm(pb[3][:, :], Dm[:, 4 + jl, :], xc[:, jl, :], True, True, r=["Dm", "xc.%d" % jl], w=["pb3"])
                            P.copy("dve", qT[:, jl, :], pb[2][:, :], r=["pb2"], w=["qT"])
                            P.act(khT[:, jl, :], pb[3][:, :], AF.Copy, r=["pb3"], w=["khT"], scale=KS)
                            for kc in range(8):
                                P.mm(pb[4][:, :], wbuf[:, kc, 512 + jl * 128:512 + (jl + 1) * 128], xn[:, kc, t0:t0 + 512], kc == 0, kc == 7,
                                     r=["wBz", xnk(ti)], w=["pb4"])
                            P.act(szT[:, jl, :], pb[4][:, :], AF.Silu, r=["pb4"], w=["szT"])
                        for c4 in range(4):
                            c = 4 * ti + c4
                            hc = h * 16 + c
                            cs_ = slice(c4 * 128, (c4 + 1) * 128)
                            xs_ = slice(3 + c4 * 128, 3 + (c4 + 1) * 128)
                            for jl in range(4):
                                js = slice(jl * 128, (jl + 1) * 128)
                                P.mm(pb[2][:, js], xr[jl][:, xs_], Dm[:, 8 + jl, :], True, True, r=["xr%d" % jl, "Dm"], w=["pb2"])
                                P.mm(pb[3][:, js], xc[:, jl, cs_], Dm[:, 4 + jl, :], True, True, r=["xc.%d" % jl, "Dm"], w=["pb3"])
                                P.mm(pb[4][:, js], xr[jl][:, xs_], Dm[:, 12 + jl, :], True, True, r=["xr%d" % jl, "Dm"], w=["pb4"])
                            P.copy("act", vtok[:, :], pb[2][:, :], r=["pb2"], w=["vtok"])
                            P.act(kwt[:, :], pb[3][:, :], AF.Copy, r=["pb3", "tokq"], w=["kwt"], scale=tokq[:, 3, hc:hc + 1])
                            P.tt("dve", num[:, :], pb[4][:, :], bobc[:, :], ALU.add, r=["pb4", "bobc"], w=["num"])
                            P.act(otok[:, :], num[:, :], AF.Sigmoid, r=["num"], w=["otok"])
                            for jl in range(4):
                                P.mm(pb[5][:, 0:128], qT[:, jl, cs_], khT[:, jl, cs_], jl == 0, jl == 3, r=["qT", "khT"], w=["pb5"])
                            P.mm(pb[5][:, 128:256], bc(ident_f[0:64, hc:hc + 1], [64, 128]), ga[:, :], True, True, r=["cst", "gq0"], w=["pb5"])
                            for jl in range(4):
                                P.mm(pb[5][:, 256:257], qT[:, jl, cs_], nstb[:, jl:jl + 1], jl == 0, jl == 3, r=["qT", "nstb"], w=["pb5"])
                            P.tt("dve", Sm[:, :], pb[5][:, 0:128], gemask, ALU.mult, r=["pb5", "cst"], w=["Sm"])
                            P.ts("dve", Eg_[:, :], pb[5][:, 128:256], tokq[:, 0, hc:hc + 1], 0.0, ALU.subtract, ALU.min, r=["pb5", "tokq"], w=["Eg_"])
                            P.act(Eg_[:, :], Eg_[:, :], AF.Exp, r=["Eg_"], w=["Eg_"])
                            P.S.add("dve", (lambda e, o=wls[:, :], a=Eg_[:, :], b=Sm[:, :], ac=sm[:, 0:1]:
                                            e.scalar_tensor_tensor(out=o, in0=a, scalar=1.0, in1=b, op0=ALU.mult, op1=ALU.mult, accum_out=ac)),
                                    ["Eg_", "Sm"], ["wls", "sm0"])
                            P.tr(pbT1[:, 0:128], wls[:, :], ident_b[:, :], r=["wls", "ident_b"], w=["pb1"])
                            P.copy("dve", wT[:, :], pbT1[:, 0:128], r=["pb1"], w=["wT"])
                            P.mm(pb[6][:, :], wT[:, :], vtok[:, :], True, True, r=["wT", "vtok"], w=["pb6"])
                            for jl in range(4):
                                P.mm(pb[7][:, :], qT[:, jl, cs_], CTb[:, jl, :], jl == 0, jl == 3, r=["qT", "CTb"], w=["pb7"])
                            P.copy("act", Asb[:, :], pb[6][:, :], r=["pb6"], w=["Asb"])
                            wic = tokq[:, 1, hc:hc + 1]
                            P.stt(num[:, :], pb[7][:, :], wic, Asb[:, :], ALU.mult, ALU.add, r=["pb7", "tokq", "Asb"], w=["num"])
                            P.stt(sm[:, 1:2], pb[5][:, 256:257], wic, sm[:, 0:1], ALU.mult, ALU.add, r=["pb5", "tokq", "sm0"], w=["sm1"])
                            P.act(sm[:, 1:2], sm[:, 1:2], AF.Abs, r=["sm1"], w=["sm1"])
                            P.tt("dve", sm[:, 1:2], sm[:, 1:2], tokq[:, 2, hc:hc + 1], ALU.max, r=["sm1", "tokq"], w=["sm1"])
                            P.recip(sm[:, 1:2], sm[:, 1:2], r=["sm1"], w=["sm1"])
                            P.stt(num[:, :], num[:, :], sm[:, 1:2], otok[:, :], ALU.mult, ALU.mult, r=["num", "sm1", "otok"], w=["num"])
                            P.S.add("dve", (lambda e, o=st6[:, 0:6], i=num[:, :]: e.bn_stats(out=o, in_=i)), ["num"], ["st6"])
                            P.S.add("dve", (lambda e, o=sm[:, 2:4], i=st6[:, 0:6]: e.bn_aggr(out=o, in_=i)), ["st6"], ["sm2"])
                            P.act(sm[:, 4:5], sm[:, 3:4], AF.Sqrt, r=["sm2", "epsc"], w=["sm4"], bias=epsc[:, 0:1])
                            P.recip(sm[:, 4:5], sm[:, 4:5], r=["sm4"], w=["sm4"])
                            P.stt(sm[:, 5:6], sm[:, 2:3], -1.0, sm[:, 4:5], ALU.mult, ALU.mult, r=["sm2", "sm4"], w=["sm5"])
                            P.ts("dve", hn[:, :], num[:, :], sm[:, 4:5], sm[:, 5:6], ALU.mult, ALU.add, r=["num", "sm4", "sm5"], w=["hn"])
                            for jl in range(4):
                                P.tr(pbT1[:, 256 + jl * 128:256 + (jl + 1) * 128], hn[:, jl * 128:(jl + 1) * 128], ident_b[:, :], r=["hn", "ident_b"], w=["pb1"])
                            for jl in range(4):
                                jt = 4 * h + jl
                                P.ts("dve", sxc[:, :], xc[:, jl, cs_], skp[:, jt:jt + 1], None, ALU.mult, r=["xc.%d" % jl, "skp"], w=["sxc"])
                                P.stt(g1[:, :], pbT1[:, 256 + jl * 128:256 + (jl + 1) * 128], mhw[:, jt:jt + 1], sxc[:, :], ALU.mult, ALU.add,
                                      r=["pb1", "mhw", "sxc"], w=["g1"])
                                P.tt("pool", gT[:, jl, cs_], g1[:, :], szT[:, jl, cs_], ALU.mult, r=["g1", "szT"], w=["gT"])
                            dpc = tokq[:, 4, hc:hc + 1]
                            for kt in range(4):
                                pu, puk = (pb[0], "pb0") if kt % 2 == 0 else (pb[6], "pb6")
                                P.mm(pu[:, :], kwt[:, kt * 128:(kt + 1) * 128], vtok[:, :], True, True, r=["kwt", "vtok"], w=[puk])
                                P.stt(CT[:, kt, :], CT[:, kt, :], dpc, pu[:, :], ALU.mult, ALU.add, r=["CT", "tokq", puk], w=["CT"])
                                P.copy("act", CTb[:, kt, :], CT[:, kt, :], r=["CT"], w=["CTb"])
                                P.mm(pb[5][:, 260 + kt:261 + kt], kwt[:, kt * 128:(kt + 1) * 128], ones_b[:, 0:1], True, True, r=["kwt", "ones_b"], w=["pb5"])
                            P.stt(nst[:, :], nst[:, :], dpc, pb[5][:, 260:264], ALU.mult, ALU.add, r=["nst", "tokq", "pb5"], w=["nst"])
                            P.copy("dve", nstb[:, :], nst[:, :], r=["nst"], w=["nstb"])
                        for dt_ in range(8):
                            po, pk = pb[dt_ % 2], "pb%d" % (dt_ % 2)
                            for jl in range(4):
                                P.mm(po[:, :], wo[:, jl, dt_ * 128:(dt_ + 1) * 128], gT[:, jl, :], jl == 0, jl == 3, r=["wo", "gT"], w=[pk])
                            P.tt("dve", xres[:, dt_, t0:t0 + 512], xres[:, dt_, t0:t0 + 512], po[:, :], ALU.add, r=[pk, xk(dt_, ti)], w=[xk(dt_, ti)])
                    for vt in range(4):
                        for kt in range(4):
                            P.tr(pb[2][:, kt * 128:(kt + 1) * 128], CT[:, kt, vt * 128:(vt + 1) * 128], ident_f, r=["CT", "cst"], w=["pb2"])
                        P.copy("dve", Cout[:, :], pb[2][:, :], r=["pb2"], w=["Cout"])
                        P.dma("sp", o_pC[h, vt * 128:(vt + 1) * 128, :], Cout[:, :], r=["Cout"], dkey="o_Cout")
                    P.tr(pb[3][0:4, 0:128], nst[:, :], ident_f, r=["nst", "cst"], w=["pb3"])
                    P.copy("dve", ctout[:, :].bitcast(F32)[0:3, :], pb[3][0:3, 0:128], r=["pb3"], w=["ctout"]) if False else None
                    P.copy("dve", st6[0:4, 0:8].bitcast(F32), pb[3][0:4, 0:8], r=["pb3"], w=["st6"]) if False else None
                    P.copy("dve", Cout[0:4, 0:128], pb[3][0:4, 0:128], r=["pb3"], w=["Cout"])
                    P.dma("sp", o_pn[h].rearrange("(kt k) -> kt k", k=128), Cout[0:4, 0:128], r=["Cout"], dkey="o_pn")

                    if cfg.get("ml_nosample"):
                        continue
                    for jl in range(4):
                        jt = 4 * h + jl
                        P.copy("dve", sB[:, 0, :], xcs_all[:, jt, :], r=["xcs_all"], w=["sB"])
                        P.copy("dve", sB[:, 1, :], xms_all[:, jt, :], r=["xms_all"], w=["sB"])
                        js = slice(jl * 128, (jl + 1) * 128)
                        P.mm(pb[2][0:NS, js], sB[:, 0, :], Dm[:, 4 + jl, :], True, True, r=["sB", "Dm"], w=["pb2"])
                        P.mm(pb[3][0:NS, js], sB[:, 0, :], Dm[:, 0 + jl, :], True, True, r=["sB", "Dm"], w=["pb3"])
                        P.mm(pb[4][0:NS, js], sB[:, 1, :], Dm[:, 8 + jl, :], True, True, r=["sB", "Dm"], w=["pb4"])
                        P.mm(pb[5][:, jl * NS:(jl + 1) * NS], Dm[:, 12 + jl, :], sB[:, 1, :], True, True, r=["sB", "Dm"], w=["pb5"])
                        for kc in range(8):
                            P.mm(pb[5][:, 64 + jl * NS:64 + (jl + 1) * NS], wbuf[:, kc, 512 + jl * 128:512 + (jl + 1) * 128], xn[:, kc, SEQ:NT], kc == 0, kc == 7,
                                 r=["wBz", xnk(4)], w=["pb5"])
                        P.act(mrs[:, jl, :], pb[5][:, jl * NS:(jl + 1) * NS], AF.Sigmoid, r=["pb5", "bocol"], w=["mrs"], bias=bocol[:, jt:jt + 1])
                        P.act(hs2[:, jl, :], pb[5][:, 64 + jl * NS:64 + (jl + 1) * NS], AF.Silu, r=["pb5"], w=["hs2"])
                    ktok, vwt, kmk, qmk = bobc[0:NS, :], otok[0:NS, :], vtok[0:NS, :], kwt[0:NS, :]
                    C0t, Cnt = [Asb, num], [Cout, Cnt1]
                    P.ts("dve", ktok, pb[2][0:NS, :], KS, None, ALU.mult, r=["pb2"], w=["bobc"])
                    P.copy("dve", qtok[:, :], pb[3][0:NS, :], r=["pb3"], w=["qtok"])
                    P.ts("dve", vwt, pb[4][0:NS, :], sg[:, 16 + h:17 + h], None, ALU.mult, r=["pb4", "sg"], w=["otok"])
                    P.dma("sp", n0t[:, :], st_n[:, h, :], w=["n0t"], dkey="n0t")
                    P.ts("dve", n0t[:, :], n0t[:, :], sg[:, 20 + h:21 + h], None, ALU.mult, r=["n0t", "sg"], w=["n0t"])
                    P.stt(n0t[:, :], ktok, sg[:, 16 + h:17 + h], n0t[:, :], ALU.mult, ALU.add, r=["bobc", "sg", "n0t"], w=["n0t"])
                    P.dma("sp", o_sn[:, h, :], n0t[:, :], r=["n0t"], dkey="o_sn")
                    P.S.add("dve", (lambda e, o=Cout[0:NS, :], a=n0t[:, :], b=qtok[:, :], ac=sg[:, 36:37]:
                                    e.scalar_tensor_tensor(out=o, in0=a, scalar=1.0, in1=b, op0=ALU.mult, op1=ALU.mult, accum_out=ac)),
                            ["n0t", "qtok", "Cout"], ["Cout", "sg36"])
                    P.act(sg[:, 36:37], sg[:, 36:37], AF.Abs, r=["sg36"], w=["sg36"])
                    P.tt("dve", sg[:, 36:37], sg[:, 36:37], sg[:, 24 + h:25 + h], ALU.max, r=["sg36", "sg"], w=["sg36"])
                    P.recip(sg[:, 36:37], sg[:, 36:37], r=["sg36"], w=["sg36"])
                    P.tt("dve", wid[:, :, 4:5], bc(sg[:, 36:37].rearrange("p (o h) -> p o h", o=1), [NS, NS, 1]),
                         bc(ident_f[0:NS, 0:NS].rearrange("p (b o) -> p b o", o=1), [NS, NS, 1]), ALU.mult, r=["sg36", "cst"], w=["wid"])
                    P.mm(pb[5][:, 128:256], ones_f[0:NS, :], wid[:, :, :].rearrange("p b x -> p (b x)"), True, True, r=["cst", "wid"], w=["pb5"])
                    P.copy("dve", wibc[:, :, :], pb[5][:, 128:256].rearrange("p (b x) -> p b x", x=8), r=["pb5"], w=["wibc"])
                    for b in range(NS):
                        P.ts("dve", kmk, ktok, ident_f[0:NS, b:b + 1], None, ALU.mult, r=["bobc", "cst"], w=["vtok"])
                        P.ts("dve", qmk, qtok[:, :], ident_f[0:NS, b:b + 1], None, ALU.mult, r=["qtok", "cst"], w=["kwt"])
                        P.mm(pb[7][:, :], ones_b[0:NS, :], qmk, True, True, r=["ones_b", "kwt"], w=["pb7"])
                        for vt in range(4):
                            i2 = (b * 4 + vt) % 2
                            c0, c0k = C0t[i2], ("Asb", "num")[i2]
                            cn, cnk = Cnt[i2], ("Cout", "Cnt1")[i2]
                            pu, puk = (pb[2], "pb2") if vt % 2 == 0 else (pb[3], "pb3")
                            P.dma("sp", c0[:, :], st_C[b, h, vt * 128:(vt + 1) * 128, :], w=[c0k], dkey=c0k)
                            P.mm(pu[:, :], vwt[:, vt * 128:(vt + 1) * 128], kmk, True, True, r=["otok", "vtok"], w=[puk])
                            P.stt(cn[:, :], c0[:, :], wibc[:, b, h:h + 1], pu[:, :], ALU.mult, ALU.add, r=[c0k, "wibc", puk], w=[cnk])
                            P.dma("sp", o_sC[b, h, vt * 128:(vt + 1) * 128, :], cn[:, :], r=[cnk], dkey="o_" + cnk)
                            P.S.add("dve", (lambda e, o=c0[:, :], a=cn[:, :], bb=pb[7][:, :], ac=numT[:, vt, b:b + 1]:
                                            e.scalar_tensor_tensor(out=o, in0=a, scalar=1.0, in1=bb, op0=ALU.mult, op1=ALU.mult, accum_out=ac)),
                                    [cnk, "pb7", c0k], [c0k, "numT"])
                    P.tt("dve", hsT[:, :, :], numT[:, :, :], bc(wibc[:, :, 4:5].rearrange("p b o -> p o b"), [128, 4, NS]), ALU.mult, r=["numT", "wibc"], w=["hsT"])
                    P.tt("dve", hsT[:, :, :], hsT[:, :, :], mrs[:, :, :], ALU.mult, r=["hsT", "mrs"], w=["hsT"])
                    P.tt("dve", numT[:, :, :], hsT[:, :, :], hsT[:, :, :], ALU.mult, r=["hsT"], w=["numT"])
                    for vt in range(4):
                        P.mm(pb[4][:, 0:NS], ones_f, hsT[:, vt, :], vt == 0, vt == 3, r=["cst", "hsT"], w=["pb4"])
                    for vt in range(4):
                        P.mm(pb[4][:, NS:2 * NS], ones_f, numT[:, vt, :], vt == 0, vt == 3, r=["cst", "numT"], w=["pb4"])
                    mean = sA[:, 0, :]
                    var = sA[:, 1, :]
                    P.ts("dve", mean, pb[4][:, 0:NS], 1.0 / 512, None, ALU.mult, r=["pb4"], w=["sA"])
                    P.ts("dve", var, pb[4][:, NS:2 * NS], 1.0 / 512, None, ALU.mult, r=["pb4"], w=["sA"])
                    P.tt("dve", sA[:, 3, :], mean, mean, ALU.mult, r=["sA"], w=["sA"])
                    P.tt("dve", var, var, sA[:, 3, :], ALU.subtract, r=["sA"], w=["sA"])
                    P.act(var, var, AF.Sqrt, r=["sA", "epsc"], w=["sA"], bias=epsc[:, 0:1])
                    P.recip(var, var, r=["sA"], w=["sA"])
                    P.tt("dve", hsT[:, :, :], hsT[:, :, :], bc(mean.rearrange("p (o b) -> p o b", o=1), [128, 4, NS]), ALU.subtract, r=["hsT", "sA"], w=["hsT"])
                    P.tt("dve", hsT[:, :, :], hsT[:, :, :], bc(var.rearrange("p (o b) -> p o b", o=1), [128, 4, NS]), ALU.mult, r=["hsT", "sA"], w=["hsT"])
                    for jl in range(4):
                        jt = 4 * h + jl
                        P.ts("dve", hsT[:, jl, :], hsT[:, jl, :], mhw[:, jt:jt + 1], None, ALU.mult, r=["hsT", "mhw"], w=["hsT"])
                        P.stt(hsT[:, jl, :], xcs_all[:, jt, :], skp[:, jt:jt + 1], hsT[:, jl, :], ALU.mult, ALU.add, r=["xcs_all", "skp", "hsT"], w=["hsT"])
                    P.tt("dve", gTs[:, :, :], hsT[:, :, :], hs2[:, :, :], ALU.mult, r=["hsT", "hs2"], w=["gTs"])
                    for dt_ in range(8):
                        po, pk = pb[dt_ % 2], "pb%d" % (dt_ % 2)
                        for jl in range(4):
                            P.mm(po[:, 0:NS], wo[:, jl, dt_ * 128:(dt_ + 1) * 128], gTs[:, jl, :], jl == 0, jl == 3, r=["wo", "gTs"], w=[pk])
                        P.tt("dve", xres[:, dt_, SEQ:NT], xres[:, dt_, SEQ:NT], po[:, 0:NS], ALU.add, r=[pk, xk(dt_, 4)], w=[xk(dt_, 4)])
                P.S.barrier()

        for li_ in (0, 1, 2, 3):
            if li_ in layers:
                if li_ in (0, 3):
                    layer_ssd(li_)
                elif li_ == 1:
                    layer_gmlp()
                else:
                    layer_mlstm()

        with ExitStack() as L:
            ystage = [P.sb(L, "ystage%d" % i, [128, D], F32) for i in range(2)]
            ysq = P.sb(L, "ysq", [128, D], F32)
            fnw_r = P.sb(L, "fnw_r", [128, D], F32)
            ss = P.sb(L, "ss", [128, 2], F32)
            if final_norm:
                P.dma("sp", fnw_r[:, :], bc(W["final_norm_w"].rearrange("(o n) -> o n", o=1), [128, D]), w=["fnw_r"], dkey="fnw_r")
            for c in range(17):
                if c < 16:
                    t0, M, ti = c * 128, 128, c // 4
                    dst = y_p[t0:t0 + 128, :]
                else:
                    t0, M, ti = SEQ, NS, 4
                    dst = y_s[:, :]
                yb = ystage[c % 2]
                yk = "ystage%d" % (c % 2)
                for h in range(2):
                    ps = pb[(2 * c + h) % 8]
                    pk = "pb%d" % ((2 * c + h) % 8)
                    for k4 in range(4):
                        kc = h * 4 + k4
                        P.tr(ps[:M, k4 * 128:(k4 + 1) * 128], xres[:, kc, t0:t0 + M], ident_f, r=[xk(kc, ti), "cst"], w=[pk])
                    P.copy("act" if h else "dve", yb[:M, h * 512:(h + 1) * 512], ps[:M, :], r=[pk], w=[yk + ".%d" % h])
                if final_norm:
                    P.act(ysq[:M, :], yb[:M, :], AF.Square, r=[yk + ".0", yk + ".1"], w=["ysq", "ss"], accum=ss[:M, 0:1])
                    P.act(ss[:M, 1:2], ss[:M, 0:1], AF.Sqrt, r=["ss", "epsc"], w=["ss"], bias=epsc[:M, 0:1], scale=1.0 / D)
                    P.recip(ss[:M, 1:2], ss[:M, 1:2], r=["ss"], w=["ss"])
                    P.stt(yb[:M, :], yb[:M, :], ss[:M, 1:2], fnw_r[:M, :], ALU.mult, ALU.mult,
                          r=[yk + ".0", yk + ".1", "ss", "fnw_r"], w=[yk + ".0", yk + ".1"])
                P.dma("sp", dst, yb[:M, :], r=[yk + ".0", yk + ".1"], dkey="o_" + yk)

        P.S.emit(root)
    return P


def make_consts():
    c = np.zeros((128, 7 * 128), np.float32)
    i = np.arange(128)
    c[:, 0:128] = np.eye(128, dtype=np.float32)
    c[:, 128:256] = 1.0
    c[:, 256:384] = (i[:, None] <= i[None, :]).astype(np.float32)
    c[:, 384:512] = (i[:, None] > i[None, :]).astype(np.float32)
    c[:, 512:640] = (i[None, :] < i[:, None]).astype(np.float32)
    c[:, 640:768] = (i[:, None] // 4 == i[None, :] // 4).astype(np.float32)
    c[:, 768:896] = (i[:, None] >= i[None, :]).astype(np.float32)
    return c


_CACHE = {}


def run(cfg, inputs, ncores=NCORES):
    key = repr(sorted(cfg.items()))
    if key not in _CACHE:
        _CACHE[key] = build(cfg)
    P = _CACHE[key]
    consts = make_consts()
    in_maps = []
    for c in range(ncores):
        m = {}
        for name in P.dram:
            t = P.dram[name]
            if name in ("y_prompt", "y_sample") or name.startswith(("p0_", "s0_", "s1_", "p2_", "s2_", "p3_", "s3_")):
                continue
            if name == "consts":
                m[name] = consts
            elif name == "consts2":
                m[name] = (np.arange(32)[:, None] == (np.arange(2048)[None, :] // 64)).astype(np.float32)
            elif name == "x_prompt":
                m[name] = np.ascontiguousarray(inputs["x_prompt"][c])
            elif name == "x_sample":
                m[name] = np.ascontiguousarray(inputs["x_sample"][c * NS:(c + 1) * NS, 0, :])
            elif name.startswith("state_"):
                m[name] = np.ascontiguousarray(inputs[name][c * NS:(c + 1) * NS])
            else:
                m[name] = np.ascontiguousarray(inputs[name])
        in_maps.append(m)
    res = run_bass_kernel_spmd(P.nc, in_maps, core_ids=list(range(ncores)))
    return res.results


def kernel(**inputs):
    cfg = {"layers": (0, 1, 2, 3), "final_norm": True}
    r = run(cfg, inputs)
    def pstack(name):
        return np.stack([r[c][name] for c in range(NCORES)], 0)
    def scat(name):
        return np.concatenate([r[c][name] for c in range(NCORES)], 0)
    outs = [pstack("y_prompt"), scat("y_sample")[:, None, :]]
    outs += [pstack("p0_ssm"), pstack("p0_conv"), scat("s0_ssm"), scat("s0_conv")]
    outs += [scat("s1_v")[:, None, :]]
    outs += [pstack("p2_C"), pstack("p2_n"), pstack("p2_m"), pstack("p2_conv"), scat("s2_C"), scat("s2_n"), scat("s2_m"), scat("s2_conv")]
    outs += [pstack("p3_ssm"), pstack("p3_conv"), scat("s3_ssm"), scat("s3_conv")]
    return tuple(np.ascontiguousarray(o, dtype=np.float32) for o in outs)
```

```python
import sys
import math
from contextlib import ExitStack
import numpy as np
import concourse.bass as bass
import concourse.mybir as mybir
from concourse.bass_utils import run_bass_kernel_spmd

F32 = mybir.dt.float32
BF16 = mybir.dt.bfloat16
AF = mybir.ActivationFunctionType
ALU = mybir.AluOpType
AX = mybir.AxisListType

ALL_INPUT_NAMES = (
    "x_prompt",
    "x_sample",
    "state_l0_ssm",
    "state_l0_conv",
    "state_l2_C",
    "state_l2_n",
    "state_l2_m",
    "state_l2_conv",
    "state_l3_ssm",
    "state_l3_conv",
    "l0_norm_w",
    "l0_in_proj",
    "l0_conv_w",
    "l0_conv_b",
    "l0_dt_bias",
    "l0_A_log",
    "l0_D_skip",
    "l0_gnorm_w",
    "l0_out_proj",
    "l1_norm_w",
    "l1_in_proj",
    "l1_v_ln_w",
    "l1_v_ln_b",
    "l1_spatial_w",
    "l1_spatial_b",
    "l1_out_proj",
    "l2_norm_w",
    "l2_in_proj",
    "l2_conv_w",
    "l2_conv_b",
    "l2_w_q",
    "l2_w_k",
    "l2_w_v",
    "l2_w_o",
    "l2_b_o",
    "l2_w_if",
    "l2_b_if",
    "l2_mh_norm_w",
    "l2_skip",
    "l2_out_proj",
    "l3_norm_w",
    "l3_in_proj",
    "l3_conv_w",
    "l3_conv_b",
    "l3_dt_bias",
    "l3_A_log",
    "l3_D_skip",
    "l3_gnorm_w",
    "l3_out_proj",
    "final_norm_w",
)

NCORES = 8
D = 1024
SEQ = 2048
NS = 16
NT = SEQ + NS
EPS = 1e-6
TT = [(0, 512), (512, 512), (1024, 512), (1536, 512), (2048, 16)]
NEG = -30000.0


class Op:
    __slots__ = ("eng", "fn", "deps", "dkey", "token", "hasdep", "idx", "where", "seg")


class Sched:
    ENGS = ("pe", "act", "dve", "pool", "sp")
    EPOCH = 12000

    def __init__(self, nc):
        self.nc = nc
        self.ops = []
        self.lw = {}
        self.rd = {}
        self.last_dma = {}
        self.last_on = {}
        self.psum_rd = {}
        self.seg = 0

    def add(self, eng, fn, reads=(), writes=(), dkey=None):
        op = Op()
        op.eng, op.fn, op.dkey, op.hasdep, op.token = eng, fn, dkey, False, None
        op.idx = len(self.ops)
        op.seg = self.seg
        f = sys._getframe(1)
        wl = []
        while f is not None and len(wl) < 4:
            wl.append(f.f_lineno)
            f = f.f_back
        op.where = wl
        deps = {}
        for k in reads:
            w = self.lw.get(k)
            if w is not None:
                deps[w.idx] = w
            if k.startswith("pb") and eng in ("act", "dve"):
                bank = k.split(".")[0]
                other = self.psum_rd.get((bank, "dve" if eng == "act" else "act"))
                if other is not None:
                    deps[other.idx] = other
                self.psum_rd[(bank, eng)] = op
        for k in writes:
            w = self.lw.get(k)
            if w is not None:
                deps[w.idx] = w
            for r in self.rd.get(k, ()):
                deps[r.idx] = r
        op.deps = list(deps.values())
        for d in op.deps:
            d.hasdep = True
        for k in reads:
            lst = self.rd.setdefault(k, [])
            if dkey is None:
                lst[:] = [r for r in lst if r.dkey is not None or r.eng != eng]
            lst.append(op)
        for k in writes:
            self.lw[k] = op
            self.rd[k] = []
        self.ops.append(op)
        if dkey is not None:
            self.last_dma[dkey] = op
        else:
            self.last_on[eng] = op
        return op

    def barrier(self):
        prev = list(self.last_on.values()) + list(self.last_dma.values())
        for e in self.ENGS:
            op = self.add(e, None)
            op.deps = list(prev)
            for d in prev:
                d.hasdep = True
        self.lw.clear()
        self.rd.clear()
        self.seg += 1

    def emit(self, stack):
        nc = self.nc
        cnt = {}
        sems = {}

        def sem_for(key):
            if key not in sems:
                sems[key] = stack.enter_context(nc.semaphore("s%d" % len(sems)))
            return sems[key]

        slot_of = {}
        cur_seg = -1
        for op in self.ops:
            if op.seg != cur_seg:
                cur_seg = op.seg
                slot_of = {}
            if op.dkey is not None:
                sk_ = (op.eng, op.dkey)
                if sk_ not in slot_of:
                    slot_of[sk_] = (op.eng, sum(1 for x in slot_of if x[0] == op.eng))
                k = ("dma", slot_of[sk_])
                cnt[k] = cnt.get(k, 0) + 16
                op.token = (k, cnt[k])
            elif op.hasdep:
                c = cnt.get(op.eng, 0) + 1
                cnt[op.eng] = c
                ep = (c - 1) // self.EPOCH
                op.token = ((op.eng, ep), c - ep * self.EPOCH)
        for op in self.ops:
            if op.token is not None:
                sem_for(op.token[0])
        final = [v.token for k, v in self.last_dma.items()]
        block = stack.enter_context(nc.Block())
        handles = {"pe": nc.tensor, "act": nc.scalar, "dve": nc.vector, "pool": nc.gpsimd, "sp": nc.sync}
        deco = {"pe": block.tensor, "act": block.scalar, "dve": block.vector, "pool": block.gpsimd, "sp": block.sync}
        for eng in self.ENGS:
            myops = [o for o in self.ops if o.eng == eng]

            def body(e, myops=myops, eng=eng):
                waited = {}
                for op in myops:
                    need = {}
                    for d in op.deps:
                        if d.token is None:
                            continue
                        if d.dkey is None and d.eng == eng == "pe":
                            continue
                        sk, val = d.token
                        if val > need.get(sk, 0):
                            need[sk] = val
                    for sk, val in need.items():
                        if waited.get(sk, 0) >= val:
                            continue
                        waited[sk] = val
                        e.wait_ge(sems[sk], val)
                    if op.fn is None:
                        if op.token is not None:
                            e.nop().then_inc(sems[op.token[0]], 1)
                        continue
                    try:
                        inst = op.fn(e)
                    except Exception:
                        print('EMIT FAILED for op added at lines', op.where, 'eng', op.eng)
                        raise
                    if op.token is not None:
                        inst.then_inc(sems[op.token[0]], 16 if op.dkey is not None else 1)
                if eng == "sp":
                    for sk, val in final:
                        e.wait_ge(sems[sk], val)

            deco[eng](body)
        self.nsems = len(sems)


class Prog:
    def __init__(self, cfg):
        self.cfg = cfg
        self.nc = bass.Bass("TRN2", target_bir_lowering=False)
        self.S = Sched(self.nc)
        self.dram = {}
        self.uid = 0

    def din(self, name, shape):
        t = self.nc.dram_tensor(name, list(shape), F32, kind="ExternalInput")
        self.dram[name] = t
        return t.ap()

    def dout(self, name, shape):
        t = self.nc.dram_tensor(name, list(shape), F32, kind="ExternalOutput")
        self.dram[name] = t
        return t.ap()

    def sb(self, stack, name, shape, dt):
        self.uid += 1
        return stack.enter_context(self.nc.sbuf_tensor("%s_%d" % (name, self.uid), list(shape), dt))

    def key(self, base):
        self.uid += 1
        return "%s#%d" % (base, self.uid)

    def op(self, eng, fn, r=(), w=()):
        return self.S.add(eng, fn, r, w)

    def dma(self, q, out, in_, r=(), w=(), dkey=None, nonc=False):
        if nonc:
            fn = lambda e: e.dma_start(out=out, in_=in_, allow_slow_non_contiguous=True)
        else:
            fn = lambda e: e.dma_start(out=out, in_=in_)
        return self.S.add(q, fn, r, w, dkey=dkey)

    def mm(self, out, lhsT, rhs, start, stop, r=(), w=()):
        return self.S.add("pe", lambda e: e.matmul(out, lhsT=lhsT, rhs=rhs, start=start, stop=stop), r, w)

    def tr(self, out, in_, ident, r=(), w=()):
        return self.S.add("pe", lambda e: e.transpose(out, in_, ident), r, w)

    def act(self, out, in_, func, r=(), w=(), bias=None, scale=None, accum=None):
        kw = {}
        if bias is not None:
            kw["bias"] = bias
        if scale is not None:
            kw["scale"] = scale
        if accum is not None:
            kw["accum_out"] = accum
        return self.S.add("act", lambda e: e.activation(out=out, in_=in_, func=func, **kw), r, w)

    def tt(self, eng, out, in0, in1, op, r=(), w=()):
        return self.S.add(eng, lambda e: e.tensor_tensor(out=out, in0=in0, in1=in1, op=op), r, w)

    def ts(self, eng, out, in0, s1, s2, op0, op1=None, r=(), w=()):
        if op1 is None:
            return self.S.add(eng, lambda e: e.tensor_scalar(out=out, in0=in0, scalar1=s1, scalar2=None, op0=op0), r, w)
        return self.S.add(eng, lambda e: e.tensor_scalar(out=out, in0=in0, scalar1=s1, scalar2=s2, op0=op0, op1=op1), r, w)

    def stt(self, out, in0, scalar, in1, op0, op1, r=(), w=()):
        return self.S.add("dve", lambda e: e.scalar_tensor_tensor(out=out, in0=in0, scalar=scalar, in1=in1, op0=op0, op1=op1), r, w)

    def copy(self, eng, out, in_, r=(), w=()):
        if eng == "act":
            return self.S.add("act", lambda e: e.activation(out=out, in_=in_, func=AF.Copy), r, w)
        return self.S.add(eng, lambda e: e.tensor_copy(out=out, in_=in_), r, w)

    def recip(self, out, in_, r=(), w=()):
        return self.S.add("dve", lambda e: e.reciprocal(out=out, in_=in_), r, w)

    def memset(self, eng, ap, val, r=(), w=()):
        return self.S.add(eng, lambda e: e.memset(ap, val), r, w)


def bc(ap, shape):
    return ap.to_broadcast(list(shape))


def build(cfg):
    P = Prog(cfg)
    nc = P.nc
    layers = cfg.get("layers", [0, 1, 2, 3])
    final_norm = cfg.get("final_norm", True)

    x_p = P.din("x_prompt", [SEQ, D])
    x_s = P.din("x_sample", [NS, D])
    consts_d = P.din("consts", [128, 7 * 128])
    W = {}
    wshapes = {
        "final_norm_w": [D],
        "l1_norm_w": [D], "l1_in_proj": [D, 6144], "l1_v_ln_w": [2048], "l1_v_ln_b": [2048],
        "l1_spatial_w": [8, 128, 128], "l1_spatial_b": [8, 128], "l1_out_proj": [2048, D],
    }
    for li_ in (0, 3):
        p_ = "l%d_" % li_
        wshapes.update({p_ + "norm_w": [D], p_ + "in_proj": [D, 6176], p_ + "conv_w": [4, 4096], p_ + "conv_b": [4096],
                        p_ + "dt_bias": [32], p_ + "A_log": [32], p_ + "D_skip": [32], p_ + "gnorm_w": [2048], p_ + "out_proj": [2048, D]})
    for k, s in wshapes.items():
        if k == "final_norm_w" or int(k[1]) in layers:
            W[k] = P.din(k, s)
    DR = {}
    if 2 in layers:
        for k_, s_ in {"l2_norm_w": [D], "l2_in_proj": [D, 4096], "l2_conv_w": [4, 2048], "l2_conv_b": [2048], "l2_w_q": [512, 4, 4],
                       "l2_w_k": [512, 4, 4], "l2_w_v": [512, 4, 4], "l2_w_o": [512, 4, 4], "l2_b_o": [2048], "l2_w_if": [6144, 8],
                       "l2_b_if": [8], "l2_mh_norm_w": [2048], "l2_skip": [2048], "l2_out_proj": [2048, D]}.items():
            W[k_] = P.din(k_, s_)
        for k_, s_ in {"state_l2_C": [NS, 4, 512, 512], "state_l2_n": [NS, 4, 512], "state_l2_m": [NS, 4], "state_l2_conv": [NS, 3, 2048]}.items():
            DR[k_] = P.din(k_, s_)
        for k_, s_ in {"p2_C": [4, 512, 512], "p2_n": [4, 512], "p2_m": [4], "p2_conv": [3, 2048],
                       "s2_C": [NS, 4, 512, 512], "s2_n": [NS, 4, 512], "s2_m": [NS, 4], "s2_conv": [NS, 3, 2048]}.items():
            DR[k_] = P.dout(k_, s_)
    consts2_d = P.din("consts2", [32, 2048])
    for li_ in (0, 3):
        if li_ in layers:
            DR["state_l%d_ssm" % li_] = P.din("state_l%d_ssm" % li_, [NS, 32, 64, 128])
            DR["state_l%d_conv" % li_] = P.din("state_l%d_conv" % li_, [NS, 3, 4096])
            DR["p%d_ssm" % li_] = P.dout("p%d_ssm" % li_, [32, 64, 128])
            DR["p%d_conv" % li_] = P.dout("p%d_conv" % li_, [3, 4096])
            DR["s%d_ssm" % li_] = P.dout("s%d_ssm" % li_, [NS, 32, 64, 128])
            DR["s%d_conv" % li_] = P.dout("s%d_conv" % li_, [NS, 3, 4096])
    y_p = P.dout("y_prompt", [SEQ, D])
    y_s = P.dout("y_sample", [NS, D])
    s1_v = P.dout("s1_v", [NS, 2048]) if 1 in layers else None

    root = ExitStack()
    with root:
        G = root
        xres = P.sb(G, "xres", [128, 8, NT], F32)
        xn = P.sb(G, "xn", [128, 8, NT], BF16)
        cst = P.sb(G, "cst", [128, 7 * 128], F32)
        ident_b = P.sb(G, "ident_b", [128, 128], BF16)
        ones_b = P.sb(G, "ones_b", [128, 128], BF16)
        nw = P.sb(G, "nw", [128, 5, 8], F32)
        rs_s = P.sb(G, "rs_s", [128, 512], F32)
        rs_r = P.sb(G, "rs_r", [128, 512], F32)
        sq = [P.sb(G, "sq%d" % i, [128, 512], BF16) for i in range(2)]
        pb = [G.enter_context(nc.psum_tensor("pb%d" % i, [128, 512], F32)) for i in range(8)]
        ident_f = cst[:, 0:128]

        def xk(kc, ti):
            return "xres.%d.%d" % (kc, ti)

        def xnk(ti):
            return "xn.%d" % ti

        P.dma("sp", cst[:, :], consts_d[:, :], w=["cst"], dkey="cst")
        P.copy("dve", ident_b[:, :], cst[:, 0:128], r=["cst"], w=["ident_b"])
        P.copy("dve", ones_b[:, :], cst[:, 128:256], r=["cst"], w=["ones_b"])
        nwn = {0: "l0_norm_w", 1: "l1_norm_w", 2: "l2_norm_w", 3: "l3_norm_w", 4: "final_norm_w"}
        for li in list(layers) + [4]:
            if nwn[li] in W:
                P.dma("sp", nw[:, li, :], W[nwn[li]].rearrange("(k p) -> p k", p=128), w=["nw%d" % li],
                      dkey="nw%d" % li, nonc=True)

        with ExitStack() as L:
            xin = [P.sb(L, "xin%d" % i, [128, 4, D], F32) for i in range(2)]
            xsin = P.sb(L, "xsin", [NS, D], F32)
            for ti in range(4):
                buf = xin[ti % 2]
                bk = "xin%d" % (ti % 2)
                P.dma("sp", buf[:, :, :], x_p[ti * 512:(ti + 1) * 512, :].rearrange("(c p) d -> p c d", p=128),
                      w=[bk], dkey=bk)
                for kc in range(8):
                    ps = pb[kc % 8]
                    pk = "pb%d" % (kc % 8)
                    for c in range(4):
                        P.tr(ps[:, c * 128:(c + 1) * 128], buf[:, c, kc * 128:(kc + 1) * 128], ident_f,
                             r=[bk, "cst"], w=[pk])
                    P.copy("act" if kc % 2 else "dve", xres[:, kc, ti * 512:(ti + 1) * 512], ps[:, :], r=[pk], w=[xk(kc, ti)])
            P.dma("sp", xsin[:, :], x_s[:, :], w=["xsin"], dkey="xsin")
            for kc in range(8):
                P.tr(pb[0][:, kc * 16:(kc + 1) * 16], xsin[:, kc * 128:(kc + 1) * 128], ident_f[0:NS, 0:NS],
                     r=["xsin", "cst"], w=["pb0"])
            P.copy("dve", xres[:, :, SEQ:NT], pb[0][:, 0:128].rearrange("p (k t) -> p k t", k=8),
                   r=["pb0"], w=[xk(kc, 4) for kc in range(8)])
            P.S.barrier()

        def emit_norm(li):
            for ti, (t0, tn) in enumerate(TT):
                for kc in range(8):
                    P.act(sq[kc % 2][:, :tn], xres[:, kc, t0:t0 + tn], AF.Square, r=[xk(kc, ti)], w=["sq%d" % (kc % 2)])
                    P.mm(pb[7][:, :tn], ones_b[:, :], sq[kc % 2][:, :tn], kc == 0, kc == 7,
                         r=["sq%d" % (kc % 2), "ones_b"], w=["pb7"])
                P.act(rs_s[:, :tn], pb[7][:, :tn], AF.Sqrt, r=["pb7", "epsc"], w=["rs_s"], bias=epsc[:, 0:1], scale=1.0 / D)
                P.recip(rs_r[:, :tn], rs_s[:, :tn], r=["rs_s"], w=["rs_r"])
                for kc in range(8):
                    P.stt(xn[:, kc, t0:t0 + tn], xres[:, kc, t0:t0 + tn], nw[:, li, kc:kc + 1], rs_r[:, :tn],
                          ALU.mult, ALU.mult, r=[xk(kc, ti), "rs_r", "nw%d" % li], w=[xnk(ti)])

        epsc = P.sb(G, "epsc", [128, 2], F32)
        P.memset("dve", epsc[:, 0:1], EPS, w=["epsc"])
        P.memset("dve", epsc[:, 1:2], -0.5, w=["epsc"])

        def rstd_pool(out, in_, scale, r, w):
            P.ts("pool", out, in_, scale, EPS, ALU.mult, ALU.add, r=r, w=w)
            npart, nfree = out.shape[0], out.shape[1]
            P.tt("pool", out, out, bc(epsc[0:npart, 1:2], [npart, nfree]), ALU.pow, r=list(w) + ["epsc"], w=w)

        def layer_gmlp():
            li = 1
            emit_norm(li)
            Win = W["l1_in_proj"].rearrange("(k p) j -> p k j", p=128)
            Wout = W["l1_out_proj"].rearrange("(k p) j -> p k j", p=128)
            with ExitStack() as L:
                vtok = P.sb(L, "vtok", [128, 9, 2048], BF16)
                wv = [P.sb(L, "wv%d" % i, [128, 8, 512], BF16) for i in range(2)]
                wuz = wv
                wo = [P.sb(L, "wo%d" % i, [128, 2, D], BF16) for i in range(2)]
                stats = P.sb(L, "stats", [128, 9, 4, 6], F32)
                mv = P.sb(L, "mv", [128, 9, 2], F32)
                rstd = P.sb(L, "rstd", [128, 9], F32)
                nmr = P.sb(L, "nmr", [128, 9], F32)
                wsT = P.sb(L, "wsT", [128, 8, 128], BF16)
                lnw = P.sb(L, "lnw", [128, 16], F32)
                lnb = P.sb(L, "lnb", [128, 16], F32)
                Ec = P.sb(L, "Ec", [128, 2, 128], F32)
                rsb = P.sb(L, "rsb", [128, 8, 128], F32)
                sbb = P.sb(L, "sbb", [128, 8, 128], F32)
                lnr = P.sb(L, "lnr", [NS, 2048], F32)
                ws00 = P.sb(L, "ws00", [NS, 8], F32)
                sb00 = P.sb(L, "sb00", [NS, 8], F32)
                vns = P.sb(L, "vns", [NS, 2048], F32)
                mxsT = P.sb(L, "mxsT", [128, 16, NS], F32)
                sz = [P.sb(L, "sz%d" % i, [128, 512], BF16) for i in range(2)]
                mx = [P.sb(L, "mx%d" % i, [128, 512], F32) for i in range(2)]
                gt = [P.sb(L, "gt%d" % i, [128, 2, 512], BF16) for i in range(2)]
                onesw = P.sb(L, "onesw", [128, 128], BF16)

                P.dma("sp", lnw[:, :], W["l1_v_ln_w"].rearrange("(k p) -> p k", p=128), w=["lnw"], dkey="lnw", nonc=True)
                P.dma("sp", lnb[:, :], W["l1_v_ln_b"].rearrange("(k p) -> p k", p=128), w=["lnb"], dkey="lnb", nonc=True)
                P.dma("sp", ws00[:, :], bc(W["l1_spatial_w"][:, 0:1, 0:1].rearrange("g a b -> (a b) g"), [NS, 8]),
                      w=["ws00"], dkey="ws00", nonc=True)
                P.dma("sp", sb00[:, :], bc(W["l1_spatial_b"][:, 0:1].rearrange("g a -> a g"), [NS, 8]),
                      w=["sb00"], dkey="sb00", nonc=True)
                for h2 in range(2):
                    P.dma("sp", mx[h2][:, :].rearrange("p (g s) -> p g s", g=4),
                          W["l1_spatial_w"][h2 * 4:(h2 + 1) * 4].rearrange("g t s -> t g s"), w=["mx%d" % h2], dkey="wsl%d" % h2)
                P.dma("sp", sbb[:, :, :], bc(W["l1_spatial_b"].rearrange("(o g) t -> o g t", o=1), [128, 8, 128]),
                      w=["sbb"], dkey="sbb")
                causal = cst[:, 256:384]
                for g in range(8):
                    ps = pb[g % 2]
                    pk = "pb%d" % (g % 2)
                    P.tr(ps[:, 0:128], mx[g // 4][:, (g % 4) * 128:(g % 4 + 1) * 128], ident_f, r=["mx%d" % (g // 4), "cst"], w=[pk])
                    P.tt("dve", wsT[:, g, :], ps[:, 0:128], causal, ALU.mult, r=[pk, "cst"], w=["wsT"])
                P.copy("dve", onesw[:, :], cst[:, 128:256], r=["cst"], w=["onesw"])
                for g2 in range(2):
                    P.mm(pb[2 + g2][:, :], onesw[:, :], wsT[:, g2 * 4:(g2 + 1) * 4, :], True, True,
                         r=["onesw", "wsT"], w=["pb%d" % (2 + g2)])
                    P.copy("dve", rsb[:, g2 * 4:(g2 + 1) * 4, :], pb[2 + g2][:, :].rearrange("p (g t) -> p g t", g=4),
                           r=["pb%d" % (2 + g2)], w=["rsb"])

                wq = 0
                stg = cfg.get('gm_stage', 9)
                for half in range(cfg.get('gm_halves', 2)):
                    chunks = list(range(8 * half, 8 * half + 8))
                    ttiles = [2 * half, 2 * half + 1] + ([4] if half == 1 else [])
                    nslot = 8 + (1 if half == 1 else 0)
                    bank = 0
                    for ct in range(cfg.get('v_nct', 4) if stg >= 1 else 0):
                        wb = wv[ct % 2]
                        wk = "wv%d" % (ct % 2)
                        P.dma("pool", wb[:, :, :], Win[:, :, 2048 + ct * 512: 2048 + (ct + 1) * 512], w=[wk + "u", wk + "z"], dkey=wk + "u")
                        for sl in range(cfg.get("v_nsl", nslot)):
                            if sl < 8:
                                c = chunks[sl]
                                t0, M, ti = c * 128, 128, c // 4
                            else:
                                t0, M, ti = SEQ, NS, 4
                            ps = pb[bank % 7]
                            pk = "pb%d" % (bank % 7)
                            bank += 1
                            for kc in range(8):
                                P.mm(ps[:M, :], xn[:, kc, t0:t0 + M], wb[:, kc, :], kc == 0, kc == 7,
                                     r=[xnk(ti), wk + "u", wk + "z"], w=[pk])
                            if cfg.get("v_statcopy"):
                                P.copy("dve", mx[0][:M, :], ps[:M, :], r=[pk], w=["mx0"])
                            elif not cfg.get("v_nostats"):
                                P.S.add("dve", (lambda e, o=stats[:M, sl, ct, :], i=ps[:M, :]: e.bn_stats(out=o, in_=i)),
                                        [pk], ["stats.%d" % sl])
                            if not cfg.get("v_nocopy"):
                                P.copy(cfg.get("v_copyeng", "act"), vtok[:M, sl, ct * 512:(ct + 1) * 512], ps[:M, :], r=[pk] + (["stats.%d" % sl] if cfg.get("v_serial") else []), w=["vtok.%d" % sl])
                    for sl in range(nslot if stg >= 2 else 0):
                        M = 128 if sl < 8 else NS
                        P.S.add("dve", (lambda e, o=mv[:M, sl, :], i=stats[:M, sl, :, :]: e.bn_aggr(out=o, in_=i)),
                                ["stats.%d" % sl], ["mv.%d" % sl])
                        P.act(rstd[:M, sl:sl + 1], mv[:M, sl, 1:2], AF.Sqrt, r=["mv.%d" % sl, "epsc"], w=["rstd.%d" % sl],
                              bias=epsc[:M, 0:1], scale=1.0)
                        P.recip(rstd[:M, sl:sl + 1], rstd[:M, sl:sl + 1], r=["rstd.%d" % sl], w=["rstd.%d" % sl])
                        P.stt(nmr[:M, sl:sl + 1], mv[:M, sl, 0:1], -1.0, rstd[:M, sl:sl + 1], ALU.mult, ALU.mult,
                              r=["mv.%d" % sl, "rstd.%d" % sl], w=["nmr.%d" % sl])
                        if sl < 8:
                            P.ts("dve", vtok[:M, sl, :], vtok[:M, sl, :], rstd[:M, sl:sl + 1], nmr[:M, sl:sl + 1],
                                 ALU.mult, ALU.add, r=["vtok.%d" % sl, "rstd.%d" % sl, "nmr.%d" % sl], w=["vtok.%d" % sl])
                        else:
                            P.ts("dve", vns[:, :], vtok[:M, sl, :], rstd[:M, sl:sl + 1], nmr[:M, sl:sl + 1],
                                 ALU.mult, ALU.add, r=["vtok.%d" % sl, "rstd.%d" % sl, "nmr.%d" % sl], w=["vns"])
                            P.dma("sp", lnr[:, :], bc(W["l1_v_ln_w"].rearrange("(o n) -> o n", o=1), [NS, 2048]), w=["lnr"], dkey="lnr")
                            P.tt("dve", vns[:, :], vns[:, :], lnr[:, :], ALU.mult, r=["vns", "lnr"], w=["vns"])
                            P.dma("sp", lnr[:, :], bc(W["l1_v_ln_b"].rearrange("(o n) -> o n", o=1), [NS, 2048]), w=["lnr"], dkey="lnr")
                            P.tt("dve", vns[:, :], vns[:, :], lnr[:, :], ALU.add, r=["vns", "lnr"], w=["vns"])
                            P.dma("sp", s1_v[:, :], vns[:, :], r=["vns"], dkey="o_s1v")
                            mxs = lnr
                            P.tt("dve", mxs[:, :].rearrange("p (g d) -> p g d", g=8), vns[:, :].rearrange("p (g d) -> p g d", g=8),
                                 bc(ws00[:, :].rearrange("p (g o) -> p g o", o=1), [NS, 8, 256]), ALU.mult,
                                 r=["vns", "ws00"], w=["lnr"])
                            P.tt("dve", mxs[:, :].rearrange("p (g d) -> p g d", g=8), mxs[:, :].rearrange("p (g d) -> p g d", g=8),
                                 bc(sb00[:, :].rearrange("p (g o) -> p g o", o=1), [NS, 8, 256]), ALU.add,
                                 r=["lnr", "sb00"], w=["lnr"])
                            for jt in range(16):
                                P.tr(pb[7][:, jt * NS:(jt + 1) * NS], mxs[:, jt * 128:(jt + 1) * 128], ident_f[0:NS, 0:NS],
                                     r=["lnr", "cst"], w=["pb7"])
                            P.copy("dve", mxsT[:, :, :], pb[7][:, 0:16 * NS].rearrange("p (j t) -> p j t", j=16),
                                   r=["pb7"], w=["mxsT"])
                    def load_g(g_, slot_):
                        wb_, wk_ = wuz[slot_], "wv%d" % slot_
                        ob_, ok_ = wo[slot_], "wo%d" % slot_
                        P.dma("pool", wb_[:, :, 0:256], Win[:, :, g_ * 256:(g_ + 1) * 256], w=[wk_ + "u"], dkey=wk_ + "u")
                        P.dma("pool", wb_[:, :, 256:512], Win[:, :, 4096 + g_ * 256: 4096 + (g_ + 1) * 256], w=[wk_ + "z"], dkey=wk_ + "z")
                        P.dma("pool", ob_[:, :, :], Wout[:, 2 * g_:2 * g_ + 2, :], w=[ok_], dkey=ok_)
                    if stg >= 3:
                        load_g(0, wq % 2)
                    for g in range(8 if stg >= 3 else 0):
                        slot = wq % 2
                        wq += 1
                        wb, wk = wuz[slot], "wv%d" % slot
                        ob, ok = wo[slot], "wo%d" % slot
                        if g < 7:
                            load_g(g + 1, wq % 2)
                        for ti in ttiles:
                            t0, tn = TT[ti]
                            gb = gt[ti % 2]
                            gk = "gt%d" % (ti % 2)
                            for jl in range(2):
                                jt = 2 * g + jl
                                o = 3 * jl
                                pu, pz, pm = pb[o], pb[o + 1], pb[o + 2]
                                ku, kz, km = "pb%d" % o, "pb%d" % (o + 1), "pb%d" % (o + 2)
                                for kc in range(8):
                                    P.mm(pu[:, :tn], wb[:, kc, jl * 128:(jl + 1) * 128], xn[:, kc, t0:t0 + tn], kc == 0, kc == 7,
                                         r=[wk + "u", xnk(ti)], w=[ku])
                                for kc in range(8):
                                    P.mm(pz[:, :tn], wb[:, kc, 256 + jl * 128:256 + (jl + 1) * 128], xn[:, kc, t0:t0 + tn],
                                         kc == 0, kc == 7, r=[wk + "z", xnk(ti)], w=[kz])
                                s_ = sz[jl]
                                sk_ = "sz%d" % jl
                                m_ = mx[jl]
                                mk_ = "mx%d" % jl
                                P.act(s_[:, :tn], pz[:, :tn], AF.Silu, r=[kz], w=[sk_])
                                if ti < 4:
                                    for c4 in range(4):
                                        c = ti * 4 + c4
                                        sl = c - 8 * half
                                        P.mm(pm[:, c4 * 128:(c4 + 1) * 128], vtok[:, sl, jt * 128:(jt + 1) * 128], wsT[:, g, :],
                                             True, True, r=["vtok.%d" % sl, "wsT"], w=[km])
                                    P.stt(Ec[:, jl, :], rsb[:, g, :], lnb[:, jt:jt + 1], sbb[:, g, :], ALU.mult, ALU.add,
                                          r=["rsb", "lnb", "sbb"], w=["Ec%d" % jl])
                                    P.stt(m_[:, :].rearrange("p (c t) -> p c t", c=4), pm[:, :].rearrange("p (c t) -> p c t", c=4),
                                          lnw[:, jt:jt + 1], bc(Ec[:, jl:jl + 1, :], [128, 4, 128]), ALU.mult, ALU.add,
                                          r=[km, "lnw", "Ec%d" % jl], w=[mk_])
                                    min_ = m_[:, :tn]
                                    rk = [mk_]
                                else:
                                    min_ = mxsT[:, jt, :]
                                    rk = ["mxsT"]
                                P.tt("dve", m_[:, :tn], pu[:, :tn], min_, ALU.mult, r=[ku] + rk, w=[mk_])
                                P.tt("pool", gb[:, jl, :tn], m_[:, :tn], s_[:, :tn], ALU.mult, r=[mk_, sk_], w=[gk])
                            for dt_ in range(8):
                                po = pb[6 + dt_ % 2]
                                pk = "pb%d" % (6 + dt_ % 2)
                                for jl in range(2):
                                    P.mm(po[:, :tn], ob[:, jl, dt_ * 128:(dt_ + 1) * 128], gb[:, jl, :tn], jl == 0, jl == 1,
                                         r=[ok, gk], w=[pk])
                                P.tt("dve", xres[:, dt_, t0:t0 + tn], xres[:, dt_, t0:t0 + tn], po[:, :tn], ALU.add,
                                     r=[pk, xk(dt_, ti)], w=[xk(dt_, ti)])
                P.S.barrier()

        def layer_ssd(li):
            p_ = "l%d_" % li
            emit_norm(li)
            Win = W[p_ + "in_proj"].rearrange("(k p) j -> p k j", p=128)
            Wout = W[p_ + "out_proj"].rearrange("(k p) j -> p k j", p=128)
            st_ssm = DR["state_l%d_ssm" % li]
            st_conv = DR["state_l%d_conv" % li]
            o_pssm, o_pconv, o_sssm, o_sconv = DR["p%d_ssm" % li], DR["p%d_conv" % li], DR["s%d_ssm" % li], DR["s%d_conv" % li]
            triu_f, sgt_f, ltm_f = cst[:, 256:384], cst[:, 384:512], cst[:, 512:640]
            ones_f = cst[:, 128:256]
            with ExitStack() as L:
                cw = P.sb(L, "cw", [128, 32, 5], F32)
                dtb = P.sb(L, "dtb", [32, 4], F32)
                Dcol = P.sb(L, "Dcol", [128, 16], F32)
                gnw = P.sb(L, "gnw", [128, 16], F32)
                wdt = P.sb(L, "wdt", [128, 8, 32], BF16)
                ddt = P.sb(L, "ddt", [128, 17, 64], F32)
                ecum = P.sb(L, "ecum", [128, 16, 32], F32)
                wend = P.sb(L, "wend", [128, 16, 32], F32)
                eL = P.sb(L, "eL", [128, 16, 32], F32)
                dtmp = P.sb(L, "dtmp", [32, 2, 512], F32)
                dsT = P.sb(L, "dsT", [32, 2, NS], F32)
                csb = P.sb(L, "csb", [128, 32], F32)
                negI = P.sb(L, "negI", [128, 128], F32)
                wblk = [P.sb(L, "wblk%d" % i, [128, 8, 768], BF16) for i in range(2)]
                wo = [P.sb(L, "wo0", [128, 2, D], BF16)] * 2
                dg = P.sb(L, "dg", [128, 16, 128], BF16)
                dgD = P.sb(L, "dgD", [128, 2, 128], BF16)
                cwg = P.sb(L, "cwg", [128, 4, 5], F32)
                xr = [[P.sb(L, "xr%d_0" % j, [128, 515], BF16)] * 2 for j in range(4)]
                xbc = [P.sb(L, "xbc%d" % i, [128, 4, 512], BF16) for i in range(2)]
                ctail = P.sb(L, "ctail", [128, 4, 3], F32)
                xBtr = [P.sb(L, "xBt%d" % i, [128, 384], BF16) for i in range(2)]
                sztile = P.sb(L, "sztile", [128, 4, 256], BF16)
                scrA = P.sb(L, "scrA", [128, 512], F32)
                scrB = P.sb(L, "scrB", [128, 512], F32)
                Ub = [P.sb(L, "Ub%d" % i, [128, 128], F32) for i in range(2)]
                dec = P.sb(L, "dec", [128, 512], BF16)
                mixT = P.sb(L, "mixT", [128, 4, 128], BF16)
                ssq = P.sb(L, "ssq", [128, 2], F32)
                yn = P.sb(L, "yn", [128, 256], BF16)
                gT = [P.sb(L, "gT%d" % i, [128, 2, 512], BF16) for i in range(2)]
                xw = P.sb(L, "xw", [128, 256], BF16)
                hT = P.sb(L, "hT", [128, 256], F32)
                hTb = P.sb(L, "hTb", [128, 256], BF16)
                htmp = P.sb(L, "htmp", [128, 256], F32)
                hout = scrA[:, 0:256].rearrange("p (j n) -> p j n", j=2)
                ctout = scrB[0:3, :]
                xrs = P.sb(L, "xrs", [128, 4, NS], F32)
                cstk = scrA[0:48, :]
                cs = P.sb(L, "cs", [128, 4, 48], F32)
                acc = P.sb(L, "acc", [128, 4, NS], F32)
                atmp = P.sb(L, "atmp", [128, 4, NS], F32)
                xbs = P.sb(L, "xbs", [128, 4, NS], F32)
                ncs = P.sb(L, "ncs", [128, 4, 48], F32)
                szs = P.sb(L, "szs", [128, 2, NS], F32)
                Eg = P.sb(L, "Eg", [32, 256], F32)
                dtc = P.sb(L, "dtc", [128, 2, 2, NS], F32)
                xdtc = P.sb(L, "xdtc", [128, 2, NS], F32)
                rhsB = P.sb(L, "rhsB", [128, NS, 128], BF16)
                h0t = [P.sb(L, "h0t%d" % i, [128, 4, 128], F32) for i in range(2)]
                hnt = [P.sb(L, "hnt%d" % i, [128, 4, 128], F32) for i in range(2)]
                s1 = P.sb(L, "s1", [128, 4, 128], F32)
                ys = P.sb(L, "ys", [128, 2, NS], F32)
                gsm = P.sb(L, "gsm", [128, 2, NS], F32)
                gsq = P.sb(L, "gsq", [128, 2, NS], BF16)
                gTs = P.sb(L, "gTs", [128, 2, NS], BF16)
                rsd = P.sb(L, "rsd", [128, NS], F32)
                pbT2 = pb[2].bitcast(BF16)
                pbT7 = pb[7].bitcast(BF16)
                pbT1 = pb[1].bitcast(BF16)

                for k in range(4):
                    P.dma("sp", cw[:, :, k], W[p_ + "conv_w"][k].rearrange("(j p) -> p j", p=128), w=["cw"], dkey="cw%d" % k, nonc=True)
                P.dma("sp", cw[:, :, 4], W[p_ + "conv_b"].rearrange("(j p) -> p j", p=128), w=["cw"], dkey="cw4", nonc=True)
                P.dma("sp", dtb[:, 0:1], W[p_ + "dt_bias"].rearrange("(h o) -> h o", o=1), w=["dtb"], dkey="dtb0", nonc=True)
                P.dma("sp", dtb[:, 1:2], W[p_ + "A_log"].rearrange("(h o) -> h o", o=1), w=["dtb"], dkey="dtb1", nonc=True)
                P.act(dtb[:, 2:3], dtb[:, 1:2], AF.Exp, r=["dtb"], w=["dtb"])
                P.ts("dve", dtb[:, 2:3], dtb[:, 2:3], -1.0, None, ALU.mult, r=["dtb"], w=["dtb"])
                P.memset("dve", dtb[:, 3:4], 1.0, w=["dtb"])
                Dsk = W[p_ + "D_skip"].rearrange("(j two o) -> two o j", two=2, o=1)
                P.dma("sp", Dcol[0:64, :], bc(Dsk[0], [64, 16]), w=["Dcol"], dkey="Dcol0", nonc=True)
                P.dma("sp", Dcol[64:128, :], bc(Dsk[1], [64, 16]), w=["Dcol"], dkey="Dcol1", nonc=True)
                P.dma("sp", gnw[:, :], W[p_ + "gnorm_w"].rearrange("(k p) -> p k", p=128), w=["gnw"], dkey="gnw", nonc=True)
                P.dma("pool", wdt[:, :, :], Win[:, :, 6144:6176], w=["wdt"], dkey="wdt")
                P.ts("dve", negI[:, :], ident_f, NEG, None, ALU.mult, r=["cst"], w=["negI"])

                for ti, (t0, tn) in enumerate(TT):
                    for kc in range(8):
                        P.mm(pb[0][:32, :tn], wdt[:, kc, :], xn[:, kc, t0:t0 + tn], kc == 0, kc == 7, r=["wdt", xnk(ti)], w=["pb0"])
                    P.act(dtmp[:, 0, :tn], pb[0][:32, :tn], AF.Exp, r=["pb0", "dtb"], w=["dtmp0"], bias=dtb[:, 0:1])
                    P.act(dtmp[:, 0, :tn], dtmp[:, 0, :tn], AF.Ln, r=["dtmp0", "dtb"], w=["dtmp0"], bias=dtb[:, 3:4])
                    P.ts("dve", dtmp[:, 1, :tn], dtmp[:, 0, :tn], dtb[:, 2:3], None, ALU.mult, r=["dtmp0", "dtb"], w=["dtmp1"])
                    if ti == 4:
                        P.copy("dve", dsT[:, :, :], dtmp[:, :, 0:NS], r=["dtmp0", "dtmp1"], w=["dsT"])
                        continue
                    for c4 in range(4):
                        c = ti * 4 + c4
                        P.tr(pb[1][:, 0:32], dtmp[:, 0, c4 * 128:(c4 + 1) * 128], ident_f[0:32, 0:32], r=["dtmp0", "cst"], w=["pb1"])
                        P.tr(pb[1][:, 32:64], dtmp[:, 1, c4 * 128:(c4 + 1) * 128], ident_f[0:32, 0:32], r=["dtmp1", "cst"], w=["pb1"])
                        P.copy("dve", ddt[:, c, :], pb[1][:, 0:64], r=["pb1"], w=["ddt.%d" % c])
                        P.mm(pb[2][:, 0:32], triu_f, ddt[:, c, 32:64], True, True, r=["cst", "ddt.%d" % c], w=["pb2"])
                        P.mm(pb[2][:, 32:64], ones_f, ddt[:, c, 32:64], True, True, r=["cst", "ddt.%d" % c], w=["pb2"])
                        P.act(ecum[:, c, :], pb[2][:, 0:32], AF.Exp, r=["pb2"], w=["ecum.%d" % c])
                        P.act(eL[:, c, :], pb[2][:, 32:64], AF.Exp, r=["pb2"], w=["eL.%d" % c])
                        P.copy("dve", csb[:, :], pb[2][:, 0:32], r=["pb2"], w=["csb"])
                        P.tt("dve", csb[:, :], pb[2][:, 32:64], csb[:, :], ALU.subtract, r=["pb2", "csb"], w=["csb"])
                        P.act(wend[:, c, :], csb[:, :], AF.Exp, r=["csb"], w=["wend.%d" % c])
                        P.tt("dve", wend[:, c, :], wend[:, c, :], ddt[:, c, 0:32], ALU.mult, r=["wend.%d" % c, "ddt.%d" % c], w=["wend.%d" % c])

                ngroups = cfg.get("ssd_groups", 8)

                def load_g(g, slot):
                    wb, wk = wblk[slot], "wblk%d" % slot
                    P.dma("pool", wb[:, :, 0:256], Win[:, :, 2048 + 256 * g: 2048 + 256 * (g + 1)], w=[wk + "x"], dkey=wk + "x")
                    P.dma("pool", wb[:, :, 256:384], Win[:, :, 4096 + 128 * g: 4096 + 128 * (g + 1)], w=[wk + "B"], dkey=wk + "B")
                    P.dma("pool", wb[:, :, 384:512], Win[:, :, 5120 + 128 * g: 5120 + 128 * (g + 1)], w=[wk + "C"], dkey=wk + "C")
                    P.dma("pool", wb[:, :, 512:768], Win[:, :, 256 * g: 256 * (g + 1)], w=[wk + "z"], dkey=wk + "z")

                load_g(0, 0)
                for g in range(ngroups):
                    slot = g % 2
                    wb, wk = wblk[slot], "wblk%d" % slot
                    wkeys = [wk + "x", wk + "x", wk + "B", wk + "C"]
                    ob, ok = wo[slot], "wo0"
                    P.dma("pool", ob[:, :, :], Wout[:, 2 * g:2 * g + 2, :], w=["wo0"], dkey="wo0")
                    if g + 1 < ngroups:
                        load_g(g + 1, 1 - slot)
                    jts = [2 * g, 2 * g + 1, 16 + g, 24 + g]
                    for j in range(4):
                        P.copy("dve", cwg[:, j, :], cw[:, jts[j], :], r=["cw"], w=["cwg"])
                        for k in range(4):
                            P.ts("dve", dg[:, j * 4 + k, :], ident_f, cw[:, jts[j], k:k + 1], None, ALU.mult, r=["cst", "cw"], w=["dg"])
                    for jl in range(2):
                        P.ts("dve", dgD[:, jl, :], ident_f, Dcol[:, 2 * g + jl:2 * g + jl + 1], None, ALU.mult, r=["cst", "Dcol"], w=["dgD"])
                    P.memset("dve", hT[:, :], 0.0, w=["hT"])
                    P.memset("dve", hTb[:, :], 0.0, w=["hTb"])
                    for j in range(4):
                        P.memset("dve", xr[j][0][:, 0:3], 0.0, w=["xr%d_0" % j])

                    def tile_prep(ti):
                        t0 = ti * 512
                        xb = xbc[ti % 2]
                        xbk = "xbc%d" % (ti % 2)
                        for j in range(4):
                            ps, pk = pb[0], "pb0"
                            xrb, xrk = xr[j][0], "xr%d_0" % j
                            for kc in range(8):
                                P.mm(ps[:, :], wb[:, kc, j * 128:(j + 1) * 128], xn[:, kc, t0:t0 + 512], kc == 0, kc == 7,
                                     r=[wkeys[j], xnk(ti)], w=[pk])
                            P.copy("act", xrb[:, 3:515], ps[:, :], r=[pk], w=[xrk])
                            if ti == 3:
                                P.copy("dve", ctail[:, j, :], ps[:, 509:512], r=[pk], w=["ctail"])
                            pc, pck = pb[6], "pb6"
                            for k in range(4):
                                P.mm(pc[:, :], dg[:, j * 4 + k, :], xrb[:, k:k + 512], k == 0, k == 3, r=["dg", xrk], w=[pck])
                            if ti < 3:
                                P.copy("pool", xrb[:, 0:3], xrb[:, 512:515], r=[xrk], w=[xrk])
                            P.act(xb[:, j, :], pc[:, :], AF.Silu, r=[pck, "cw"], w=[xbk + ".%d" % j], bias=cw[:, jts[j], 4:5])

                    def tile_z(ti):
                        for c4 in range(4):
                            tok = slice((ti * 4 + c4) * 128, (ti * 4 + c4 + 1) * 128)
                            for kc in range(8):
                                P.mm(pb[6][:, 0:256], xn[:, kc, tok], wb[:, kc, 512:768], kc == 0, kc == 7, r=[xnk(ti), wk + "z"], w=["pb6"])
                            P.act(sztile[:, c4, :], pb[6][:, 0:256], AF.Silu, r=["pb6"], w=["szt.%d" % c4])

                    def front(c):
                        ti, c4 = c // 4, c % 4
                        xb, xbk = xbc[ti % 2], "xbc%d" % (ti % 2)
                        cs_ = slice(c4 * 128, (c4 + 1) * 128)
                        tok = slice(c * 128, (c + 1) * 128)
                        xBt, xBk = xBtr[c % 2], "xBt%d" % (c % 2)
                        ysb, ysk = scrA[:, (c % 2) * 256:(c % 2 + 1) * 256], "scrA.%d" % (c % 2)
                        for j in range(3):
                            P.tr(pbT2[:, j * 128:(j + 1) * 128], xb[:, j, cs_], ident_b[:, :], r=[xbk + ".%d" % j, "ident_b"], w=["pb2"])
                        P.copy("dve", xBt[:, :], pbT2[:, 0:384], r=["pb2"], w=[xBk])
                        P.mm(pb[3][:, 0:128], xb[:, 2, cs_], xb[:, 3, cs_], True, True, r=[xbk + ".2", xbk + ".3"], w=["pb3"])
                        for hh in range(4):
                            h = 4 * g + hh
                            U, Uk = Ub[hh % 2], "Ub%d" % (hh % 2)
                            P.ts("dve", U[:, :], sgt_f, ddt[:, c, 32 + h:33 + h], None, ALU.mult, r=["cst", "ddt.%d" % c], w=[Uk])
                            P.mm(pb[4][:, hh * 128:(hh + 1) * 128], U[:, :], triu_f, True, False, r=[Uk, "cst"], w=["pb4"])
                            P.mm(pb[4][:, hh * 128:(hh + 1) * 128], negI[:, :], ltm_f, False, True, r=["negI", "cst"], w=["pb4"])
                        P.act(dec[:, :], pb[4][:, :], AF.Exp, r=["pb4"], w=["dec"])
                        for hh in range(4):
                            h = 4 * g + hh
                            P.stt(mixT[:, hh, :], dec[:, hh * 128:(hh + 1) * 128], ddt[:, c, h:h + 1], pb[3][:, 0:128],
                                  ALU.mult, ALU.mult, r=["dec", "ddt.%d" % c, "pb3"], w=["mixT"])
                        for jl in range(2):
                            P.mm(pb[5][:, jl * 128:(jl + 1) * 128], xb[:, jl, cs_], dgD[:, jl, :], True, False, r=[xbk + ".%d" % jl, "dgD"], w=["pb5"])
                            for hh in (2 * jl, 2 * jl + 1):
                                P.mm(pb[5][:, hh * 64:(hh + 1) * 64], mixT[:, hh, :], xBt[:, hh * 64:(hh + 1) * 64], False, hh == 2 * jl + 1,
                                     r=["mixT", xBk], w=["pb5"])
                        P.copy("act", ysb, pb[5][:, 0:256], r=["pb5"], w=[ysk])

                    def tail(c):
                        ti, c4 = c // 4, c % 4
                        xb, xbk = xbc[ti % 2], "xbc%d" % (ti % 2)
                        cs_ = slice(c4 * 128, (c4 + 1) * 128)
                        gb, gk = gT[ti % 2], "gT%d" % (ti % 2)
                        xBt, xBk = xBtr[c % 2], "xBt%d" % (c % 2)
                        szt, szk = sztile[:, c4, :], "szt.%d" % c4
                        ysb, ysk = scrA[:, (c % 2) * 256:(c % 2 + 1) * 256], "scrA.%d" % (c % 2)
                        t1, t2 = scrB[:, 0:256], scrB[:, 256:512]
                        P.mm(pb[7][:, 0:256], xb[:, 3, cs_], hTb[:, :], True, True, r=[xbk + ".3", "hTb"], w=["pb7"])
                        P.tt("dve", t1.rearrange("p (h q) -> p h q", h=4), pb[7][:, 0:256].rearrange("p (h q) -> p h q", h=4),
                             bc(ecum[:, c, 4 * g:4 * g + 4].rearrange("p (h o) -> p h o", o=1), [128, 4, 64]), ALU.mult,
                             r=["pb7", "ecum.%d" % c], w=["scrB.0"])
                        P.tt("dve", t2, ysb, t1, ALU.add, r=[ysk, "scrB.0"], w=["scrB.1"])
                        P.tt("pool", t2, t2, szt, ALU.mult, r=["scrB.1", szk], w=["scrB.1"])
                        P.act(t1, t2, AF.Square, r=["scrB.1"], w=["scrB.0", "ssq"], accum=ssq[:, 0:1])
                        rstd_pool(ssq[:, 1:2], ssq[:, 0:1], 1.0 / 256, ["ssq"], ["ssq"])
                        P.ts("dve", yn[:, :], t2, ssq[:, 1:2], None, ALU.mult, r=["scrB.1", "ssq"], w=["yn"])
                        for jl in range(2):
                            P.tr(pbT1[:, jl * 128:(jl + 1) * 128], yn[:, jl * 128:(jl + 1) * 128], ident_b[:, :], r=["yn", "ident_b"], w=["pb1"])
                        for jl in range(2):
                            P.ts("dve", gb[:, jl, cs_], pbT1[:, jl * 128:(jl + 1) * 128], gnw[:, 2 * g + jl:2 * g + jl + 1], None, ALU.mult,
                                 r=["pb1", "gnw"], w=[gk])
                        P.tt("pool", xw[:, :].rearrange("p (h q) -> p h q", h=4), xBt[:, 0:256].rearrange("p (h q) -> p h q", h=4),
                             bc(wend[:, c, 4 * g:4 * g + 4].rearrange("p (h o) -> p h o", o=1), [128, 4, 64]), ALU.mult,
                             r=[xBk, "wend.%d" % c], w=["xw"])
                        P.mm(pb[7][:, 256:512], xBt[:, 256:384], xw[:, :], True, True, r=[xBk, "xw"], w=["pb7"])
                        P.tt("pool", htmp[:, :].rearrange("p (h q) -> p h q", h=4), hT[:, :].rearrange("p (h q) -> p h q", h=4),
                             bc(eL[:, c, 4 * g:4 * g + 4].rearrange("p (h o) -> p h o", o=1), [128, 4, 64]), ALU.mult,
                             r=["hT", "eL.%d" % c], w=["htmp"])
                        P.tt("dve", hT[:, :], htmp[:, :], pb[7][:, 256:512], ALU.add, r=["htmp", "pb7"], w=["hT"])
                        P.copy("act", hTb[:, :], hT[:, :], r=["hT"], w=["hTb"])

                    def outproj(ti):
                        t0 = ti * 512
                        gb, gk = gT[ti % 2], "gT%d" % (ti % 2)
                        for dt_ in range(8):
                            po, pk = pb[dt_ % 2], "pb%d" % (dt_ % 2)
                            for jl in range(2):
                                P.mm(po[:, :], ob[:, jl, dt_ * 128:(dt_ + 1) * 128], gb[:, jl, :], jl == 0, jl == 1, r=[ok, gk], w=[pk])
                            P.tt("dve", xres[:, dt_, t0:t0 + 512], xres[:, dt_, t0:t0 + 512], po[:, :], ALU.add, r=[pk, xk(dt_, ti)], w=[xk(dt_, ti)])

                    tile_prep(0)
                    tile_z(0)
                    front(0)
                    for c in range(16):
                        if c + 1 < 16:
                            if (c + 1) % 4 == 0:
                                tile_prep((c + 1) // 4)
                            front(c + 1)
                        tail(c)
                        if c % 4 == 3:
                            outproj(c // 4)
                            if c < 15:
                                tile_z(c // 4 + 1)
                    for jl in range(2):
                        P.tr(pb[2][:, jl * 128:(jl + 1) * 128], hT[:, jl * 128:(jl + 1) * 128], ident_f, r=["hT", "cst"], w=["pb2"])
                    P.copy("dve", hout[:, :, :], pb[2][:, 0:256].rearrange("p (j n) -> p j n", j=2), r=["pb2"], w=["scrA.0"])
                    P.dma("sp", o_pssm[4 * g:4 * g + 4].rearrange("(j hh) q n -> (hh q) j n", j=2), hout[:, :, :], r=["scrA.0"], dkey="o_hout")
                    for j in range(4):
                        P.tr(pb[3][0:3, j * 128:(j + 1) * 128], ctail[:, j, :], ident_f, r=["ctail", "cst"], w=["pb3"])
                    P.copy("dve", ctout[:, :], pb[3][0:3, :], r=["pb3"], w=["scrB.0", "scrB.1"])
                    P.dma("sp", o_pconv[:, 256 * g:256 * (g + 1)], ctout[:, 0:256], r=["scrB.0", "scrB.1"], dkey="o_ctx")
                    P.dma("sp", o_pconv[:, 2048 + 128 * g:2048 + 128 * (g + 1)], ctout[:, 256:384], r=["scrB.0", "scrB.1"], dkey="o_ctB")
                    P.dma("sp", o_pconv[:, 3072 + 128 * g:3072 + 128 * (g + 1)], ctout[:, 384:512], r=["scrB.0", "scrB.1"], dkey="o_ctC")

                    if cfg.get('ssd_nosample'):
                        continue
                    ti = 4
                    for j in range(4):
                        for kc in range(8):
                            P.mm(pb[0][:, j * NS:(j + 1) * NS], wb[:, kc, j * 128:(j + 1) * 128], xn[:, kc, SEQ:NT], kc == 0, kc == 7,
                                 r=[wkeys[j], xnk(4)], w=["pb0"])
                    P.copy("dve", xrs[:, :, :], pb[0][:, 0:4 * NS].rearrange("p (j b) -> p j b", j=4), r=["pb0"], w=["xrs"])
                    cranges = [(256 * g, 256), (2048 + 128 * g, 128), (3072 + 128 * g, 128)]
                    off = 0
                    for (c0, cn) in cranges:
                        P.dma("sp", cstk[:, off:off + cn], st_conv[:, :, c0:c0 + cn].rearrange("b k c -> (b k) c"), w=["scrA.0", "scrA.1"], dkey="cstk%d" % off)
                        off += cn
                    for j in range(4):
                        P.tr(pb[1][:, j * 48:(j + 1) * 48], cstk[:, j * 128:(j + 1) * 128], ident_f[0:48, 0:48], r=["scrA.0", "scrA.1", "cst"], w=["pb1"])
                    P.copy("dve", cs[:, :, :], pb[1][:, 0:192].rearrange("p (j q) -> p j q", j=4), r=["pb1"], w=["cs"])
                    cs4 = cs[:, :, :].rearrange("p j (b k) -> p j b k", k=3)
                    ncs4 = ncs[:, :, :].rearrange("p j (b k) -> p j b k", k=3)
                    P.tt("dve", acc[:, :, :], xrs[:, :, :], bc(cwg[:, :, 3:4], [128, 4, NS]), ALU.mult, r=["xrs", "cwg"], w=["acc"])
                    P.tt("dve", acc[:, :, :], acc[:, :, :], bc(cwg[:, :, 4:5], [128, 4, NS]), ALU.add, r=["acc", "cwg"], w=["acc"])
                    for k in range(3):
                        P.tt("dve", atmp[:, :, :], cs4[:, :, :, k], bc(cwg[:, :, k:k + 1], [128, 4, NS]), ALU.mult, r=["cs", "cwg"], w=["atmp"])
                        P.tt("dve", acc[:, :, :], acc[:, :, :], atmp[:, :, :], ALU.add, r=["acc", "atmp"], w=["acc"])
                    P.act(xbs[:, :, :], acc[:, :, :], AF.Silu, r=["acc"], w=["xbs"])
                    P.copy("dve", ncs4[:, :, :, 0:2], cs4[:, :, :, 1:3], r=["cs"], w=["ncs"])
                    P.copy("dve", ncs4[:, :, :, 2], xrs[:, :, :], r=["xrs", "ncs"], w=["ncs"])
                    for j in range(4):
                        P.tr(pb[1][0:48, j * 128:(j + 1) * 128], ncs[:, j, :], ident_f, r=["ncs", "cst"], w=["pb1"])
                    P.copy("dve", cstk[:, :], pb[1][0:48, :], r=["pb1"], w=["scrA.0", "scrA.1"])
                    off = 0
                    for (c0, cn) in cranges:
                        P.dma("sp", o_sconv[:, :, c0:c0 + cn].rearrange("b k c -> (b k) c"), cstk[:, off:off + cn], r=["scrA.0", "scrA.1"], dkey="o_cstk%d" % off)
                        off += cn
                    for jl in range(2):
                        for kc in range(8):
                            P.mm(pb[2][:, jl * NS:(jl + 1) * NS], wb[:, kc, 512 + jl * 128:512 + (jl + 1) * 128], xn[:, kc, SEQ:NT], kc == 0, kc == 7,
                                 r=[wk + "z", xnk(4)], w=["pb2"])
                    P.act(szs[:, :, :], pb[2][:, 0:2 * NS].rearrange("p (j b) -> p j b", j=2), AF.Silu, r=["pb2"], w=["szs"])
                    P.dma("sp", Eg[:, :], consts2_d[:, 256 * g:256 * (g + 1)], w=["Eg"], dkey="Eg")
                    for jl in range(2):
                        for q in range(2):
                            P.mm(pb[2][:, 64 + (jl * 2 + q) * NS: 64 + (jl * 2 + q + 1) * NS], Eg[:, jl * 128:(jl + 1) * 128], dsT[:, q, :], True, True,
                                 r=["Eg", "dsT"], w=["pb2"])
                    dview = pb[2][:, 64:64 + 4 * NS].rearrange("p (j q b) -> p j q b", j=2, q=2)
                    P.copy("dve", dtc[:, :, 0, :], dview[:, :, 0, :], r=["pb2"], w=["dtc"])
                    P.act(dtc[:, :, 1, :], dview[:, :, 1, :], AF.Exp, r=["pb2"], w=["dtc"])
                    P.tt("dve", xdtc[:, :, :], xbs[:, 0:2, :], dtc[:, :, 0, :], ALU.mult, r=["xbs", "dtc"], w=["xdtc"])
                    for which, j, base in (("B", 2, 0), ("C", 3, 4)):
                        P.tt("dve", rhsB[:, :, :], bc(ident_b[:, :].rearrange("p (o n) -> p o n", o=1), [128, NS, 128]),
                             bc(xbs[:, j, :].rearrange("p (b o) -> p b o", o=1), [128, NS, 128]), ALU.mult, r=["ident_b", "xbs"], w=["rhsB"])
                        for q in range(4):
                            P.mm(pb[base + q][:, :], ones_b[:, :], rhsB[:, 4 * q:4 * q + 4, :], True, True, r=["ones_b", "rhsB"],
                                 w=["pb%d" % (base + q)])
                    for jl in range(2):
                        for q in range(4):
                            hb, hk = h0t[q % 2], "h0t%d" % (q % 2)
                            nb, nk = hnt[q % 2], "hnt%d" % (q % 2)
                            bs = slice(4 * q, 4 * q + 4)
                            bk = ["pb%d" % q]
                            ck = ["pb%d" % (4 + q)]
                            P.dma("sp", hb[:, :, :], st_ssm[4 * q:4 * q + 4, 4 * g + 2 * jl:4 * g + 2 * jl + 2].rearrange("b h q n -> (h q) b n"),
                                  w=[hk], dkey=hk)
                            P.tt("dve", s1[:, :, :], pb[q][:, :].rearrange("p (b n) -> p b n", b=4),
                                 bc(xdtc[:, jl, bs].rearrange("p (b o) -> p b o", o=1), [128, 4, 128]), ALU.mult, r=bk + ["xdtc"], w=["s1"])
                            P.tt("pool", hb[:, :, :], hb[:, :, :],
                                 bc(dtc[:, jl, 1, bs].rearrange("p (b o) -> p b o", o=1), [128, 4, 128]), ALU.mult, r=[hk, "dtc"], w=[hk])
                            P.tt("pool", nb[:, :, :], hb[:, :, :], s1[:, :, :], ALU.add, r=["s1", hk], w=[nk])
                            P.tt("dve", s1[:, :, :], nb[:, :, :], pb[4 + q][:, :].rearrange("p (b n) -> p b n", b=4), ALU.mult,
                                 r=[nk] + ck, w=["s1"])
                            P.S.add("dve", (lambda e, o=ys[:, jl, bs], i=s1[:, :, :]: e.tensor_reduce(out=o, in_=i, axis=AX.X, op=ALU.add)),
                                    ["s1"], ["ys"])
                            P.dma("pool", o_sssm[4 * q:4 * q + 4, 4 * g + 2 * jl:4 * g + 2 * jl + 2].rearrange("b h q n -> (h q) b n"), nb[:, :, :],
                                  r=[nk], dkey="o_" + nk)
                    P.tt("dve", gsm[:, :, :], xbs[:, 0:2, :], bc(Dcol[:, 2 * g:2 * g + 2].rearrange("p (j o) -> p j o", o=1), [128, 2, NS]), ALU.mult,
                         r=["xbs", "Dcol"], w=["gsm"])
                    P.tt("dve", gsm[:, :, :], gsm[:, :, :], ys[:, :, :], ALU.add, r=["gsm", "ys"], w=["gsm"])
                    P.tt("dve", gsm[:, :, :], gsm[:, :, :], szs[:, :, :], ALU.mult, r=["gsm", "szs"], w=["gsm"])
                    P.tt("dve", gsq[:, :, :], gsm[:, :, :], gsm[:, :, :], ALU.mult, r=["gsm"], w=["gsq"])
                    for jl in range(2):
                        P.mm(pb[0][:, 0:NS], ones_b[:, :], gsq[:, jl, :], jl == 0, jl == 1, r=["ones_b", "gsq"], w=["pb0"])
                    P.copy("dve", rsd[:, :], pb[0][:, 0:NS], r=["pb0"], w=["rsd"])
                    rstd_pool(rsd[:, :], rsd[:, :], 1.0 / 256, ["rsd"], ["rsd"])
                    for jl in range(2):
                        P.stt(gTs[:, jl, :], gsm[:, jl, :], gnw[:, 2 * g + jl:2 * g + jl + 1], rsd[:, :], ALU.mult, ALU.mult,
                              r=["gsm", "gnw", "rsd"], w=["gTs"])
                    for dt_ in range(8):
                        po, pk = pb[dt_ % 2], "pb%d" % (dt_ % 2)
                        for jl in range(2):
                            P.mm(po[:, 0:NS], ob[:, jl, dt_ * 128:(dt_ + 1) * 128], gTs[:, jl, :], jl == 0, jl == 1, r=[ok, "gTs"], w=[pk])
                        P.tt("dve", xres[:, dt_, SEQ:NT], xres[:, dt_, SEQ:NT], po[:, 0:NS], ALU.add, r=[pk, xk(dt_, 4)], w=[xk(dt_, 4)])
                P.S.barrier()

        def layer_mlstm():
            li = 2
            emit_norm(li)
            Win = W["l2_in_proj"].rearrange("(k p) j -> p k j", p=128)
            Wout = W["l2_out_proj"].rearrange("(k p) j -> p k j", p=128)
            st_C, st_n, st_m, st_conv = DR["state_l2_C"], DR["state_l2_n"], DR["state_l2_m"], DR["state_l2_conv"]
            o_pC, o_pn, o_pm, o_pconv = DR["p2_C"], DR["p2_n"], DR["p2_m"], DR["p2_conv"]
            o_sC, o_sn, o_sm, o_sconv = DR["s2_C"], DR["s2_n"], DR["s2_m"], DR["s2_conv"]
            bmask = cst[:, 640:768]
            gemask = cst[:, 768:896]
            ones_f = cst[:, 128:256]
            KS = 512.0 ** -0.5
            with ExitStack() as L:
                cw = P.sb(L, "cw", [128, 16, 5], F32)
                wq4 = P.sb(L, "wq4", [128, 16, 4, 4], F32)
                wif = P.sb(L, "wif", [128, 48, 8], BF16)
                mhw = P.sb(L, "mhw", [128, 16], F32)
                skp = P.sb(L, "skp", [128, 16], F32)
                bocol = P.sb(L, "bocol", [128, 16], F32)
                gacc = P.sb(L, "gacc", [128, 17, 8], F32)
                xcs_all = P.sb(L, "xcs_all", [128, 16, NS], F32)
                xms_all = P.sb(L, "xms_all", [128, 16, NS], F32)
                dg = P.sb(L, "dg", [128, 16, 128], BF16)
                Dm = P.sb(L, "Dm", [128, 16, 128], BF16)
                xr = [P.sb(L, "xr%d" % j, [128, 515], BF16) for j in range(4)]
                xc = P.sb(L, "xc", [128, 4, 512], BF16)
                qT = P.sb(L, "qT", [128, 4, 512], BF16)
                khT = P.sb(L, "khT", [128, 4, 512], BF16)
                szT = P.sb(L, "szT", [128, 4, 512], BF16)
                gT = P.sb(L, "gT", [128, 4, 512], BF16)
                wbuf = P.sb(L, "wbuf", [128, 8, 1024], BF16)
                wo = P.sb(L, "wo", [128, 4, D], BF16)
                ctail = P.sb(L, "ctail", [128, 3], F32)
                ctout = P.sb(L, "ctout", [3, 128], F32)
                cstk = P.sb(L, "cstk", [48, 128], F32)
                cs = P.sb(L, "cs", [128, 48], F32)
                ncs = P.sb(L, "ncs", [128, 48], F32)
                sA = P.sb(L, "sA", [128, 4, NS], F32)
                sB = P.sb(L, "sB", [128, 4, NS], BF16)
                gcol = P.sb(L, "gcol", [128, 128], F32)
                ga = P.sb(L, "ga", [64, 128], F32)
                gs1 = P.sb(L, "gs1", [64, 8], F32)
                ends4 = P.sb(L, "ends4", [4, 16, 2], F32)
                m4 = P.sb(L, "m4", [4, 2, 16], F32)
                tokq = P.sb(L, "tokq", [128, 5, 64], F32)
                vtok = P.sb(L, "vtok", [128, 512], BF16)
                kwt = P.sb(L, "kwt", [128, 512], BF16)
                otok = P.sb(L, "otok", [128, 512], BF16)
                bobc = P.sb(L, "bobc", [128, 512], F32)
                Sm = P.sb(L, "Sm", [128, 128], F32)
                Eg_ = P.sb(L, "Eg_", [128, 128], F32)
                wls = P.sb(L, "wls", [128, 128], BF16)
                wT = P.sb(L, "wT", [128, 128], BF16)
                Asb = P.sb(L, "Asb", [128, 512], F32)
                num = P.sb(L, "num", [128, 512], F32)
                sm = P.sb(L, "sm", [128, 16], F32)
                st6 = P.sb(L, "st6", [128, 8], F32)
                hn = P.sb(L, "hn", [128, 512], BF16)
                g1 = P.sb(L, "g1", [128, 128], F32)
                sxc = P.sb(L, "sxc", [128, 128], F32)
                CT = P.sb(L, "CT", [128, 4, 512], F32)
                CTb = P.sb(L, "CTb", [128, 4, 512], BF16)
                nst = P.sb(L, "nst", [128, 4], F32)
                nstb = P.sb(L, "nstb", [128, 4], BF16)
                Cout = P.sb(L, "Cout", [128, 512], F32)
                sg = P.sb(L, "sg", [NS, 64], F32)
                n0t = P.sb(L, "n0t", [NS, 512], F32)
                qtok = P.sb(L, "qtok", [NS, 512], F32)
                wid = P.sb(L, "wid", [NS, NS, 8], F32)
                wibc = P.sb(L, "wibc", [128, NS, 8], F32)
                Cnt1 = P.sb(L, "Cnt1", [128, 512], F32)
                numT = P.sb(L, "numT", [128, 4, NS], F32)
                hsT = P.sb(L, "hsT", [128, 4, NS], F32)
                hs2 = P.sb(L, "hs2", [128, 4, NS], F32)
                mrs = P.sb(L, "mrs", [128, 4, NS], F32)
                gTs = P.sb(L, "gTs", [128, 4, NS], BF16)
                pbT1 = pb[1].bitcast(BF16)
                scr = P.nc.dram_tensor("ml_scr", [64, 4], F32, kind="Internal").ap()

                for k in range(4):
                    P.dma("sp", cw[:, :, k], W["l2_conv_w"][k].rearrange("(j p) -> p j", p=128), w=["cw"], dkey="cw%d" % k, nonc=True)
                P.dma("sp", cw[:, :, 4], W["l2_conv_b"].rearrange("(j p) -> p j", p=128), w=["cw"], dkey="cw4", nonc=True)
                for pi, nm in enumerate(("w_q", "w_k", "w_v", "w_o")):
                    P.dma("sp", wq4[:, :, pi, :], W["l2_" + nm].rearrange("n j i -> (n j) i").rearrange("(t p) i -> p t i", p=128),
                          w=["wq4"], dkey="wq4%d" % pi, nonc=True)
                P.dma("pool", wif[:, :, :], W["l2_w_if"].rearrange("(t p) g -> p t g", p=128), w=["wif"], dkey="wif")
                P.dma("sp", mhw[:, :], W["l2_mh_norm_w"].rearrange("(k p) -> p k", p=128), w=["mhw"], dkey="mhw", nonc=True)
                P.dma("sp", skp[:, :], W["l2_skip"].rearrange("(k p) -> p k", p=128), w=["skp"], dkey="skp", nonc=True)
                P.dma("sp", bocol[:, :], W["l2_b_o"].rearrange("(k p) -> p k", p=128), w=["bocol"], dkey="bocol", nonc=True)
                bif2 = W["l2_b_if"].rearrange("(g o) -> g o", o=1)
                for h in range(4):
                    P.dma("sp", gs1[h * 16:(h + 1) * 16, 0:1], bc(bif2[h:h + 1, :], [16, 1]), w=["gs1"], dkey="gs1a%d" % h, nonc=True)
                    P.dma("sp", gs1[h * 16:(h + 1) * 16, 1:2], bc(bif2[4 + h:5 + h, :], [16, 1]), w=["gs1"], dkey="gs1b%d" % h, nonc=True)
                P.ts("dve", gs1[:, 2:3], gs1[:, 1:2], -1.0, None, ALU.mult, r=["gs1"], w=["gs1"])
                P.memset("dve", gs1[:, 3:4], 1.0, w=["gs1"])
                P.memset("dve", wid[:, :, :], 0.0, w=["wid"])

                def build_D(dst_idx, jt, pi):
                    P.tt("dve", Dm[:, dst_idx, :].rearrange("p (n i) -> p n i", i=4),
                         bc(wq4[:, jt, pi:pi + 1, :], [128, 32, 4]), bmask.rearrange("p (n i) -> p n i", i=4), ALU.mult,
                         r=["wq4", "cst"], w=["Dm"])

                def build_dg(dst_base, jt):
                    for k in range(4):
                        P.ts("dve", dg[:, dst_base + k, :], ident_f, cw[:, jt, k:k + 1], None, ALU.mult, r=["cst", "cw"], w=["dg"])

                def proj_conv(jt, jl, ti, wcols, wkey):
                    t0 = ti * 512
                    for kc in range(8):
                        P.mm(pb[0][:, :], wbuf[:, kc, wcols:wcols + 128], xn[:, kc, t0:t0 + 512], kc == 0, kc == 7, r=[wkey, xnk(ti)], w=["pb0"])
                    P.copy("act", xr[jl][:, 3:515], pb[0][:, :], r=["pb0"], w=["xr%d" % jl])
                    for k in range(4):
                        P.mm(pb[1][:, :], dg[:, jl * 4 + k, :], xr[jl][:, k:k + 512], k == 0, k == 3, r=["dg", "xr%d" % jl], w=["pb1"])
                    P.act(xc[:, jl, :], pb[1][:, :], AF.Silu, r=["pb1", "cw"], w=["xc.%d" % jl], bias=cw[:, jt, 4:5])

                stgA = cfg.get("ml_stage", 9)
                for jt in range(16 if stgA >= 1 else 0):
                    slot = jt % 4
                    wk = "wA%d" % slot
                    P.dma("pool", wbuf[:, :, slot * 128:(slot + 1) * 128], Win[:, :, jt * 128:(jt + 1) * 128], w=[wk], dkey=wk)
                    build_dg(0, jt)
                    for pi in range(3):
                        build_D(pi * 4, jt, pi)
                    P.memset("dve", xr[0][:, 0:3], 0.0, w=["xr0"])
                    for ti in range(4):
                        proj_conv(jt, 0, ti, slot * 128, wk)
                        if ti == 3:
                            P.copy("dve", ctail[:, :], pb[0][:, 509:512], r=["pb0"], w=["ctail"])
                        else:
                            P.copy("pool", xr[0][:, 0:3], xr[0][:, 512:515], r=["xr0"], w=["xr0"])
                        P.mm(pb[2][:, :], Dm[:, 0, :], xc[:, 0, :], True, True, r=["Dm", "xc.0"], w=["pb2"])
                        P.mm(pb[3][:, :], Dm[:, 4, :], xc[:, 0, :], True, True, r=["Dm", "xc.0"], w=["pb3"])
                        P.mm(pb[4][:, :], Dm[:, 8, :], xr[0][:, 3:515], True, True, r=["Dm", "xr0"], w=["pb4"])
                        P.copy("act", qT[:, 0, :], pb[2][:, :], r=["pb2"], w=["qT"])
                        P.copy("dve", khT[:, 0, :], pb[3][:, :], r=["pb3"], w=["khT"])
                        P.copy("act", szT[:, 0, :], pb[4][:, :], r=["pb4"], w=["szT"])
                        for c4 in range(4):
                            cs_ = slice(c4 * 128, (c4 + 1) * 128)
                            o_ = pb[5][:, c4 * 8:(c4 + 1) * 8]
                            P.mm(o_, qT[:, 0, cs_], wif[:, jt, :], True, False, r=["qT", "wif"], w=["pb5"])
                            P.mm(o_, khT[:, 0, cs_], wif[:, 16 + jt, :], False, False, r=["khT", "wif"], w=["pb5"])
                            P.mm(o_, szT[:, 0, cs_], wif[:, 32 + jt, :], False, True, r=["szT", "wif"], w=["pb5"])
                        gv = gacc[:, 4 * ti:4 * ti + 4, :]
                        pv = pb[5][:, 0:32].rearrange("p (c g) -> p c g", g=8)
                        if jt == 0:
                            P.copy("dve", gv, pv, r=["pb5"], w=["gacc.%d" % ti])
                        else:
                            P.tt("dve", gv, gv, pv, ALU.add, r=["pb5", "gacc.%d" % ti], w=["gacc.%d" % ti])
                    P.tr(pb[6][0:3, 0:128], ctail[:, :], ident_f, r=["ctail", "cst"], w=["pb6"])
                    P.copy("dve", ctout[:, :], pb[6][0:3, 0:128], r=["pb6"], w=["ctout"])
                    P.dma("sp", o_pconv[:, jt * 128:(jt + 1) * 128], ctout[:, :], r=["ctout"], dkey="o_ctout")
                    for kc in range(8):
                        P.mm(pb[6][:, 128:128 + NS], wbuf[:, kc, slot * 128:(slot + 1) * 128], xn[:, kc, SEQ:NT], kc == 0, kc == 7, r=[wk, xnk(4)], w=["pb6"])
                    P.copy("dve", xms_all[:, jt, :], pb[6][:, 128:128 + NS], r=["pb6"], w=["xms_all"])
                    P.dma("sp", cstk[:, :], st_conv[:, :, jt * 128:(jt + 1) * 128].rearrange("b k c -> (b k) c"), w=["cstk"], dkey="cstk")
                    P.tr(pb[6][:, 256:304], cstk[:, :], ident_f[0:48, 0:48], r=["cstk", "cst"], w=["pb6"])
                    P.copy("dve", cs[:, :], pb[6][:, 256:304], r=["pb6"], w=["cs"])
                    cs3 = cs[:, :].rearrange("p (b k) -> p b k", k=3)
                    ncs3 = ncs[:, :].rearrange("p (b k) -> p b k", k=3)
                    a0 = sA[:, 0, :]
                    a1 = sA[:, 1, :]
                    P.ts("dve", a0, xms_all[:, jt, :], cw[:, jt, 3:4], cw[:, jt, 4:5], ALU.mult, ALU.add, r=["xms_all", "cw"], w=["sA"])
                    for k in range(3):
                        P.stt(a0, cs3[:, :, k], cw[:, jt, k:k + 1], a0, ALU.mult, ALU.add, r=["cs", "cw", "sA"], w=["sA"])
                    P.act(xcs_all[:, jt, :], a0, AF.Silu, r=["sA"], w=["xcs_all"])
                    P.copy("dve", ncs3[:, :, 0:2], cs3[:, :, 1:3], r=["cs"], w=["ncs"])
                    P.copy("dve", ncs3[:, :, 2], xms_all[:, jt, :], r=["xms_all", "ncs"], w=["ncs"])
                    P.tr(pb[6][0:48, 320:448], ncs[:, :], ident_f, r=["ncs", "cst"], w=["pb6"])
                    P.copy("dve", cstk[:, :], pb[6][0:48, 320:448], r=["pb6"], w=["cstk"])
                    P.dma("sp", o_sconv[:, :, jt * 128:(jt + 1) * 128].rearrange("b k c -> (b k) c"), cstk[:, :], r=["cstk"], dkey="o_cstk")
                    P.copy("dve", sB[:, 0, :], xcs_all[:, jt, :], r=["xcs_all"], w=["sB"])
                    P.copy("dve", sB[:, 1, :], xms_all[:, jt, :], r=["xms_all"], w=["sB"])
                    for pi in range(3):
                        P.mm(pb[7][:, pi * NS:(pi + 1) * NS], Dm[:, pi * 4, :], sB[:, 0 if pi < 2 else 1, :], True, True, r=["Dm", "sB"], w=["pb7"])
                    P.copy("dve", sB[:, 2:4, :].rearrange("p a b -> p (a b)")[:, 0:2 * NS], pb[7][:, 0:2 * NS], r=["pb7"], w=["sB2"])
                    P.copy("dve", sA[:, 2, :].bitcast(BF16)[:, 0:NS], pb[7][:, 2 * NS:3 * NS], r=["pb7"], w=["sA2"])
                    sv = sA[:, 2, :].bitcast(BF16)[:, 0:NS]
                    P.mm(pb[7][0:NS, 64:72], sB[:, 2, :], wif[:, jt, :], True, False, r=["sB2", "wif"], w=["pb7"])
                    P.mm(pb[7][0:NS, 64:72], sB[:, 3, :], wif[:, 16 + jt, :], False, False, r=["sB2", "wif"], w=["pb7"])
                    P.mm(pb[7][0:NS, 64:72], sv, wif[:, 32 + jt, :], False, True, r=["sA2", "wif"], w=["pb7"])
                    if jt == 0:
                        P.copy("dve", gacc[0:NS, 16, :], pb[7][0:NS, 64:72], r=["pb7"], w=["gacc.4"])
                    else:
                        P.tt("dve", gacc[0:NS, 16, :], gacc[0:NS, 16, :], pb[7][0:NS, 64:72], ALU.add, r=["pb7", "gacc.4"], w=["gacc.4"])

                if stgA >= 2:
                    gk = ["gacc.%d" % i for i in range(4)]
                    P.copy("dve", gcol[:, :].rearrange("p (g c) -> p c g", c=16), gacc[:, 0:16, :], r=gk, w=["gcol"])
                    P.memset("dve", num[0:64, 256:384], 1.0, w=["gq7"])
                    P.tr(pb[0][0:64, 0:128], gcol[:, 0:64], ident_f, r=["gcol", "cst"], w=["pb0"])
                    P.tr(pb[0][0:64, 128:256], gcol[:, 64:128], ident_f, r=["gcol", "cst"], w=["pb0"])
                    ig = ga[:, :]
                    lf, cm, wi_, em_ = [Asb[0:64, i * 128:(i + 1) * 128] for i in range(4)]
                    we_, dpb, one_ = [num[0:64, i * 128:(i + 1) * 128] for i in range(3)]
                    P.ts("dve", ig, pb[0][0:64, 0:128], gs1[:, 0:1], None, ALU.add, r=["pb0", "gs1"], w=["gq0"])
                    P.act(lf, pb[0][0:64, 128:256], AF.Exp, r=["pb0", "gs1"], w=["gq1"], bias=gs1[:, 2:3], scale=-1.0)
                    P.act(lf, lf, AF.Ln, r=["gq1", "gs1"], w=["gq1"], bias=gs1[:, 3:4])
                    P.ts("dve", lf, lf, -1.0, None, ALU.mult, r=["gq1"], w=["gq1"])
                    P.S.add("dve", (lambda e: e.tensor_tensor_scan(out=lf, data0=one_, data1=lf, initial=0.0, op0=ALU.mult, op1=ALU.add)),
                            ["gq1", "gq7"], ["gq1"])
                    P.tt("dve", ig, ig, lf, ALU.subtract, r=["gq0", "gq1"], w=["gq0"])
                    P.S.add("dve", (lambda e: e.tensor_tensor_scan(out=cm, data0=ig, data1=ig, initial=-1e30, op0=ALU.max, op1=ALU.max)),
                            ["gq0"], ["gq2"])
                    P.copy("dve", gs1[:, 4:5], cm[:, 127:128], r=["gq2"], w=["gs1e"])
                    P.copy("dve", gs1[:, 5:6], lf[:, 127:128], r=["gq1"], w=["gs1e"])
                    P.dma("sp", scr[:, 0:2], gs1[:, 4:6], r=["gs1e"], w=["scr01"], dkey="scr_a", nonc=True)
                    P.dma("sp", ends4[:, :, :], scr[:, 0:2].rearrange("(h c) t -> h c t", c=16), r=["scr01"], w=["ends4"], dkey="scr_b", nonc=True)
                    P.S.add("dve", (lambda e: e.tensor_tensor_scan(out=m4[:, 0, :], data0=ends4[:, :, 0], data1=ends4[:, :, 1], initial=0.0,
                                                                   op0=ALU.max, op1=ALU.add)), ["ends4"], ["m4"])
                    P.memset("dve", m4[:, 1, 0:1], 0.0, w=["m4b"])
                    P.copy("dve", m4[:, 1, 1:16], m4[:, 0, 0:15], r=["m4"], w=["m4b"])
                    P.dma("sp", o_pm.rearrange("(h o) -> h o", o=1), m4[:, 0, 15:16], r=["m4"], dkey="o_pm", nonc=True)
                    P.dma("sp", scr[:, 2:3].rearrange("(h c) t -> h c t", c=16), m4[:, 1, :].rearrange("h (c o) -> h c o", o=1), r=["m4b"], w=["scr2"],
                          dkey="scr_c", nonc=True)
                    P.dma("sp", gs1[:, 6:7], scr[:, 2:3], r=["scr2"], w=["gs1m"], dkey="scr_d", nonc=True)
                    mprev = gs1[:, 6:7]
                    P.ts("dve", cm, cm, mprev, None, ALU.max, r=["gq2", "gs1m"], w=["gq2"])
                    P.act(wi_, cm, AF.Exp, r=["gq2", "gs1m"], w=["gq3"], bias=mprev, scale=-1.0)
                    P.tt("dve", em_, lf, cm, ALU.add, r=["gq1", "gq2"], w=["gq4"])
                    P.act(em_, em_, AF.Exp, r=["gq4"], w=["gq4"], scale=-1.0)
                    P.ts("dve", gs1[:, 7:8], cm[:, 127:128], -1.0, math.log(KS), ALU.mult, ALU.add, r=["gq2"], w=["gs1n"])
                    P.act(we_, ig, AF.Exp, r=["gq0", "gs1n"], w=["gq5"], bias=gs1[:, 7:8])
                    P.tt("dve", gs1[:, 4:5], mprev, cm[:, 127:128], ALU.subtract, r=["gs1m", "gq2", "gs1e"], w=["gs1e"])
                    P.act(gs1[:, 4:5], gs1[:, 4:5], AF.Exp, r=["gs1e"], w=["gs1e"])
                    P.ts("dve", dpb, one_, gs1[:, 4:5], None, ALU.mult, r=["gq7", "gs1e"], w=["gq6"])
                    for qi, src in enumerate((cm, wi_, em_, we_, dpb)):
                        P.tr(pb[2][:, qi * 64:(qi + 1) * 64], src, ident_f[0:64, 0:64], r=["gq%d" % (2 + qi), "cst"], w=["pb2"])
                    P.copy("dve", tokq[:, :, :], pb[2][:, 0:320].rearrange("p (q x) -> p q x", q=5), r=["pb2"], w=["tokq"])

                    P.dma("sp", sg[:, 28:36], bc(W["l2_b_if"].rearrange("(o g) -> o g", o=1), [NS, 8]), w=["sg"], dkey="sg_b")
                    P.dma("sp", sg[:, 8:12], st_m[:, :], w=["sg"], dkey="sg_m")
                    P.tt("dve", sg[:, 0:8], gacc[0:NS, 16, :], sg[:, 28:36], ALU.add, r=["gacc.4", "sg"], w=["sg"])
                    P.act(sg[:, 4:8], sg[:, 4:8], AF.Exp, r=["sg"], w=["sg"], scale=-1.0)
                    P.act(sg[:, 4:8], sg[:, 4:8], AF.Ln, r=["sg", "gs1"], w=["sg"], bias=gs1[0:NS, 3:4])
                    P.ts("dve", sg[:, 4:8], sg[:, 4:8], -1.0, None, ALU.mult, r=["sg"], w=["sg"])
                    P.tt("dve", sg[:, 8:12], sg[:, 8:12], sg[:, 4:8], ALU.add, r=["sg"], w=["sg"])
                    P.tt("dve", sg[:, 12:16], sg[:, 8:12], sg[:, 0:4], ALU.max, r=["sg"], w=["sg"])
                    P.dma("sp", o_sm[:, :], sg[:, 12:16], r=["sg"], dkey="o_sm")
                    P.tt("dve", sg[:, 16:20], sg[:, 0:4], sg[:, 12:16], ALU.subtract, r=["sg"], w=["sg"])
                    P.act(sg[:, 16:20], sg[:, 16:20], AF.Exp, r=["sg"], w=["sg"])
                    P.tt("dve", sg[:, 20:24], sg[:, 8:12], sg[:, 12:16], ALU.subtract, r=["sg"], w=["sg"])
                    P.act(sg[:, 20:24], sg[:, 20:24], AF.Exp, r=["sg"], w=["sg"])
                    P.act(sg[:, 24:28], sg[:, 12:16], AF.Exp, r=["sg"], w=["sg"], scale=-1.0)
                    P.tt("dve", wid[:, :, 0:4], bc(sg[:, 20:24].rearrange("p (o h) -> p o h", o=1), [NS, NS, 4]),
                         bc(ident_f[0:NS, 0:NS].rearrange("p (b o) -> p b o", o=1), [NS, NS, 4]), ALU.mult, r=["sg", "cst"], w=["wid"])

                nheads = cfg.get("ml_heads", 4) if stgA >= 3 else 0
                for h in range(nheads):
                    P.dma("pool", wbuf[:, :, 0:512], Win[:, :, 512 * h:512 * (h + 1)], w=["wA0", "wA1", "wA2", "wA3"], dkey="wBx")
                    P.dma("pool", wbuf[:, :, 512:1024], Win[:, :, 2048 + 512 * h:2048 + 512 * (h + 1)], w=["wBz"], dkey="wBz")
                    P.dma("pool", wo[:, :, :], Wout[:, 4 * h:4 * h + 4, :], w=["wo"], dkey="wo")
                    P.dma("sp", bobc[:, :], bc(W["l2_b_o"][512 * h:512 * (h + 1)].rearrange("(o n) -> o n", o=1), [128, 512]), w=["bobc"], dkey="bobc")
                    wxk = ["wA0", "wA1", "wA2", "wA3"]
                    for jl in range(4):
                        build_dg(jl * 4, 4 * h + jl)
                        for pi in range(4):
                            build_D(pi * 4 + jl, 4 * h + jl, pi)
                        P.memset("dve", xr[jl][:, 0:3], 0.0, w=["xr%d" % jl])
                    P.memset("dve", CT[:, :, :], 0.0, w=["CT"])
                    P.memset("pool", CTb[:, :, :], 0.0, w=["CTb"])
                    P.memset("dve", nst[:, :], 0.0, w=["nst"])
                    P.memset("dve", nstb[:, :], 0.0, w=["nstb"])
                    for ti in range(4):
                        t0 = ti * 512
                        for jl in range(4):
                            jt = 4 * h + jl
                            proj_conv(jt, jl, ti, jl * 128, wxk[jl])
                            if ti < 3:
                                P.copy("pool", xr[jl][:, 0:3], xr[jl][:, 512:515], r=["xr%d" % jl], w=["xr%d" % jl])
                            P.mm(pb[2][:, :], Dm[:, 0 + jl, :], xc[:, jl, :], True, True, r=["Dm", "xc.%d" % jl], w=["pb2"])
                            P.mm(pb[3][:, :], Dm[:, 4 + jl, :], xc[:, jl, :], True, True, r=["Dm", "xc.%d" % jl], w=["pb3"])
                            P.copy("dve", qT[:, jl, :], pb[2][:, :], r=["pb2"], w=["qT"])
                            P.act(khT[:, jl, :], pb[3][:, :], AF.Copy, r=["pb3"], w=["khT"], scale=KS)
                            for kc in range(8):
                                P.mm(pb[4][:, :], wbuf[:, kc, 512 + jl * 128:512 + (jl + 1) * 128], xn[:, kc, t0:t0 + 512], kc == 0, kc == 7,
                                     r=["wBz", xnk(ti)], w=["pb4"])
                            P.act(szT[:, jl, :], pb[4][:, :], AF.Silu, r=["pb4"], w=["szT"])
                        for c4 in range(4):
                            c = 4 * ti + c4
                            hc = h * 16 + c
                            cs_ = slice(c4 * 128, (c4 + 1) * 128)
                            xs_ = slice(3 + c4 * 128, 3 + (c4 + 1) * 128)
                            for jl in range(4):
                                js = slice(jl * 128, (jl + 1) * 128)
                                P.mm(pb[2][:, js], xr[jl][:, xs_], Dm[:, 8 + jl, :], True, True, r=["xr%d" % jl, "Dm"], w=["pb2"])
                                P.mm(pb[3][:, js], xc[:, jl, cs_], Dm[:, 4 + jl, :], True, True, r=["xc.%d" % jl, "Dm"], w=["pb3"])
                                P.mm(pb[4][:, js], xr[jl][:, xs_], Dm[:, 12 + jl, :], True, True, r=["xr%d" % jl, "Dm"], w=["pb4"])
                            P.copy("act", vtok[:, :], pb[2][:, :], r=["pb2"], w=["vtok"])
                            P.act(kwt[:, :], pb[3][:, :], AF.Copy, r=["pb3", "tokq"], w=["kwt"], scale=tokq[:, 3, hc:hc + 1])
                            P.tt("dve", num[:, :], pb[4][:, :], bobc[:, :], ALU.add, r=["pb4", "bobc"], w=["num"])
                            P.act(otok[:, :], num[:, :], AF.Tanh, r=["num"], w=["otok"], scale=0.5)
                            P.ts("pool", otok[:, :], otok[:, :], 0.5, 0.5, ALU.mult, ALU.add, r=["otok"], w=["otok"])
                            for jl in range(4):
                                P.mm(pb[5][:, 0:128], qT[:, jl, cs_], khT[:, jl, cs_], jl == 0, jl == 3, r=["qT", "khT"], w=["pb5"])
                            P.mm(pb[5][:, 128:256], bc(ident_f[0:64, hc:hc + 1], [64, 128]), ga[:, :], True, True, r=["cst", "gq0"], w=["pb5"])
                            for jl in range(4):
                                P.mm(pb[5][:, 256:257], qT[:, jl, cs_], nstb[:, jl:jl + 1], jl == 0, jl == 3, r=["qT", "nstb"], w=["pb5"])
                            P.tt("dve", Sm[:, :], pb[5][:, 0:128], gemask, ALU.mult, r=["pb5", "cst"], w=["Sm"])
                            P.ts("dve", Eg_[:, :], pb[5][:, 128:256], tokq[:, 0, hc:hc + 1], 0.0, ALU.subtract, ALU.min, r=["pb5", "tokq"], w=["Eg_"])
                            P.act(Eg_[:, :], Eg_[:, :], AF.Exp, r=["Eg_"], w=["Eg_"])
                            P.S.add("dve", (lambda e, o=wls[:, :], a=Eg_[:, :], b=Sm[:, :], ac=sm[:, 0:1]:
                                            e.scalar_tensor_tensor(out=o, in0=a, scalar=1.0, in1=b, op0=ALU.mult, op1=ALU.mult, accum_out=ac)),
                                    ["Eg_", "Sm"], ["wls", "sm0"])
                            P.tr(pbT1[:, 0:128], wls[:, :], ident_b[:, :], r=["wls", "ident_b"], w=["pb1"])
                            P.copy("dve", wT[:, :], pbT1[:, 0:128], r=["pb1"], w=["wT"])
                            P.mm(pb[6][:, :], wT[:, :], vtok[:, :], True, True, r=["wT", "vtok"], w=["pb6"])
                            for jl in range(4):
                                P.mm(pb[7][:, :], qT[:, jl, cs_], CTb[:, jl, :], jl == 0, jl == 3, r=["qT", "CTb"], w=["pb7"])
                            P.copy("act", Asb[:, :], pb[6][:, :], r=["pb6"], w=["Asb"])
                            wic = tokq[:, 1, hc:hc + 1]
                            P.stt(num[:, :], pb[7][:, :], wic, Asb[:, :], ALU.mult, ALU.add, r=["pb7", "tokq", "Asb"], w=["num"])
                            P.stt(sm[:, 1:2], pb[5][:, 256:257], wic, sm[:, 0:1], ALU.mult, ALU.add, r=["pb5", "tokq", "sm0"], w=["sm1"])
                            P.stt(sm[:, 1:2], sm[:, 1:2], -1.0, sm[:, 1:2], ALU.mult, ALU.max, r=["sm1"], w=["sm1"])
                            P.tt("dve", sm[:, 1:2], sm[:, 1:2], tokq[:, 2, hc:hc + 1], ALU.max, r=["sm1", "tokq"], w=["sm1"])
                            P.recip(sm[:, 1:2], sm[:, 1:2], r=["sm1"], w=["sm1"])
                            P.stt(num[:, :], num[:, :], sm[:, 1:2], otok[:, :], ALU.mult, ALU.mult, r=["num", "sm1", "otok"], w=["num"])
                            P.S.add("dve", (lambda e, o=st6[:, 0:6], i=num[:, :]: e.bn_stats(out=o, in_=i)), ["num"], ["st6"])
                            P.S.add("dve", (lambda e, o=sm[:, 2:4], i=st6[:, 0:6]: e.bn_aggr(out=o, in_=i)), ["st6"], ["sm2"])
                            rstd_pool(sm[:, 4:5], sm[:, 3:4], 1.0, ["sm2"], ["sm4"])
                            P.stt(sm[:, 5:6], sm[:, 2:3], -1.0, sm[:, 4:5], ALU.mult, ALU.mult, r=["sm2", "sm4"], w=["sm5"])
                            P.ts("dve", hn[:, :], num[:, :], sm[:, 4:5], sm[:, 5:6], ALU.mult, ALU.add, r=["num", "sm4", "sm5"], w=["hn"])
                            for jl in range(4):
                                P.tr(pbT1[:, 256 + jl * 128:256 + (jl + 1) * 128], hn[:, jl * 128:(jl + 1) * 128], ident_b[:, :], r=["hn", "ident_b"], w=["pb1"])
                            for jl in range(4):
                                jt = 4 * h + jl
                                P.ts("dve", sxc[:, :], xc[:, jl, cs_], skp[:, jt:jt + 1], None, ALU.mult, r=["xc.%d" % jl, "skp"], w=["sxc"])
                                P.stt(g1[:, :], pbT1[:, 256 + jl * 128:256 + (jl + 1) * 128], mhw[:, jt:jt + 1], sxc[:, :], ALU.mult, ALU.add,
                                      r=["pb1", "mhw", "sxc"], w=["g1"])
                                P.tt("pool", gT[:, jl, cs_], g1[:, :], szT[:, jl, cs_], ALU.mult, r=["g1", "szT"], w=["gT"])
                            dpc = tokq[:, 4, hc:hc + 1]
                            for kt in range(4):
                                pu, puk = (pb[0], "pb0") if kt % 2 == 0 else (pb[6], "pb6")
                                P.mm(pu[:, :], kwt[:, kt * 128:(kt + 1) * 128], vtok[:, :], True, True, r=["kwt", "vtok"], w=[puk])
                                P.stt(CT[:, kt, :], CT[:, kt, :], dpc, pu[:, :], ALU.mult, ALU.add, r=["CT", "tokq", puk], w=["CT"])
                                P.copy("act", CTb[:, kt, :], CT[:, kt, :], r=["CT"], w=["CTb"])
                                P.mm(pb[5][:, 260 + kt:261 + kt], kwt[:, kt * 128:(kt + 1) * 128], ones_b[:, 0:1], True, True, r=["kwt", "ones_b"], w=["pb5"])
                            P.stt(nst[:, :], nst[:, :], dpc, pb[5][:, 260:264], ALU.mult, ALU.add, r=["nst", "tokq", "pb5"], w=["nst"])
                            P.copy("dve", nstb[:, :], nst[:, :], r=["nst"], w=["nstb"])
                        for dt_ in range(8):
                            po, pk = pb[dt_ % 2], "pb%d" % (dt_ % 2)
                            for jl in range(4):
                                P.mm(po[:, :], wo[:, jl, dt_ * 128:(dt_ + 1) * 128], gT[:, jl, :], jl == 0, jl == 3, r=["wo", "gT"], w=[pk])
                            P.tt("dve", xres[:, dt_, t0:t0 + 512], xres[:, dt_, t0:t0 + 512], po[:, :], ALU.add, r=[pk, xk(dt_, ti)], w=[xk(dt_, ti)])
                    for vt in range(4):
                        for kt in range(4):
                            P.tr(pb[2][:, kt * 128:(kt + 1) * 128], CT[:, kt, vt * 128:(vt + 1) * 128], ident_f, r=["CT", "cst"], w=["pb2"])
                        P.copy("dve", Cout[:, :], pb[2][:, :], r=["pb2"], w=["Cout"])
                        P.dma("sp", o_pC[h, vt * 128:(vt + 1) * 128, :], Cout[:, :], r=["Cout"], dkey="o_Cout")
                    P.tr(pb[3][0:4, 0:128], nst[:, :], ident_f, r=["nst", "cst"], w=["pb3"])
                    P.copy("dve", ctout[:, :].bitcast(F32)[0:3, :], pb[3][0:3, 0:128], r=["pb3"], w=["ctout"]) if False else None
                    P.copy("dve", st6[0:4, 0:8].bitcast(F32), pb[3][0:4, 0:8], r=["pb3"], w=["st6"]) if False else None
                    P.copy("dve", Cout[0:4, 0:128], pb[3][0:4, 0:128], r=["pb3"], w=["Cout"])
                    P.dma("sp", o_pn[h].rearrange("(kt k) -> kt k", k=128), Cout[0:4, 0:128], r=["Cout"], dkey="o_pn")

                    if cfg.get("ml_nosample"):
                        continue
                    for jl in range(4):
                        jt = 4 * h + jl
                        P.copy("dve", sB[:, 0, :], xcs_all[:, jt, :], r=["xcs_all"], w=["sB"])
                        P.copy("dve", sB[:, 1, :], xms_all[:, jt, :], r=["xms_all"], w=["sB"])
                        js = slice(jl * 128, (jl + 1) * 128)
                        P.mm(pb[2][0:NS, js], sB[:, 0, :], Dm[:, 4 + jl, :], True, True, r=["sB", "Dm"], w=["pb2"])
                        P.mm(pb[3][0:NS, js], sB[:, 0, :], Dm[:, 0 + jl, :], True, True, r=["sB", "Dm"], w=["pb3"])
                        P.mm(pb[4][0:NS, js], sB[:, 1, :], Dm[:, 8 + jl, :], True, True, r=["sB", "Dm"], w=["pb4"])
                        P.mm(pb[5][:, jl * NS:(jl + 1) * NS], Dm[:, 12 + jl, :], sB[:, 1, :], True, True, r=["sB", "Dm"], w=["pb5"])
                        for kc in range(8):
                            P.mm(pb[5][:, 64 + jl * NS:64 + (jl + 1) * NS], wbuf[:, kc, 512 + jl * 128:512 + (jl + 1) * 128], xn[:, kc, SEQ:NT], kc == 0, kc == 7,
                                 r=["wBz", xnk(4)], w=["pb5"])
                        P.ts("dve", mrs[:, jl, :], pb[5][:, jl * NS:(jl + 1) * NS], bocol[:, jt:jt + 1], None, ALU.add, r=["pb5", "bocol"], w=["mrs"])
                        P.act(mrs[:, jl, :], mrs[:, jl, :], AF.Tanh, r=["mrs"], w=["mrs"], scale=0.5)
                        P.ts("dve", mrs[:, jl, :], mrs[:, jl, :], 0.5, 0.5, ALU.mult, ALU.add, r=["mrs"], w=["mrs"])
                        P.act(hs2[:, jl, :], pb[5][:, 64 + jl * NS:64 + (jl + 1) * NS], AF.Silu, r=["pb5"], w=["hs2"])
                    ktok, vwt, kmk, qmk = bobc[0:NS, :], otok[0:NS, :], vtok[0:NS, :], kwt[0:NS, :]
                    C0t, Cnt = [Asb, num], [Cout, Cnt1]
                    P.ts("dve", ktok, pb[2][0:NS, :], KS, None, ALU.mult, r=["pb2"], w=["bobc"])
                    P.copy("dve", qtok[:, :], pb[3][0:NS, :], r=["pb3"], w=["qtok"])
                    P.ts("dve", vwt, pb[4][0:NS, :], sg[:, 16 + h:17 + h], None, ALU.mult, r=["pb4", "sg"], w=["otok"])
                    P.dma("sp", n0t[:, :], st_n[:, h, :], w=["n0t"], dkey="n0t")
                    P.ts("dve", n0t[:, :], n0t[:, :], sg[:, 20 + h:21 + h], None, ALU.mult, r=["n0t", "sg"], w=["n0t"])
                    P.stt(n0t[:, :], ktok, sg[:, 16 + h:17 + h], n0t[:, :], ALU.mult, ALU.add, r=["bobc", "sg", "n0t"], w=["n0t"])
                    P.dma("sp", o_sn[:, h, :], n0t[:, :], r=["n0t"], dkey="o_sn")
                    P.S.add("dve", (lambda e, o=Cout[0:NS, :], a=n0t[:, :], b=qtok[:, :], ac=sg[:, 36:37]:
                                    e.scalar_tensor_tensor(out=o, in0=a, scalar=1.0, in1=b, op0=ALU.mult, op1=ALU.mult, accum_out=ac)),
                            ["n0t", "qtok", "Cout"], ["Cout", "sg36"])
                    P.stt(sg[:, 36:37], sg[:, 36:37], -1.0, sg[:, 36:37], ALU.mult, ALU.max, r=["sg36"], w=["sg36"])
                    P.tt("dve", sg[:, 36:37], sg[:, 36:37], sg[:, 24 + h:25 + h], ALU.max, r=["sg36", "sg"], w=["sg36"])
                    P.recip(sg[:, 36:37], sg[:, 36:37], r=["sg36"], w=["sg36"])
                    P.tt("dve", wid[:, :, 4:5], bc(sg[:, 36:37].rearrange("p (o h) -> p o h", o=1), [NS, NS, 1]),
                         bc(ident_f[0:NS, 0:NS].rearrange("p (b o) -> p b o", o=1), [NS, NS, 1]), ALU.mult, r=["sg36", "cst"], w=["wid"])
                    P.mm(pb[5][:, 128:256], ones_f[0:NS, :], wid[:, :, :].rearrange("p b x -> p (b x)"), True, True, r=["cst", "wid"], w=["pb5"])
                    P.copy("dve", wibc[:, :, :], pb[5][:, 128:256].rearrange("p (b x) -> p b x", x=8), r=["pb5"], w=["wibc"])
                    for b in range(NS):
                        P.ts("dve", kmk, ktok, ident_f[0:NS, b:b + 1], None, ALU.mult, r=["bobc", "cst"], w=["vtok"])
                        P.ts("dve", qmk, qtok[:, :], ident_f[0:NS, b:b + 1], None, ALU.mult, r=["qtok", "cst"], w=["kwt"])
                        P.mm(pb[7][:, :], ones_b[0:NS, :], qmk, True, True, r=["ones_b", "kwt"], w=["pb7"])
                        for vt in range(4):
                            i2 = (b * 4 + vt) % 2
                            c0, c0k = C0t[i2], ("Asb", "num")[i2]
                            cn, cnk = Cnt[i2], ("Cout", "Cnt1")[i2]
                            pu, puk = (pb[2], "pb2") if vt % 2 == 0 else (pb[3], "pb3")
                            P.dma("sp", c0[:, :], st_C[b, h, vt * 128:(vt + 1) * 128, :], w=[c0k], dkey=c0k)
                            P.mm(pu[:, :], vwt[:, vt * 128:(vt + 1) * 128], kmk, True, True, r=["otok", "vtok"], w=[puk])
                            P.stt(cn[:, :], c0[:, :], wibc[:, b, h:h + 1], pu[:, :], ALU.mult, ALU.add, r=[c0k, "wibc", puk], w=[cnk])
                            P.dma("pool", o_sC[b, h, vt * 128:(vt + 1) * 128, :], cn[:, :], r=[cnk], dkey="o_" + cnk)
                            P.S.add("dve", (lambda e, o=c0[:, :], a=cn[:, :], bb=pb[7][:, :], ac=numT[:, vt, b:b + 1]:
                                            e.scalar_tensor_tensor(out=o, in0=a, scalar=1.0, in1=bb, op0=ALU.mult, op1=ALU.mult, accum_out=ac)),
                                    [cnk, "pb7", c0k], [c0k, "numT"])
                    P.tt("dve", hsT[:, :, :], numT[:, :, :], bc(wibc[:, :, 4:5].rearrange("p b o -> p o b"), [128, 4, NS]), ALU.mult, r=["numT", "wibc"], w=["hsT"])
                    P.tt("dve", hsT[:, :, :], hsT[:, :, :], mrs[:, :, :], ALU.mult, r=["hsT", "mrs"], w=["hsT"])
                    P.tt("dve", numT[:, :, :], hsT[:, :, :], hsT[:, :, :], ALU.mult, r=["hsT"], w=["numT"])
                    for vt in range(4):
                        P.mm(pb[4][:, 0:NS], ones_f, hsT[:, vt, :], vt == 0, vt == 3, r=["cst", "hsT"], w=["pb4"])
                    for vt in range(4):
                        P.mm(pb[4][:, NS:2 * NS], ones_f, numT[:, vt, :], vt == 0, vt == 3, r=["cst", "numT"], w=["pb4"])
                    mean = sA[:, 0, :]
                    var = sA[:, 1, :]
                    P.ts("dve", mean, pb[4][:, 0:NS], 1.0 / 512, None, ALU.mult, r=["pb4"], w=["sA"])
                    P.ts("dve", var, pb[4][:, NS:2 * NS], 1.0 / 512, None, ALU.mult, r=["pb4"], w=["sA"])
                    P.tt("dve", sA[:, 3, :], mean, mean, ALU.mult, r=["sA"], w=["sA"])
                    P.tt("dve", var, var, sA[:, 3, :], ALU.subtract, r=["sA"], w=["sA"])
                    rstd_pool(var, var, 1.0, ["sA"], ["sA"])
                    P.tt("dve", hsT[:, :, :], hsT[:, :, :], bc(mean.rearrange("p (o b) -> p o b", o=1), [128, 4, NS]), ALU.subtract, r=["hsT", "sA"], w=["hsT"])
                    P.tt("dve", hsT[:, :, :], hsT[:, :, :], bc(var.rearrange("p (o b) -> p o b", o=1), [128, 4, NS]), ALU.mult, r=["hsT", "sA"], w=["hsT"])
                    for jl in range(4):
                        jt = 4 * h + jl
                        P.ts("dve", hsT[:, jl, :], hsT[:, jl, :], mhw[:, jt:jt + 1], None, ALU.mult, r=["hsT", "mhw"], w=["hsT"])
                        P.stt(hsT[:, jl, :], xcs_all[:, jt, :], skp[:, jt:jt + 1], hsT[:, jl, :], ALU.mult, ALU.add, r=["xcs_all", "skp", "hsT"], w=["hsT"])
                    P.tt("dve", gTs[:, :, :], hsT[:, :, :], hs2[:, :, :], ALU.mult, r=["hsT", "hs2"], w=["gTs"])
                    for dt_ in range(8):
                        po, pk = pb[dt_ % 2], "pb%d" % (dt_ % 2)
                        for jl in range(4):
                            P.mm(po[:, 0:NS], wo[:, jl, dt_ * 128:(dt_ + 1) * 128], gTs[:, jl, :], jl == 0, jl == 3, r=["wo", "gTs"], w=[pk])
                        P.tt("dve", xres[:, dt_, SEQ:NT], xres[:, dt_, SEQ:NT], po[:, 0:NS], ALU.add, r=[pk, xk(dt_, 4)], w=[xk(dt_, 4)])
                P.S.barrier()

        for li_ in (0, 1, 2, 3):
            if li_ in layers:
                if li_ in (0, 3):
                    layer_ssd(li_)
                elif li_ == 1:
                    layer_gmlp()
                else:
                    layer_mlstm()

        with ExitStack() as L:
            ystage = [P.sb(L, "ystage%d" % i, [128, D], F32) for i in range(2)]
            ysq = P.sb(L, "ysq", [128, D], F32)
            fnw_r = P.sb(L, "fnw_r", [128, D], F32)
            ss = P.sb(L, "ss", [128, 2], F32)
            if final_norm:
                P.dma("sp", fnw_r[:, :], bc(W["final_norm_w"].rearrange("(o n) -> o n", o=1), [128, D]), w=["fnw_r"], dkey="fnw_r")
            for c in range(17):
                if c < 16:
                    t0, M, ti = c * 128, 128, c // 4
                    dst = y_p[t0:t0 + 128, :]
                else:
                    t0, M, ti = SEQ, NS, 4
                    dst = y_s[:, :]
                yb = ystage[c % 2]
                yk = "ystage%d" % (c % 2)
                for h in range(2):
                    ps = pb[(2 * c + h) % 8]
                    pk = "pb%d" % ((2 * c + h) % 8)
                    for k4 in range(4):
                        kc = h * 4 + k4
                        P.tr(ps[:M, k4 * 128:(k4 + 1) * 128], xres[:, kc, t0:t0 + M], ident_f, r=[xk(kc, ti), "cst"], w=[pk])
                    P.copy("act" if h else "dve", yb[:M, h * 512:(h + 1) * 512], ps[:M, :], r=[pk], w=[yk + ".%d" % h])
                if final_norm:
                    P.act(ysq[:M, :], yb[:M, :], AF.Square, r=[yk + ".0", yk + ".1"], w=["ysq", "ss"], accum=ss[:M, 0:1])
                    P.act(ss[:M, 1:2], ss[:M, 0:1], AF.Sqrt, r=["ss", "epsc"], w=["ss"], bias=epsc[:M, 0:1], scale=1.0 / D)
                    P.recip(ss[:M, 1:2], ss[:M, 1:2], r=["ss"], w=["ss"])
                    P.stt(yb[:M, :], yb[:M, :], ss[:M, 1:2], fnw_r[:M, :], ALU.mult, ALU.mult,
                          r=[yk + ".0", yk + ".1", "ss", "fnw_r"], w=[yk + ".0", yk + ".1"])
                P.dma("sp", dst, yb[:M, :], r=[yk + ".0", yk + ".1"], dkey="o_" + yk)

        P.S.emit(root)
    return P


def make_consts():
    c = np.zeros((128, 7 * 128), np.float32)
    i = np.arange(128)
    c[:, 0:128] = np.eye(128, dtype=np.float32)
    c[:, 128:256] = 1.0
    c[:, 256:384] = (i[:, None] <= i[None, :]).astype(np.float32)
    c[:, 384:512] = (i[:, None] > i[None, :]).astype(np.float32)
    c[:, 512:640] = (i[None, :] < i[:, None]).astype(np.float32)
    c[:, 640:768] = (i[:, None] // 4 == i[None, :] // 4).astype(np.float32)
    c[:, 768:896] = (i[:, None] >= i[None, :]).astype(np.float32)
    return c


_CACHE = {}


def run(cfg, inputs, ncores=NCORES):
    key = repr(sorted(cfg.items()))
    if key not in _CACHE:
        _CACHE[key] = build(cfg)
    P = _CACHE[key]
    missing = [n for n in ALL_INPUT_NAMES if n not in inputs]
    assert not missing or len(cfg.get('layers', ())) < 4, missing
    consts = make_consts()
    in_maps = []
    for c in range(ncores):
        m = {}
        for name in P.dram:
            t = P.dram[name]
            if name in ("y_prompt", "y_sample") or name.startswith(("p0_", "s0_", "s1_", "p2_", "s2_", "p3_", "s3_")):
                continue
            if name == "consts":
                m[name] = consts
            elif name == "consts2":
                m[name] = (np.arange(32)[:, None] == (np.arange(2048)[None, :] // 64)).astype(np.float32)
            elif name == "x_prompt":
                m[name] = np.ascontiguousarray(inputs["x_prompt"][c])
            elif name == "x_sample":
                m[name] = np.ascontiguousarray(inputs["x_sample"][c * NS:(c + 1) * NS, 0, :])
            elif name.startswith("state_"):
                m[name] = np.ascontiguousarray(inputs[name][c * NS:(c + 1) * NS])
            else:
                m[name] = np.ascontiguousarray(inputs[name])
        in_maps.append(m)
    res = run_bass_kernel_spmd(P.nc, in_maps, core_ids=list(range(ncores)))
    return res.results


def kernel(**inputs):
    cfg = {"layers": (0, 1, 2, 3), "final_norm": True}
    r = run(cfg, inputs)
    def pstack(name):
        return np.stack([r[c][name] for c in range(NCORES)], 0)
    def scat(name):
        return np.concatenate([r[c][name] for c in range(NCORES)], 0)
    outs = [pstack("y_prompt"), scat("y_sample")[:, None, :]]
    outs += [pstack("p0_ssm"), pstack("p0_conv"), scat("s0_ssm"), scat("s0_conv")]
    outs += [scat("s1_v")[:, None, :]]
    outs += [pstack("p2_C"), pstack("p2_n"), pstack("p2_m"), pstack("p2_conv"), scat("s2_C"), scat("s2_n"), scat("s2_m"), scat("s2_conv")]
    outs += [pstack("p3_ssm"), pstack("p3_conv"), scat("s3_ssm"), scat("s3_conv")]
    return tuple(np.ascontiguousarray(o, dtype=np.float32) for o in outs)
```

```python
import sys
import math
from contextlib import ExitStack
import numpy as np
import concourse.bass as bass
import concourse.mybir as mybir
from concourse.bass_utils import run_bass_kernel_spmd

F32 = mybir.dt.float32
BF16 = mybir.dt.bfloat16
AF = mybir.ActivationFunctionType
ALU = mybir.AluOpType
AX = mybir.AxisListType

ALL_INPUT_NAMES = (
    "x_prompt",
    "x_sample",
    "state_l0_ssm",
    "state_l0_conv",
    "state_l2_C",
    "state_l2_n",
    "state_l2_m",
    "state_l2_conv",
    "state_l3_ssm",
    "state_l3_conv",
    "l0_norm_w",
    "l0_in_proj",
    "l0_conv_w",
    "l0_conv_b",
    "l0_dt_bias",
    "l0_A_log",
    "l0_D_skip",
    "l0_gnorm_w",
    "l0_out_proj",
    "l1_norm_w",
    "l1_in_proj",
    "l1_v_ln_w",
    "l1_v_ln_b",
    "l1_spatial_w",
    "l1_spatial_b",
    "l1_out_proj",
    "l2_norm_w",
    "l2_in_proj",
    "l2_conv_w",
    "l2_conv_b",
    "l2_w_q",
    "l2_w_k",
    "l2_w_v",
    "l2_w_o",
    "l2_b_o",
    "l2_w_if",
    "l2_b_if",
    "l2_mh_norm_w",
    "l2_skip",
    "l2_out_proj",
    "l3_norm_w",
    "l3_in_proj",
    "l3_conv_w",
    "l3_conv_b",
    "l3_dt_bias",
    "l3_A_log",
    "l3_D_skip",
    "l3_gnorm_w",
    "l3_out_proj",
    "final_norm_w",
)

NCORES = 8
D = 1024
SEQ = 2048
NS = 16
NT = SEQ + NS
EPS = 1e-6
TT = [(0, 512), (512, 512), (1024, 512), (1536, 512), (2048, 16)]
NEG = -30000.0


class Op:
    __slots__ = ("eng", "fn", "deps", "dkey", "token", "hasdep", "idx", "where", "seg", "cost")


class Sched:
    ENGS = ("pe", "act", "dve", "pool", "sp")
    EPOCH = 12000

    def __init__(self, nc):
        self.nc = nc
        self.ops = []
        self.lw = {}
        self.rd = {}
        self.last_dma = {}
        self.last_on = {}
        self.psum_rd = {}
        self.seg = 0
        self.keep_all_readers = False

    def add(self, eng, fn, reads=(), writes=(), dkey=None, cost=None):
        op = Op()
        op.eng, op.fn, op.dkey, op.hasdep, op.token = eng, fn, dkey, False, None
        op.idx = len(self.ops)
        op.cost = cost if cost is not None else {"pe": 200, "act": 550, "dve": 450, "pool": 650, "sp": 100}.get(eng, 300)
        op.seg = self.seg
        f = sys._getframe(1)
        wl = []
        while f is not None and len(wl) < 4:
            wl.append(f.f_lineno)
            f = f.f_back
        op.where = wl
        deps = {}
        for k in reads:
            w = self.lw.get(k)
            if w is not None:
                deps[w.idx] = w
            if k.startswith("pb") and eng in ("act", "dve"):
                bank = k.split(".")[0]
                lst_ = self.psum_rd.setdefault(bank, [])
                for r_ in lst_:
                    if r_.eng != eng:
                        deps[r_.idx] = r_
                lst_.append(op)
        for k in writes:
            w = self.lw.get(k)
            if w is not None:
                deps[w.idx] = w
            for r in self.rd.get(k, ()):
                deps[r.idx] = r
        op.deps = list(deps.values())
        for d in op.deps:
            if not (d.eng == "pe" and eng == "pe" and d.dkey is None and dkey is None):
                d.hasdep = True
        for k in reads:
            lst = self.rd.setdefault(k, [])
            if dkey is None and not self.keep_all_readers:
                lst[:] = [r for r in lst if r.dkey is not None or r.eng != eng]
            lst.append(op)
        for k in writes:
            self.lw[k] = op
            self.rd[k] = []
            if k.startswith("pb"):
                self.psum_rd[k.split(".")[0]] = []
        self.ops.append(op)
        if dkey is not None:
            self.last_dma[dkey] = op
        else:
            self.last_on[eng] = op
        return op

    def barrier(self):
        prev = list(self.last_on.values()) + list(self.last_dma.values())
        for e in self.ENGS:
            op = self.add(e, None)
            op.deps = list(prev)
            for d in prev:
                d.hasdep = True
        self.lw.clear()
        self.rd.clear()
        self.seg += 1

    def reorder(self, sync_lat=150, dma_lat=2500):
        import heapq
        out = []
        n = len(self.ops)
        i = 0
        while i < n:
            j = i
            seg = self.ops[i].seg
            while j < n and self.ops[j].seg == seg:
                j += 1
            ops = self.ops[i:j]
            body = [o for o in ops if o.fn is not None]
            bars = [o for o in ops if o.fn is None]
            inseg = {id(o) for o in body}
            succ = {id(o): [] for o in body}
            ndep = {}
            for o in body:
                ds = [d for d in o.deps if id(d) in inseg]
                ndep[id(o)] = len(ds)
                for d in ds:
                    succ[id(d)].append(o)
            finish = {}
            eng_free = {}
            heaps = {}
            ready_t = {}

            def push(o):
                rt = 0
                for d in o.deps:
                    if id(d) in finish:
                        lat = 0 if (d.eng == o.eng == "pe" and d.dkey is None) else sync_lat
                        rt = max(rt, finish[id(d)] + lat)
                ready_t[id(o)] = rt
                heapq.heappush(heaps.setdefault(o.eng, []), (o.idx, id(o), o))

            for o in body:
                if ndep[id(o)] == 0:
                    push(o)
            order = []
            LOOK = 24
            while True:
                best = None
                for eng, h in heaps.items():
                    if not h:
                        continue
                    ef = eng_free.get(eng, 0)
                    cands = heapq.nsmallest(LOOK, h)
                    c = min(cands, key=lambda t: (max(ef, ready_t[t[1]]), t[0]))
                    st = max(ef, ready_t[c[1]])
                    if best is None or (st, c[0]) < (best[0], best[1][0]):
                        best = (st, c, eng)
                if best is None:
                    break
                st, c, eng = best
                heaps[eng].remove(c)
                heapq.heapify(heaps[eng])
                o = c[2]
                if o.dkey is not None:
                    eng_free[eng] = st + o.cost
                    finish[id(o)] = st + dma_lat
                else:
                    eng_free[eng] = st + o.cost
                    finish[id(o)] = st + o.cost
                order.append((st, o))
                for s_ in succ[id(o)]:
                    ndep[id(s_)] -= 1
                    if ndep[id(s_)] == 0:
                        push(s_)
            assert len(order) == len(body), (len(order), len(body))
            order.sort(key=lambda t: (t[0], t[1].idx))
            out.extend(o for _, o in order)
            out.extend(bars)
            i = j
        self.ops = out
        for k, o in enumerate(self.ops):
            o.idx = k

    def emit(self, stack):
        nc = self.nc
        cnt = {}
        sems = {}

        def sem_for(key):
            if key not in sems:
                sems[key] = stack.enter_context(nc.semaphore("s%d" % len(sems)))
            return sems[key]

        slot_of = {}
        cur_seg = -1
        for op in self.ops:
            if op.seg != cur_seg:
                cur_seg = op.seg
                slot_of = {}
            if op.dkey is not None:
                sk_ = (op.eng, op.dkey)
                if sk_ not in slot_of:
                    slot_of[sk_] = (op.eng, sum(1 for x in slot_of if x[0] == op.eng))
                k = ("dma", slot_of[sk_])
                cnt[k] = cnt.get(k, 0) + 16
                op.token = (k, cnt[k])
            elif op.hasdep:
                c = cnt.get(op.eng, 0) + 1
                cnt[op.eng] = c
                ep = (c - 1) // self.EPOCH
                op.token = ((op.eng, ep), c - ep * self.EPOCH)
        for op in self.ops:
            if op.token is not None:
                sem_for(op.token[0])
        final = [v.token for k, v in self.last_dma.items()]
        block = stack.enter_context(nc.Block())
        handles = {"pe": nc.tensor, "act": nc.scalar, "dve": nc.vector, "pool": nc.gpsimd, "sp": nc.sync}
        deco = {"pe": block.tensor, "act": block.scalar, "dve": block.vector, "pool": block.gpsimd, "sp": block.sync}
        for eng in self.ENGS:
            myops = [o for o in self.ops if o.eng == eng]

            def body(e, myops=myops, eng=eng):
                waited = {}
                for op in myops:
                    need = {}
                    for d in op.deps:
                        if d.token is None:
                            continue
                        if d.dkey is None and d.eng == eng == "pe":
                            continue
                        sk, val = d.token
                        if val > need.get(sk, 0):
                            need[sk] = val
                    for sk, val in need.items():
                        if waited.get(sk, 0) >= val:
                            continue
                        waited[sk] = val
                        e.wait_ge(sems[sk], val)
                    if op.fn is None:
                        if op.token is not None:
                            e.nop().then_inc(sems[op.token[0]], 1)
                        continue
                    try:
                        inst = op.fn(e)
                    except Exception:
                        print('EMIT FAILED for op added at lines', op.where, 'eng', op.eng)
                        raise
                    if op.token is not None:
                        inst.then_inc(sems[op.token[0]], 16 if op.dkey is not None else 1)
                if eng == "sp":
                    for sk, val in final:
                        e.wait_ge(sems[sk], val)

            deco[eng](body)
        self.nsems = len(sems)


class Prog:
    def __init__(self, cfg):
        self.cfg = cfg
        self.nc = bass.Bass("TRN2", target_bir_lowering=False)
        self.S = Sched(self.nc)
        self.dram = {}
        self.uid = 0

    def din(self, name, shape):
        t = self.nc.dram_tensor(name, list(shape), F32, kind="ExternalInput")
        self.dram[name] = t
        return t.ap()

    def dout(self, name, shape):
        t = self.nc.dram_tensor(name, list(shape), F32, kind="ExternalOutput")
        self.dram[name] = t
        return t.ap()

    def sb(self, stack, name, shape, dt):
        self.uid += 1
        return stack.enter_context(self.nc.sbuf_tensor("%s_%d" % (name, self.uid), list(shape), dt))

    def key(self, base):
        self.uid += 1
        return "%s#%d" % (base, self.uid)

    def op(self, eng, fn, r=(), w=()):
        return self.S.add(eng, fn, r, w)

    def dma(self, q, out, in_, r=(), w=(), dkey=None, nonc=False):
        if nonc:
            fn = lambda e: e.dma_start(out=out, in_=in_, allow_slow_non_contiguous=True)
        else:
            fn = lambda e: e.dma_start(out=out, in_=in_)
        return self.S.add(q, fn, r, w, dkey=dkey)

    def mm(self, out, lhsT, rhs, start, stop, r=(), w=()):
        nfree = int(np.prod(rhs.shape[1:]))
        cost = max(64, nfree) / 1.2 * (2 if rhs.dtype == F32 else 1) + 20
        return self.S.add("pe", lambda e: e.matmul(out, lhsT=lhsT, rhs=rhs, start=start, stop=stop), r, w, cost=cost)

    def tr(self, out, in_, ident, r=(), w=()):
        return self.S.add("pe", lambda e: e.transpose(out, in_, ident), r, w, cost=130)

    def act(self, out, in_, func, r=(), w=(), bias=None, scale=None, accum=None):
        kw = {}
        if bias is not None:
            kw["bias"] = bias
        if scale is not None:
            kw["scale"] = scale
        if accum is not None:
            kw["accum_out"] = accum
        return self.S.add("act", lambda e: e.activation(out=out, in_=in_, func=func, **kw), r, w, cost=300 + 0.7 * int(np.prod(out.shape[1:])))

    def tt(self, eng, out, in0, in1, op, r=(), w=()):
        return self.S.add(eng, lambda e: e.tensor_tensor(out=out, in0=in0, in1=in1, op=op), r, w, cost=(220 if eng == "dve" else 400) + 0.9 * int(np.prod(out.shape[1:])))

    def ts(self, eng, out, in0, s1, s2, op0, op1=None, r=(), w=()):
        if op1 is None:
            return self.S.add(eng, lambda e: e.tensor_scalar(out=out, in0=in0, scalar1=s1, scalar2=None, op0=op0), r, w, cost=(220 if eng == "dve" else 400) + 0.6 * int(np.prod(out.shape[1:])))
        return self.S.add(eng, lambda e: e.tensor_scalar(out=out, in0=in0, scalar1=s1, scalar2=s2, op0=op0, op1=op1), r, w, cost=(220 if eng == "dve" else 400) + 0.6 * int(np.prod(out.shape[1:])))

    def stt(self, out, in0, scalar, in1, op0, op1, r=(), w=()):
        return self.S.add("dve", lambda e: e.scalar_tensor_tensor(out=out, in0=in0, scalar=scalar, in1=in1, op0=op0, op1=op1), r, w, cost=220 + 1.0 * int(np.prod(out.shape[1:])))

    def copy(self, eng, out, in_, r=(), w=()):
        if eng == "act":
            return self.S.add("act", lambda e: e.activation(out=out, in_=in_, func=AF.Copy), r, w)
        return self.S.add(eng, lambda e: e.tensor_copy(out=out, in_=in_), r, w)

    def recip(self, out, in_, r=(), w=()):
        return self.S.add("dve", lambda e: e.reciprocal(out=out, in_=in_), r, w)

    def memset(self, eng, ap, val, r=(), w=()):
        return self.S.add(eng, lambda e: e.memset(ap, val), r, w)


def bc(ap, shape):
    return ap.to_broadcast(list(shape))


def build(cfg):
    P = Prog(cfg)
    nc = P.nc
    P.S.keep_all_readers = bool(cfg.get("resched", False))
    layers = cfg.get("layers", [0, 1, 2, 3])
    final_norm = cfg.get("final_norm", True)

    x_p = P.din("x_prompt", [SEQ, D])
    x_s = P.din("x_sample", [NS, D])
    consts_d = P.din("consts", [128, 7 * 128])
    W = {}
    wshapes = {
        "final_norm_w": [D],
        "l1_norm_w": [D], "l1_in_proj": [D, 6144], "l1_v_ln_w": [2048], "l1_v_ln_b": [2048],
        "l1_spatial_w": [8, 128, 128], "l1_spatial_b": [8, 128], "l1_out_proj": [2048, D],
    }
    for li_ in (0, 3):
        p_ = "l%d_" % li_
        wshapes.update({p_ + "norm_w": [D], p_ + "in_proj": [D, 6176], p_ + "conv_w": [4, 4096], p_ + "conv_b": [4096],
                        p_ + "dt_bias": [32], p_ + "A_log": [32], p_ + "D_skip": [32], p_ + "gnorm_w": [2048], p_ + "out_proj": [2048, D]})
    for k, s in wshapes.items():
        if k == "final_norm_w" or int(k[1]) in layers:
            W[k] = P.din(k, s)
    DR = {}
    if 2 in layers:
        for k_, s_ in {"l2_norm_w": [D], "l2_in_proj": [D, 4096], "l2_conv_w": [4, 2048], "l2_conv_b": [2048], "l2_w_q": [512, 4, 4],
                       "l2_w_k": [512, 4, 4], "l2_w_v": [512, 4, 4], "l2_w_o": [512, 4, 4], "l2_b_o": [2048], "l2_w_if": [6144, 8],
                       "l2_b_if": [8], "l2_mh_norm_w": [2048], "l2_skip": [2048], "l2_out_proj": [2048, D]}.items():
            W[k_] = P.din(k_, s_)
        for k_, s_ in {"state_l2_C": [NS, 4, 512, 512], "state_l2_n": [NS, 4, 512], "state_l2_m": [NS, 4], "state_l2_conv": [NS, 3, 2048]}.items():
            DR[k_] = P.din(k_, s_)
        for k_, s_ in {"p2_C": [4, 512, 512], "p2_n": [4, 512], "p2_m": [4], "p2_conv": [3, 2048],
                       "s2_C": [NS, 4, 512, 512], "s2_n": [NS, 4, 512], "s2_m": [NS, 4], "s2_conv": [NS, 3, 2048]}.items():
            DR[k_] = P.dout(k_, s_)
    consts2_d = P.din("consts2", [32, 2048])
    for li_ in (0, 3):
        if li_ in layers:
            DR["state_l%d_ssm" % li_] = P.din("state_l%d_ssm" % li_, [NS, 32, 64, 128])
            DR["state_l%d_conv" % li_] = P.din("state_l%d_conv" % li_, [NS, 3, 4096])
            DR["p%d_ssm" % li_] = P.dout("p%d_ssm" % li_, [32, 64, 128])
            DR["p%d_conv" % li_] = P.dout("p%d_conv" % li_, [3, 4096])
            DR["s%d_ssm" % li_] = P.dout("s%d_ssm" % li_, [NS, 32, 64, 128])
            DR["s%d_conv" % li_] = P.dout("s%d_conv" % li_, [NS, 3, 4096])
    y_p = P.dout("y_prompt", [SEQ, D])
    y_s = P.dout("y_sample", [NS, D])
    s1_v = P.dout("s1_v", [NS, 2048]) if 1 in layers else None

    root = ExitStack()
    with root:
        G = root
        xres = P.sb(G, "xres", [128, 8, NT], F32)
        xn = P.sb(G, "xn", [128, 8, NT], BF16)
        cst = P.sb(G, "cst", [128, 7 * 128], F32)
        ident_b = P.sb(G, "ident_b", [128, 128], BF16)
        ones_b = P.sb(G, "ones_b", [128, 128], BF16)
        nw = P.sb(G, "nw", [128, 5, 8], F32)
        rs_s = P.sb(G, "rs_s", [128, 512], F32)
        rs_r = P.sb(G, "rs_r", [128, 512], F32)
        sq = [P.sb(G, "sq%d" % i, [128, 512], BF16) for i in range(2)]
        pb = [G.enter_context(nc.psum_tensor("pb%d" % i, [128, 512], F32)) for i in range(8)]
        ident_f = cst[:, 0:128]

        def xk(kc, ti):
            return "xres.%d.%d" % (kc, ti)

        def xnk(ti):
            return "xn.%d" % ti

        P.dma("sp", cst[:, :], consts_d[:, :], w=["cst"], dkey="cst")
        P.copy("dve", ident_b[:, :], cst[:, 0:128], r=["cst"], w=["ident_b"])
        P.copy("dve", ones_b[:, :], cst[:, 128:256], r=["cst"], w=["ones_b"])
        nwn = {0: "l0_norm_w", 1: "l1_norm_w", 2: "l2_norm_w", 3: "l3_norm_w", 4: "final_norm_w"}
        for li in list(layers) + [4]:
            if nwn[li] in W:
                P.dma("sp", nw[:, li, :], W[nwn[li]].rearrange("(k p) -> p k", p=128), w=["nw%d" % li],
                      dkey="nw%d" % li, nonc=True)

        with ExitStack() as L:
            xin = [P.sb(L, "xin%d" % i, [128, 4, D], F32) for i in range(2)]
            xsin = P.sb(L, "xsin", [NS, D], F32)
            for ti in range(4):
                buf = xin[ti % 2]
                bk = "xin%d" % (ti % 2)
                P.dma("sp", buf[:, :, :], x_p[ti * 512:(ti + 1) * 512, :].rearrange("(c p) d -> p c d", p=128),
                      w=[bk], dkey=bk)
                for kc in range(8):
                    ps = pb[kc % 8]
                    pk = "pb%d" % (kc % 8)
                    for c in range(4):
                        P.tr(ps[:, c * 128:(c + 1) * 128], buf[:, c, kc * 128:(kc + 1) * 128], ident_f,
                             r=[bk, "cst"], w=[pk])
                    P.copy("act" if kc % 2 else "dve", xres[:, kc, ti * 512:(ti + 1) * 512], ps[:, :], r=[pk], w=[xk(kc, ti)])
            P.dma("sp", xsin[:, :], x_s[:, :], w=["xsin"], dkey="xsin")
            for kc in range(8):
                P.tr(pb[0][:, kc * 16:(kc + 1) * 16], xsin[:, kc * 128:(kc + 1) * 128], ident_f[0:NS, 0:NS],
                     r=["xsin", "cst"], w=["pb0"])
            P.copy("dve", xres[:, :, SEQ:NT], pb[0][:, 0:128].rearrange("p (k t) -> p k t", k=8),
                   r=["pb0"], w=[xk(kc, 4) for kc in range(8)])
            P.S.barrier()

        def emit_norm(li):
            for ti, (t0, tn) in enumerate(TT):
                for kc in range(8):
                    P.act(sq[kc % 2][:, :tn], xres[:, kc, t0:t0 + tn], AF.Square, r=[xk(kc, ti)], w=["sq%d" % (kc % 2)])
                    P.mm(pb[7][:, :tn], ones_b[:, :], sq[kc % 2][:, :tn], kc == 0, kc == 7,
                         r=["sq%d" % (kc % 2), "ones_b"], w=["pb7"])
                P.act(rs_s[:, :tn], pb[7][:, :tn], AF.Sqrt, r=["pb7", "epsc"], w=["rs_s"], bias=epsc[:, 0:1], scale=1.0 / D)
                P.recip(rs_r[:, :tn], rs_s[:, :tn], r=["rs_s"], w=["rs_r"])
                for kc in range(8):
                    P.stt(xn[:, kc, t0:t0 + tn], xres[:, kc, t0:t0 + tn], nw[:, li, kc:kc + 1], rs_r[:, :tn],
                          ALU.mult, ALU.mult, r=[xk(kc, ti), "rs_r", "nw%d" % li], w=[xnk(ti)])

        epsc = P.sb(G, "epsc", [128, 2], F32)
        P.memset("dve", epsc[:, 0:1], EPS, w=["epsc"])
        P.memset("dve", epsc[:, 1:2], -0.5, w=["epsc"])

        def rstd_pool(out, in_, scale, r, w):
            P.ts("pool", out, in_, scale, EPS, ALU.mult, ALU.add, r=r, w=w)
            npart, nfree = out.shape[0], out.shape[1]
            P.tt("pool", out, out, bc(epsc[0:npart, 1:2], [npart, nfree]), ALU.pow, r=list(w) + ["epsc"], w=w)

        def layer_gmlp():
            li = 1
            emit_norm(li)
            Win = W["l1_in_proj"].rearrange("(k p) j -> p k j", p=128)
            Wout = W["l1_out_proj"].rearrange("(k p) j -> p k j", p=128)
            with ExitStack() as L:
                vtok = P.sb(L, "vtok", [128, 9, 2048], BF16)
                wv = [P.sb(L, "wv%d" % i, [128, 8, 512], BF16) for i in range(2)]
                wuz = wv
                wo = [P.sb(L, "wo%d" % i, [128, 2, D], BF16) for i in range(2)]
                stats = P.sb(L, "stats", [128, 9, 4, 6], F32)
                mv = P.sb(L, "mv", [128, 9, 2], F32)
                rstd = P.sb(L, "rstd", [128, 9], F32)
                nmr = P.sb(L, "nmr", [128, 9], F32)
                wsT = P.sb(L, "wsT", [128, 8, 128], BF16)
                lnw = P.sb(L, "lnw", [128, 16], F32)
                lnb = P.sb(L, "lnb", [128, 16], F32)
                Ec = P.sb(L, "Ec", [128, 2, 128], F32)
                rsb = P.sb(L, "rsb", [128, 8, 128], F32)
                sbb = P.sb(L, "sbb", [128, 8, 128], F32)
                lnr = P.sb(L, "lnr", [NS, 2048], F32)
                ws00 = P.sb(L, "ws00", [NS, 8], F32)
                sb00 = P.sb(L, "sb00", [NS, 8], F32)
                vns = P.sb(L, "vns", [NS, 2048], F32)
                mxsT = P.sb(L, "mxsT", [128, 16, NS], F32)
                sz = [P.sb(L, "sz%d" % i, [128, 512], BF16) for i in range(2)]
                mx = [P.sb(L, "mx%d" % i, [128, 512], F32) for i in range(2)]
                gt = [P.sb(L, "gt%d" % i, [128, 2, 512], BF16) for i in range(2)]
                onesw = P.sb(L, "onesw", [128, 128], BF16)

                P.dma("sp", lnw[:, :], W["l1_v_ln_w"].rearrange("(k p) -> p k", p=128), w=["lnw"], dkey="lnw", nonc=True)
                P.dma("sp", lnb[:, :], W["l1_v_ln_b"].rearrange("(k p) -> p k", p=128), w=["lnb"], dkey="lnb", nonc=True)
                P.dma("sp", ws00[:, :], bc(W["l1_spatial_w"][:, 0:1, 0:1].rearrange("g a b -> (a b) g"), [NS, 8]),
                      w=["ws00"], dkey="ws00", nonc=True)
                P.dma("sp", sb00[:, :], bc(W["l1_spatial_b"][:, 0:1].rearrange("g a -> a g"), [NS, 8]),
                      w=["sb00"], dkey="sb00", nonc=True)
                for h2 in range(2):
                    P.dma("sp", mx[h2][:, :].rearrange("p (g s) -> p g s", g=4),
                          W["l1_spatial_w"][h2 * 4:(h2 + 1) * 4].rearrange("g t s -> t g s"), w=["mx%d" % h2], dkey="wsl%d" % h2)
                P.dma("sp", sbb[:, :, :], bc(W["l1_spatial_b"].rearrange("(o g) t -> o g t", o=1), [128, 8, 128]),
                      w=["sbb"], dkey="sbb")
                causal = cst[:, 256:384]
                for g in range(8):
                    ps = pb[g % 2]
                    pk = "pb%d" % (g % 2)
                    P.tr(ps[:, 0:128], mx[g // 4][:, (g % 4) * 128:(g % 4 + 1) * 128], ident_f, r=["mx%d" % (g // 4), "cst"], w=[pk])
                    P.tt("dve", wsT[:, g, :], ps[:, 0:128], causal, ALU.mult, r=[pk, "cst"], w=["wsT"])
                P.copy("dve", onesw[:, :], cst[:, 128:256], r=["cst"], w=["onesw"])
                for g2 in range(2):
                    P.mm(pb[2 + g2][:, :], onesw[:, :], wsT[:, g2 * 4:(g2 + 1) * 4, :], True, True,
                         r=["onesw", "wsT"], w=["pb%d" % (2 + g2)])
                    P.copy("dve", rsb[:, g2 * 4:(g2 + 1) * 4, :], pb[2 + g2][:, :].rearrange("p (g t) -> p g t", g=4),
                           r=["pb%d" % (2 + g2)], w=["rsb"])

                wq = 0
                stg = cfg.get('gm_stage', 9)
                for half in range(cfg.get('gm_halves', 2)):
                    chunks = list(range(8 * half, 8 * half + 8))
                    ttiles = [2 * half, 2 * half + 1] + ([4] if half == 1 else [])
                    nslot = 8 + (1 if half == 1 else 0)
                    bank = 0
                    for ct in range(cfg.get('v_nct', 4) if stg >= 1 else 0):
                        wb = wv[ct % 2]
                        wk = "wv%d" % (ct % 2)
                        P.dma("pool", wb[:, :, :], Win[:, :, 2048 + ct * 512: 2048 + (ct + 1) * 512], w=[wk + "u", wk + "z"], dkey=wk + "u")
                        for sl in range(cfg.get("v_nsl", nslot)):
                            if sl < 8:
                                c = chunks[sl]
                                t0, M, ti = c * 128, 128, c // 4
                            else:
                                t0, M, ti = SEQ, NS, 4
                            ps = pb[bank % 7]
                            pk = "pb%d" % (bank % 7)
                            bank += 1
                            for kc in range(8):
                                P.mm(ps[:M, :], xn[:, kc, t0:t0 + M], wb[:, kc, :], kc == 0, kc == 7,
                                     r=[xnk(ti), wk + "u", wk + "z"], w=[pk])
                            if cfg.get("v_statcopy"):
                                P.copy("dve", mx[0][:M, :], ps[:M, :], r=[pk], w=["mx0"])
                            elif not cfg.get("v_nostats"):
                                P.S.add("dve", (lambda e, o=stats[:M, sl, ct, :], i=ps[:M, :]: e.bn_stats(out=o, in_=i)),
                                        [pk], ["stats.%d" % sl])
                            if not cfg.get("v_nocopy"):
                                P.copy(cfg.get("v_copyeng", "act"), vtok[:M, sl, ct * 512:(ct + 1) * 512], ps[:M, :], r=[pk] + (["stats.%d" % sl] if cfg.get("v_serial") else []), w=["vtok.%d" % sl])
                    for sl in range(nslot if stg >= 2 else 0):
                        M = 128 if sl < 8 else NS
                        P.S.add("dve", (lambda e, o=mv[:M, sl, :], i=stats[:M, sl, :, :]: e.bn_aggr(out=o, in_=i)),
                                ["stats.%d" % sl], ["mv.%d" % sl])
                        P.act(rstd[:M, sl:sl + 1], mv[:M, sl, 1:2], AF.Sqrt, r=["mv.%d" % sl, "epsc"], w=["rstd.%d" % sl],
                              bias=epsc[:M, 0:1], scale=1.0)
                        P.recip(rstd[:M, sl:sl + 1], rstd[:M, sl:sl + 1], r=["rstd.%d" % sl], w=["rstd.%d" % sl])
                        P.stt(nmr[:M, sl:sl + 1], mv[:M, sl, 0:1], -1.0, rstd[:M, sl:sl + 1], ALU.mult, ALU.mult,
                              r=["mv.%d" % sl, "rstd.%d" % sl], w=["nmr.%d" % sl])
                        if sl < 8:
                            P.ts("dve", vtok[:M, sl, :], vtok[:M, sl, :], rstd[:M, sl:sl + 1], nmr[:M, sl:sl + 1],
                                 ALU.mult, ALU.add, r=["vtok.%d" % sl, "rstd.%d" % sl, "nmr.%d" % sl], w=["vtok.%d" % sl])
                        else:
                            P.ts("dve", vns[:, :], vtok[:M, sl, :], rstd[:M, sl:sl + 1], nmr[:M, sl:sl + 1],
                                 ALU.mult, ALU.add, r=["vtok.%d" % sl, "rstd.%d" % sl, "nmr.%d" % sl], w=["vns"])
                            P.dma("sp", lnr[:, :], bc(W["l1_v_ln_w"].rearrange("(o n) -> o n", o=1), [NS, 2048]), w=["lnr"], dkey="lnr")
                            P.tt("dve", vns[:, :], vns[:, :], lnr[:, :], ALU.mult, r=["vns", "lnr"], w=["vns"])
                            P.dma("sp", lnr[:, :], bc(W["l1_v_ln_b"].rearrange("(o n) -> o n", o=1), [NS, 2048]), w=["lnr"], dkey="lnr")
                            P.tt("dve", vns[:, :], vns[:, :], lnr[:, :], ALU.add, r=["vns", "lnr"], w=["vns"])
                            P.dma("sp", s1_v[:, :], vns[:, :], r=["vns"], dkey="o_s1v")
                            mxs = lnr
                            P.tt("dve", mxs[:, :].rearrange("p (g d) -> p g d", g=8), vns[:, :].rearrange("p (g d) -> p g d", g=8),
                                 bc(ws00[:, :].rearrange("p (g o) -> p g o", o=1), [NS, 8, 256]), ALU.mult,
                                 r=["vns", "ws00"], w=["lnr"])
                            P.tt("dve", mxs[:, :].rearrange("p (g d) -> p g d", g=8), mxs[:, :].rearrange("p (g d) -> p g d", g=8),
                                 bc(sb00[:, :].rearrange("p (g o) -> p g o", o=1), [NS, 8, 256]), ALU.add,
                                 r=["lnr", "sb00"], w=["lnr"])
                            for jt in range(16):
                                P.tr(pb[7][:, jt * NS:(jt + 1) * NS], mxs[:, jt * 128:(jt + 1) * 128], ident_f[0:NS, 0:NS],
                                     r=["lnr", "cst"], w=["pb7"])
                            P.copy("dve", mxsT[:, :, :], pb[7][:, 0:16 * NS].rearrange("p (j t) -> p j t", j=16),
                                   r=["pb7"], w=["mxsT"])
                    def load_g(g_, slot_):
                        wb_, wk_ = wuz[slot_], "wv%d" % slot_
                        ob_, ok_ = wo[slot_], "wo%d" % slot_
                        P.dma("pool", wb_[:, :, 0:256], Win[:, :, g_ * 256:(g_ + 1) * 256], w=[wk_ + "u"], dkey=wk_ + "u")
                        P.dma("pool", wb_[:, :, 256:512], Win[:, :, 4096 + g_ * 256: 4096 + (g_ + 1) * 256], w=[wk_ + "z"], dkey=wk_ + "z")
                        P.dma("pool", ob_[:, :, :], Wout[:, 2 * g_:2 * g_ + 2, :], w=[ok_], dkey=ok_)
                    if stg >= 3:
                        load_g(0, wq % 2)
                    for g in range(8 if stg >= 3 else 0):
                        slot = wq % 2
                        wq += 1
                        wb, wk = wuz[slot], "wv%d" % slot
                        ob, ok = wo[slot], "wo%d" % slot
                        if g < 7:
                            load_g(g + 1, wq % 2)
                        for ti in ttiles:
                            t0, tn = TT[ti]
                            gb = gt[ti % 2]
                            gk = "gt%d" % (ti % 2)
                            for jl in range(2):
                                jt = 2 * g + jl
                                o = 3 * jl
                                pu, pz, pm = pb[o], pb[o + 1], pb[o + 2]
                                ku, kz, km = "pb%d" % o, "pb%d" % (o + 1), "pb%d" % (o + 2)
                                for kc in range(8):
                                    P.mm(pu[:, :tn], wb[:, kc, jl * 128:(jl + 1) * 128], xn[:, kc, t0:t0 + tn], kc == 0, kc == 7,
                                         r=[wk + "u", xnk(ti)], w=[ku])
                                for kc in range(8):
                                    P.mm(pz[:, :tn], wb[:, kc, 256 + jl * 128:256 + (jl + 1) * 128], xn[:, kc, t0:t0 + tn],
                                         kc == 0, kc == 7, r=[wk + "z", xnk(ti)], w=[kz])
                                s_ = sz[jl]
                                sk_ = "sz%d" % jl
                                m_ = mx[jl]
                                mk_ = "mx%d" % jl
                                P.act(s_[:, :tn], pz[:, :tn], AF.Silu, r=[kz], w=[sk_])
                                if ti < 4:
                                    for c4 in range(4):
                                        c = ti * 4 + c4
                                        sl = c - 8 * half
                                        P.mm(pm[:, c4 * 128:(c4 + 1) * 128], vtok[:, sl, jt * 128:(jt + 1) * 128], wsT[:, g, :],
                                             True, True, r=["vtok.%d" % sl, "wsT"], w=[km])
                                    P.stt(Ec[:, jl, :], rsb[:, g, :], lnb[:, jt:jt + 1], sbb[:, g, :], ALU.mult, ALU.add,
                                          r=["rsb", "lnb", "sbb"], w=["Ec%d" % jl])
                                    P.stt(m_[:, :].rearrange("p (c t) -> p c t", c=4), pm[:, :].rearrange("p (c t) -> p c t", c=4),
                                          lnw[:, jt:jt + 1], bc(Ec[:, jl:jl + 1, :], [128, 4, 128]), ALU.mult, ALU.add,
                                          r=[km, "lnw", "Ec%d" % jl], w=[mk_])
                                    min_ = m_[:, :tn]
                                    rk = [mk_]
                                else:
                                    min_ = mxsT[:, jt, :]
                                    rk = ["mxsT"]
                                P.tt("dve", m_[:, :tn], pu[:, :tn], min_, ALU.mult, r=[ku] + rk, w=[mk_])
                                P.tt("pool", gb[:, jl, :tn], m_[:, :tn], s_[:, :tn], ALU.mult, r=[mk_, sk_], w=[gk])
                            for dt_ in range(8):
                                po = pb[6 + dt_ % 2]
                                pk = "pb%d" % (6 + dt_ % 2)
                                for jl in range(2):
                                    P.mm(po[:, :tn], ob[:, jl, dt_ * 128:(dt_ + 1) * 128], gb[:, jl, :tn], jl == 0, jl == 1,
                                         r=[ok, gk], w=[pk])
                                P.tt("dve", xres[:, dt_, t0:t0 + tn], xres[:, dt_, t0:t0 + tn], po[:, :tn], ALU.add,
                                     r=[pk, xk(dt_, ti)], w=[xk(dt_, ti)])
                P.S.barrier()

        def layer_ssd(li):
            p_ = "l%d_" % li
            emit_norm(li)
            Win = W[p_ + "in_proj"].rearrange("(k p) j -> p k j", p=128)
            Wout = W[p_ + "out_proj"].rearrange("(k p) j -> p k j", p=128)
            st_ssm = DR["state_l%d_ssm" % li]
            st_conv = DR["state_l%d_conv" % li]
            o_pssm, o_pconv, o_sssm, o_sconv = DR["p%d_ssm" % li], DR["p%d_conv" % li], DR["s%d_ssm" % li], DR["s%d_conv" % li]
            triu_f, sgt_f, ltm_f = cst[:, 256:384], cst[:, 384:512], cst[:, 512:640]
            ones_f = cst[:, 128:256]
            with ExitStack() as L:
                cw = P.sb(L, "cw", [128, 32, 5], F32)
                dtb = P.sb(L, "dtb", [32, 4], F32)
                Dcol = P.sb(L, "Dcol", [128, 16], F32)
                gnw = P.sb(L, "gnw", [128, 16], F32)
                wdt = P.sb(L, "wdt", [128, 8, 32], BF16)
                ddt = P.sb(L, "ddt", [128, 17, 64], F32)
                ecum = P.sb(L, "ecum", [128, 16, 32], F32)
                wend = P.sb(L, "wend", [128, 16, 32], F32)
                eL = P.sb(L, "eL", [128, 16, 32], F32)
                dtmp = P.sb(L, "dtmp", [32, 2, 512], F32)
                dsT = P.sb(L, "dsT", [32, 2, NS], F32)
                csb = P.sb(L, "csb", [128, 32], F32)
                negI = P.sb(L, "negI", [128, 128], F32)
                wblk = [P.sb(L, "wblk%d" % i, [128, 8, 768], BF16) for i in range(2)]
                wo = [P.sb(L, "wo0", [128, 2, D], BF16)] * 2
                dg = P.sb(L, "dg", [128, 16, 128], BF16)
                dgD = P.sb(L, "dgD", [128, 2, 128], BF16)
                cwg = P.sb(L, "cwg", [128, 4, 5], F32)
                xr = [[P.sb(L, "xr%d_0" % j, [128, 515], BF16)] * 2 for j in range(4)]
                xbc = [P.sb(L, "xbc%d" % i, [128, 4, 512], BF16) for i in range(2)]
                ctail = P.sb(L, "ctail", [128, 4, 3], F32)
                xBtr = [P.sb(L, "xBt%d" % i, [128, 384], BF16) for i in range(2)]
                sztiles = [P.sb(L, "sztile%d" % i, [128, 4, 256], BF16) for i in range(2)]
                scrA = P.sb(L, "scrA", [128, 512], F32)
                scrB = P.sb(L, "scrB", [128, 512], F32)
                Ub = [P.sb(L, "Ub%d" % i, [128, 128], F32) for i in range(2)]
                dec = P.sb(L, "dec", [128, 512], BF16)
                mixT = P.sb(L, "mixT", [128, 4, 128], BF16)
                ssq = P.sb(L, "ssq", [128, 2], F32)
                yn = P.sb(L, "yn", [128, 256], BF16)
                gT = [P.sb(L, "gT%d" % i, [128, 2, 512], BF16) for i in range(2)]
                xw = P.sb(L, "xw", [128, 256], BF16)
                hT = P.sb(L, "hT", [128, 256], F32)
                hTb = P.sb(L, "hTb", [128, 256], BF16)
                htmp = P.sb(L, "htmp", [128, 256], F32)
                hout = scrA[:, 0:256].rearrange("p (j n) -> p j n", j=2)
                ctout = scrB[0:3, :]
                xrs = P.sb(L, "xrs", [128, 4, NS], F32)
                cstk = scrA[0:48, :]
                cs = P.sb(L, "cs", [128, 4, 48], F32)
                acc = P.sb(L, "acc", [128, 4, NS], F32)
                atmp = P.sb(L, "atmp", [128, 4, NS], F32)
                xbs = P.sb(L, "xbs", [128, 4, NS], F32)
                ncs = P.sb(L, "ncs", [128, 4, 48], F32)
                szs = P.sb(L, "szs", [128, 2, NS], F32)
                Eg = P.sb(L, "Eg", [32, 256], F32)
                dtc = P.sb(L, "dtc", [128, 2, 2, NS], F32)
                xdtc = P.sb(L, "xdtc", [128, 2, NS], F32)
                rhsB = P.sb(L, "rhsB", [128, NS, 128], BF16)
                h0t = [P.sb(L, "h0t%d" % i, [128, 4, 128], F32) for i in range(2)]
                hnt = [P.sb(L, "hnt%d" % i, [128, 4, 128], F32) for i in range(2)]
                s1 = P.sb(L, "s1", [128, 4, 128], F32)
                ys = P.sb(L, "ys", [128, 2, NS], F32)
                gsm = P.sb(L, "gsm", [128, 2, NS], F32)
                gsq = P.sb(L, "gsq", [128, 2, NS], BF16)
                gTs = P.sb(L, "gTs", [128, 2, NS], BF16)
                rsd = P.sb(L, "rsd", [128, NS], F32)
                pbT2 = pb[2].bitcast(BF16)
                pbT7 = pb[7].bitcast(BF16)
                pbT1 = pb[1].bitcast(BF16)

                for k in range(4):
                    P.dma("sp", cw[:, :, k], W[p_ + "conv_w"][k].rearrange("(j p) -> p j", p=128), w=["cw"], dkey="cw%d" % k, nonc=True)
                P.dma("sp", cw[:, :, 4], W[p_ + "conv_b"].rearrange("(j p) -> p j", p=128), w=["cw"], dkey="cw4", nonc=True)
                P.dma("sp", dtb[:, 0:1], W[p_ + "dt_bias"].rearrange("(h o) -> h o", o=1), w=["dtb"], dkey="dtb0", nonc=True)
                P.dma("sp", dtb[:, 1:2], W[p_ + "A_log"].rearrange("(h o) -> h o", o=1), w=["dtb"], dkey="dtb1", nonc=True)
                P.act(dtb[:, 2:3], dtb[:, 1:2], AF.Exp, r=["dtb"], w=["dtb"])
                P.ts("dve", dtb[:, 2:3], dtb[:, 2:3], -1.0, None, ALU.mult, r=["dtb"], w=["dtb"])
                P.memset("dve", dtb[:, 3:4], 1.0, w=["dtb"])
                Dsk = W[p_ + "D_skip"].rearrange("(j two o) -> two o j", two=2, o=1)
                P.dma("sp", Dcol[0:64, :], bc(Dsk[0], [64, 16]), w=["Dcol"], dkey="Dcol0", nonc=True)
                P.dma("sp", Dcol[64:128, :], bc(Dsk[1], [64, 16]), w=["Dcol"], dkey="Dcol1", nonc=True)
                P.dma("sp", gnw[:, :], W[p_ + "gnorm_w"].rearrange("(k p) -> p k", p=128), w=["gnw"], dkey="gnw", nonc=True)
                P.dma("pool", wdt[:, :, :], Win[:, :, 6144:6176], w=["wdt"], dkey="wdt")
                P.ts("dve", negI[:, :], ident_f, NEG, None, ALU.mult, r=["cst"], w=["negI"])

                for ti, (t0, tn) in enumerate(TT):
                    for kc in range(8):
                        P.mm(pb[0][:32, :tn], wdt[:, kc, :], xn[:, kc, t0:t0 + tn], kc == 0, kc == 7, r=["wdt", xnk(ti)], w=["pb0"])
                    P.act(dtmp[:, 0, :tn], pb[0][:32, :tn], AF.Exp, r=["pb0", "dtb"], w=["dtmp0"], bias=dtb[:, 0:1])
                    P.act(dtmp[:, 0, :tn], dtmp[:, 0, :tn], AF.Ln, r=["dtmp0", "dtb"], w=["dtmp0"], bias=dtb[:, 3:4])
                    P.ts("dve", dtmp[:, 1, :tn], dtmp[:, 0, :tn], dtb[:, 2:3], None, ALU.mult, r=["dtmp0", "dtb"], w=["dtmp1"])
                    if ti == 4:
                        P.copy("dve", dsT[:, :, :], dtmp[:, :, 0:NS], r=["dtmp0", "dtmp1"], w=["dsT"])
                        continue
                    for c4 in range(4):
                        c = ti * 4 + c4
                        P.tr(pb[1][:, 0:32], dtmp[:, 0, c4 * 128:(c4 + 1) * 128], ident_f[0:32, 0:32], r=["dtmp0", "cst"], w=["pb1"])
                        P.tr(pb[1][:, 32:64], dtmp[:, 1, c4 * 128:(c4 + 1) * 128], ident_f[0:32, 0:32], r=["dtmp1", "cst"], w=["pb1"])
                        P.copy("dve", ddt[:, c, :], pb[1][:, 0:64], r=["pb1"], w=["ddt.%d" % c])
                        P.mm(pb[2][:, 0:32], triu_f, ddt[:, c, 32:64], True, True, r=["cst", "ddt.%d" % c], w=["pb2"])
                        P.mm(pb[2][:, 32:64], ones_f, ddt[:, c, 32:64], True, True, r=["cst", "ddt.%d" % c], w=["pb2"])
                        P.act(ecum[:, c, :], pb[2][:, 0:32], AF.Exp, r=["pb2"], w=["ecum.%d" % c])
                        P.act(eL[:, c, :], pb[2][:, 32:64], AF.Exp, r=["pb2"], w=["eL.%d" % c])
                        P.copy("dve", csb[:, :], pb[2][:, 0:32], r=["pb2"], w=["csb"])
                        P.tt("dve", csb[:, :], pb[2][:, 32:64], csb[:, :], ALU.subtract, r=["pb2", "csb"], w=["csb"])
                        P.act(wend[:, c, :], csb[:, :], AF.Exp, r=["csb"], w=["wend.%d" % c])
                        P.tt("dve", wend[:, c, :], wend[:, c, :], ddt[:, c, 0:32], ALU.mult, r=["wend.%d" % c, "ddt.%d" % c], w=["wend.%d" % c])

                ngroups = cfg.get("ssd_groups", 8)

                def load_g(g, slot):
                    wb, wk = wblk[slot], "wblk%d" % slot
                    P.dma("pool", wb[:, :, 0:256], Win[:, :, 2048 + 256 * g: 2048 + 256 * (g + 1)], w=[wk + "x"], dkey=wk + "x")
                    P.dma("pool", wb[:, :, 256:384], Win[:, :, 4096 + 128 * g: 4096 + 128 * (g + 1)], w=[wk + "B"], dkey=wk + "B")
                    P.dma("pool", wb[:, :, 384:512], Win[:, :, 5120 + 128 * g: 5120 + 128 * (g + 1)], w=[wk + "C"], dkey=wk + "C")
                    P.dma("pool", wb[:, :, 512:768], Win[:, :, 256 * g: 256 * (g + 1)], w=[wk + "z"], dkey=wk + "z")

                load_g(0, 0)
                for g in range(ngroups):
                    slot = g % 2
                    wb, wk = wblk[slot], "wblk%d" % slot
                    wkeys = [wk + "x", wk + "x", wk + "B", wk + "C"]
                    ob, ok = wo[slot], "wo0"
                    P.dma("pool", ob[:, :, :], Wout[:, 2 * g:2 * g + 2, :], w=["wo0"], dkey="wo0")
                    if g + 1 < ngroups:
                        load_g(g + 1, 1 - slot)
                    jts = [2 * g, 2 * g + 1, 16 + g, 24 + g]
                    for j in range(4):
                        P.copy("dve", cwg[:, j, :], cw[:, jts[j], :], r=["cw"], w=["cwg"])
                        for k in range(4):
                            P.ts("dve", dg[:, j * 4 + k, :], ident_f, cw[:, jts[j], k:k + 1], None, ALU.mult, r=["cst", "cw"], w=["dg"])
                    for jl in range(2):
                        P.ts("dve", dgD[:, jl, :], ident_f, Dcol[:, 2 * g + jl:2 * g + jl + 1], None, ALU.mult, r=["cst", "Dcol"], w=["dgD"])
                    P.memset("dve", hT[:, :], 0.0, w=["hT"])
                    P.memset("dve", hTb[:, :], 0.0, w=["hTb"])
                    for j in range(4):
                        P.memset("dve", xr[j][0][:, 0:3], 0.0, w=["xr%d_0" % j])

                    fill = []

                    def pump(n=1):
                        for _ in range(n):
                            if fill:
                                fill.pop(0)()

                    def u_proj(ti, j):
                        t0 = ti * 512
                        xrb, xrk = xr[j][0], "xr%d_0" % j
                        for kc in range(8):
                            P.mm(pb[0][:, :], wb[:, kc, j * 128:(j + 1) * 128], xn[:, kc, t0:t0 + 512], kc == 0, kc == 7,
                                 r=[wkeys[j], xnk(ti)], w=["pb0"])
                        P.copy("act", xrb[:, 3:515], pb[0][:, :], r=["pb0"], w=[xrk])
                        if ti == 3:
                            P.copy("dve", ctail[:, j, :], pb[0][:, 509:512], r=["pb0"], w=["ctail"])

                    def u_conv(ti, j):
                        xb, xbk = xbc[ti % 2], "xbc%d" % (ti % 2)
                        xrb, xrk = xr[j][0], "xr%d_0" % j
                        for k in range(4):
                            P.mm(pb[6][:, :], dg[:, j * 4 + k, :], xrb[:, k:k + 512], k == 0, k == 3, r=["dg", xrk], w=["pb6"])
                        if ti < 3:
                            P.copy("pool", xrb[:, 0:3], xrb[:, 512:515], r=[xrk], w=[xrk])
                        P.act(xb[:, j, :], pb[6][:, :], AF.Silu, r=["pb6", "cw"], w=[xbk + ".%d" % j], bias=cw[:, jts[j], 4:5])

                    def u_z(ti, c4):
                        tok = slice((ti * 4 + c4) * 128, (ti * 4 + c4 + 1) * 128)
                        for kc in range(8):
                            P.mm(pb[6][:, 0:256], xn[:, kc, tok], wb[:, kc, 512:768], kc == 0, kc == 7, r=[xnk(ti), wk + "z"], w=["pb6"])
                        P.act(sztiles[ti % 2][:, c4, :], pb[6][:, 0:256], AF.Silu, r=["pb6"], w=["szt%d.%d" % (ti % 2, c4)])

                    def u_op(ti, dt_):
                        t0 = ti * 512
                        gb, gk = gT[ti % 2], "gT%d" % (ti % 2)
                        for jl in range(2):
                            P.mm(pb[0][:, :], ob[:, jl, dt_ * 128:(dt_ + 1) * 128], gb[:, jl, :], jl == 0, jl == 1, r=[ok, gk], w=["pb0"])
                        P.tt("dve", xres[:, dt_, t0:t0 + 512], xres[:, dt_, t0:t0 + 512], pb[0][:, :], ALU.add, r=["pb0", xk(dt_, ti)], w=[xk(dt_, ti)])

                    def front(c):
                        ti, c4 = c // 4, c % 4
                        xb, xbk = xbc[ti % 2], "xbc%d" % (ti % 2)
                        cs_ = slice(c4 * 128, (c4 + 1) * 128)
                        tok = slice(c * 128, (c + 1) * 128)
                        xBt, xBk = xBtr[c % 2], "xBt%d" % (c % 2)
                        ysb, ysk = scrA[:, (c % 2) * 256:(c % 2 + 1) * 256], "scrA.%d" % (c % 2)
                        for j in range(3):
                            P.tr(pbT2[:, j * 128:(j + 1) * 128], xb[:, j, cs_], ident_b[:, :], r=[xbk + ".%d" % j, "ident_b"], w=["pb2"])
                        P.copy("dve", xBt[:, :], pbT2[:, 0:384], r=["pb2"], w=[xBk])
                        pump()
                        P.mm(pb[3][:, 0:128], xb[:, 2, cs_], xb[:, 3, cs_], True, True, r=[xbk + ".2", xbk + ".3"], w=["pb3"])
                        for hh in range(4):
                            h = 4 * g + hh
                            U, Uk = Ub[hh % 2], "Ub%d" % (hh % 2)
                            P.ts("dve", U[:, :], sgt_f, ddt[:, c, 32 + h:33 + h], None, ALU.mult, r=["cst", "ddt.%d" % c], w=[Uk])
                            P.mm(pb[4][:, hh * 128:(hh + 1) * 128], U[:, :], triu_f, True, False, r=[Uk, "cst"], w=["pb4"])
                            P.mm(pb[4][:, hh * 128:(hh + 1) * 128], negI[:, :], ltm_f, False, True, r=["negI", "cst"], w=["pb4"])
                        P.act(dec[:, :], pb[4][:, :], AF.Exp, r=["pb4"], w=["dec"])
                        pump()
                        for hh in range(4):
                            h = 4 * g + hh
                            P.stt(mixT[:, hh, :], dec[:, hh * 128:(hh + 1) * 128], ddt[:, c, h:h + 1], pb[3][:, 0:128],
                                  ALU.mult, ALU.mult, r=["dec", "ddt.%d" % c, "pb3"], w=["mixT"])
                        for jl in range(2):
                            P.mm(pb[5][:, jl * 128:(jl + 1) * 128], xb[:, jl, cs_], dgD[:, jl, :], True, False, r=[xbk + ".%d" % jl, "dgD"], w=["pb5"])
                            for hh in (2 * jl, 2 * jl + 1):
                                P.mm(pb[5][:, hh * 64:(hh + 1) * 64], mixT[:, hh, :], xBt[:, hh * 64:(hh + 1) * 64], False, hh == 2 * jl + 1,
                                     r=["mixT", xBk], w=["pb5"])
                        P.copy("act", ysb, pb[5][:, 0:256], r=["pb5"], w=[ysk])
                        pump()

                    def tail(c):
                        ti, c4 = c // 4, c % 4
                        xb, xbk = xbc[ti % 2], "xbc%d" % (ti % 2)
                        cs_ = slice(c4 * 128, (c4 + 1) * 128)
                        gb, gk = gT[ti % 2], "gT%d" % (ti % 2)
                        xBt, xBk = xBtr[c % 2], "xBt%d" % (c % 2)
                        szt, szk = sztiles[ti % 2][:, c4, :], "szt%d.%d" % (ti % 2, c4)
                        ysb, ysk = scrA[:, (c % 2) * 256:(c % 2 + 1) * 256], "scrA.%d" % (c % 2)
                        t1, t2 = scrB[:, 0:256], scrB[:, 256:512]
                        P.mm(pb[7][:, 0:256], xb[:, 3, cs_], hTb[:, :], True, True, r=[xbk + ".3", "hTb"], w=["pb7"])
                        pump()
                        P.tt("dve", t1.rearrange("p (h q) -> p h q", h=4), pb[7][:, 0:256].rearrange("p (h q) -> p h q", h=4),
                             bc(ecum[:, c, 4 * g:4 * g + 4].rearrange("p (h o) -> p h o", o=1), [128, 4, 64]), ALU.mult,
                             r=["pb7", "ecum.%d" % c], w=["scrB.0"])
                        P.tt("dve", t2, ysb, t1, ALU.add, r=[ysk, "scrB.0"], w=["scrB.1"])
                        P.tt("pool", t2, t2, szt, ALU.mult, r=["scrB.1", szk], w=["scrB.1"])
                        P.act(t1, t2, AF.Square, r=["scrB.1"], w=["scrB.0", "ssq"], accum=ssq[:, 0:1])
                        rstd_pool(ssq[:, 1:2], ssq[:, 0:1], 1.0 / 256, ["ssq"], ["ssq"])
                        P.ts("dve", yn[:, :], t2, ssq[:, 1:2], None, ALU.mult, r=["scrB.1", "ssq"], w=["yn"])
                        for jl in range(2):
                            P.tr(pbT1[:, jl * 128:(jl + 1) * 128], yn[:, jl * 128:(jl + 1) * 128], ident_b[:, :], r=["yn", "ident_b"], w=["pb1"])
                        pump()
                        for jl in range(2):
                            P.ts("dve", gb[:, jl, cs_], pbT1[:, jl * 128:(jl + 1) * 128], gnw[:, 2 * g + jl:2 * g + jl + 1], None, ALU.mult,
                                 r=["pb1", "gnw"], w=[gk])
                        P.tt("pool", xw[:, :].rearrange("p (h q) -> p h q", h=4), xBt[:, 0:256].rearrange("p (h q) -> p h q", h=4),
                             bc(wend[:, c, 4 * g:4 * g + 4].rearrange("p (h o) -> p h o", o=1), [128, 4, 64]), ALU.mult,
                             r=[xBk, "wend.%d" % c], w=["xw"])
                        P.mm(pb[7][:, 256:512], xBt[:, 256:384], xw[:, :], True, True, r=[xBk, "xw"], w=["pb7"])
                        P.tt("pool", htmp[:, :].rearrange("p (h q) -> p h q", h=4), hT[:, :].rearrange("p (h q) -> p h q", h=4),
                             bc(eL[:, c, 4 * g:4 * g + 4].rearrange("p (h o) -> p h o", o=1), [128, 4, 64]), ALU.mult,
                             r=["hT", "eL.%d" % c], w=["htmp"])
                        P.tt("dve", hT[:, :], htmp[:, :], pb[7][:, 256:512], ALU.add, r=["htmp", "pb7"], w=["hT"])
                        P.copy("act", hTb[:, :], hT[:, :], r=["hT"], w=["hTb"])

                    def refill(ti):
                        prep = []
                        if ti < 3:
                            for j in range(4):
                                prep.append(lambda ti=ti, j=j: u_proj(ti + 1, j))
                                prep.append(lambda ti=ti, j=j: u_conv(ti + 1, j))
                        ops_ = [(lambda ti=ti, d=d: u_op(ti - 1, d)) for d in range(8)] if ti >= 1 else []
                        while prep or ops_:
                            if prep:
                                fill.append(prep.pop(0))
                            if ops_:
                                fill.append(ops_.pop(0))

                    for j in range(4):
                        u_proj(0, j)
                        u_conv(0, j)
                    for c4 in range(4):
                        u_z(0, c4)
                    front(0)
                    for c in range(16):
                        ti, c4 = c // 4, c % 4
                        if c4 == 0:
                            pump(len(fill))
                            refill(ti)
                        if c + 1 < 16:
                            if c4 == 3:
                                pump(len(fill))
                            front(c + 1)
                        tail(c)
                        if ti < 3:
                            u_z(ti + 1, c4)
                    pump(len(fill))
                    for d in range(8):
                        u_op(3, d)
                    for jl in range(2):
                        P.tr(pb[2][:, jl * 128:(jl + 1) * 128], hT[:, jl * 128:(jl + 1) * 128], ident_f, r=["hT", "cst"], w=["pb2"])
                    P.copy("dve", hout[:, :, :], pb[2][:, 0:256].rearrange("p (j n) -> p j n", j=2), r=["pb2"], w=["scrA.0"])
                    P.dma("sp", o_pssm[4 * g:4 * g + 4].rearrange("(j hh) q n -> (hh q) j n", j=2), hout[:, :, :], r=["scrA.0"], dkey="o_hout")
                    for j in range(4):
                        P.tr(pb[3][0:3, j * 128:(j + 1) * 128], ctail[:, j, :], ident_f, r=["ctail", "cst"], w=["pb3"])
                    P.copy("dve", ctout[:, :], pb[3][0:3, :], r=["pb3"], w=["scrB.0", "scrB.1"])
                    P.dma("sp", o_pconv[:, 256 * g:256 * (g + 1)], ctout[:, 0:256], r=["scrB.0", "scrB.1"], dkey="o_ctx")
                    P.dma("sp", o_pconv[:, 2048 + 128 * g:2048 + 128 * (g + 1)], ctout[:, 256:384], r=["scrB.0", "scrB.1"], dkey="o_ctB")
                    P.dma("sp", o_pconv[:, 3072 + 128 * g:3072 + 128 * (g + 1)], ctout[:, 384:512], r=["scrB.0", "scrB.1"], dkey="o_ctC")

                    if cfg.get('ssd_nosample'):
                        continue
                    ti = 4
                    for j in range(4):
                        for kc in range(8):
                            P.mm(pb[0][:, j * NS:(j + 1) * NS], wb[:, kc, j * 128:(j + 1) * 128], xn[:, kc, SEQ:NT], kc == 0, kc == 7,
                                 r=[wkeys[j], xnk(4)], w=["pb0"])
                    P.copy("dve", xrs[:, :, :], pb[0][:, 0:4 * NS].rearrange("p (j b) -> p j b", j=4), r=["pb0"], w=["xrs"])
                    cranges = [(256 * g, 256), (2048 + 128 * g, 128), (3072 + 128 * g, 128)]
                    off = 0
                    for (c0, cn) in cranges:
                        P.dma("sp", cstk[:, off:off + cn], st_conv[:, :, c0:c0 + cn].rearrange("b k c -> (b k) c"), w=["scrA.0", "scrA.1"], dkey="cstk%d" % off)
                        off += cn
                    for j in range(4):
                        P.tr(pb[1][:, j * 48:(j + 1) * 48], cstk[:, j * 128:(j + 1) * 128], ident_f[0:48, 0:48], r=["scrA.0", "scrA.1", "cst"], w=["pb1"])
                    P.copy("dve", cs[:, :, :], pb[1][:, 0:192].rearrange("p (j q) -> p j q", j=4), r=["pb1"], w=["cs"])
                    cs4 = cs[:, :, :].rearrange("p j (b k) -> p j b k", k=3)
                    ncs4 = ncs[:, :, :].rearrange("p j (b k) -> p j b k", k=3)
                    P.tt("dve", acc[:, :, :], xrs[:, :, :], bc(cwg[:, :, 3:4], [128, 4, NS]), ALU.mult, r=["xrs", "cwg"], w=["acc"])
                    P.tt("dve", acc[:, :, :], acc[:, :, :], bc(cwg[:, :, 4:5], [128, 4, NS]), ALU.add, r=["acc", "cwg"], w=["acc"])
                    for k in range(3):
                        P.tt("dve", atmp[:, :, :], cs4[:, :, :, k], bc(cwg[:, :, k:k + 1], [128, 4, NS]), ALU.mult, r=["cs", "cwg"], w=["atmp"])
                        P.tt("dve", acc[:, :, :], acc[:, :, :], atmp[:, :, :], ALU.add, r=["acc", "atmp"], w=["acc"])
                    P.act(xbs[:, :, :], acc[:, :, :], AF.Silu, r=["acc"], w=["xbs"])
                    P.copy("dve", ncs4[:, :, :, 0:2], cs4[:, :, :, 1:3], r=["cs"], w=["ncs"])
                    P.copy("dve", ncs4[:, :, :, 2], xrs[:, :, :], r=["xrs", "ncs"], w=["ncs"])
                    for j in range(4):
                        P.tr(pb[1][0:48, j * 128:(j + 1) * 128], ncs[:, j, :], ident_f, r=["ncs", "cst"], w=["pb1"])
                    P.copy("dve", cstk[:, :], pb[1][0:48, :], r=["pb1"], w=["scrA.0", "scrA.1"])
                    off = 0
                    for (c0, cn) in cranges:
                        P.dma("sp", o_sconv[:, :, c0:c0 + cn].rearrange("b k c -> (b k) c"), cstk[:, off:off + cn], r=["scrA.0", "scrA.1"], dkey="o_cstk%d" % off)
                        off += cn
                    for jl in range(2):
                        for kc in range(8):
                            P.mm(pb[2][:, jl * NS:(jl + 1) * NS], wb[:, kc, 512 + jl * 128:512 + (jl + 1) * 128], xn[:, kc, SEQ:NT], kc == 0, kc == 7,
                                 r=[wk + "z", xnk(4)], w=["pb2"])
                    P.act(szs[:, :, :], pb[2][:, 0:2 * NS].rearrange("p (j b) -> p j b", j=2), AF.Silu, r=["pb2"], w=["szs"])
                    P.dma("sp", Eg[:, :], consts2_d[:, 256 * g:256 * (g + 1)], w=["Eg"], dkey="Eg")
                    for jl in range(2):
                        for q in range(2):
                            P.mm(pb[2][:, 64 + (jl * 2 + q) * NS: 64 + (jl * 2 + q + 1) * NS], Eg[:, jl * 128:(jl + 1) * 128], dsT[:, q, :], True, True,
                                 r=["Eg", "dsT"], w=["pb2"])
                    dview = pb[2][:, 64:64 + 4 * NS].rearrange("p (j q b) -> p j q b", j=2, q=2)
                    P.copy("dve", dtc[:, :, 0, :], dview[:, :, 0, :], r=["pb2"], w=["dtc"])
                    P.act(dtc[:, :, 1, :], dview[:, :, 1, :], AF.Exp, r=["pb2"], w=["dtc"])
                    P.tt("dve", xdtc[:, :, :], xbs[:, 0:2, :], dtc[:, :, 0, :], ALU.mult, r=["xbs", "dtc"], w=["xdtc"])
                    for which, j, base in (("B", 2, 0), ("C", 3, 4)):
                        P.tt("dve", rhsB[:, :, :], bc(ident_b[:, :].rearrange("p (o n) -> p o n", o=1), [128, NS, 128]),
                             bc(xbs[:, j, :].rearrange("p (b o) -> p b o", o=1), [128, NS, 128]), ALU.mult, r=["ident_b", "xbs"], w=["rhsB"])
                        for q in range(4):
                            P.mm(pb[base + q][:, :], ones_b[:, :], rhsB[:, 4 * q:4 * q + 4, :], True, True, r=["ones_b", "rhsB"],
                                 w=["pb%d" % (base + q)])
                    for jl in range(2):
                        for q in range(4):
                            hb, hk = h0t[q % 2], "h0t%d" % (q % 2)
                            nb, nk = hnt[q % 2], "hnt%d" % (q % 2)
                            bs = slice(4 * q, 4 * q + 4)
                            bk = ["pb%d" % q]
                            ck = ["pb%d" % (4 + q)]
                            P.dma("sp", hb[:, :, :], st_ssm[4 * q:4 * q + 4, 4 * g + 2 * jl:4 * g + 2 * jl + 2].rearrange("b h q n -> (h q) b n"),
                                  w=[hk], dkey=hk)
                            P.tt("dve", s1[:, :, :], pb[q][:, :].rearrange("p (b n) -> p b n", b=4),
                                 bc(xdtc[:, jl, bs].rearrange("p (b o) -> p b o", o=1), [128, 4, 128]), ALU.mult, r=bk + ["xdtc"], w=["s1"])
                            P.tt("pool", hb[:, :, :], hb[:, :, :],
                                 bc(dtc[:, jl, 1, bs].rearrange("p (b o) -> p b o", o=1), [128, 4, 128]), ALU.mult, r=[hk, "dtc"], w=[hk])
                            P.tt("pool", nb[:, :, :], hb[:, :, :], s1[:, :, :], ALU.add, r=["s1", hk], w=[nk])
                            P.tt("dve", s1[:, :, :], nb[:, :, :], pb[4 + q][:, :].rearrange("p (b n) -> p b n", b=4), ALU.mult,
                                 r=[nk] + ck, w=["s1"])
                            P.S.add("dve", (lambda e, o=ys[:, jl, bs], i=s1[:, :, :]: e.tensor_reduce(out=o, in_=i, axis=AX.X, op=ALU.add)),
                                    ["s1"], ["ys"])
                            P.dma("pool", o_sssm[4 * q:4 * q + 4, 4 * g + 2 * jl:4 * g + 2 * jl + 2].rearrange("b h q n -> (h q) b n"), nb[:, :, :],
                                  r=[nk], dkey="o_" + nk)
                    P.tt("dve", gsm[:, :, :], xbs[:, 0:2, :], bc(Dcol[:, 2 * g:2 * g + 2].rearrange("p (j o) -> p j o", o=1), [128, 2, NS]), ALU.mult,
                         r=["xbs", "Dcol"], w=["gsm"])
                    P.tt("dve", gsm[:, :, :], gsm[:, :, :], ys[:, :, :], ALU.add, r=["gsm", "ys"], w=["gsm"])
                    P.tt("dve", gsm[:, :, :], gsm[:, :, :], szs[:, :, :], ALU.mult, r=["gsm", "szs"], w=["gsm"])
                    P.tt("dve", gsq[:, :, :], gsm[:, :, :], gsm[:, :, :], ALU.mult, r=["gsm"], w=["gsq"])
                    for jl in range(2):
                        P.mm(pb[0][:, 0:NS], ones_b[:, :], gsq[:, jl, :], jl == 0, jl == 1, r=["ones_b", "gsq"], w=["pb0"])
                    P.copy("dve", rsd[:, :], pb[0][:, 0:NS], r=["pb0"], w=["rsd"])
                    rstd_pool(rsd[:, :], rsd[:, :], 1.0 / 256, ["rsd"], ["rsd"])
                    for jl in range(2):
                        P.stt(gTs[:, jl, :], gsm[:, jl, :], gnw[:, 2 * g + jl:2 * g + jl + 1], rsd[:, :], ALU.mult, ALU.mult,
                              r=["gsm", "gnw", "rsd"], w=["gTs"])
                    for dt_ in range(8):
                        po, pk = pb[dt_ % 2], "pb%d" % (dt_ % 2)
                        for jl in range(2):
                            P.mm(po[:, 0:NS], ob[:, jl, dt_ * 128:(dt_ + 1) * 128], gTs[:, jl, :], jl == 0, jl == 1, r=[ok, "gTs"], w=[pk])
                        P.tt("dve", xres[:, dt_, SEQ:NT], xres[:, dt_, SEQ:NT], po[:, 0:NS], ALU.add, r=[pk, xk(dt_, 4)], w=[xk(dt_, 4)])
                P.S.barrier()

        def layer_mlstm():
            li = 2
            emit_norm(li)
            Win = W["l2_in_proj"].rearrange("(k p) j -> p k j", p=128)
            Wout = W["l2_out_proj"].rearrange("(k p) j -> p k j", p=128)
            st_C, st_n, st_m, st_conv = DR["state_l2_C"], DR["state_l2_n"], DR["state_l2_m"], DR["state_l2_conv"]
            o_pC, o_pn, o_pm, o_pconv = DR["p2_C"], DR["p2_n"], DR["p2_m"], DR["p2_conv"]
            o_sC, o_sn, o_sm, o_sconv = DR["s2_C"], DR["s2_n"], DR["s2_m"], DR["s2_conv"]
            bmask = cst[:, 640:768]
            gemask = cst[:, 768:896]
            ones_f = cst[:, 128:256]
            KS = 512.0 ** -0.5
            with ExitStack() as L:
                cw = P.sb(L, "cw", [128, 16, 5], F32)
                wq4 = P.sb(L, "wq4", [128, 16, 4, 4], F32)
                wif = P.sb(L, "wif", [128, 48, 8], BF16)
                mhw = P.sb(L, "mhw", [128, 16], F32)
                skp = P.sb(L, "skp", [128, 16], F32)
                bocol = P.sb(L, "bocol", [128, 16], F32)
                gacc = P.sb(L, "gacc", [128, 17, 8], F32)
                xcs_all = P.sb(L, "xcs_all", [128, 16, NS], F32)
                xms_all = P.sb(L, "xms_all", [128, 16, NS], F32)
                dg = P.sb(L, "dg", [128, 16, 128], BF16)
                Dm = P.sb(L, "Dm", [128, 16, 128], BF16)
                xr = [P.sb(L, "xr%d" % j, [128, 515], BF16) for j in range(4)]
                xc = P.sb(L, "xc", [128, 4, 512], BF16)
                qT = P.sb(L, "qT", [128, 4, 512], BF16)
                khT = P.sb(L, "khT", [128, 4, 512], BF16)
                szT = P.sb(L, "szT", [128, 4, 512], BF16)
                gT = P.sb(L, "gT", [128, 4, 512], BF16)
                wbuf = P.sb(L, "wbuf", [128, 8, 1024], BF16)
                wo = P.sb(L, "wo", [128, 4, D], BF16)
                ctail = P.sb(L, "ctail", [128, 3], F32)
                ctout = P.sb(L, "ctout", [3, 128], F32)
                cstk = P.sb(L, "cstk", [48, 128], F32)
                cs = P.sb(L, "cs", [128, 48], F32)
                ncs = P.sb(L, "ncs", [128, 48], F32)
                sA = P.sb(L, "sA", [128, 4, NS], F32)
                sB = P.sb(L, "sB", [128, 4, NS], BF16)
                gcol = P.sb(L, "gcol", [128, 128], F32)
                ga = P.sb(L, "ga", [64, 128], F32)
                gs1 = P.sb(L, "gs1", [64, 8], F32)
                ends4 = P.sb(L, "ends4", [4, 16, 2], F32)
                m4 = P.sb(L, "m4", [4, 2, 16], F32)
                tokq = P.sb(L, "tokq", [128, 5, 64], F32)
                vtok = P.sb(L, "vtok", [128, 512], BF16)
                kwt = P.sb(L, "kwt", [128, 512], BF16)
                otok = P.sb(L, "otok", [128, 512], BF16)
                bobc = P.sb(L, "bobc", [128, 512], F32)
                Sm = P.sb(L, "Sm", [128, 128], F32)
                Eg_ = P.sb(L, "Eg_", [128, 128], F32)
                wls = P.sb(L, "wls", [128, 128], BF16)
                wT = P.sb(L, "wT", [128, 128], BF16)
                Asb = P.sb(L, "Asb", [128, 512], F32)
                num = P.sb(L, "num", [128, 512], F32)
                sm = P.sb(L, "sm", [128, 16], F32)
                st6 = P.sb(L, "st6", [128, 8], F32)
                hn = P.sb(L, "hn", [128, 512], BF16)
                g1 = P.sb(L, "g1", [128, 128], F32)
                sxc = P.sb(L, "sxc", [128, 128], F32)
                CT = P.sb(L, "CT", [128, 4, 512], F32)
                CTb = P.sb(L, "CTb", [128, 4, 512], BF16)
                nst = P.sb(L, "nst", [128, 4], F32)
                nstb = P.sb(L, "nstb", [128, 4], BF16)
                Cout = P.sb(L, "Cout", [128, 512], F32)
                sg = P.sb(L, "sg", [NS, 64], F32)
                n0t = P.sb(L, "n0t", [NS, 512], F32)
                qtok = P.sb(L, "qtok", [NS, 512], F32)
                wid = P.sb(L, "wid", [NS, NS, 8], F32)
                wibc = P.sb(L, "wibc", [128, NS, 8], F32)
                Cnt1 = P.sb(L, "Cnt1", [128, 512], F32)
                numT = P.sb(L, "numT", [128, 4, NS], F32)
                hsT = P.sb(L, "hsT", [128, 4, NS], F32)
                hs2 = P.sb(L, "hs2", [128, 4, NS], F32)
                mrs = P.sb(L, "mrs", [128, 4, NS], F32)
                gTs = P.sb(L, "gTs", [128, 4, NS], BF16)
                pbT1 = pb[1].bitcast(BF16)
                scr = P.nc.dram_tensor("ml_scr", [64, 4], F32, kind="Internal").ap()

                for k in range(4):
                    P.dma("sp", cw[:, :, k], W["l2_conv_w"][k].rearrange("(j p) -> p j", p=128), w=["cw"], dkey="cw%d" % k, nonc=True)
                P.dma("sp", cw[:, :, 4], W["l2_conv_b"].rearrange("(j p) -> p j", p=128), w=["cw"], dkey="cw4", nonc=True)
                for pi, nm in enumerate(("w_q", "w_k", "w_v", "w_o")):
                    P.dma("sp", wq4[:, :, pi, :], W["l2_" + nm].rearrange("n j i -> (n j) i").rearrange("(t p) i -> p t i", p=128),
                          w=["wq4"], dkey="wq4%d" % pi, nonc=True)
                P.dma("pool", wif[:, :, :], W["l2_w_if"].rearrange("(t p) g -> p t g", p=128), w=["wif"], dkey="wif")
                P.dma("sp", mhw[:, :], W["l2_mh_norm_w"].rearrange("(k p) -> p k", p=128), w=["mhw"], dkey="mhw", nonc=True)
                P.dma("sp", skp[:, :], W["l2_skip"].rearrange("(k p) -> p k", p=128), w=["skp"], dkey="skp", nonc=True)
                P.dma("sp", bocol[:, :], W["l2_b_o"].rearrange("(k p) -> p k", p=128), w=["bocol"], dkey="bocol", nonc=True)
                bif2 = W["l2_b_if"].rearrange("(g o) -> g o", o=1)
                for h in range(4):
                    P.dma("sp", gs1[h * 16:(h + 1) * 16, 0:1], bc(bif2[h:h + 1, :], [16, 1]), w=["gs1"], dkey="gs1a%d" % h, nonc=True)
                    P.dma("sp", gs1[h * 16:(h + 1) * 16, 1:2], bc(bif2[4 + h:5 + h, :], [16, 1]), w=["gs1"], dkey="gs1b%d" % h, nonc=True)
                P.ts("dve", gs1[:, 2:3], gs1[:, 1:2], -1.0, None, ALU.mult, r=["gs1"], w=["gs1"])
                P.memset("dve", gs1[:, 3:4], 1.0, w=["gs1"])
                P.memset("dve", wid[:, :, :], 0.0, w=["wid"])

                def build_D(dst_idx, jt, pi):
                    P.tt("dve", Dm[:, dst_idx, :].rearrange("p (n i) -> p n i", i=4),
                         bc(wq4[:, jt, pi:pi + 1, :], [128, 32, 4]), bmask.rearrange("p (n i) -> p n i", i=4), ALU.mult,
                         r=["wq4", "cst"], w=["Dm"])

                def build_dg(dst_base, jt):
                    for k in range(4):
                        P.ts("dve", dg[:, dst_base + k, :], ident_f, cw[:, jt, k:k + 1], None, ALU.mult, r=["cst", "cw"], w=["dg"])

                def proj_conv(jt, jl, ti, wcols, wkey):
                    t0 = ti * 512
                    for kc in range(8):
                        P.mm(pb[0][:, :], wbuf[:, kc, wcols:wcols + 128], xn[:, kc, t0:t0 + 512], kc == 0, kc == 7, r=[wkey, xnk(ti)], w=["pb0"])
                    P.copy("act", xr[jl][:, 3:515], pb[0][:, :], r=["pb0"], w=["xr%d" % jl])
                    for k in range(4):
                        P.mm(pb[1][:, :], dg[:, jl * 4 + k, :], xr[jl][:, k:k + 512], k == 0, k == 3, r=["dg", "xr%d" % jl], w=["pb1"])
                    P.act(xc[:, jl, :], pb[1][:, :], AF.Silu, r=["pb1", "cw"], w=["xc.%d" % jl], bias=cw[:, jt, 4:5])

                stgA = cfg.get("ml_stage", 9)
                for jt in range(16 if stgA >= 1 else 0):
                    slot = jt % 4
                    wk = "wA%d" % slot
                    P.dma("pool", wbuf[:, :, slot * 128:(slot + 1) * 128], Win[:, :, jt * 128:(jt + 1) * 128], w=[wk], dkey=wk)
                    build_dg(0, jt)
                    for pi in range(3):
                        build_D(pi * 4, jt, pi)
                    P.memset("dve", xr[0][:, 0:3], 0.0, w=["xr0"])
                    for ti in range(4):
                        proj_conv(jt, 0, ti, slot * 128, wk)
                        if ti == 3:
                            P.copy("dve", ctail[:, :], pb[0][:, 509:512], r=["pb0"], w=["ctail"])
                        else:
                            P.copy("pool", xr[0][:, 0:3], xr[0][:, 512:515], r=["xr0"], w=["xr0"])
                        P.mm(pb[2][:, :], Dm[:, 0, :], xc[:, 0, :], True, True, r=["Dm", "xc.0"], w=["pb2"])
                        P.mm(pb[3][:, :], Dm[:, 4, :], xc[:, 0, :], True, True, r=["Dm", "xc.0"], w=["pb3"])
                        P.mm(pb[4][:, :], Dm[:, 8, :], xr[0][:, 3:515], True, True, r=["Dm", "xr0"], w=["pb4"])
                        P.copy("act", qT[:, 0, :], pb[2][:, :], r=["pb2"], w=["qT"])
                        P.copy("dve", khT[:, 0, :], pb[3][:, :], r=["pb3"], w=["khT"])
                        P.copy("act", szT[:, 0, :], pb[4][:, :], r=["pb4"], w=["szT"])
                        for c4 in range(4):
                            cs_ = slice(c4 * 128, (c4 + 1) * 128)
                            o_ = pb[5][:, c4 * 8:(c4 + 1) * 8]
                            P.mm(o_, qT[:, 0, cs_], wif[:, jt, :], True, False, r=["qT", "wif"], w=["pb5"])
                            P.mm(o_, khT[:, 0, cs_], wif[:, 16 + jt, :], False, False, r=["khT", "wif"], w=["pb5"])
                            P.mm(o_, szT[:, 0, cs_], wif[:, 32 + jt, :], False, True, r=["szT", "wif"], w=["pb5"])
                        gv = gacc[:, 4 * ti:4 * ti + 4, :]
                        pv = pb[5][:, 0:32].rearrange("p (c g) -> p c g", g=8)
                        if jt == 0:
                            P.copy("dve", gv, pv, r=["pb5"], w=["gacc.%d" % ti])
                        else:
                            P.tt("dve", gv, gv, pv, ALU.add, r=["pb5", "gacc.%d" % ti], w=["gacc.%d" % ti])
                    P.tr(pb[6][0:3, 0:128], ctail[:, :], ident_f, r=["ctail", "cst"], w=["pb6"])
                    P.copy("dve", ctout[:, :], pb[6][0:3, 0:128], r=["pb6"], w=["ctout"])
                    P.dma("sp", o_pconv[:, jt * 128:(jt + 1) * 128], ctout[:, :], r=["ctout"], dkey="o_ctout")
                    for kc in range(8):
                        P.mm(pb[6][:, 128:128 + NS], wbuf[:, kc, slot * 128:(slot + 1) * 128], xn[:, kc, SEQ:NT], kc == 0, kc == 7, r=[wk, xnk(4)], w=["pb6"])
                    P.copy("dve", xms_all[:, jt, :], pb[6][:, 128:128 + NS], r=["pb6"], w=["xms_all"])
                    P.dma("sp", cstk[:, :], st_conv[:, :, jt * 128:(jt + 1) * 128].rearrange("b k c -> (b k) c"), w=["cstk"], dkey="cstk")
                    P.tr(pb[6][:, 256:304], cstk[:, :], ident_f[0:48, 0:48], r=["cstk", "cst"], w=["pb6"])
                    P.copy("dve", cs[:, :], pb[6][:, 256:304], r=["pb6"], w=["cs"])
                    cs3 = cs[:, :].rearrange("p (b k) -> p b k", k=3)
                    ncs3 = ncs[:, :].rearrange("p (b k) -> p b k", k=3)
                    a0 = sA[:, 0, :]
                    a1 = sA[:, 1, :]
                    P.ts("dve", a0, xms_all[:, jt, :], cw[:, jt, 3:4], cw[:, jt, 4:5], ALU.mult, ALU.add, r=["xms_all", "cw"], w=["sA"])
                    for k in range(3):
                        P.stt(a0, cs3[:, :, k], cw[:, jt, k:k + 1], a0, ALU.mult, ALU.add, r=["cs", "cw", "sA"], w=["sA"])
                    P.act(xcs_all[:, jt, :], a0, AF.Silu, r=["sA"], w=["xcs_all"])
                    P.copy("dve", ncs3[:, :, 0:2], cs3[:, :, 1:3], r=["cs"], w=["ncs"])
                    P.copy("dve", ncs3[:, :, 2], xms_all[:, jt, :], r=["xms_all", "ncs"], w=["ncs"])
                    P.tr(pb[6][0:48, 320:448], ncs[:, :], ident_f, r=["ncs", "cst"], w=["pb6"])
                    P.copy("dve", cstk[:, :], pb[6][0:48, 320:448], r=["pb6"], w=["cstk"])
                    P.dma("sp", o_sconv[:, :, jt * 128:(jt + 1) * 128].rearrange("b k c -> (b k) c"), cstk[:, :], r=["cstk"], dkey="o_cstk")
                    P.copy("dve", sB[:, 0, :], xcs_all[:, jt, :], r=["xcs_all"], w=["sB"])
                    P.copy("dve", sB[:, 1, :], xms_all[:, jt, :], r=["xms_all"], w=["sB"])
                    for pi in range(3):
                        P.mm(pb[7][:, pi * NS:(pi + 1) * NS], Dm[:, pi * 4, :], sB[:, 0 if pi < 2 else 1, :], True, True, r=["Dm", "sB"], w=["pb7"])
                    P.copy("dve", sB[:, 2:4, :].rearrange("p a b -> p (a b)")[:, 0:2 * NS], pb[7][:, 0:2 * NS], r=["pb7"], w=["sB2"])
                    P.copy("dve", sA[:, 2, :].bitcast(BF16)[:, 0:NS], pb[7][:, 2 * NS:3 * NS], r=["pb7"], w=["sA2"])
                    sv = sA[:, 2, :].bitcast(BF16)[:, 0:NS]
                    P.mm(pb[7][0:NS, 64:72], sB[:, 2, :], wif[:, jt, :], True, False, r=["sB2", "wif"], w=["pb7"])
                    P.mm(pb[7][0:NS, 64:72], sB[:, 3, :], wif[:, 16 + jt, :], False, False, r=["sB2", "wif"], w=["pb7"])
                    P.mm(pb[7][0:NS, 64:72], sv, wif[:, 32 + jt, :], False, True, r=["sA2", "wif"], w=["pb7"])
                    if jt == 0:
                        P.copy("dve", gacc[0:NS, 16, :], pb[7][0:NS, 64:72], r=["pb7"], w=["gacc.4"])
                    else:
                        P.tt("dve", gacc[0:NS, 16, :], gacc[0:NS, 16, :], pb[7][0:NS, 64:72], ALU.add, r=["pb7", "gacc.4"], w=["gacc.4"])

                if stgA >= 2:
                    gk = ["gacc.%d" % i for i in range(4)]
                    P.copy("dve", gcol[:, :].rearrange("p (g c) -> p c g", c=16), gacc[:, 0:16, :], r=gk, w=["gcol"])
                    P.memset("dve", num[0:64, 256:384], 1.0, w=["gq7"])
                    P.tr(pb[0][0:64, 0:128], gcol[:, 0:64], ident_f, r=["gcol", "cst"], w=["pb0"])
                    P.tr(pb[0][0:64, 128:256], gcol[:, 64:128], ident_f, r=["gcol", "cst"], w=["pb0"])
                    ig = ga[:, :]
                    lf, cm, wi_, em_ = [Asb[0:64, i * 128:(i + 1) * 128] for i in range(4)]
                    we_, dpb, one_ = [num[0:64, i * 128:(i + 1) * 128] for i in range(3)]
                    P.ts("dve", ig, pb[0][0:64, 0:128], gs1[:, 0:1], None, ALU.add, r=["pb0", "gs1"], w=["gq0"])
                    P.act(lf, pb[0][0:64, 128:256], AF.Exp, r=["pb0", "gs1"], w=["gq1"], bias=gs1[:, 2:3], scale=-1.0)
                    P.act(lf, lf, AF.Ln, r=["gq1", "gs1"], w=["gq1"], bias=gs1[:, 3:4])
                    P.ts("dve", lf, lf, -1.0, None, ALU.mult, r=["gq1"], w=["gq1"])
                    P.S.add("dve", (lambda e: e.tensor_tensor_scan(out=lf, data0=one_, data1=lf, initial=0.0, op0=ALU.mult, op1=ALU.add)),
                            ["gq1", "gq7"], ["gq1"])
                    P.tt("dve", ig, ig, lf, ALU.subtract, r=["gq0", "gq1"], w=["gq0"])
                    P.S.add("dve", (lambda e: e.tensor_tensor_scan(out=cm, data0=ig, data1=ig, initial=-1e30, op0=ALU.max, op1=ALU.max)),
                            ["gq0"], ["gq2"])
                    P.copy("dve", gs1[:, 4:5], cm[:, 127:128], r=["gq2"], w=["gs1e"])
                    P.copy("dve", gs1[:, 5:6], lf[:, 127:128], r=["gq1"], w=["gs1e"])
                    P.dma("sp", scr[:, 0:2], gs1[:, 4:6], r=["gs1e"], w=["scr01"], dkey="scr_a", nonc=True)
                    P.dma("sp", ends4[:, :, :], scr[:, 0:2].rearrange("(h c) t -> h c t", c=16), r=["scr01"], w=["ends4"], dkey="scr_b", nonc=True)
                    P.S.add("dve", (lambda e: e.tensor_tensor_scan(out=m4[:, 0, :], data0=ends4[:, :, 0], data1=ends4[:, :, 1], initial=0.0,
                                                                   op0=ALU.max, op1=ALU.add)), ["ends4"], ["m4"])
                    P.memset("dve", m4[:, 1, 0:1], 0.0, w=["m4b"])
                    P.copy("dve", m4[:, 1, 1:16], m4[:, 0, 0:15], r=["m4"], w=["m4b"])
                    P.dma("sp", o_pm.rearrange("(h o) -> h o", o=1), m4[:, 0, 15:16], r=["m4"], dkey="o_pm", nonc=True)
                    P.dma("sp", scr[:, 2:3].rearrange("(h c) t -> h c t", c=16), m4[:, 1, :].rearrange("h (c o) -> h c o", o=1), r=["m4b"], w=["scr2"],
                          dkey="scr_c", nonc=True)
                    P.dma("sp", gs1[:, 6:7], scr[:, 2:3], r=["scr2"], w=["gs1m"], dkey="scr_d", nonc=True)
                    mprev = gs1[:, 6:7]
                    P.ts("dve", cm, cm, mprev, None, ALU.max, r=["gq2", "gs1m"], w=["gq2"])
                    P.act(wi_, cm, AF.Exp, r=["gq2", "gs1m"], w=["gq3"], bias=mprev, scale=-1.0)
                    P.tt("dve", em_, lf, cm, ALU.add, r=["gq1", "gq2"], w=["gq4"])
                    P.act(em_, em_, AF.Exp, r=["gq4"], w=["gq4"], scale=-1.0)
                    P.ts("dve", gs1[:, 7:8], cm[:, 127:128], -1.0, math.log(KS), ALU.mult, ALU.add, r=["gq2"], w=["gs1n"])
                    P.act(we_, ig, AF.Exp, r=["gq0", "gs1n"], w=["gq5"], bias=gs1[:, 7:8])
                    P.tt("dve", gs1[:, 4:5], mprev, cm[:, 127:128], ALU.subtract, r=["gs1m", "gq2", "gs1e"], w=["gs1e"])
                    P.act(gs1[:, 4:5], gs1[:, 4:5], AF.Exp, r=["gs1e"], w=["gs1e"])
                    P.ts("dve", dpb, one_, gs1[:, 4:5], None, ALU.mult, r=["gq7", "gs1e"], w=["gq6"])
                    for qi, src in enumerate((cm, wi_, em_, we_, dpb)):
                        P.tr(pb[2][:, qi * 64:(qi + 1) * 64], src, ident_f[0:64, 0:64], r=["gq%d" % (2 + qi), "cst"], w=["pb2"])
                    P.copy("dve", tokq[:, :, :], pb[2][:, 0:320].rearrange("p (q x) -> p q x", q=5), r=["pb2"], w=["tokq"])

                    P.dma("sp", sg[:, 28:36], bc(W["l2_b_if"].rearrange("(o g) -> o g", o=1), [NS, 8]), w=["sg"], dkey="sg_b")
                    P.dma("sp", sg[:, 8:12], st_m[:, :], w=["sg"], dkey="sg_m")
                    P.tt("dve", sg[:, 0:8], gacc[0:NS, 16, :], sg[:, 28:36], ALU.add, r=["gacc.4", "sg"], w=["sg"])
                    P.act(sg[:, 4:8], sg[:, 4:8], AF.Exp, r=["sg"], w=["sg"], scale=-1.0)
                    P.act(sg[:, 4:8], sg[:, 4:8], AF.Ln, r=["sg", "gs1"], w=["sg"], bias=gs1[0:NS, 3:4])
                    P.ts("dve", sg[:, 4:8], sg[:, 4:8], -1.0, None, ALU.mult, r=["sg"], w=["sg"])
                    P.tt("dve", sg[:, 8:12], sg[:, 8:12], sg[:, 4:8], ALU.add, r=["sg"], w=["sg"])
                    P.tt("dve", sg[:, 12:16], sg[:, 8:12], sg[:, 0:4], ALU.max, r=["sg"], w=["sg"])
                    P.dma("sp", o_sm[:, :], sg[:, 12:16], r=["sg"], dkey="o_sm")
                    P.tt("dve", sg[:, 16:20], sg[:, 0:4], sg[:, 12:16], ALU.subtract, r=["sg"], w=["sg"])
                    P.act(sg[:, 16:20], sg[:, 16:20], AF.Exp, r=["sg"], w=["sg"])
                    P.tt("dve", sg[:, 20:24], sg[:, 8:12], sg[:, 12:16], ALU.subtract, r=["sg"], w=["sg"])
                    P.act(sg[:, 20:24], sg[:, 20:24], AF.Exp, r=["sg"], w=["sg"])
                    P.act(sg[:, 24:28], sg[:, 12:16], AF.Exp, r=["sg"], w=["sg"], scale=-1.0)
                    P.tt("dve", wid[:, :, 0:4], bc(sg[:, 20:24].rearrange("p (o h) -> p o h", o=1), [NS, NS, 4]),
                         bc(ident_f[0:NS, 0:NS].rearrange("p (b o) -> p b o", o=1), [NS, NS, 4]), ALU.mult, r=["sg", "cst"], w=["wid"])

                nheads = cfg.get("ml_heads", 4) if stgA >= 3 else 0
                for h in range(nheads):
                    P.dma("pool", wbuf[:, :, 0:512], Win[:, :, 512 * h:512 * (h + 1)], w=["wA0", "wA1", "wA2", "wA3"], dkey="wBx")
                    P.dma("pool", wbuf[:, :, 512:1024], Win[:, :, 2048 + 512 * h:2048 + 512 * (h + 1)], w=["wBz"], dkey="wBz")
                    P.dma("pool", wo[:, :, :], Wout[:, 4 * h:4 * h + 4, :], w=["wo"], dkey="wo")
                    P.dma("sp", bobc[:, :], bc(W["l2_b_o"][512 * h:512 * (h + 1)].rearrange("(o n) -> o n", o=1), [128, 512]), w=["bobc"], dkey="bobc")
                    wxk = ["wA0", "wA1", "wA2", "wA3"]
                    for jl in range(4):
                        build_dg(jl * 4, 4 * h + jl)
                        for pi in range(4):
                            build_D(pi * 4 + jl, 4 * h + jl, pi)
                        P.memset("dve", xr[jl][:, 0:3], 0.0, w=["xr%d" % jl])
                    P.memset("dve", CT[:, :, :], 0.0, w=["CT"])
                    P.memset("pool", CTb[:, :, :], 0.0, w=["CTb"])
                    P.memset("dve", nst[:, :], 0.0, w=["nst"])
                    P.memset("dve", nstb[:, :], 0.0, w=["nstb"])
                    for ti in range(4):
                        t0 = ti * 512
                        for jl in range(4):
                            jt = 4 * h + jl
                            proj_conv(jt, jl, ti, jl * 128, wxk[jl])
                            if ti < 3:
                                P.copy("pool", xr[jl][:, 0:3], xr[jl][:, 512:515], r=["xr%d" % jl], w=["xr%d" % jl])
                            P.mm(pb[2][:, :], Dm[:, 0 + jl, :], xc[:, jl, :], True, True, r=["Dm", "xc.%d" % jl], w=["pb2"])
                            P.mm(pb[3][:, :], Dm[:, 4 + jl, :], xc[:, jl, :], True, True, r=["Dm", "xc.%d" % jl], w=["pb3"])
                            P.copy("dve", qT[:, jl, :], pb[2][:, :], r=["pb2"], w=["qT"])
                            P.act(khT[:, jl, :], pb[3][:, :], AF.Copy, r=["pb3"], w=["khT"], scale=KS)
                            for kc in range(8):
                                P.mm(pb[4][:, :], wbuf[:, kc, 512 + jl * 128:512 + (jl + 1) * 128], xn[:, kc, t0:t0 + 512], kc == 0, kc == 7,
                                     r=["wBz", xnk(ti)], w=["pb4"])
                            P.act(szT[:, jl, :], pb[4][:, :], AF.Silu, r=["pb4"], w=["szT"])
                        for c4 in range(4):
                            c = 4 * ti + c4
                            hc = h * 16 + c
                            cs_ = slice(c4 * 128, (c4 + 1) * 128)
                            xs_ = slice(3 + c4 * 128, 3 + (c4 + 1) * 128)
                            for jl in range(4):
                                js = slice(jl * 128, (jl + 1) * 128)
                                P.mm(pb[2][:, js], xr[jl][:, xs_], Dm[:, 8 + jl, :], True, True, r=["xr%d" % jl, "Dm"], w=["pb2"])
                                P.mm(pb[3][:, js], xc[:, jl, cs_], Dm[:, 4 + jl, :], True, True, r=["xc.%d" % jl, "Dm"], w=["pb3"])
                                P.mm(pb[4][:, js], xr[jl][:, xs_], Dm[:, 12 + jl, :], True, True, r=["xr%d" % jl, "Dm"], w=["pb4"])
                            P.copy("act", vtok[:, :], pb[2][:, :], r=["pb2"], w=["vtok"])
                            P.act(kwt[:, :], pb[3][:, :], AF.Copy, r=["pb3", "tokq"], w=["kwt"], scale=tokq[:, 3, hc:hc + 1])
                            P.tt("dve", num[:, :], pb[4][:, :], bobc[:, :], ALU.add, r=["pb4", "bobc"], w=["num"])
                            P.act(otok[:, :], num[:, :], AF.Tanh, r=["num"], w=["otok"], scale=0.5)
                            P.ts("pool", otok[:, :], otok[:, :], 0.5, 0.5, ALU.mult, ALU.add, r=["otok"], w=["otok"])
                            for jl in range(4):
                                P.mm(pb[5][:, 0:128], qT[:, jl, cs_], khT[:, jl, cs_], jl == 0, jl == 3, r=["qT", "khT"], w=["pb5"])
                            P.mm(pb[5][:, 128:256], bc(ident_f[0:64, hc:hc + 1], [64, 128]), ga[:, :], True, True, r=["cst", "gq0"], w=["pb5"])
                            for jl in range(4):
                                P.mm(pb[5][:, 256:257], qT[:, jl, cs_], nstb[:, jl:jl + 1], jl == 0, jl == 3, r=["qT", "nstb"], w=["pb5"])
                            P.tt("dve", Sm[:, :], pb[5][:, 0:128], gemask, ALU.mult, r=["pb5", "cst"], w=["Sm"])
                            P.ts("dve", Eg_[:, :], pb[5][:, 128:256], tokq[:, 0, hc:hc + 1], 0.0, ALU.subtract, ALU.min, r=["pb5", "tokq"], w=["Eg_"])
                            P.act(Eg_[:, :], Eg_[:, :], AF.Exp, r=["Eg_"], w=["Eg_"])
                            P.S.add("dve", (lambda e, o=wls[:, :], a=Eg_[:, :], b=Sm[:, :], ac=sm[:, 0:1]:
                                            e.scalar_tensor_tensor(out=o, in0=a, scalar=1.0, in1=b, op0=ALU.mult, op1=ALU.mult, accum_out=ac)),
                                    ["Eg_", "Sm"], ["wls", "sm0"])
                            P.tr(pbT1[:, 0:128], wls[:, :], ident_b[:, :], r=["wls", "ident_b"], w=["pb1"])
                            P.copy("dve", wT[:, :], pbT1[:, 0:128], r=["pb1"], w=["wT"])
                            P.mm(pb[6][:, :], wT[:, :], vtok[:, :], True, True, r=["wT", "vtok"], w=["pb6"])
                            for jl in range(4):
                                P.mm(pb[7][:, :], qT[:, jl, cs_], CTb[:, jl, :], jl == 0, jl == 3, r=["qT", "CTb"], w=["pb7"])
                            P.copy("act", Asb[:, :], pb[6][:, :], r=["pb6"], w=["Asb"])
                            wic = tokq[:, 1, hc:hc + 1]
                            P.stt(num[:, :], pb[7][:, :], wic, Asb[:, :], ALU.mult, ALU.add, r=["pb7", "tokq", "Asb"], w=["num"])
                            P.stt(sm[:, 1:2], pb[5][:, 256:257], wic, sm[:, 0:1], ALU.mult, ALU.add, r=["pb5", "tokq", "sm0"], w=["sm1"])
                            P.stt(sm[:, 1:2], sm[:, 1:2], -1.0, sm[:, 1:2], ALU.mult, ALU.max, r=["sm1"], w=["sm1"])
                            P.tt("dve", sm[:, 1:2], sm[:, 1:2], tokq[:, 2, hc:hc + 1], ALU.max, r=["sm1", "tokq"], w=["sm1"])
                            P.recip(sm[:, 1:2], sm[:, 1:2], r=["sm1"], w=["sm1"])
                            P.stt(num[:, :], num[:, :], sm[:, 1:2], otok[:, :], ALU.mult, ALU.mult, r=["num", "sm1", "otok"], w=["num"])
                            P.S.add("dve", (lambda e, o=st6[:, 0:6], i=num[:, :]: e.bn_stats(out=o, in_=i)), ["num"], ["st6"])
                            P.S.add("dve", (lambda e, o=sm[:, 2:4], i=st6[:, 0:6]: e.bn_aggr(out=o, in_=i)), ["st6"], ["sm2"])
                            rstd_pool(sm[:, 4:5], sm[:, 3:4], 1.0, ["sm2"], ["sm4"])
                            P.stt(sm[:, 5:6], sm[:, 2:3], -1.0, sm[:, 4:5], ALU.mult, ALU.mult, r=["sm2", "sm4"], w=["sm5"])
                            P.ts("dve", hn[:, :], num[:, :], sm[:, 4:5], sm[:, 5:6], ALU.mult, ALU.add, r=["num", "sm4", "sm5"], w=["hn"])
                            for jl in range(4):
                                P.tr(pbT1[:, 256 + jl * 128:256 + (jl + 1) * 128], hn[:, jl * 128:(jl + 1) * 128], ident_b[:, :], r=["hn", "ident_b"], w=["pb1"])
                            for jl in range(4):
                                jt = 4 * h + jl
                                P.ts("dve", sxc[:, :], xc[:, jl, cs_], skp[:, jt:jt + 1], None, ALU.mult, r=["xc.%d" % jl, "skp"], w=["sxc"])
                                P.stt(g1[:, :], pbT1[:, 256 + jl * 128:256 + (jl + 1) * 128], mhw[:, jt:jt + 1], sxc[:, :], ALU.mult, ALU.add,
                                      r=["pb1", "mhw", "sxc"], w=["g1"])
                                P.tt("pool", gT[:, jl, cs_], g1[:, :], szT[:, jl, cs_], ALU.mult, r=["g1", "szT"], w=["gT"])
                            dpc = tokq[:, 4, hc:hc + 1]
                            for kt in range(4):
                                pu, puk = (pb[0], "pb0") if kt % 2 == 0 else (pb[6], "pb6")
                                P.mm(pu[:, :], kwt[:, kt * 128:(kt + 1) * 128], vtok[:, :], True, True, r=["kwt", "vtok"], w=[puk])
                                P.stt(CT[:, kt, :], CT[:, kt, :], dpc, pu[:, :], ALU.mult, ALU.add, r=["CT", "tokq", puk], w=["CT"])
                                P.copy("act", CTb[:, kt, :], CT[:, kt, :], r=["CT"], w=["CTb"])
                                P.mm(pb[5][:, 260 + kt:261 + kt], kwt[:, kt * 128:(kt + 1) * 128], ones_b[:, 0:1], True, True, r=["kwt", "ones_b"], w=["pb5"])
                            P.stt(nst[:, :], nst[:, :], dpc, pb[5][:, 260:264], ALU.mult, ALU.add, r=["nst", "tokq", "pb5"], w=["nst"])
                            P.copy("dve", nstb[:, :], nst[:, :], r=["nst"], w=["nstb"])
                        for dt_ in range(8):
                            po, pk = pb[dt_ % 2], "pb%d" % (dt_ % 2)
                            for jl in range(4):
                                P.mm(po[:, :], wo[:, jl, dt_ * 128:(dt_ + 1) * 128], gT[:, jl, :], jl == 0, jl == 3, r=["wo", "gT"], w=[pk])
                            P.tt("dve", xres[:, dt_, t0:t0 + 512], xres[:, dt_, t0:t0 + 512], po[:, :], ALU.add, r=[pk, xk(dt_, ti)], w=[xk(dt_, ti)])
                    for vt in range(4):
                        for kt in range(4):
                            P.tr(pb[2][:, kt * 128:(kt + 1) * 128], CT[:, kt, vt * 128:(vt + 1) * 128], ident_f, r=["CT", "cst"], w=["pb2"])
                        P.copy("dve", Cout[:, :], pb[2][:, :], r=["pb2"], w=["Cout"])
                        P.dma("sp", o_pC[h, vt * 128:(vt + 1) * 128, :], Cout[:, :], r=["Cout"], dkey="o_Cout")
                    P.tr(pb[3][0:4, 0:128], nst[:, :], ident_f, r=["nst", "cst"], w=["pb3"])
                    P.copy("dve", ctout[:, :].bitcast(F32)[0:3, :], pb[3][0:3, 0:128], r=["pb3"], w=["ctout"]) if False else None
                    P.copy("dve", st6[0:4, 0:8].bitcast(F32), pb[3][0:4, 0:8], r=["pb3"], w=["st6"]) if False else None
                    P.copy("dve", Cout[0:4, 0:128], pb[3][0:4, 0:128], r=["pb3"], w=["Cout"])
                    P.dma("sp", o_pn[h].rearrange("(kt k) -> kt k", k=128), Cout[0:4, 0:128], r=["Cout"], dkey="o_pn")

                    if cfg.get("ml_nosample"):
                        continue
                    for jl in range(4):
                        jt = 4 * h + jl
                        P.copy("dve", sB[:, 0, :], xcs_all[:, jt, :], r=["xcs_all"], w=["sB"])
                        P.copy("dve", sB[:, 1, :], xms_all[:, jt, :], r=["xms_all"], w=["sB"])
                        js = slice(jl * 128, (jl + 1) * 128)
                        P.mm(pb[2][0:NS, js], sB[:, 0, :], Dm[:, 4 + jl, :], True, True, r=["sB", "Dm"], w=["pb2"])
                        P.mm(pb[3][0:NS, js], sB[:, 0, :], Dm[:, 0 + jl, :], True, True, r=["sB", "Dm"], w=["pb3"])
                        P.mm(pb[4][0:NS, js], sB[:, 1, :], Dm[:, 8 + jl, :], True, True, r=["sB", "Dm"], w=["pb4"])
                        P.mm(pb[5][:, jl * NS:(jl + 1) * NS], Dm[:, 12 + jl, :], sB[:, 1, :], True, True, r=["sB", "Dm"], w=["pb5"])
                        for kc in range(8):
                            P.mm(pb[5][:, 64 + jl * NS:64 + (jl + 1) * NS], wbuf[:, kc, 512 + jl * 128:512 + (jl + 1) * 128], xn[:, kc, SEQ:NT], kc == 0, kc == 7,
                                 r=["wBz", xnk(4)], w=["pb5"])
                        P.ts("dve", mrs[:, jl, :], pb[5][:, jl * NS:(jl + 1) * NS], bocol[:, jt:jt + 1], None, ALU.add, r=["pb5", "bocol"], w=["mrs"])
                        P.act(mrs[:, jl, :], mrs[:, jl, :], AF.Tanh, r=["mrs"], w=["mrs"], scale=0.5)
                        P.ts("dve", mrs[:, jl, :], mrs[:, jl, :], 0.5, 0.5, ALU.mult, ALU.add, r=["mrs"], w=["mrs"])
                        P.act(hs2[:, jl, :], pb[5][:, 64 + jl * NS:64 + (jl + 1) * NS], AF.Silu, r=["pb5"], w=["hs2"])
                    ktok, vwt, kmk, qmk = bobc[0:NS, :], otok[0:NS, :], vtok[0:NS, :], kwt[0:NS, :]
                    C0t, Cnt = [Asb, num], [Cout, Cnt1]
                    P.ts("dve", ktok, pb[2][0:NS, :], KS, None, ALU.mult, r=["pb2"], w=["bobc"])
                    P.copy("dve", qtok[:, :], pb[3][0:NS, :], r=["pb3"], w=["qtok"])
                    P.ts("dve", vwt, pb[4][0:NS, :], sg[:, 16 + h:17 + h], None, ALU.mult, r=["pb4", "sg"], w=["otok"])
                    P.dma("sp", n0t[:, :], st_n[:, h, :], w=["n0t"], dkey="n0t")
                    P.ts("dve", n0t[:, :], n0t[:, :], sg[:, 20 + h:21 + h], None, ALU.mult, r=["n0t", "sg"], w=["n0t"])
                    P.stt(n0t[:, :], ktok, sg[:, 16 + h:17 + h], n0t[:, :], ALU.mult, ALU.add, r=["bobc", "sg", "n0t"], w=["n0t"])
                    P.dma("sp", o_sn[:, h, :], n0t[:, :], r=["n0t"], dkey="o_sn")
                    P.S.add("dve", (lambda e, o=Cout[0:NS, :], a=n0t[:, :], b=qtok[:, :], ac=sg[:, 36:37]:
                                    e.scalar_tensor_tensor(out=o, in0=a, scalar=1.0, in1=b, op0=ALU.mult, op1=ALU.mult, accum_out=ac)),
                            ["n0t", "qtok", "Cout"], ["Cout", "sg36"])
                    P.stt(sg[:, 36:37], sg[:, 36:37], -1.0, sg[:, 36:37], ALU.mult, ALU.max, r=["sg36"], w=["sg36"])
                    P.tt("dve", sg[:, 36:37], sg[:, 36:37], sg[:, 24 + h:25 + h], ALU.max, r=["sg36", "sg"], w=["sg36"])
                    P.recip(sg[:, 36:37], sg[:, 36:37], r=["sg36"], w=["sg36"])
                    P.tt("dve", wid[:, :, 4:5], bc(sg[:, 36:37].rearrange("p (o h) -> p o h", o=1), [NS, NS, 1]),
                         bc(ident_f[0:NS, 0:NS].rearrange("p (b o) -> p b o", o=1), [NS, NS, 1]), ALU.mult, r=["sg36", "cst"], w=["wid"])
                    P.mm(pb[5][:, 128:256], ones_f[0:NS, :], wid[:, :, :].rearrange("p b x -> p (b x)"), True, True, r=["cst", "wid"], w=["pb5"])
                    P.copy("dve", wibc[:, :, :], pb[5][:, 128:256].rearrange("p (b x) -> p b x", x=8), r=["pb5"], w=["wibc"])
                    for b in range(NS):
                        P.ts("dve", kmk, ktok, ident_f[0:NS, b:b + 1], None, ALU.mult, r=["bobc", "cst"], w=["vtok"])
                        P.ts("dve", qmk, qtok[:, :], ident_f[0:NS, b:b + 1], None, ALU.mult, r=["qtok", "cst"], w=["kwt"])
                        P.mm(pb[7][:, :], ones_b[0:NS, :], qmk, True, True, r=["ones_b", "kwt"], w=["pb7"])
                        for vt in range(4):
                            i2 = (b * 4 + vt) % 2
                            c0, c0k = C0t[i2], ("Asb", "num")[i2]
                            cn, cnk = Cnt[i2], ("Cout", "Cnt1")[i2]
                            pu, puk = (pb[2], "pb2") if vt % 2 == 0 else (pb[3], "pb3")
                            P.dma("sp", c0[:, :], st_C[b, h, vt * 128:(vt + 1) * 128, :], w=[c0k], dkey=c0k)
                            P.mm(pu[:, :], vwt[:, vt * 128:(vt + 1) * 128], kmk, True, True, r=["otok", "vtok"], w=[puk])
                            P.stt(cn[:, :], c0[:, :], wibc[:, b, h:h + 1], pu[:, :], ALU.mult, ALU.add, r=[c0k, "wibc", puk], w=[cnk])
                            P.dma("pool", o_sC[b, h, vt * 128:(vt + 1) * 128, :], cn[:, :], r=[cnk], dkey="o_" + cnk)
                            P.S.add("dve", (lambda e, o=c0[:, :], a=cn[:, :], bb=pb[7][:, :], ac=numT[:, vt, b:b + 1]:
                                            e.scalar_tensor_tensor(out=o, in0=a, scalar=1.0, in1=bb, op0=ALU.mult, op1=ALU.mult, accum_out=ac)),
                                    [cnk, "pb7", c0k], [c0k, "numT"])
                    P.tt("dve", hsT[:, :, :], numT[:, :, :], bc(wibc[:, :, 4:5].rearrange("p b o -> p o b"), [128, 4, NS]), ALU.mult, r=["numT", "wibc"], w=["hsT"])
                    P.tt("dve", hsT[:, :, :], hsT[:, :, :], mrs[:, :, :], ALU.mult, r=["hsT", "mrs"], w=["hsT"])
                    P.tt("dve", numT[:, :, :], hsT[:, :, :], hsT[:, :, :], ALU.mult, r=["hsT"], w=["numT"])
                    for vt in range(4):
                        P.mm(pb[4][:, 0:NS], ones_f, hsT[:, vt, :], vt == 0, vt == 3, r=["cst", "hsT"], w=["pb4"])
                    for vt in range(4):
                        P.mm(pb[4][:, NS:2 * NS], ones_f, numT[:, vt, :], vt == 0, vt == 3, r=["cst", "numT"], w=["pb4"])
                    mean = sA[:, 0, :]
                    var = sA[:, 1, :]
                    P.ts("dve", mean, pb[4][:, 0:NS], 1.0 / 512, None, ALU.mult, r=["pb4"], w=["sA"])
                    P.ts("dve", var, pb[4][:, NS:2 * NS], 1.0 / 512, None, ALU.mult, r=["pb4"], w=["sA"])
                    P.tt("dve", sA[:, 3, :], mean, mean, ALU.mult, r=["sA"], w=["sA"])
                    P.tt("dve", var, var, sA[:, 3, :], ALU.subtract, r=["sA"], w=["sA"])
                    rstd_pool(var, var, 1.0, ["sA"], ["sA"])
                    P.tt("dve", hsT[:, :, :], hsT[:, :, :], bc(mean.rearrange("p (o b) -> p o b", o=1), [128, 4, NS]), ALU.subtract, r=["hsT", "sA"], w=["hsT"])
                    P.tt("dve", hsT[:, :, :], hsT[:, :, :], bc(var.rearrange("p (o b) -> p o b", o=1), [128, 4, NS]), ALU.mult, r=["hsT", "sA"], w=["hsT"])
                    for jl in range(4):
                        jt = 4 * h + jl
                        P.ts("dve", hsT[:, jl, :], hsT[:, jl, :], mhw[:, jt:jt + 1], None, ALU.mult, r=["hsT", "mhw"], w=["hsT"])
                        P.stt(hsT[:, jl, :], xcs_all[:, jt, :], skp[:, jt:jt + 1], hsT[:, jl, :], ALU.mult, ALU.add, r=["xcs_all", "skp", "hsT"], w=["hsT"])
                    P.tt("dve", gTs[:, :, :], hsT[:, :, :], hs2[:, :, :], ALU.mult, r=["hsT", "hs2"], w=["gTs"])
                    for dt_ in range(8):
                        po, pk = pb[dt_ % 2], "pb%d" % (dt_ % 2)
                        for jl in range(4):
                            P.mm(po[:, 0:NS], wo[:, jl, dt_ * 128:(dt_ + 1) * 128], gTs[:, jl, :], jl == 0, jl == 3, r=["wo", "gTs"], w=[pk])
                        P.tt("dve", xres[:, dt_, SEQ:NT], xres[:, dt_, SEQ:NT], po[:, 0:NS], ALU.add, r=[pk, xk(dt_, 4)], w=[xk(dt_, 4)])
                P.S.barrier()

        for li_ in (0, 1, 2, 3):
            if li_ in layers:
                if li_ in (0, 3):
                    layer_ssd(li_)
                elif li_ == 1:
                    layer_gmlp()
                else:
                    layer_mlstm()

        with ExitStack() as L:
            ystage = [P.sb(L, "ystage%d" % i, [128, D], F32) for i in range(2)]
            ysq = P.sb(L, "ysq", [128, D], F32)
            fnw_r = P.sb(L, "fnw_r", [128, D], F32)
            ss = P.sb(L, "ss", [128, 2], F32)
            if final_norm:
                P.dma("sp", fnw_r[:, :], bc(W["final_norm_w"].rearrange("(o n) -> o n", o=1), [128, D]), w=["fnw_r"], dkey="fnw_r")
            for c in range(17):
                if c < 16:
                    t0, M, ti = c * 128, 128, c // 4
                    dst = y_p[t0:t0 + 128, :]
                else:
                    t0, M, ti = SEQ, NS, 4
                    dst = y_s[:, :]
                yb = ystage[c % 2]
                yk = "ystage%d" % (c % 2)
                for h in range(2):
                    ps = pb[(2 * c + h) % 8]
                    pk = "pb%d" % ((2 * c + h) % 8)
                    for k4 in range(4):
                        kc = h * 4 + k4
                        P.tr(ps[:M, k4 * 128:(k4 + 1) * 128], xres[:, kc, t0:t0 + M], ident_f, r=[xk(kc, ti), "cst"], w=[pk])
                    P.copy("act" if h else "dve", yb[:M, h * 512:(h + 1) * 512], ps[:M, :], r=[pk], w=[yk + ".%d" % h])
                if final_norm:
                    P.act(ysq[:M, :], yb[:M, :], AF.Square, r=[yk + ".0", yk + ".1"], w=["ysq", "ss"], accum=ss[:M, 0:1])
                    P.act(ss[:M, 1:2], ss[:M, 0:1], AF.Sqrt, r=["ss", "epsc"], w=["ss"], bias=epsc[:M, 0:1], scale=1.0 / D)
                    P.recip(ss[:M, 1:2], ss[:M, 1:2], r=["ss"], w=["ss"])
                    P.stt(yb[:M, :], yb[:M, :], ss[:M, 1:2], fnw_r[:M, :], ALU.mult, ALU.mult,
                          r=[yk + ".0", yk + ".1", "ss", "fnw_r"], w=[yk + ".0", yk + ".1"])
                P.dma("sp", dst, yb[:M, :], r=[yk + ".0", yk + ".1"], dkey="o_" + yk)

        if cfg.get("resched", False):
            P.S.reorder()
        P.S.emit(root)
    return P


def make_consts():
    c = np.zeros((128, 7 * 128), np.float32)
    i = np.arange(128)
    c[:, 0:128] = np.eye(128, dtype=np.float32)
    c[:, 128:256] = 1.0
    c[:, 256:384] = (i[:, None] <= i[None, :]).astype(np.float32)
    c[:, 384:512] = (i[:, None] > i[None, :]).astype(np.float32)
    c[:, 512:640] = (i[None, :] < i[:, None]).astype(np.float32)
    c[:, 640:768] = (i[:, None] // 4 == i[None, :] // 4).astype(np.float32)
    c[:, 768:896] = (i[:, None] >= i[None, :]).astype(np.float32)
    return c


_CACHE = {}


def run(cfg, inputs, ncores=NCORES):
    key = repr(sorted(cfg.items()))
    if key not in _CACHE:
        _CACHE[key] = build(cfg)
    P = _CACHE[key]
    missing = [n for n in ALL_INPUT_NAMES if n not in inputs]
    assert not missing or len(cfg.get('layers', ())) < 4, missing
    consts = make_consts()
    in_maps = []
    for c in range(ncores):
        m = {}
        for name in P.dram:
            t = P.dram[name]
            if name in ("y_prompt", "y_sample") or name.startswith(("p0_", "s0_", "s1_", "p2_", "s2_", "p3_", "s3_")):
                continue
            if name == "consts":
                m[name] = consts
            elif name == "consts2":
                m[name] = (np.arange(32)[:, None] == (np.arange(2048)[None, :] // 64)).astype(np.float32)
            elif name == "x_prompt":
                m[name] = np.ascontiguousarray(inputs["x_prompt"][c])
            elif name == "x_sample":
                m[name] = np.ascontiguousarray(inputs["x_sample"][c * NS:(c + 1) * NS, 0, :])
            elif name.startswith("state_"):
                m[name] = np.ascontiguousarray(inputs[name][c * NS:(c + 1) * NS])
            else:
                m[name] = np.ascontiguousarray(inputs[name])
        in_maps.append(m)
    res = run_bass_kernel_spmd(P.nc, in_maps, core_ids=list(range(ncores)))
    return res.results


def kernel(**inputs):
    cfg = {"layers": (0, 1, 2, 3), "final_norm": True, "resched": True}
    r = run(cfg, inputs)
    def pstack(name):
        return np.stack([r[c][name] for c in range(NCORES)], 0)
    def scat(name):
        return np.concatenate([r[c][name] for c in range(NCORES)], 0)
    outs = [pstack("y_prompt"), scat("y_sample")[:, None, :]]
    outs += [pstack("p0_ssm"), pstack("p0_conv"), scat("s0_ssm"), scat("s0_conv")]
    outs += [scat("s1_v")[:, None, :]]
    outs += [pstack("p2_C"), pstack("p2_n"), pstack("p2_m"), pstack("p2_conv"), scat("s2_C"), scat("s2_n"), scat("s2_m"), scat("s2_conv")]
    outs += [pstack("p3_ssm"), pstack("p3_conv"), scat("s3_ssm"), scat("s3_conv")]
    return tuple(np.ascontiguousarray(o, dtype=np.float32) for o in outs)
```

```python
import sys
import math
from contextlib import ExitStack
import numpy as np
import concourse.bass as bass
import concourse.mybir as mybir
from concourse.bass_utils import run_bass_kernel_spmd

F32 = mybir.dt.float32
BF16 = mybir.dt.bfloat16
AF = mybir.ActivationFunctionType
ALU = mybir.AluOpType
AX = mybir.AxisListType

ALL_INPUT_NAMES = (
    "x_prompt",
    "x_sample",
    "state_l0_ssm",
    "state_l0_conv",
    "state_l2_C",
    "state_l2_n",
    "state_l2_m",
    "state_l2_conv",
    "state_l3_ssm",
    "state_l3_conv",
    "l0_norm_w",
    "l0_in_proj",
    "l0_conv_w",
    "l0_conv_b",
    "l0_dt_bias",
    "l0_A_log",
    "l0_D_skip",
    "l0_gnorm_w",
    "l0_out_proj",
    "l1_norm_w",
    "l1_in_proj",
    "l1_v_ln_w",
    "l1_v_ln_b",
    "l1_spatial_w",
    "l1_spatial_b",
    "l1_out_proj",
    "l2_norm_w",
    "l2_in_proj",
    "l2_conv_w",
    "l2_conv_b",
    "l2_w_q",
    "l2_w_k",
    "l2_w_v",
    "l2_w_o",
    "l2_b_o",
    "l2_w_if",
    "l2_b_if",
    "l2_mh_norm_w",
    "l2_skip",
    "l2_out_proj",
    "l3_norm_w",
    "l3_in_proj",
    "l3_conv_w",
    "l3_conv_b",
    "l3_dt_bias",
    "l3_A_log",
    "l3_D_skip",
    "l3_gnorm_w",
    "l3_out_proj",
    "final_norm_w",
)

NCORES = 8
D = 1024
SEQ = 2048
NS = 16
NT = SEQ + NS
EPS = 1e-6
TT = [(0, 512), (512, 512), (1024, 512), (1536, 512), (2048, 16)]
NEG = -30000.0


class Op:
    __slots__ = ("eng", "fn", "deps", "dkey", "token", "hasdep", "idx", "where", "seg", "cost", "tab")


class Sched:
    ENGS = ("pe", "act", "dve", "pool", "sp")
    EPOCH = 12000

    def __init__(self, nc):
        self.nc = nc
        self.ops = []
        self.lw = {}
        self.rd = {}
        self.last_dma = {}
        self.last_on = {}
        self.psum_rd = {}
        self.seg = 0
        self.keep_all_readers = False
        self.look = 24

    def add(self, eng, fn, reads=(), writes=(), dkey=None, cost=None, tab=None):
        op = Op()
        op.eng, op.fn, op.dkey, op.hasdep, op.token = eng, fn, dkey, False, None
        op.idx = len(self.ops)
        op.tab = tab
        op.cost = cost if cost is not None else {"pe": 200, "act": 550, "dve": 450, "pool": 650, "sp": 100}.get(eng, 300)
        op.seg = self.seg
        f = sys._getframe(1)
        wl = []
        while f is not None and len(wl) < 4:
            wl.append(f.f_lineno)
            f = f.f_back
        op.where = wl
        deps = {}
        for k in reads:
            w = self.lw.get(k)
            if w is not None:
                deps[w.idx] = w
            if k.startswith("pb") and eng in ("act", "dve"):
                bank = k.split(".")[0]
                lst_ = self.psum_rd.setdefault(bank, [])
                for r_ in lst_:
                    if r_.eng != eng:
                        deps[r_.idx] = r_
                lst_.append(op)
        for k in writes:
            w = self.lw.get(k)
            if w is not None:
                deps[w.idx] = w
            for r in self.rd.get(k, ()):
                deps[r.idx] = r
        op.deps = list(deps.values())
        for d in op.deps:
            if not (d.eng == "pe" and eng == "pe" and d.dkey is None and dkey is None):
                d.hasdep = True
        for k in reads:
            lst = self.rd.setdefault(k, [])
            if dkey is None and not self.keep_all_readers:
                lst[:] = [r for r in lst if r.dkey is not None or r.eng != eng]
            lst.append(op)
        for k in writes:
            self.lw[k] = op
            self.rd[k] = []
            if k.startswith("pb"):
                self.psum_rd[k.split(".")[0]] = []
        self.ops.append(op)
        if dkey is not None:
            self.last_dma[dkey] = op
        else:
            self.last_on[eng] = op
        return op

    def barrier(self):
        prev = list(self.last_on.values()) + list(self.last_dma.values())
        for e in self.ENGS:
            op = self.add(e, None)
            op.deps = list(prev)
            for d in prev:
                d.hasdep = True
        self.lw.clear()
        self.rd.clear()
        self.seg += 1

    def reorder(self, sync_lat=150, dma_lat=2500):
        import heapq
        out = []
        n = len(self.ops)
        i = 0
        while i < n:
            j = i
            seg = self.ops[i].seg
            while j < n and self.ops[j].seg == seg:
                j += 1
            ops = self.ops[i:j]
            body = [o for o in ops if o.fn is not None]
            bars = [o for o in ops if o.fn is None]
            inseg = {id(o) for o in body}
            succ = {id(o): [] for o in body}
            ndep = {}
            for o in body:
                ds = [d for d in o.deps if id(d) in inseg]
                ndep[id(o)] = len(ds)
                for d in ds:
                    succ[id(d)].append(o)
            finish = {}
            eng_free = {}
            heaps = {}
            ready_t = {}

            def push(o):
                rt = 0
                for d in o.deps:
                    if id(d) in finish:
                        lat = 0 if (d.eng == o.eng == "pe" and d.dkey is None) else sync_lat
                        rt = max(rt, finish[id(d)] + lat)
                ready_t[id(o)] = rt
                heapq.heappush(heaps.setdefault(o.eng, []), (o.idx, id(o), o))

            for o in body:
                if ndep[id(o)] == 0:
                    push(o)
            order = []
            LOOK = self.look
            cur_tab = [None]
            TABSW = 1300

            def tpen(o):
                return TABSW if (o.eng == "act" and o.tab is not None and o.tab != cur_tab[0]) else 0
            while True:
                best = None
                for eng, h in heaps.items():
                    if not h:
                        continue
                    ef = eng_free.get(eng, 0)
                    cands = heapq.nsmallest(LOOK, h)
                    c = min(cands, key=lambda t: (max(ef, ready_t[t[1]]) + tpen(t[2]), t[0]))
                    st = max(ef, ready_t[c[1]]) + tpen(c[2])
                    if best is None or (st, c[0]) < (best[0], best[1][0]):
                        best = (st, c, eng)
                if best is None:
                    break
                st, c, eng = best
                heaps[eng].remove(c)
                heapq.heapify(heaps[eng])
                o = c[2]
                if o.eng == "act" and o.tab is not None:
                    cur_tab[0] = o.tab
                if o.dkey is not None:
                    eng_free[eng] = st + o.cost
                    finish[id(o)] = st + dma_lat
                else:
                    eng_free[eng] = st + o.cost
                    finish[id(o)] = st + o.cost
                order.append((st, o))
                for s_ in succ[id(o)]:
                    ndep[id(s_)] -= 1
                    if ndep[id(s_)] == 0:
                        push(s_)
            assert len(order) == len(body), (len(order), len(body))
            order.sort(key=lambda t: (t[0], t[1].idx))
            out.extend(o for _, o in order)
            out.extend(bars)
            i = j
        self.ops = out
        for k, o in enumerate(self.ops):
            o.idx = k

    def emit(self, stack):
        nc = self.nc
        cnt = {}
        sems = {}

        def sem_for(key):
            if key not in sems:
                sems[key] = stack.enter_context(nc.semaphore("s%d" % len(sems)))
            return sems[key]

        slot_of = {}
        cur_seg = -1
        for op in self.ops:
            if op.seg != cur_seg:
                cur_seg = op.seg
                slot_of = {}
            if op.dkey is not None:
                sk_ = (op.eng, op.dkey)
                if sk_ not in slot_of:
                    slot_of[sk_] = (op.eng, sum(1 for x in slot_of if x[0] == op.eng))
                k = ("dma", slot_of[sk_])
                cnt[k] = cnt.get(k, 0) + 16
                op.token = (k, cnt[k])
            elif op.hasdep:
                c = cnt.get(op.eng, 0) + 1
                cnt[op.eng] = c
                ep = (c - 1) // self.EPOCH
                op.token = ((op.eng, ep), c - ep * self.EPOCH)
        for op in self.ops:
            if op.token is not None:
                sem_for(op.token[0])
        final = [v.token for k, v in self.last_dma.items()]
        block = stack.enter_context(nc.Block())
        handles = {"pe": nc.tensor, "act": nc.scalar, "dve": nc.vector, "pool": nc.gpsimd, "sp": nc.sync}
        deco = {"pe": block.tensor, "act": block.scalar, "dve": block.vector, "pool": block.gpsimd, "sp": block.sync}
        for eng in self.ENGS:
            myops = [o for o in self.ops if o.eng == eng]

            def body(e, myops=myops, eng=eng):
                waited = {}
                for op in myops:
                    need = {}
                    for d in op.deps:
                        if d.token is None:
                            continue
                        if d.dkey is None and d.eng == eng == "pe":
                            continue
                        sk, val = d.token
                        if val > need.get(sk, 0):
                            need[sk] = val
                    for sk, val in need.items():
                        if waited.get(sk, 0) >= val:
                            continue
                        waited[sk] = val
                        e.wait_ge(sems[sk], val)
                    if op.fn is None:
                        if op.token is not None:
                            e.nop().then_inc(sems[op.token[0]], 1)
                        continue
                    try:
                        inst = op.fn(e)
                    except Exception:
                        print('EMIT FAILED for op added at lines', op.where, 'eng', op.eng)
                        raise
                    if op.token is not None:
                        inst.then_inc(sems[op.token[0]], 16 if op.dkey is not None else 1)
                if eng == "sp":
                    for sk, val in final:
                        e.wait_ge(sems[sk], val)

            deco[eng](body)
        self.nsems = len(sems)


class Prog:
    def __init__(self, cfg):
        self.cfg = cfg
        self.nc = bass.Bass("TRN2", target_bir_lowering=False)
        self.S = Sched(self.nc)
        self.dram = {}
        self.uid = 0

    def din(self, name, shape):
        t = self.nc.dram_tensor(name, list(shape), F32, kind="ExternalInput")
        self.dram[name] = t
        return t.ap()

    def dout(self, name, shape):
        t = self.nc.dram_tensor(name, list(shape), F32, kind="ExternalOutput")
        self.dram[name] = t
        return t.ap()

    def sb(self, stack, name, shape, dt):
        self.uid += 1
        return stack.enter_context(self.nc.sbuf_tensor("%s_%d" % (name, self.uid), list(shape), dt))

    def key(self, base):
        self.uid += 1
        return "%s#%d" % (base, self.uid)

    def op(self, eng, fn, r=(), w=()):
        return self.S.add(eng, fn, r, w)

    def dma(self, q, out, in_, r=(), w=(), dkey=None, nonc=False):
        if nonc:
            fn = lambda e: e.dma_start(out=out, in_=in_, allow_slow_non_contiguous=True)
        else:
            fn = lambda e: e.dma_start(out=out, in_=in_)
        return self.S.add(q, fn, r, w, dkey=dkey)

    def mm(self, out, lhsT, rhs, start, stop, r=(), w=()):
        nfree = int(np.prod(rhs.shape[1:]))
        cost = max(64, nfree) / 1.2 * (2 if rhs.dtype == F32 else 1) + 20
        return self.S.add("pe", lambda e: e.matmul(out, lhsT=lhsT, rhs=rhs, start=start, stop=stop), r, w, cost=cost)

    def tr(self, out, in_, ident, r=(), w=()):
        return self.S.add("pe", lambda e: e.transpose(out, in_, ident), r, w, cost=130)

    def act(self, out, in_, func, r=(), w=(), bias=None, scale=None, accum=None):
        kw = {}
        if bias is not None:
            kw["bias"] = bias
        if scale is not None:
            kw["scale"] = scale
        if accum is not None:
            kw["accum_out"] = accum
        tab = {AF.Silu: "silu", AF.Exp: "exp", AF.Tanh: "exp", AF.Ln: "ln", AF.Sqrt: "sqrt", AF.Sigmoid: "sig"}.get(func)
        return self.S.add("act", lambda e: e.activation(out=out, in_=in_, func=func, **kw), r, w,
                          cost=300 + 0.7 * int(np.prod(out.shape[1:])), tab=tab)

    def tt(self, eng, out, in0, in1, op, r=(), w=()):
        return self.S.add(eng, lambda e: e.tensor_tensor(out=out, in0=in0, in1=in1, op=op), r, w, cost=(220 if eng == "dve" else 400) + 0.9 * int(np.prod(out.shape[1:])))

    def ts(self, eng, out, in0, s1, s2, op0, op1=None, r=(), w=()):
        if op1 is None:
            return self.S.add(eng, lambda e: e.tensor_scalar(out=out, in0=in0, scalar1=s1, scalar2=None, op0=op0), r, w, cost=(220 if eng == "dve" else 400) + 0.6 * int(np.prod(out.shape[1:])))
        return self.S.add(eng, lambda e: e.tensor_scalar(out=out, in0=in0, scalar1=s1, scalar2=s2, op0=op0, op1=op1), r, w, cost=(220 if eng == "dve" else 400) + 0.6 * int(np.prod(out.shape[1:])))

    def stt(self, out, in0, scalar, in1, op0, op1, r=(), w=()):
        return self.S.add("dve", lambda e: e.scalar_tensor_tensor(out=out, in0=in0, scalar=scalar, in1=in1, op0=op0, op1=op1), r, w, cost=220 + 1.0 * int(np.prod(out.shape[1:])))

    def copy(self, eng, out, in_, r=(), w=()):
        if eng == "act":
            return self.S.add("act", lambda e: e.activation(out=out, in_=in_, func=AF.Copy), r, w)
        return self.S.add(eng, lambda e: e.tensor_copy(out=out, in_=in_), r, w)

    def recip(self, out, in_, r=(), w=()):
        return self.S.add("dve", lambda e: e.reciprocal(out=out, in_=in_), r, w)

    def memset(self, eng, ap, val, r=(), w=()):
        return self.S.add(eng, lambda e: e.memset(ap, val), r, w)


def bc(ap, shape):
    return ap.to_broadcast(list(shape))


def build(cfg):
    P = Prog(cfg)
    nc = P.nc
    P.S.keep_all_readers = bool(cfg.get("resched", False))
    layers = cfg.get("layers", [0, 1, 2, 3])
    final_norm = cfg.get("final_norm", True)

    x_p = P.din("x_prompt", [SEQ, D])
    x_s = P.din("x_sample", [NS, D])
    consts_d = P.din("consts", [128, 7 * 128])
    W = {}
    wshapes = {
        "final_norm_w": [D],
        "l1_norm_w": [D], "l1_in_proj": [D, 6144], "l1_v_ln_w": [2048], "l1_v_ln_b": [2048],
        "l1_spatial_w": [8, 128, 128], "l1_spatial_b": [8, 128], "l1_out_proj": [2048, D],
    }
    for li_ in (0, 3):
        p_ = "l%d_" % li_
        wshapes.update({p_ + "norm_w": [D], p_ + "in_proj": [D, 6176], p_ + "conv_w": [4, 4096], p_ + "conv_b": [4096],
                        p_ + "dt_bias": [32], p_ + "A_log": [32], p_ + "D_skip": [32], p_ + "gnorm_w": [2048], p_ + "out_proj": [2048, D]})
    for k, s in wshapes.items():
        if k == "final_norm_w" or int(k[1]) in layers:
            W[k] = P.din(k, s)
    DR = {}
    if 2 in layers:
        for k_, s_ in {"l2_norm_w": [D], "l2_in_proj": [D, 4096], "l2_conv_w": [4, 2048], "l2_conv_b": [2048], "l2_w_q": [512, 4, 4],
                       "l2_w_k": [512, 4, 4], "l2_w_v": [512, 4, 4], "l2_w_o": [512, 4, 4], "l2_b_o": [2048], "l2_w_if": [6144, 8],
                       "l2_b_if": [8], "l2_mh_norm_w": [2048], "l2_skip": [2048], "l2_out_proj": [2048, D]}.items():
            W[k_] = P.din(k_, s_)
        for k_, s_ in {"state_l2_C": [NS, 4, 512, 512], "state_l2_n": [NS, 4, 512], "state_l2_m": [NS, 4], "state_l2_conv": [NS, 3, 2048]}.items():
            DR[k_] = P.din(k_, s_)
        for k_, s_ in {"p2_C": [4, 512, 512], "p2_n": [4, 512], "p2_m": [4], "p2_conv": [3, 2048],
                       "s2_C": [NS, 4, 512, 512], "s2_n": [NS, 4, 512], "s2_m": [NS, 4], "s2_conv": [NS, 3, 2048]}.items():
            DR[k_] = P.dout(k_, s_)
    consts2_d = P.din("consts2", [32, 2048])
    for li_ in (0, 3):
        if li_ in layers:
            DR["state_l%d_ssm" % li_] = P.din("state_l%d_ssm" % li_, [NS, 32, 64, 128])
            DR["state_l%d_conv" % li_] = P.din("state_l%d_conv" % li_, [NS, 3, 4096])
            DR["p%d_ssm" % li_] = P.dout("p%d_ssm" % li_, [32, 64, 128])
            DR["p%d_conv" % li_] = P.dout("p%d_conv" % li_, [3, 4096])
            DR["s%d_ssm" % li_] = P.dout("s%d_ssm" % li_, [NS, 32, 64, 128])
            DR["s%d_conv" % li_] = P.dout("s%d_conv" % li_, [NS, 3, 4096])
    y_p = P.dout("y_prompt", [SEQ, D])
    y_s = P.dout("y_sample", [NS, D])
    s1_v = P.dout("s1_v", [NS, 2048]) if 1 in layers else None

    root = ExitStack()
    with root:
        G = root
        xres = P.sb(G, "xres", [128, 8, NT], F32)
        xn = P.sb(G, "xn", [128, 8, NT], BF16)
        cst = P.sb(G, "cst", [128, 7 * 128], F32)
        ident_b = P.sb(G, "ident_b", [128, 128], BF16)
        ones_b = P.sb(G, "ones_b", [128, 128], BF16)
        nw = P.sb(G, "nw", [128, 5, 8], F32)
        rs_s = P.sb(G, "rs_s", [128, 512], F32)
        rs_r = P.sb(G, "rs_r", [128, 512], F32)
        sq = [P.sb(G, "sq%d" % i, [128, 512], BF16) for i in range(2)]
        pb = [G.enter_context(nc.psum_tensor("pb%d" % i, [128, 512], F32)) for i in range(8)]
        ident_f = cst[:, 0:128]

        def xk(kc, ti):
            return "xres.%d.%d" % (kc, ti)

        def xnk(ti):
            return "xn.%d" % ti

        P.dma("sp", cst[:, :], consts_d[:, :], w=["cst"], dkey="cst")
        P.copy("dve", ident_b[:, :], cst[:, 0:128], r=["cst"], w=["ident_b"])
        P.copy("dve", ones_b[:, :], cst[:, 128:256], r=["cst"], w=["ones_b"])
        nwn = {0: "l0_norm_w", 1: "l1_norm_w", 2: "l2_norm_w", 3: "l3_norm_w", 4: "final_norm_w"}
        for li in list(layers) + [4]:
            if nwn[li] in W:
                P.dma("sp", nw[:, li, :], W[nwn[li]].rearrange("(k p) -> p k", p=128), w=["nw%d" % li],
                      dkey="nw%d" % li, nonc=True)

        with ExitStack() as L:
            xin = [P.sb(L, "xin%d" % i, [128, 4, D], F32) for i in range(2)]
            xsin = P.sb(L, "xsin", [NS, D], F32)
            for ti in range(4):
                buf = xin[ti % 2]
                bk = "xin%d" % (ti % 2)
                P.dma("sp", buf[:, :, :], x_p[ti * 512:(ti + 1) * 512, :].rearrange("(c p) d -> p c d", p=128),
                      w=[bk], dkey=bk)
                for kc in range(8):
                    ps = pb[kc % 8]
                    pk = "pb%d" % (kc % 8)
                    for c in range(4):
                        P.tr(ps[:, c * 128:(c + 1) * 128], buf[:, c, kc * 128:(kc + 1) * 128], ident_f,
                             r=[bk, "cst"], w=[pk])
                    P.copy("act" if kc % 2 else "dve", xres[:, kc, ti * 512:(ti + 1) * 512], ps[:, :], r=[pk], w=[xk(kc, ti)])
            P.dma("sp", xsin[:, :], x_s[:, :], w=["xsin"], dkey="xsin")
            for kc in range(8):
                P.tr(pb[0][:, kc * 16:(kc + 1) * 16], xsin[:, kc * 128:(kc + 1) * 128], ident_f[0:NS, 0:NS],
                     r=["xsin", "cst"], w=["pb0"])
            P.copy("dve", xres[:, :, SEQ:NT], pb[0][:, 0:128].rearrange("p (k t) -> p k t", k=8),
                   r=["pb0"], w=[xk(kc, 4) for kc in range(8)])
            P.S.barrier()

        def emit_norm(li):
            for ti, (t0, tn) in enumerate(TT):
                for kc in range(8):
                    P.act(sq[kc % 2][:, :tn], xres[:, kc, t0:t0 + tn], AF.Square, r=[xk(kc, ti)], w=["sq%d" % (kc % 2)])
                    P.mm(pb[7][:, :tn], ones_b[:, :], sq[kc % 2][:, :tn], kc == 0, kc == 7,
                         r=["sq%d" % (kc % 2), "ones_b"], w=["pb7"])
                P.act(rs_s[:, :tn], pb[7][:, :tn], AF.Sqrt, r=["pb7", "epsc"], w=["rs_s"], bias=epsc[:, 0:1], scale=1.0 / D)
                P.recip(rs_r[:, :tn], rs_s[:, :tn], r=["rs_s"], w=["rs_r"])
                for kc in range(8):
                    P.stt(xn[:, kc, t0:t0 + tn], xres[:, kc, t0:t0 + tn], nw[:, li, kc:kc + 1], rs_r[:, :tn],
                          ALU.mult, ALU.mult, r=[xk(kc, ti), "rs_r", "nw%d" % li], w=[xnk(ti)])

        epsc = P.sb(G, "epsc", [128, 2], F32)
        P.memset("dve", epsc[:, 0:1], EPS, w=["epsc"])
        P.memset("dve", epsc[:, 1:2], -0.5, w=["epsc"])

        def rstd_pool(out, in_, scale, r, w):
            P.ts("pool", out, in_, scale, EPS, ALU.mult, ALU.add, r=r, w=w)
            npart, nfree = out.shape[0], out.shape[1]
            P.tt("pool", out, out, bc(epsc[0:npart, 1:2], [npart, nfree]), ALU.pow, r=list(w) + ["epsc"], w=w)

        def layer_gmlp():
            li = 1
            emit_norm(li)
            Win = W["l1_in_proj"].rearrange("(k p) j -> p k j", p=128)
            Wout = W["l1_out_proj"].rearrange("(k p) j -> p k j", p=128)
            with ExitStack() as L:
                vtok = P.sb(L, "vtok", [128, 9, 2048], BF16)
                wv = [P.sb(L, "wv%d" % i, [128, 8, 512], BF16) for i in range(2)]
                wuz = wv
                wo = [P.sb(L, "wo%d" % i, [128, 2, D], BF16) for i in range(2)]
                stats = P.sb(L, "stats", [128, 9, 4, 6], F32)
                mv = P.sb(L, "mv", [128, 9, 2], F32)
                rstd = P.sb(L, "rstd", [128, 9], F32)
                nmr = P.sb(L, "nmr", [128, 9], F32)
                wsT = P.sb(L, "wsT", [128, 8, 128], BF16)
                lnw = P.sb(L, "lnw", [128, 16], F32)
                lnb = P.sb(L, "lnb", [128, 16], F32)
                Ec = P.sb(L, "Ec", [128, 2, 128], F32)
                rsb = P.sb(L, "rsb", [128, 8, 128], F32)
                sbb = P.sb(L, "sbb", [128, 8, 128], F32)
                lnr = P.sb(L, "lnr", [NS, 2048], F32)
                ws00 = P.sb(L, "ws00", [NS, 8], F32)
                sb00 = P.sb(L, "sb00", [NS, 8], F32)
                vns = P.sb(L, "vns", [NS, 2048], F32)
                mxsT = P.sb(L, "mxsT", [128, 16, NS], F32)
                sz = [P.sb(L, "sz%d" % i, [128, 512], BF16) for i in range(2)]
                mx = [P.sb(L, "mx%d" % i, [128, 512], F32) for i in range(2)]
                gt = [P.sb(L, "gt%d" % i, [128, 2, 512], BF16) for i in range(2)]
                onesw = P.sb(L, "onesw", [128, 128], BF16)

                P.dma("sp", lnw[:, :], W["l1_v_ln_w"].rearrange("(k p) -> p k", p=128), w=["lnw"], dkey="lnw", nonc=True)
                P.dma("sp", lnb[:, :], W["l1_v_ln_b"].rearrange("(k p) -> p k", p=128), w=["lnb"], dkey="lnb", nonc=True)
                P.dma("sp", ws00[:, :], bc(W["l1_spatial_w"][:, 0:1, 0:1].rearrange("g a b -> (a b) g"), [NS, 8]),
                      w=["ws00"], dkey="ws00", nonc=True)
                P.dma("sp", sb00[:, :], bc(W["l1_spatial_b"][:, 0:1].rearrange("g a -> a g"), [NS, 8]),
                      w=["sb00"], dkey="sb00", nonc=True)
                for h2 in range(2):
                    P.dma("sp", mx[h2][:, :].rearrange("p (g s) -> p g s", g=4),
                          W["l1_spatial_w"][h2 * 4:(h2 + 1) * 4].rearrange("g t s -> t g s"), w=["mx%d" % h2], dkey="wsl%d" % h2)
                P.dma("sp", sbb[:, :, :], bc(W["l1_spatial_b"].rearrange("(o g) t -> o g t", o=1), [128, 8, 128]),
                      w=["sbb"], dkey="sbb")
                causal = cst[:, 256:384]
                for g in range(8):
                    ps = pb[g % 2]
                    pk = "pb%d" % (g % 2)
                    P.tr(ps[:, 0:128], mx[g // 4][:, (g % 4) * 128:(g % 4 + 1) * 128], ident_f, r=["mx%d" % (g // 4), "cst"], w=[pk])
                    P.tt("dve", wsT[:, g, :], ps[:, 0:128], causal, ALU.mult, r=[pk, "cst"], w=["wsT"])
                P.copy("dve", onesw[:, :], cst[:, 128:256], r=["cst"], w=["onesw"])
                for g2 in range(2):
                    P.mm(pb[2 + g2][:, :], onesw[:, :], wsT[:, g2 * 4:(g2 + 1) * 4, :], True, True,
                         r=["onesw", "wsT"], w=["pb%d" % (2 + g2)])
                    P.copy("dve", rsb[:, g2 * 4:(g2 + 1) * 4, :], pb[2 + g2][:, :].rearrange("p (g t) -> p g t", g=4),
                           r=["pb%d" % (2 + g2)], w=["rsb"])

                wq = 0
                stg = cfg.get('gm_stage', 9)
                for half in range(cfg.get('gm_halves', 2)):
                    chunks = list(range(8 * half, 8 * half + 8))
                    ttiles = [2 * half, 2 * half + 1] + ([4] if half == 1 else [])
                    nslot = 8 + (1 if half == 1 else 0)
                    bank = 0
                    for ct in range(cfg.get('v_nct', 4) if stg >= 1 else 0):
                        wb = wv[ct % 2]
                        wk = "wv%d" % (ct % 2)
                        P.dma("pool", wb[:, :, :], Win[:, :, 2048 + ct * 512: 2048 + (ct + 1) * 512], w=[wk + "u", wk + "z"], dkey=wk + "u")
                        for sl in range(cfg.get("v_nsl", nslot)):
                            if sl < 8:
                                c = chunks[sl]
                                t0, M, ti = c * 128, 128, c // 4
                            else:
                                t0, M, ti = SEQ, NS, 4
                            ps = pb[bank % 7]
                            pk = "pb%d" % (bank % 7)
                            bank += 1
                            for kc in range(8):
                                P.mm(ps[:M, :], xn[:, kc, t0:t0 + M], wb[:, kc, :], kc == 0, kc == 7,
                                     r=[xnk(ti), wk + "u", wk + "z"], w=[pk])
                            if cfg.get("v_statcopy"):
                                P.copy("dve", mx[0][:M, :], ps[:M, :], r=[pk], w=["mx0"])
                            elif not cfg.get("v_nostats"):
                                P.S.add("dve", (lambda e, o=stats[:M, sl, ct, :], i=ps[:M, :]: e.bn_stats(out=o, in_=i)),
                                        [pk], ["stats.%d" % sl])
                            if not cfg.get("v_nocopy"):
                                P.copy(cfg.get("v_copyeng", "act"), vtok[:M, sl, ct * 512:(ct + 1) * 512], ps[:M, :], r=[pk] + (["stats.%d" % sl] if cfg.get("v_serial") else []), w=["vtok.%d" % sl])
                    for sl in range(nslot if stg >= 2 else 0):
                        M = 128 if sl < 8 else NS
                        P.S.add("dve", (lambda e, o=mv[:M, sl, :], i=stats[:M, sl, :, :]: e.bn_aggr(out=o, in_=i)),
                                ["stats.%d" % sl], ["mv.%d" % sl])
                        P.act(rstd[:M, sl:sl + 1], mv[:M, sl, 1:2], AF.Sqrt, r=["mv.%d" % sl, "epsc"], w=["rstd.%d" % sl],
                              bias=epsc[:M, 0:1], scale=1.0)
                        P.recip(rstd[:M, sl:sl + 1], rstd[:M, sl:sl + 1], r=["rstd.%d" % sl], w=["rstd.%d" % sl])
                        P.stt(nmr[:M, sl:sl + 1], mv[:M, sl, 0:1], -1.0, rstd[:M, sl:sl + 1], ALU.mult, ALU.mult,
                              r=["mv.%d" % sl, "rstd.%d" % sl], w=["nmr.%d" % sl])
                        if sl < 8:
                            P.ts("dve", vtok[:M, sl, :], vtok[:M, sl, :], rstd[:M, sl:sl + 1], nmr[:M, sl:sl + 1],
                                 ALU.mult, ALU.add, r=["vtok.%d" % sl, "rstd.%d" % sl, "nmr.%d" % sl], w=["vtok.%d" % sl])
                        else:
                            P.ts("dve", vns[:, :], vtok[:M, sl, :], rstd[:M, sl:sl + 1], nmr[:M, sl:sl + 1],
                                 ALU.mult, ALU.add, r=["vtok.%d" % sl, "rstd.%d" % sl, "nmr.%d" % sl], w=["vns"])
                            P.dma("sp", lnr[:, :], bc(W["l1_v_ln_w"].rearrange("(o n) -> o n", o=1), [NS, 2048]), w=["lnr"], dkey="lnr")
                            P.tt("dve", vns[:, :], vns[:, :], lnr[:, :], ALU.mult, r=["vns", "lnr"], w=["vns"])
                            P.dma("sp", lnr[:, :], bc(W["l1_v_ln_b"].rearrange("(o n) -> o n", o=1), [NS, 2048]), w=["lnr"], dkey="lnr")
                            P.tt("dve", vns[:, :], vns[:, :], lnr[:, :], ALU.add, r=["vns", "lnr"], w=["vns"])
                            P.dma("sp", s1_v[:, :], vns[:, :], r=["vns"], dkey="o_s1v")
                            mxs = lnr
                            P.tt("dve", mxs[:, :].rearrange("p (g d) -> p g d", g=8), vns[:, :].rearrange("p (g d) -> p g d", g=8),
                                 bc(ws00[:, :].rearrange("p (g o) -> p g o", o=1), [NS, 8, 256]), ALU.mult,
                                 r=["vns", "ws00"], w=["lnr"])
                            P.tt("dve", mxs[:, :].rearrange("p (g d) -> p g d", g=8), mxs[:, :].rearrange("p (g d) -> p g d", g=8),
                                 bc(sb00[:, :].rearrange("p (g o) -> p g o", o=1), [NS, 8, 256]), ALU.add,
                                 r=["lnr", "sb00"], w=["lnr"])
                            for jt in range(16):
                                P.tr(pb[7][:, jt * NS:(jt + 1) * NS], mxs[:, jt * 128:(jt + 1) * 128], ident_f[0:NS, 0:NS],
                                     r=["lnr", "cst"], w=["pb7"])
                            P.copy("dve", mxsT[:, :, :], pb[7][:, 0:16 * NS].rearrange("p (j t) -> p j t", j=16),
                                   r=["pb7"], w=["mxsT"])
                    def load_g(g_, slot_):
                        wb_, wk_ = wuz[slot_], "wv%d" % slot_
                        ob_, ok_ = wo[slot_], "wo%d" % slot_
                        P.dma("pool", wb_[:, :, 0:256], Win[:, :, g_ * 256:(g_ + 1) * 256], w=[wk_ + "u"], dkey=wk_ + "u")
                        P.dma("pool", wb_[:, :, 256:512], Win[:, :, 4096 + g_ * 256: 4096 + (g_ + 1) * 256], w=[wk_ + "z"], dkey=wk_ + "z")
                        P.dma("pool", ob_[:, :, :], Wout[:, 2 * g_:2 * g_ + 2, :], w=[ok_], dkey=ok_)
                    if stg >= 3:
                        load_g(0, wq % 2)
                    for g in range(8 if stg >= 3 else 0):
                        slot = wq % 2
                        wq += 1
                        wb, wk = wuz[slot], "wv%d" % slot
                        ob, ok = wo[slot], "wo%d" % slot
                        if g < 7:
                            load_g(g + 1, wq % 2)
                        for ti in ttiles:
                            t0, tn = TT[ti]
                            gb = gt[ti % 2]
                            gk = "gt%d" % (ti % 2)
                            for jl in range(2):
                                jt = 2 * g + jl
                                o = 3 * jl
                                pu, pz, pm = pb[o], pb[o + 1], pb[o + 2]
                                ku, kz, km = "pb%d" % o, "pb%d" % (o + 1), "pb%d" % (o + 2)
                                for kc in range(8):
                                    P.mm(pu[:, :tn], wb[:, kc, jl * 128:(jl + 1) * 128], xn[:, kc, t0:t0 + tn], kc == 0, kc == 7,
                                         r=[wk + "u", xnk(ti)], w=[ku])
                                for kc in range(8):
                                    P.mm(pz[:, :tn], wb[:, kc, 256 + jl * 128:256 + (jl + 1) * 128], xn[:, kc, t0:t0 + tn],
                                         kc == 0, kc == 7, r=[wk + "z", xnk(ti)], w=[kz])
                                s_ = sz[jl]
                                sk_ = "sz%d" % jl
                                m_ = mx[jl]
                                mk_ = "mx%d" % jl
                                P.act(s_[:, :tn], pz[:, :tn], AF.Silu, r=[kz], w=[sk_])
                                if ti < 4:
                                    for c4 in range(4):
                                        c = ti * 4 + c4
                                        sl = c - 8 * half
                                        P.mm(pm[:, c4 * 128:(c4 + 1) * 128], vtok[:, sl, jt * 128:(jt + 1) * 128], wsT[:, g, :],
                                             True, True, r=["vtok.%d" % sl, "wsT"], w=[km])
                                    P.stt(Ec[:, jl, :], rsb[:, g, :], lnb[:, jt:jt + 1], sbb[:, g, :], ALU.mult, ALU.add,
                                          r=["rsb", "lnb", "sbb"], w=["Ec%d" % jl])
                                    P.stt(m_[:, :].rearrange("p (c t) -> p c t", c=4), pm[:, :].rearrange("p (c t) -> p c t", c=4),
                                          lnw[:, jt:jt + 1], bc(Ec[:, jl:jl + 1, :], [128, 4, 128]), ALU.mult, ALU.add,
                                          r=[km, "lnw", "Ec%d" % jl], w=[mk_])
                                    min_ = m_[:, :tn]
                                    rk = [mk_]
                                else:
                                    min_ = mxsT[:, jt, :]
                                    rk = ["mxsT"]
                                P.tt("dve", m_[:, :tn], pu[:, :tn], min_, ALU.mult, r=[ku] + rk, w=[mk_])
                                P.tt("pool", gb[:, jl, :tn], m_[:, :tn], s_[:, :tn], ALU.mult, r=[mk_, sk_], w=[gk])
                            for dt_ in range(8):
                                po = pb[6 + dt_ % 2]
                                pk = "pb%d" % (6 + dt_ % 2)
                                for jl in range(2):
                                    P.mm(po[:, :tn], ob[:, jl, dt_ * 128:(dt_ + 1) * 128], gb[:, jl, :tn], jl == 0, jl == 1,
                                         r=[ok, gk], w=[pk])
                                P.tt("dve", xres[:, dt_, t0:t0 + tn], xres[:, dt_, t0:t0 + tn], po[:, :tn], ALU.add,
                                     r=[pk, xk(dt_, ti)], w=[xk(dt_, ti)])
                P.S.barrier()

        def layer_ssd(li):
            p_ = "l%d_" % li
            emit_norm(li)
            Win = W[p_ + "in_proj"].rearrange("(k p) j -> p k j", p=128)
            Wout = W[p_ + "out_proj"].rearrange("(k p) j -> p k j", p=128)
            st_ssm = DR["state_l%d_ssm" % li]
            st_conv = DR["state_l%d_conv" % li]
            o_pssm, o_pconv, o_sssm, o_sconv = DR["p%d_ssm" % li], DR["p%d_conv" % li], DR["s%d_ssm" % li], DR["s%d_conv" % li]
            triu_f, sgt_f, ltm_f = cst[:, 256:384], cst[:, 384:512], cst[:, 512:640]
            ones_f = cst[:, 128:256]
            with ExitStack() as L:
                cw = P.sb(L, "cw", [128, 32, 5], F32)
                dtb = P.sb(L, "dtb", [32, 4], F32)
                Dcol = P.sb(L, "Dcol", [128, 16], F32)
                gnw = P.sb(L, "gnw", [128, 16], F32)
                wdt = P.sb(L, "wdt", [128, 8, 32], BF16)
                ddt = P.sb(L, "ddt", [128, 17, 64], F32)
                ecum = P.sb(L, "ecum", [128, 16, 32], F32)
                wend = P.sb(L, "wend", [128, 16, 32], F32)
                eL = P.sb(L, "eL", [128, 16, 32], F32)
                dtmp = P.sb(L, "dtmp", [32, 2, 512], F32)
                dsT = P.sb(L, "dsT", [32, 2, NS], F32)
                csb = P.sb(L, "csb", [128, 32], F32)
                negI = P.sb(L, "negI", [128, 128], F32)
                wblk = [P.sb(L, "wblk%d" % i, [128, 8, 768], BF16) for i in range(2)]
                wo = [P.sb(L, "wo0", [128, 2, D], BF16)] * 2
                dg = P.sb(L, "dg", [128, 16, 128], BF16)
                dgD = P.sb(L, "dgD", [128, 2, 128], BF16)
                cwg = P.sb(L, "cwg", [128, 4, 5], F32)
                xr = [[P.sb(L, "xr%d_0" % j, [128, 515], BF16)] * 2 for j in range(4)]
                xbc = [P.sb(L, "xbc%d" % i, [128, 4, 512], BF16) for i in range(2)]
                ctail = P.sb(L, "ctail", [128, 4, 3], F32)
                xBtr = [P.sb(L, "xBt%d" % i, [128, 384], BF16) for i in range(2)]
                sztiles = [P.sb(L, "sztile%d" % i, [128, 4, 256], BF16) for i in range(2)]
                scrA = P.sb(L, "scrA", [128, 512], F32)
                scrB = P.sb(L, "scrB", [128, 512], F32)
                Ub = [P.sb(L, "Ub%d" % i, [128, 128], F32) for i in range(2)]
                dec = P.sb(L, "dec", [128, 512], BF16)
                mixT = P.sb(L, "mixT", [128, 4, 128], BF16)
                ssq = P.sb(L, "ssq", [128, 2], F32)
                yn = P.sb(L, "yn", [128, 256], BF16)
                gT = [P.sb(L, "gT%d" % i, [128, 2, 512], BF16) for i in range(2)]
                xw = P.sb(L, "xw", [128, 256], BF16)
                hT = P.sb(L, "hT", [128, 256], F32)
                hTb = P.sb(L, "hTb", [128, 256], BF16)
                htmp = P.sb(L, "htmp", [128, 256], F32)
                hout = scrA[:, 0:256].rearrange("p (j n) -> p j n", j=2)
                ctout = scrB[0:3, :]
                xrs = P.sb(L, "xrs", [128, 4, NS], F32)
                cstk = scrA[0:48, :]
                cs = P.sb(L, "cs", [128, 4, 48], F32)
                acc = P.sb(L, "acc", [128, 4, NS], F32)
                atmp = P.sb(L, "atmp", [128, 4, NS], F32)
                xbs = P.sb(L, "xbs", [128, 4, NS], F32)
                ncs = P.sb(L, "ncs", [128, 4, 48], F32)
                szs = P.sb(L, "szs", [128, 2, NS], F32)
                Eg = P.sb(L, "Eg", [32, 256], F32)
                dtc = P.sb(L, "dtc", [128, 2, 2, NS], F32)
                xdtc = P.sb(L, "xdtc", [128, 2, NS], F32)
                rhsB = P.sb(L, "rhsB", [128, 8, 128], BF16)
                bcsb = P.sb(L, "bcsb", [128, 8, 128], BF16)
                gTs_all = P.sb(L, "gTs_all", [128, 16, NS], BF16)
                h0t = [P.sb(L, "h0t%d" % i, [128, 4, 128], F32) for i in range(2)]
                hnt = [P.sb(L, "hnt%d" % i, [128, 4, 128], F32) for i in range(2)]
                s1 = P.sb(L, "s1", [128, 4, 128], F32)
                ys = P.sb(L, "ys", [128, 2, NS], F32)
                gsm = P.sb(L, "gsm", [128, 2, NS], F32)
                gsq = P.sb(L, "gsq", [128, 2, NS], BF16)
                gTs = P.sb(L, "gTs", [128, 2, NS], BF16)
                rsd = P.sb(L, "rsd", [128, NS], F32)
                pbT2 = pb[2].bitcast(BF16)
                pbT7 = pb[7].bitcast(BF16)
                pbT1 = pb[1].bitcast(BF16)

                for k in range(4):
                    P.dma("sp", cw[:, :, k], W[p_ + "conv_w"][k].rearrange("(j p) -> p j", p=128), w=["cw"], dkey="cw%d" % k, nonc=True)
                P.dma("sp", cw[:, :, 4], W[p_ + "conv_b"].rearrange("(j p) -> p j", p=128), w=["cw"], dkey="cw4", nonc=True)
                P.dma("sp", dtb[:, 0:1], W[p_ + "dt_bias"].rearrange("(h o) -> h o", o=1), w=["dtb"], dkey="dtb0", nonc=True)
                P.dma("sp", dtb[:, 1:2], W[p_ + "A_log"].rearrange("(h o) -> h o", o=1), w=["dtb"], dkey="dtb1", nonc=True)
                P.act(dtb[:, 2:3], dtb[:, 1:2], AF.Exp, r=["dtb"], w=["dtb"])
                P.ts("dve", dtb[:, 2:3], dtb[:, 2:3], -1.0, None, ALU.mult, r=["dtb"], w=["dtb"])
                P.memset("dve", dtb[:, 3:4], 1.0, w=["dtb"])
                Dsk = W[p_ + "D_skip"].rearrange("(j two o) -> two o j", two=2, o=1)
                P.dma("sp", Dcol[0:64, :], bc(Dsk[0], [64, 16]), w=["Dcol"], dkey="Dcol0", nonc=True)
                P.dma("sp", Dcol[64:128, :], bc(Dsk[1], [64, 16]), w=["Dcol"], dkey="Dcol1", nonc=True)
                P.dma("sp", gnw[:, :], W[p_ + "gnorm_w"].rearrange("(k p) -> p k", p=128), w=["gnw"], dkey="gnw", nonc=True)
                P.dma("pool", wdt[:, :, :], Win[:, :, 6144:6176], w=["wdt"], dkey="wdt")
                P.ts("dve", negI[:, :], ident_f, NEG, None, ALU.mult, r=["cst"], w=["negI"])

                for ti, (t0, tn) in enumerate(TT):
                    for kc in range(8):
                        P.mm(pb[0][:32, :tn], wdt[:, kc, :], xn[:, kc, t0:t0 + tn], kc == 0, kc == 7, r=["wdt", xnk(ti)], w=["pb0"])
                    P.act(dtmp[:, 0, :tn], pb[0][:32, :tn], AF.Exp, r=["pb0", "dtb"], w=["dtmp0"], bias=dtb[:, 0:1])
                    P.act(dtmp[:, 0, :tn], dtmp[:, 0, :tn], AF.Ln, r=["dtmp0", "dtb"], w=["dtmp0"], bias=dtb[:, 3:4])
                    P.ts("dve", dtmp[:, 1, :tn], dtmp[:, 0, :tn], dtb[:, 2:3], None, ALU.mult, r=["dtmp0", "dtb"], w=["dtmp1"])
                    if ti == 4:
                        P.copy("dve", dsT[:, :, :], dtmp[:, :, 0:NS], r=["dtmp0", "dtmp1"], w=["dsT"])
                        continue
                    for c4 in range(4):
                        c = ti * 4 + c4
                        P.tr(pb[1][:, 0:32], dtmp[:, 0, c4 * 128:(c4 + 1) * 128], ident_f[0:32, 0:32], r=["dtmp0", "cst"], w=["pb1"])
                        P.tr(pb[1][:, 32:64], dtmp[:, 1, c4 * 128:(c4 + 1) * 128], ident_f[0:32, 0:32], r=["dtmp1", "cst"], w=["pb1"])
                        P.copy("dve", ddt[:, c, :], pb[1][:, 0:64], r=["pb1"], w=["ddt.%d" % c])
                        P.mm(pb[2][:, 0:32], triu_f, ddt[:, c, 32:64], True, True, r=["cst", "ddt.%d" % c], w=["pb2"])
                        P.mm(pb[2][:, 32:64], ones_f, ddt[:, c, 32:64], True, True, r=["cst", "ddt.%d" % c], w=["pb2"])
                        P.act(ecum[:, c, :], pb[2][:, 0:32], AF.Exp, r=["pb2"], w=["ecum.%d" % c])
                        P.act(eL[:, c, :], pb[2][:, 32:64], AF.Exp, r=["pb2"], w=["eL.%d" % c])
                        P.copy("dve", csb[:, :], pb[2][:, 0:32], r=["pb2"], w=["csb"])
                        P.tt("dve", csb[:, :], pb[2][:, 32:64], csb[:, :], ALU.subtract, r=["pb2", "csb"], w=["csb"])
                        P.act(wend[:, c, :], csb[:, :], AF.Exp, r=["csb"], w=["wend.%d" % c])
                        P.tt("dve", wend[:, c, :], wend[:, c, :], ddt[:, c, 0:32], ALU.mult, r=["wend.%d" % c, "ddt.%d" % c], w=["wend.%d" % c])

                ngroups = cfg.get("ssd_groups", 8)

                def load_g(g, slot):
                    wb, wk = wblk[slot], "wblk%d" % slot
                    P.dma("pool", wb[:, :, 0:256], Win[:, :, 2048 + 256 * g: 2048 + 256 * (g + 1)], w=[wk + "x"], dkey=wk + "x")
                    P.dma("pool", wb[:, :, 256:384], Win[:, :, 4096 + 128 * g: 4096 + 128 * (g + 1)], w=[wk + "B"], dkey=wk + "B")
                    P.dma("pool", wb[:, :, 384:512], Win[:, :, 5120 + 128 * g: 5120 + 128 * (g + 1)], w=[wk + "C"], dkey=wk + "C")
                    P.dma("pool", wb[:, :, 512:768], Win[:, :, 256 * g: 256 * (g + 1)], w=[wk + "z"], dkey=wk + "z")

                load_g(0, 0)
                for g in range(ngroups):
                    slot = g % 2
                    wb, wk = wblk[slot], "wblk%d" % slot
                    wkeys = [wk + "x", wk + "x", wk + "B", wk + "C"]
                    ob, ok = wo[slot], "wo0"
                    P.dma("pool", ob[:, :, :], Wout[:, 2 * g:2 * g + 2, :], w=["wo0"], dkey="wo0")
                    if g + 1 < ngroups:
                        load_g(g + 1, 1 - slot)
                    jts = [2 * g, 2 * g + 1, 16 + g, 24 + g]
                    for j in range(4):
                        P.copy("dve", cwg[:, j, :], cw[:, jts[j], :], r=["cw"], w=["cwg"])
                        for k in range(4):
                            P.ts("dve", dg[:, j * 4 + k, :], ident_f, cw[:, jts[j], k:k + 1], None, ALU.mult, r=["cst", "cw"], w=["dg"])
                    for jl in range(2):
                        P.ts("dve", dgD[:, jl, :], ident_f, Dcol[:, 2 * g + jl:2 * g + jl + 1], None, ALU.mult, r=["cst", "Dcol"], w=["dgD"])
                    P.memset("dve", hT[:, :], 0.0, w=["hT"])
                    P.memset("dve", hTb[:, :], 0.0, w=["hTb"])
                    for j in range(4):
                        P.memset("dve", xr[j][0][:, 0:3], 0.0, w=["xr%d_0" % j])

                    fill = []

                    def pump(n=1):
                        for _ in range(n):
                            if fill:
                                fill.pop(0)()

                    def u_proj(ti, j):
                        t0 = ti * 512
                        xrb, xrk = xr[j][0], "xr%d_0" % j
                        for kc in range(8):
                            P.mm(pb[0][:, :], wb[:, kc, j * 128:(j + 1) * 128], xn[:, kc, t0:t0 + 512], kc == 0, kc == 7,
                                 r=[wkeys[j], xnk(ti)], w=["pb0"])
                        P.copy("act", xrb[:, 3:515], pb[0][:, :], r=["pb0"], w=[xrk])
                        if ti == 3:
                            P.copy("dve", ctail[:, j, :], pb[0][:, 509:512], r=["pb0"], w=["ctail"])

                    def u_conv(ti, j):
                        xb, xbk = xbc[ti % 2], "xbc%d" % (ti % 2)
                        xrb, xrk = xr[j][0], "xr%d_0" % j
                        for k in range(4):
                            P.mm(pb[6][:, :], dg[:, j * 4 + k, :], xrb[:, k:k + 512], k == 0, k == 3, r=["dg", xrk], w=["pb6"])
                        if ti < 3:
                            P.copy("pool", xrb[:, 0:3], xrb[:, 512:515], r=[xrk], w=[xrk])
                        P.act(xb[:, j, :], pb[6][:, :], AF.Silu, r=["pb6", "cw"], w=[xbk + ".%d" % j], bias=cw[:, jts[j], 4:5])

                    def u_z(ti, c4):
                        tok = slice((ti * 4 + c4) * 128, (ti * 4 + c4 + 1) * 128)
                        for kc in range(8):
                            P.mm(pb[6][:, 0:256], xn[:, kc, tok], wb[:, kc, 512:768], kc == 0, kc == 7, r=[xnk(ti), wk + "z"], w=["pb6"])
                        P.act(sztiles[ti % 2][:, c4, :], pb[6][:, 0:256], AF.Silu, r=["pb6"], w=["szt%d.%d" % (ti % 2, c4)])

                    def u_op(ti, dt_):
                        t0 = ti * 512
                        gb, gk = gT[ti % 2], "gT%d" % (ti % 2)
                        for jl in range(2):
                            P.mm(pb[0][:, :], ob[:, jl, dt_ * 128:(dt_ + 1) * 128], gb[:, jl, :], jl == 0, jl == 1, r=[ok, gk], w=["pb0"])
                        P.tt("dve", xres[:, dt_, t0:t0 + 512], xres[:, dt_, t0:t0 + 512], pb[0][:, :], ALU.add, r=["pb0", xk(dt_, ti)], w=[xk(dt_, ti)])

                    def front(c):
                        ti, c4 = c // 4, c % 4
                        xb, xbk = xbc[ti % 2], "xbc%d" % (ti % 2)
                        cs_ = slice(c4 * 128, (c4 + 1) * 128)
                        tok = slice(c * 128, (c + 1) * 128)
                        xBt, xBk = xBtr[c % 2], "xBt%d" % (c % 2)
                        ysb, ysk = scrA[:, (c % 2) * 256:(c % 2 + 1) * 256], "scrA.%d" % (c % 2)
                        for j in range(3):
                            P.tr(pbT2[:, j * 128:(j + 1) * 128], xb[:, j, cs_], ident_b[:, :], r=[xbk + ".%d" % j, "ident_b"], w=["pb2"])
                        P.copy("dve", xBt[:, :], pbT2[:, 0:384], r=["pb2"], w=[xBk])
                        pump()
                        P.mm(pb[3][:, 0:128], xb[:, 2, cs_], xb[:, 3, cs_], True, True, r=[xbk + ".2", xbk + ".3"], w=["pb3"])
                        for hh in range(4):
                            h = 4 * g + hh
                            U, Uk = Ub[hh % 2], "Ub%d" % (hh % 2)
                            P.ts("dve", U[:, :], sgt_f, ddt[:, c, 32 + h:33 + h], None, ALU.mult, r=["cst", "ddt.%d" % c], w=[Uk])
                            P.mm(pb[4][:, hh * 128:(hh + 1) * 128], U[:, :], triu_f, True, False, r=[Uk, "cst"], w=["pb4"])
                            P.mm(pb[4][:, hh * 128:(hh + 1) * 128], negI[:, :], ltm_f, False, True, r=["negI", "cst"], w=["pb4"])
                        P.act(dec[:, :], pb[4][:, :], AF.Exp, r=["pb4"], w=["dec"])
                        pump()
                        for hh in range(4):
                            h = 4 * g + hh
                            P.stt(mixT[:, hh, :], dec[:, hh * 128:(hh + 1) * 128], ddt[:, c, h:h + 1], pb[3][:, 0:128],
                                  ALU.mult, ALU.mult, r=["dec", "ddt.%d" % c, "pb3"], w=["mixT"])
                        for jl in range(2):
                            P.mm(pb[5][:, jl * 128:(jl + 1) * 128], xb[:, jl, cs_], dgD[:, jl, :], True, False, r=[xbk + ".%d" % jl, "dgD"], w=["pb5"])
                            for hh in (2 * jl, 2 * jl + 1):
                                P.mm(pb[5][:, hh * 64:(hh + 1) * 64], mixT[:, hh, :], xBt[:, hh * 64:(hh + 1) * 64], False, hh == 2 * jl + 1,
                                     r=["mixT", xBk], w=["pb5"])
                        P.copy("act", ysb, pb[5][:, 0:256], r=["pb5"], w=[ysk])
                        pump()

                    def tail(c):
                        ti, c4 = c // 4, c % 4
                        xb, xbk = xbc[ti % 2], "xbc%d" % (ti % 2)
                        cs_ = slice(c4 * 128, (c4 + 1) * 128)
                        gb, gk = gT[ti % 2], "gT%d" % (ti % 2)
                        xBt, xBk = xBtr[c % 2], "xBt%d" % (c % 2)
                        szt, szk = sztiles[ti % 2][:, c4, :], "szt%d.%d" % (ti % 2, c4)
                        ysb, ysk = scrA[:, (c % 2) * 256:(c % 2 + 1) * 256], "scrA.%d" % (c % 2)
                        t1, t2 = scrB[:, 0:256], scrB[:, 256:512]
                        P.mm(pb[7][:, 0:256], xb[:, 3, cs_], hTb[:, :], True, True, r=[xbk + ".3", "hTb"], w=["pb7"])
                        pump()
                        P.tt("dve", t1.rearrange("p (h q) -> p h q", h=4), pb[7][:, 0:256].rearrange("p (h q) -> p h q", h=4),
                             bc(ecum[:, c, 4 * g:4 * g + 4].rearrange("p (h o) -> p h o", o=1), [128, 4, 64]), ALU.mult,
                             r=["pb7", "ecum.%d" % c], w=["scrB.0"])
                        P.tt("dve", t2, ysb, t1, ALU.add, r=[ysk, "scrB.0"], w=["scrB.1"])
                        P.tt("pool", t2, t2, szt, ALU.mult, r=["scrB.1", szk], w=["scrB.1"])
                        P.act(t1, t2, AF.Square, r=["scrB.1"], w=["scrB.0", "ssq"], accum=ssq[:, 0:1])
                        rstd_pool(ssq[:, 1:2], ssq[:, 0:1], 1.0 / 256, ["ssq"], ["ssq"])
                        P.ts("dve", yn[:, :], t2, ssq[:, 1:2], None, ALU.mult, r=["scrB.1", "ssq"], w=["yn"])
                        for jl in range(2):
                            P.tr(pbT1[:, jl * 128:(jl + 1) * 128], yn[:, jl * 128:(jl + 1) * 128], ident_b[:, :], r=["yn", "ident_b"], w=["pb1"])
                        pump()
                        for jl in range(2):
                            P.ts("dve", gb[:, jl, cs_], pbT1[:, jl * 128:(jl + 1) * 128], gnw[:, 2 * g + jl:2 * g + jl + 1], None, ALU.mult,
                                 r=["pb1", "gnw"], w=[gk])
                        P.tt("pool", xw[:, :].rearrange("p (h q) -> p h q", h=4), xBt[:, 0:256].rearrange("p (h q) -> p h q", h=4),
                             bc(wend[:, c, 4 * g:4 * g + 4].rearrange("p (h o) -> p h o", o=1), [128, 4, 64]), ALU.mult,
                             r=[xBk, "wend.%d" % c], w=["xw"])
                        P.mm(pb[7][:, 256:512], xBt[:, 256:384], xw[:, :], True, True, r=[xBk, "xw"], w=["pb7"])
                        P.tt("pool", htmp[:, :].rearrange("p (h q) -> p h q", h=4), hT[:, :].rearrange("p (h q) -> p h q", h=4),
                             bc(eL[:, c, 4 * g:4 * g + 4].rearrange("p (h o) -> p h o", o=1), [128, 4, 64]), ALU.mult,
                             r=["hT", "eL.%d" % c], w=["htmp"])
                        P.tt("dve", hT[:, :], htmp[:, :], pb[7][:, 256:512], ALU.add, r=["htmp", "pb7"], w=["hT"])
                        P.copy("act", hTb[:, :], hT[:, :], r=["hT"], w=["hTb"])

                    def refill(ti):
                        prep = []
                        if ti < 3:
                            for j in range(4):
                                prep.append(lambda ti=ti, j=j: u_proj(ti + 1, j))
                                prep.append(lambda ti=ti, j=j: u_conv(ti + 1, j))
                        ops_ = [(lambda ti=ti, d=d: u_op(ti - 1, d)) for d in range(8)] if ti >= 1 else []
                        while prep or ops_:
                            if prep:
                                fill.append(prep.pop(0))
                            if ops_:
                                fill.append(ops_.pop(0))

                    for j in range(4):
                        u_proj(0, j)
                        u_conv(0, j)
                    for c4 in range(4):
                        u_z(0, c4)
                    front(0)
                    for c in range(16):
                        ti, c4 = c // 4, c % 4
                        if c4 == 0:
                            pump(len(fill))
                            refill(ti)
                        if c + 1 < 16:
                            if c4 == 3:
                                pump(len(fill))
                            front(c + 1)
                        tail(c)
                        if ti < 3:
                            u_z(ti + 1, c4)
                    pump(len(fill))
                    for d in range(8):
                        u_op(3, d)
                    for jl in range(2):
                        P.tr(pb[2][:, jl * 128:(jl + 1) * 128], hT[:, jl * 128:(jl + 1) * 128], ident_f, r=["hT", "cst"], w=["pb2"])
                    P.copy("dve", hout[:, :, :], pb[2][:, 0:256].rearrange("p (j n) -> p j n", j=2), r=["pb2"], w=["scrA.0"])
                    P.dma("sp", o_pssm[4 * g:4 * g + 4].rearrange("(j hh) q n -> (hh q) j n", j=2), hout[:, :, :], r=["scrA.0"], dkey="o_hout")
                    for j in range(4):
                        P.tr(pb[3][0:3, j * 128:(j + 1) * 128], ctail[:, j, :], ident_f, r=["ctail", "cst"], w=["pb3"])
                    P.copy("dve", ctout[:, :], pb[3][0:3, :], r=["pb3"], w=["scrB.0", "scrB.1"])
                    P.dma("sp", o_pconv[:, 256 * g:256 * (g + 1)], ctout[:, 0:256], r=["scrB.0", "scrB.1"], dkey="o_ctx")
                    P.dma("sp", o_pconv[:, 2048 + 128 * g:2048 + 128 * (g + 1)], ctout[:, 256:384], r=["scrB.0", "scrB.1"], dkey="o_ctB")
                    P.dma("sp", o_pconv[:, 3072 + 128 * g:3072 + 128 * (g + 1)], ctout[:, 384:512], r=["scrB.0", "scrB.1"], dkey="o_ctC")

                    if cfg.get('ssd_nosample'):
                        continue
                    ti = 4
                    for j in range(4):
                        for kc in range(8):
                            P.mm(pb[0][:, j * NS:(j + 1) * NS], wb[:, kc, j * 128:(j + 1) * 128], xn[:, kc, SEQ:NT], kc == 0, kc == 7,
                                 r=[wkeys[j], xnk(4)], w=["pb0"])
                    P.copy("dve", xrs[:, :, :], pb[0][:, 0:4 * NS].rearrange("p (j b) -> p j b", j=4), r=["pb0"], w=["xrs"])
                    cranges = [(256 * g, 256), (2048 + 128 * g, 128), (3072 + 128 * g, 128)]
                    off = 0
                    for (c0, cn) in cranges:
                        P.dma("sp", cstk[:, off:off + cn], st_conv[:, :, c0:c0 + cn].rearrange("b k c -> (b k) c"), w=["scrA.0", "scrA.1"], dkey="cstk%d" % off)
                        off += cn
                    for j in range(4):
                        P.tr(pb[1][:, j * 48:(j + 1) * 48], cstk[:, j * 128:(j + 1) * 128], ident_f[0:48, 0:48], r=["scrA.0", "scrA.1", "cst"], w=["pb1"])
                    P.copy("dve", cs[:, :, :], pb[1][:, 0:192].rearrange("p (j q) -> p j q", j=4), r=["pb1"], w=["cs"])
                    cs4 = cs[:, :, :].rearrange("p j (b k) -> p j b k", k=3)
                    ncs4 = ncs[:, :, :].rearrange("p j (b k) -> p j b k", k=3)
                    P.tt("dve", acc[:, :, :], xrs[:, :, :], bc(cwg[:, :, 3:4], [128, 4, NS]), ALU.mult, r=["xrs", "cwg"], w=["acc"])
                    P.tt("dve", acc[:, :, :], acc[:, :, :], bc(cwg[:, :, 4:5], [128, 4, NS]), ALU.add, r=["acc", "cwg"], w=["acc"])
                    for k in range(3):
                        P.tt("dve", atmp[:, :, :], cs4[:, :, :, k], bc(cwg[:, :, k:k + 1], [128, 4, NS]), ALU.mult, r=["cs", "cwg"], w=["atmp"])
                        P.tt("dve", acc[:, :, :], acc[:, :, :], atmp[:, :, :], ALU.add, r=["acc", "atmp"], w=["acc"])
                    P.act(xbs[:, :, :], acc[:, :, :], AF.Silu, r=["acc"], w=["xbs"])
                    P.copy("dve", ncs4[:, :, :, 0:2], cs4[:, :, :, 1:3], r=["cs"], w=["ncs"])
                    P.copy("dve", ncs4[:, :, :, 2], xrs[:, :, :], r=["xrs", "ncs"], w=["ncs"])
                    for j in range(4):
                        P.tr(pb[1][0:48, j * 128:(j + 1) * 128], ncs[:, j, :], ident_f, r=["ncs", "cst"], w=["pb1"])
                    P.copy("dve", cstk[:, :], pb[1][0:48, :], r=["pb1"], w=["scrA.0", "scrA.1"])
                    off = 0
                    for (c0, cn) in cranges:
                        P.dma("sp", o_sconv[:, :, c0:c0 + cn].rearrange("b k c -> (b k) c"), cstk[:, off:off + cn], r=["scrA.0", "scrA.1"], dkey="o_cstk%d" % off)
                        off += cn
                    for jl in range(2):
                        for kc in range(8):
                            P.mm(pb[2][:, jl * NS:(jl + 1) * NS], wb[:, kc, 512 + jl * 128:512 + (jl + 1) * 128], xn[:, kc, SEQ:NT], kc == 0, kc == 7,
                                 r=[wk + "z", xnk(4)], w=["pb2"])
                    P.act(szs[:, :, :], pb[2][:, 0:2 * NS].rearrange("p (j b) -> p j b", j=2), AF.Silu, r=["pb2"], w=["szs"])
                    P.dma("sp", Eg[:, :], consts2_d[:, 256 * g:256 * (g + 1)], w=["Eg"], dkey="Eg")
                    for jl in range(2):
                        for q in range(2):
                            P.mm(pb[2][:, 64 + (jl * 2 + q) * NS: 64 + (jl * 2 + q + 1) * NS], Eg[:, jl * 128:(jl + 1) * 128], dsT[:, q, :], True, True,
                                 r=["Eg", "dsT"], w=["pb2"])
                    dview = pb[2][:, 64:64 + 4 * NS].rearrange("p (j q b) -> p j q b", j=2, q=2)
                    P.copy("dve", dtc[:, :, 0, :], dview[:, :, 0, :], r=["pb2"], w=["dtc"])
                    P.act(dtc[:, :, 1, :], dview[:, :, 1, :], AF.Exp, r=["pb2"], w=["dtc"])
                    P.tt("dve", xdtc[:, :, :], xbs[:, 0:2, :], dtc[:, :, 0, :], ALU.mult, r=["xbs", "dtc"], w=["xdtc"])
                    for q in range(4):
                        bs = slice(4 * q, 4 * q + 4)
                        for which, j, bank in (("B", 2, 2), ("C", 3, 3)):
                            rb = rhsB[:, (0 if which == "B" else 4):(4 if which == "B" else 8), :]
                            rbk = "rhsB" + which
                            P.tt("dve", rb, bc(ident_b[:, :].rearrange("p (o n) -> p o n", o=1), [128, 4, 128]),
                                 bc(xbs[:, j, bs].rearrange("p (b o) -> p b o", o=1), [128, 4, 128]), ALU.mult, r=["ident_b", "xbs"], w=[rbk])
                            P.mm(pb[bank][:, :], ones_b[:, :], rb, True, True, r=["ones_b", rbk], w=["pb%d" % bank])
                            P.copy("act", bcsb[:, (0 if which == "B" else 4):(4 if which == "B" else 8), :],
                                   pb[bank][:, :].rearrange("p (b n) -> p b n", b=4), r=["pb%d" % bank], w=["bcsb" + which])
                        for jl in range(2):
                            i2 = (2 * q + jl) % 2
                            hb, hk = h0t[i2], "h0t%d" % i2
                            nb, nk = hnt[i2], "hnt%d" % i2
                            P.dma("sp", hb[:, :, :], st_ssm[4 * q:4 * q + 4, 4 * g + 2 * jl:4 * g + 2 * jl + 2].rearrange("b h q n -> (h q) b n"),
                                  w=[hk], dkey=hk)
                            P.tt("dve", s1[:, :, :], bcsb[:, 0:4, :],
                                 bc(xdtc[:, jl, bs].rearrange("p (b o) -> p b o", o=1), [128, 4, 128]), ALU.mult, r=["bcsbB", "xdtc"], w=["s1"])
                            P.tt("pool", hb[:, :, :], hb[:, :, :],
                                 bc(dtc[:, jl, 1, bs].rearrange("p (b o) -> p b o", o=1), [128, 4, 128]), ALU.mult, r=[hk, "dtc"], w=[hk])
                            P.tt("pool", nb[:, :, :], hb[:, :, :], s1[:, :, :], ALU.add, r=["s1", hk], w=[nk])
                            P.tt("dve", s1[:, :, :], nb[:, :, :], bcsb[:, 4:8, :], ALU.mult,
                                 r=[nk, "bcsbC"], w=["s1"])
                            P.S.add("dve", (lambda e, o=ys[:, jl, bs], i=s1[:, :, :]: e.tensor_reduce(out=o, in_=i, axis=AX.X, op=ALU.add)),
                                    ["s1"], ["ys"])
                            P.dma("pool", o_sssm[4 * q:4 * q + 4, 4 * g + 2 * jl:4 * g + 2 * jl + 2].rearrange("b h q n -> (h q) b n"), nb[:, :, :],
                                  r=[nk], dkey="o_" + nk)
                    P.tt("dve", gsm[:, :, :], xbs[:, 0:2, :], bc(Dcol[:, 2 * g:2 * g + 2].rearrange("p (j o) -> p j o", o=1), [128, 2, NS]), ALU.mult,
                         r=["xbs", "Dcol"], w=["gsm"])
                    P.tt("dve", gsm[:, :, :], gsm[:, :, :], ys[:, :, :], ALU.add, r=["gsm", "ys"], w=["gsm"])
                    P.tt("dve", gsm[:, :, :], gsm[:, :, :], szs[:, :, :], ALU.mult, r=["gsm", "szs"], w=["gsm"])
                    P.tt("dve", gsq[:, :, :], gsm[:, :, :], gsm[:, :, :], ALU.mult, r=["gsm"], w=["gsq"])
                    for jl in range(2):
                        P.mm(pb[0][:, 0:NS], ones_b[:, :], gsq[:, jl, :], jl == 0, jl == 1, r=["ones_b", "gsq"], w=["pb0"])
                    P.copy("dve", rsd[:, :], pb[0][:, 0:NS], r=["pb0"], w=["rsd"])
                    rstd_pool(rsd[:, :], rsd[:, :], 1.0 / 256, ["rsd"], ["rsd"])
                    for jl in range(2):
                        P.stt(gTs_all[:, 2 * g + jl, :], gsm[:, jl, :], gnw[:, 2 * g + jl:2 * g + jl + 1], rsd[:, :], ALU.mult, ALU.mult,
                              r=["gsm", "gnw", "rsd"], w=["gTs_all"])
                if not cfg.get('ssd_nosample'):
                    for g in range(ngroups):
                        P.dma("pool", wo[0][:, :, :], Wout[:, 2 * g:2 * g + 2, :], w=["wo0"], dkey="wo0")
                        for dt_ in range(8):
                            po, pk = pb[dt_ % 2], "pb%d" % (dt_ % 2)
                            for jl in range(2):
                                P.mm(po[:, 0:NS], wo[0][:, jl, dt_ * 128:(dt_ + 1) * 128], gTs_all[:, 2 * g + jl, :], jl == 0, jl == 1,
                                     r=["wo0", "gTs_all"], w=[pk])
                            P.tt("dve", xres[:, dt_, SEQ:NT], xres[:, dt_, SEQ:NT], po[:, 0:NS], ALU.add, r=[pk, xk(dt_, 4)], w=[xk(dt_, 4)])
                P.S.barrier()

        def layer_mlstm():
            li = 2
            emit_norm(li)
            Win = W["l2_in_proj"].rearrange("(k p) j -> p k j", p=128)
            Wout = W["l2_out_proj"].rearrange("(k p) j -> p k j", p=128)
            st_C, st_n, st_m, st_conv = DR["state_l2_C"], DR["state_l2_n"], DR["state_l2_m"], DR["state_l2_conv"]
            o_pC, o_pn, o_pm, o_pconv = DR["p2_C"], DR["p2_n"], DR["p2_m"], DR["p2_conv"]
            o_sC, o_sn, o_sm, o_sconv = DR["s2_C"], DR["s2_n"], DR["s2_m"], DR["s2_conv"]
            bmask = cst[:, 640:768]
            gemask = cst[:, 768:896]
            ones_f = cst[:, 128:256]
            KS = 512.0 ** -0.5
            with ExitStack() as L:
                cw = P.sb(L, "cw", [128, 16, 5], F32)
                wq4 = P.sb(L, "wq4", [128, 16, 4, 4], F32)
                wif = P.sb(L, "wif", [128, 48, 8], BF16)
                mhw = P.sb(L, "mhw", [128, 16], F32)
                skp = P.sb(L, "skp", [128, 16], F32)
                bocol = P.sb(L, "bocol", [128, 16], F32)
                gacc = P.sb(L, "gacc", [128, 17, 8], F32)
                xcs_all = P.sb(L, "xcs_all", [128, 16, NS], F32)
                xms_all = P.sb(L, "xms_all", [128, 16, NS], F32)
                dg = P.sb(L, "dg", [128, 16, 128], BF16)
                Dm = P.sb(L, "Dm", [128, 16, 128], BF16)
                xr = [P.sb(L, "xr%d" % j, [128, 515], BF16) for j in range(4)]
                xc = P.sb(L, "xc", [128, 4, 512], BF16)
                qT = P.sb(L, "qT", [128, 4, 512], BF16)
                khT = P.sb(L, "khT", [128, 4, 512], BF16)
                szT = P.sb(L, "szT", [128, 4, 512], BF16)
                gT = P.sb(L, "gT", [128, 4, 512], BF16)
                wbuf = P.sb(L, "wbuf", [128, 8, 1024], BF16)
                wo = P.sb(L, "wo", [128, 4, D], BF16)
                ctail = P.sb(L, "ctail", [128, 3], F32)
                ctout = P.sb(L, "ctout", [3, 128], F32)
                cstk = P.sb(L, "cstk", [48, 128], F32)
                cs = P.sb(L, "cs", [128, 48], F32)
                ncs = P.sb(L, "ncs", [128, 48], F32)
                sA = P.sb(L, "sA", [128, 4, NS], F32)
                sB = P.sb(L, "sB", [128, 4, NS], BF16)
                gcol = P.sb(L, "gcol", [128, 128], F32)
                ga = P.sb(L, "ga", [64, 128], F32)
                gs1 = P.sb(L, "gs1", [64, 8], F32)
                ends4 = P.sb(L, "ends4", [4, 16, 2], F32)
                m4 = P.sb(L, "m4", [4, 2, 16], F32)
                tokq = P.sb(L, "tokq", [128, 5, 64], F32)
                vtok = P.sb(L, "vtok", [128, 512], BF16)
                kwt = P.sb(L, "kwt", [128, 512], BF16)
                otok = P.sb(L, "otok", [128, 512], BF16)
                bobc = P.sb(L, "bobc", [128, 512], F32)
                Sm = P.sb(L, "Sm", [128, 128], F32)
                Eg_ = P.sb(L, "Eg_", [128, 128], F32)
                wls = P.sb(L, "wls", [128, 128], BF16)
                wT = P.sb(L, "wT", [128, 128], BF16)
                Asb = P.sb(L, "Asb", [128, 512], F32)
                num = P.sb(L, "num", [128, 512], F32)
                sm = P.sb(L, "sm", [128, 16], F32)
                st6 = P.sb(L, "st6", [128, 8], F32)
                hn = P.sb(L, "hn", [128, 512], BF16)
                g1 = P.sb(L, "g1", [128, 128], F32)
                sxc = P.sb(L, "sxc", [128, 128], F32)
                CT = P.sb(L, "CT", [128, 4, 512], F32)
                CTb = P.sb(L, "CTb", [128, 4, 512], BF16)
                nst = P.sb(L, "nst", [128, 4], F32)
                nstb = P.sb(L, "nstb", [128, 4], BF16)
                Cout = P.sb(L, "Cout", [128, 512], F32)
                sg = P.sb(L, "sg", [NS, 64], F32)
                n0t = P.sb(L, "n0t", [NS, 512], F32)
                qtok = P.sb(L, "qtok", [NS, 512], F32)
                wid = P.sb(L, "wid", [NS, NS, 8], F32)
                wibc = P.sb(L, "wibc", [128, NS, 8], F32)
                Cnt1 = P.sb(L, "Cnt1", [128, 512], F32)
                numT = P.sb(L, "numT", [128, 4, NS], F32)
                hsT = P.sb(L, "hsT", [128, 4, NS], F32)
                hs2 = P.sb(L, "hs2", [128, 4, NS], F32)
                mrs = P.sb(L, "mrs", [128, 4, NS], F32)
                gTs = P.sb(L, "gTs", [128, 4, NS], BF16)
                pbT1 = pb[1].bitcast(BF16)
                scr = P.nc.dram_tensor("ml_scr", [64, 4], F32, kind="Internal").ap()

                for k in range(4):
                    P.dma("sp", cw[:, :, k], W["l2_conv_w"][k].rearrange("(j p) -> p j", p=128), w=["cw"], dkey="cw%d" % k, nonc=True)
                P.dma("sp", cw[:, :, 4], W["l2_conv_b"].rearrange("(j p) -> p j", p=128), w=["cw"], dkey="cw4", nonc=True)
                for pi, nm in enumerate(("w_q", "w_k", "w_v", "w_o")):
                    P.dma("sp", wq4[:, :, pi, :], W["l2_" + nm].rearrange("n j i -> (n j) i").rearrange("(t p) i -> p t i", p=128),
                          w=["wq4"], dkey="wq4%d" % pi, nonc=True)
                P.dma("pool", wif[:, :, :], W["l2_w_if"].rearrange("(t p) g -> p t g", p=128), w=["wif"], dkey="wif")
                P.dma("sp", mhw[:, :], W["l2_mh_norm_w"].rearrange("(k p) -> p k", p=128), w=["mhw"], dkey="mhw", nonc=True)
                P.dma("sp", skp[:, :], W["l2_skip"].rearrange("(k p) -> p k", p=128), w=["skp"], dkey="skp", nonc=True)
                P.dma("sp", bocol[:, :], W["l2_b_o"].rearrange("(k p) -> p k", p=128), w=["bocol"], dkey="bocol", nonc=True)
                bif2 = W["l2_b_if"].rearrange("(g o) -> g o", o=1)
                for h in range(4):
                    P.dma("sp", gs1[h * 16:(h + 1) * 16, 0:1], bc(bif2[h:h + 1, :], [16, 1]), w=["gs1"], dkey="gs1a%d" % h, nonc=True)
                    P.dma("sp", gs1[h * 16:(h + 1) * 16, 1:2], bc(bif2[4 + h:5 + h, :], [16, 1]), w=["gs1"], dkey="gs1b%d" % h, nonc=True)
                P.ts("dve", gs1[:, 2:3], gs1[:, 1:2], -1.0, None, ALU.mult, r=["gs1"], w=["gs1"])
                P.memset("dve", gs1[:, 3:4], 1.0, w=["gs1"])
                P.memset("dve", wid[:, :, :], 0.0, w=["wid"])

                def build_D(dst_idx, jt, pi):
                    P.tt("dve", Dm[:, dst_idx, :].rearrange("p (n i) -> p n i", i=4),
                         bc(wq4[:, jt, pi:pi + 1, :], [128, 32, 4]), bmask.rearrange("p (n i) -> p n i", i=4), ALU.mult,
                         r=["wq4", "cst"], w=["Dm"])

                def build_dg(dst_base, jt):
                    for k in range(4):
                        P.ts("dve", dg[:, dst_base + k, :], ident_f, cw[:, jt, k:k + 1], None, ALU.mult, r=["cst", "cw"], w=["dg"])

                def proj_conv(jt, jl, ti, wcols, wkey):
                    t0 = ti * 512
                    for kc in range(8):
                        P.mm(pb[0][:, :], wbuf[:, kc, wcols:wcols + 128], xn[:, kc, t0:t0 + 512], kc == 0, kc == 7, r=[wkey, xnk(ti)], w=["pb0"])
                    P.copy("act", xr[jl][:, 3:515], pb[0][:, :], r=["pb0"], w=["xr%d" % jl])
                    for k in range(4):
                        P.mm(pb[1][:, :], dg[:, jl * 4 + k, :], xr[jl][:, k:k + 512], k == 0, k == 3, r=["dg", "xr%d" % jl], w=["pb1"])
                    P.act(xc[:, jl, :], pb[1][:, :], AF.Silu, r=["pb1", "cw"], w=["xc.%d" % jl], bias=cw[:, jt, 4:5])

                stgA = cfg.get("ml_stage", 9)
                for jt in range(16 if stgA >= 1 else 0):
                    slot = jt % 4
                    wk = "wA%d" % slot
                    P.dma("pool", wbuf[:, :, slot * 128:(slot + 1) * 128], Win[:, :, jt * 128:(jt + 1) * 128], w=[wk], dkey=wk)
                    build_dg(0, jt)
                    for pi in range(3):
                        build_D(pi * 4, jt, pi)
                    P.memset("dve", xr[0][:, 0:3], 0.0, w=["xr0"])
                    for ti in range(4):
                        proj_conv(jt, 0, ti, slot * 128, wk)
                        if ti == 3:
                            P.copy("dve", ctail[:, :], pb[0][:, 509:512], r=["pb0"], w=["ctail"])
                        else:
                            P.copy("pool", xr[0][:, 0:3], xr[0][:, 512:515], r=["xr0"], w=["xr0"])
                        P.mm(pb[2][:, :], Dm[:, 0, :], xc[:, 0, :], True, True, r=["Dm", "xc.0"], w=["pb2"])
                        P.mm(pb[3][:, :], Dm[:, 4, :], xc[:, 0, :], True, True, r=["Dm", "xc.0"], w=["pb3"])
                        P.mm(pb[4][:, :], Dm[:, 8, :], xr[0][:, 3:515], True, True, r=["Dm", "xr0"], w=["pb4"])
                        P.copy("act", qT[:, 0, :], pb[2][:, :], r=["pb2"], w=["qT"])
                        P.copy("dve", khT[:, 0, :], pb[3][:, :], r=["pb3"], w=["khT"])
                        P.copy("act", szT[:, 0, :], pb[4][:, :], r=["pb4"], w=["szT"])
                        for c4 in range(4):
                            cs_ = slice(c4 * 128, (c4 + 1) * 128)
                            o_ = pb[5][:, c4 * 8:(c4 + 1) * 8]
                            P.mm(o_, qT[:, 0, cs_], wif[:, jt, :], True, False, r=["qT", "wif"], w=["pb5"])
                            P.mm(o_, khT[:, 0, cs_], wif[:, 16 + jt, :], False, False, r=["khT", "wif"], w=["pb5"])
                            P.mm(o_, szT[:, 0, cs_], wif[:, 32 + jt, :], False, True, r=["szT", "wif"], w=["pb5"])
                        gv = gacc[:, 4 * ti:4 * ti + 4, :]
                        pv = pb[5][:, 0:32].rearrange("p (c g) -> p c g", g=8)
                        if jt == 0:
                            P.copy("dve", gv, pv, r=["pb5"], w=["gacc.%d" % ti])
                        else:
                            P.tt("dve", gv, gv, pv, ALU.add, r=["pb5", "gacc.%d" % ti], w=["gacc.%d" % ti])
                    P.tr(pb[6][0:3, 0:128], ctail[:, :], ident_f, r=["ctail", "cst"], w=["pb6"])
                    P.copy("dve", ctout[:, :], pb[6][0:3, 0:128], r=["pb6"], w=["ctout"])
                    P.dma("sp", o_pconv[:, jt * 128:(jt + 1) * 128], ctout[:, :], r=["ctout"], dkey="o_ctout")
                    for kc in range(8):
                        P.mm(pb[6][:, 128:128 + NS], wbuf[:, kc, slot * 128:(slot + 1) * 128], xn[:, kc, SEQ:NT], kc == 0, kc == 7, r=[wk, xnk(4)], w=["pb6"])
                    P.copy("dve", xms_all[:, jt, :], pb[6][:, 128:128 + NS], r=["pb6"], w=["xms_all"])
                    P.dma("sp", cstk[:, :], st_conv[:, :, jt * 128:(jt + 1) * 128].rearrange("b k c -> (b k) c"), w=["cstk"], dkey="cstk")
                    P.tr(pb[6][:, 256:304], cstk[:, :], ident_f[0:48, 0:48], r=["cstk", "cst"], w=["pb6"])
                    P.copy("dve", cs[:, :], pb[6][:, 256:304], r=["pb6"], w=["cs"])
                    cs3 = cs[:, :].rearrange("p (b k) -> p b k", k=3)
                    ncs3 = ncs[:, :].rearrange("p (b k) -> p b k", k=3)
                    a0 = sA[:, 0, :]
                    a1 = sA[:, 1, :]
                    P.ts("dve", a0, xms_all[:, jt, :], cw[:, jt, 3:4], cw[:, jt, 4:5], ALU.mult, ALU.add, r=["xms_all", "cw"], w=["sA"])
                    for k in range(3):
                        P.stt(a0, cs3[:, :, k], cw[:, jt, k:k + 1], a0, ALU.mult, ALU.add, r=["cs", "cw", "sA"], w=["sA"])
                    P.act(xcs_all[:, jt, :], a0, AF.Silu, r=["sA"], w=["xcs_all"])
                    P.copy("dve", ncs3[:, :, 0:2], cs3[:, :, 1:3], r=["cs"], w=["ncs"])
                    P.copy("dve", ncs3[:, :, 2], xms_all[:, jt, :], r=["xms_all", "ncs"], w=["ncs"])
                    P.tr(pb[6][0:48, 320:448], ncs[:, :], ident_f, r=["ncs", "cst"], w=["pb6"])
                    P.copy("dve", cstk[:, :], pb[6][0:48, 320:448], r=["pb6"], w=["cstk"])
                    P.dma("sp", o_sconv[:, :, jt * 128:(jt + 1) * 128].rearrange("b k c -> (b k) c"), cstk[:, :], r=["cstk"], dkey="o_cstk")
                    P.copy("dve", sB[:, 0, :], xcs_all[:, jt, :], r=["xcs_all"], w=["sB"])
                    P.copy("dve", sB[:, 1, :], xms_all[:, jt, :], r=["xms_all"], w=["sB"])
                    for pi in range(3):
                        P.mm(pb[7][:, pi * NS:(pi + 1) * NS], Dm[:, pi * 4, :], sB[:, 0 if pi < 2 else 1, :], True, True, r=["Dm", "sB"], w=["pb7"])
                    P.copy("dve", sB[:, 2:4, :].rearrange("p a b -> p (a b)")[:, 0:2 * NS], pb[7][:, 0:2 * NS], r=["pb7"], w=["sB2"])
                    P.copy("dve", sA[:, 2, :].bitcast(BF16)[:, 0:NS], pb[7][:, 2 * NS:3 * NS], r=["pb7"], w=["sA2"])
                    sv = sA[:, 2, :].bitcast(BF16)[:, 0:NS]
                    P.mm(pb[7][0:NS, 64:72], sB[:, 2, :], wif[:, jt, :], True, False, r=["sB2", "wif"], w=["pb7"])
                    P.mm(pb[7][0:NS, 64:72], sB[:, 3, :], wif[:, 16 + jt, :], False, False, r=["sB2", "wif"], w=["pb7"])
                    P.mm(pb[7][0:NS, 64:72], sv, wif[:, 32 + jt, :], False, True, r=["sA2", "wif"], w=["pb7"])
                    if jt == 0:
                        P.copy("dve", gacc[0:NS, 16, :], pb[7][0:NS, 64:72], r=["pb7"], w=["gacc.4"])
                    else:
                        P.tt("dve", gacc[0:NS, 16, :], gacc[0:NS, 16, :], pb[7][0:NS, 64:72], ALU.add, r=["pb7", "gacc.4"], w=["gacc.4"])

                if stgA >= 2:
                    gk = ["gacc.%d" % i for i in range(4)]
                    P.copy("dve", gcol[:, :].rearrange("p (g c) -> p c g", c=16), gacc[:, 0:16, :], r=gk, w=["gcol"])
                    P.memset("dve", num[0:64, 256:384], 1.0, w=["gq7"])
                    P.tr(pb[0][0:64, 0:128], gcol[:, 0:64], ident_f, r=["gcol", "cst"], w=["pb0"])
                    P.tr(pb[0][0:64, 128:256], gcol[:, 64:128], ident_f, r=["gcol", "cst"], w=["pb0"])
                    ig = ga[:, :]
                    lf, cm, wi_, em_ = [Asb[0:64, i * 128:(i + 1) * 128] for i in range(4)]
                    we_, dpb, one_ = [num[0:64, i * 128:(i + 1) * 128] for i in range(3)]
                    P.ts("dve", ig, pb[0][0:64, 0:128], gs1[:, 0:1], None, ALU.add, r=["pb0", "gs1"], w=["gq0"])
                    P.act(lf, pb[0][0:64, 128:256], AF.Exp, r=["pb0", "gs1"], w=["gq1"], bias=gs1[:, 2:3], scale=-1.0)
                    P.act(lf, lf, AF.Ln, r=["gq1", "gs1"], w=["gq1"], bias=gs1[:, 3:4])
                    P.ts("dve", lf, lf, -1.0, None, ALU.mult, r=["gq1"], w=["gq1"])
                    P.S.add("dve", (lambda e: e.tensor_tensor_scan(out=lf, data0=one_, data1=lf, initial=0.0, op0=ALU.mult, op1=ALU.add)),
                            ["gq1", "gq7"], ["gq1"])
                    P.tt("dve", ig, ig, lf, ALU.subtract, r=["gq0", "gq1"], w=["gq0"])
                    P.S.add("dve", (lambda e: e.tensor_tensor_scan(out=cm, data0=ig, data1=ig, initial=-1e30, op0=ALU.max, op1=ALU.max)),
                            ["gq0"], ["gq2"])
                    P.copy("dve", gs1[:, 4:5], cm[:, 127:128], r=["gq2"], w=["gs1e"])
                    P.copy("dve", gs1[:, 5:6], lf[:, 127:128], r=["gq1"], w=["gs1e"])
                    P.dma("sp", scr[:, 0:2], gs1[:, 4:6], r=["gs1e"], w=["scr01"], dkey="scr_a", nonc=True)
                    P.dma("sp", ends4[:, :, :], scr[:, 0:2].rearrange("(h c) t -> h c t", c=16), r=["scr01"], w=["ends4"], dkey="scr_b", nonc=True)
                    P.S.add("dve", (lambda e: e.tensor_tensor_scan(out=m4[:, 0, :], data0=ends4[:, :, 0], data1=ends4[:, :, 1], initial=0.0,
                                                                   op0=ALU.max, op1=ALU.add)), ["ends4"], ["m4"])
                    P.memset("dve", m4[:, 1, 0:1], 0.0, w=["m4b"])
                    P.copy("dve", m4[:, 1, 1:16], m4[:, 0, 0:15], r=["m4"], w=["m4b"])
                    P.dma("sp", o_pm.rearrange("(h o) -> h o", o=1), m4[:, 0, 15:16], r=["m4"], dkey="o_pm", nonc=True)
                    P.dma("sp", scr[:, 2:3].rearrange("(h c) t -> h c t", c=16), m4[:, 1, :].rearrange("h (c o) -> h c o", o=1), r=["m4b"], w=["scr2"],
                          dkey="scr_c", nonc=True)
                    P.dma("sp", gs1[:, 6:7], scr[:, 2:3], r=["scr2"], w=["gs1m"], dkey="scr_d", nonc=True)
                    mprev = gs1[:, 6:7]
                    P.ts("dve", cm, cm, mprev, None, ALU.max, r=["gq2", "gs1m"], w=["gq2"])
                    P.act(wi_, cm, AF.Exp, r=["gq2", "gs1m"], w=["gq3"], bias=mprev, scale=-1.0)
                    P.tt("dve", em_, lf, cm, ALU.add, r=["gq1", "gq2"], w=["gq4"])
                    P.act(em_, em_, AF.Exp, r=["gq4"], w=["gq4"], scale=-1.0)
                    P.ts("dve", gs1[:, 7:8], cm[:, 127:128], -1.0, math.log(KS), ALU.mult, ALU.add, r=["gq2"], w=["gs1n"])
                    P.act(we_, ig, AF.Exp, r=["gq0", "gs1n"], w=["gq5"], bias=gs1[:, 7:8])
                    P.tt("dve", gs1[:, 4:5], mprev, cm[:, 127:128], ALU.subtract, r=["gs1m", "gq2", "gs1e"], w=["gs1e"])
                    P.act(gs1[:, 4:5], gs1[:, 4:5], AF.Exp, r=["gs1e"], w=["gs1e"])
                    P.ts("dve", dpb, one_, gs1[:, 4:5], None, ALU.mult, r=["gq7", "gs1e"], w=["gq6"])
                    for qi, src in enumerate((cm, wi_, em_, we_, dpb)):
                        P.tr(pb[2][:, qi * 64:(qi + 1) * 64], src, ident_f[0:64, 0:64], r=["gq%d" % (2 + qi), "cst"], w=["pb2"])
                    P.copy("dve", tokq[:, :, :], pb[2][:, 0:320].rearrange("p (q x) -> p q x", q=5), r=["pb2"], w=["tokq"])

                    P.dma("sp", sg[:, 28:36], bc(W["l2_b_if"].rearrange("(o g) -> o g", o=1), [NS, 8]), w=["sg"], dkey="sg_b")
                    P.dma("sp", sg[:, 8:12], st_m[:, :], w=["sg"], dkey="sg_m")
                    P.tt("dve", sg[:, 0:8], gacc[0:NS, 16, :], sg[:, 28:36], ALU.add, r=["gacc.4", "sg"], w=["sg"])
                    P.act(sg[:, 4:8], sg[:, 4:8], AF.Exp, r=["sg"], w=["sg"], scale=-1.0)
                    P.act(sg[:, 4:8], sg[:, 4:8], AF.Ln, r=["sg", "gs1"], w=["sg"], bias=gs1[0:NS, 3:4])
                    P.ts("dve", sg[:, 4:8], sg[:, 4:8], -1.0, None, ALU.mult, r=["sg"], w=["sg"])
                    P.tt("dve", sg[:, 8:12], sg[:, 8:12], sg[:, 4:8], ALU.add, r=["sg"], w=["sg"])
                    P.tt("dve", sg[:, 12:16], sg[:, 8:12], sg[:, 0:4], ALU.max, r=["sg"], w=["sg"])
                    P.dma("sp", o_sm[:, :], sg[:, 12:16], r=["sg"], dkey="o_sm")
                    P.tt("dve", sg[:, 16:20], sg[:, 0:4], sg[:, 12:16], ALU.subtract, r=["sg"], w=["sg"])
                    P.act(sg[:, 16:20], sg[:, 16:20], AF.Exp, r=["sg"], w=["sg"])
                    P.tt("dve", sg[:, 20:24], sg[:, 8:12], sg[:, 12:16], ALU.subtract, r=["sg"], w=["sg"])
                    P.act(sg[:, 20:24], sg[:, 20:24], AF.Exp, r=["sg"], w=["sg"])
                    P.act(sg[:, 24:28], sg[:, 12:16], AF.Exp, r=["sg"], w=["sg"], scale=-1.0)
                    P.tt("dve", wid[:, :, 0:4], bc(sg[:, 20:24].rearrange("p (o h) -> p o h", o=1), [NS, NS, 4]),
                         bc(ident_f[0:NS, 0:NS].rearrange("p (b o) -> p b o", o=1), [NS, NS, 4]), ALU.mult, r=["sg", "cst"], w=["wid"])

                nheads = cfg.get("ml_heads", 4) if stgA >= 3 else 0
                for h in range(nheads):
                    P.dma("pool", wbuf[:, :, 0:512], Win[:, :, 512 * h:512 * (h + 1)], w=["wA0", "wA1", "wA2", "wA3"], dkey="wBx")
                    P.dma("pool", wbuf[:, :, 512:1024], Win[:, :, 2048 + 512 * h:2048 + 512 * (h + 1)], w=["wBz"], dkey="wBz")
                    P.dma("pool", wo[:, :, :], Wout[:, 4 * h:4 * h + 4, :], w=["wo"], dkey="wo")
                    P.dma("sp", bobc[:, :], bc(W["l2_b_o"][512 * h:512 * (h + 1)].rearrange("(o n) -> o n", o=1), [128, 512]), w=["bobc"], dkey="bobc")
                    wxk = ["wA0", "wA1", "wA2", "wA3"]
                    for jl in range(4):
                        build_dg(jl * 4, 4 * h + jl)
                        for pi in range(4):
                            build_D(pi * 4 + jl, 4 * h + jl, pi)
                        P.memset("dve", xr[jl][:, 0:3], 0.0, w=["xr%d" % jl])
                    P.memset("dve", CT[:, :, :], 0.0, w=["CT"])
                    P.memset("pool", CTb[:, :, :], 0.0, w=["CTb"])
                    P.memset("dve", nst[:, :], 0.0, w=["nst"])
                    P.memset("dve", nstb[:, :], 0.0, w=["nstb"])
                    for ti in range(4):
                        t0 = ti * 512
                        for jl in range(4):
                            jt = 4 * h + jl
                            proj_conv(jt, jl, ti, jl * 128, wxk[jl])
                            if ti < 3:
                                P.copy("pool", xr[jl][:, 0:3], xr[jl][:, 512:515], r=["xr%d" % jl], w=["xr%d" % jl])
                            P.mm(pb[2][:, :], Dm[:, 0 + jl, :], xc[:, jl, :], True, True, r=["Dm", "xc.%d" % jl], w=["pb2"])
                            P.mm(pb[3][:, :], Dm[:, 4 + jl, :], xc[:, jl, :], True, True, r=["Dm", "xc.%d" % jl], w=["pb3"])
                            P.copy("dve", qT[:, jl, :], pb[2][:, :], r=["pb2"], w=["qT"])
                            P.act(khT[:, jl, :], pb[3][:, :], AF.Copy, r=["pb3"], w=["khT"], scale=KS)
                            for kc in range(8):
                                P.mm(pb[4][:, :], wbuf[:, kc, 512 + jl * 128:512 + (jl + 1) * 128], xn[:, kc, t0:t0 + 512], kc == 0, kc == 7,
                                     r=["wBz", xnk(ti)], w=["pb4"])
                            P.act(szT[:, jl, :], pb[4][:, :], AF.Silu, r=["pb4"], w=["szT"])
                        for c4 in range(4):
                            c = 4 * ti + c4
                            hc = h * 16 + c
                            cs_ = slice(c4 * 128, (c4 + 1) * 128)
                            xs_ = slice(3 + c4 * 128, 3 + (c4 + 1) * 128)
                            for jl in range(4):
                                js = slice(jl * 128, (jl + 1) * 128)
                                P.mm(pb[2][:, js], xr[jl][:, xs_], Dm[:, 8 + jl, :], True, True, r=["xr%d" % jl, "Dm"], w=["pb2"])
                                P.mm(pb[3][:, js], xc[:, jl, cs_], Dm[:, 4 + jl, :], True, True, r=["xc.%d" % jl, "Dm"], w=["pb3"])
                                P.mm(pb[4][:, js], xr[jl][:, xs_], Dm[:, 12 + jl, :], True, True, r=["xr%d" % jl, "Dm"], w=["pb4"])
                            P.copy("act", vtok[:, :], pb[2][:, :], r=["pb2"], w=["vtok"])
                            P.act(kwt[:, :], pb[3][:, :], AF.Copy, r=["pb3", "tokq"], w=["kwt"], scale=tokq[:, 3, hc:hc + 1])
                            P.tt("dve", num[:, :], pb[4][:, :], bobc[:, :], ALU.add, r=["pb4", "bobc"], w=["num"])
                            P.act(otok[:, :], num[:, :], AF.Tanh, r=["num"], w=["otok"], scale=0.5)
                            P.ts("pool", otok[:, :], otok[:, :], 0.5, 0.5, ALU.mult, ALU.add, r=["otok"], w=["otok"])
                            for jl in range(4):
                                P.mm(pb[5][:, 0:128], qT[:, jl, cs_], khT[:, jl, cs_], jl == 0, jl == 3, r=["qT", "khT"], w=["pb5"])
                            P.mm(pb[5][:, 128:256], bc(ident_f[0:64, hc:hc + 1], [64, 128]), ga[:, :], True, True, r=["cst", "gq0"], w=["pb5"])
                            for jl in range(4):
                                P.mm(pb[5][:, 256:257], qT[:, jl, cs_], nstb[:, jl:jl + 1], jl == 0, jl == 3, r=["qT", "nstb"], w=["pb5"])
                            P.tt("dve", Sm[:, :], pb[5][:, 0:128], gemask, ALU.mult, r=["pb5", "cst"], w=["Sm"])
                            P.ts("dve", Eg_[:, :], pb[5][:, 128:256], tokq[:, 0, hc:hc + 1], 0.0, ALU.subtract, ALU.min, r=["pb5", "tokq"], w=["Eg_"])
                            P.act(Eg_[:, :], Eg_[:, :], AF.Exp, r=["Eg_"], w=["Eg_"])
                            P.S.add("dve", (lambda e, o=wls[:, :], a=Eg_[:, :], b=Sm[:, :], ac=sm[:, 0:1]:
                                            e.scalar_tensor_tensor(out=o, in0=a, scalar=1.0, in1=b, op0=ALU.mult, op1=ALU.mult, accum_out=ac)),
                                    ["Eg_", "Sm"], ["wls", "sm0"])
                            P.tr(pbT1[:, 0:128], wls[:, :], ident_b[:, :], r=["wls", "ident_b"], w=["pb1"])
                            P.copy("dve", wT[:, :], pbT1[:, 0:128], r=["pb1"], w=["wT"])
                            P.mm(pb[6][:, :], wT[:, :], vtok[:, :], True, True, r=["wT", "vtok"], w=["pb6"])
                            for jl in range(4):
                                P.mm(pb[7][:, :], qT[:, jl, cs_], CTb[:, jl, :], jl == 0, jl == 3, r=["qT", "CTb"], w=["pb7"])
                            P.copy("act", Asb[:, :], pb[6][:, :], r=["pb6"], w=["Asb"])
                            wic = tokq[:, 1, hc:hc + 1]
                            P.stt(num[:, :], pb[7][:, :], wic, Asb[:, :], ALU.mult, ALU.add, r=["pb7", "tokq", "Asb"], w=["num"])
                            P.stt(sm[:, 1:2], pb[5][:, 256:257], wic, sm[:, 0:1], ALU.mult, ALU.add, r=["pb5", "tokq", "sm0"], w=["sm1"])
                            P.stt(sm[:, 1:2], sm[:, 1:2], -1.0, sm[:, 1:2], ALU.mult, ALU.max, r=["sm1"], w=["sm1"])
                            P.tt("dve", sm[:, 1:2], sm[:, 1:2], tokq[:, 2, hc:hc + 1], ALU.max, r=["sm1", "tokq"], w=["sm1"])
                            P.recip(sm[:, 1:2], sm[:, 1:2], r=["sm1"], w=["sm1"])
                            P.stt(num[:, :], num[:, :], sm[:, 1:2], otok[:, :], ALU.mult, ALU.mult, r=["num", "sm1", "otok"], w=["num"])
                            P.S.add("dve", (lambda e, o=st6[:, 0:6], i=num[:, :]: e.bn_stats(out=o, in_=i)), ["num"], ["st6"])
                            P.S.add("dve", (lambda e, o=sm[:, 2:4], i=st6[:, 0:6]: e.bn_aggr(out=o, in_=i)), ["st6"], ["sm2"])
                            rstd_pool(sm[:, 4:5], sm[:, 3:4], 1.0, ["sm2"], ["sm4"])
                            P.stt(sm[:, 5:6], sm[:, 2:3], -1.0, sm[:, 4:5], ALU.mult, ALU.mult, r=["sm2", "sm4"], w=["sm5"])
                            P.ts("dve", hn[:, :], num[:, :], sm[:, 4:5], sm[:, 5:6], ALU.mult, ALU.add, r=["num", "sm4", "sm5"], w=["hn"])
                            for jl in range(4):
                                P.tr(pbT1[:, 256 + jl * 128:256 + (jl + 1) * 128], hn[:, jl * 128:(jl + 1) * 128], ident_b[:, :], r=["hn", "ident_b"], w=["pb1"])
                            for jl in range(4):
                                jt = 4 * h + jl
                                P.ts("dve", sxc[:, :], xc[:, jl, cs_], skp[:, jt:jt + 1], None, ALU.mult, r=["xc.%d" % jl, "skp"], w=["sxc"])
                                P.stt(g1[:, :], pbT1[:, 256 + jl * 128:256 + (jl + 1) * 128], mhw[:, jt:jt + 1], sxc[:, :], ALU.mult, ALU.add,
                                      r=["pb1", "mhw", "sxc"], w=["g1"])
                                P.tt("pool", gT[:, jl, cs_], g1[:, :], szT[:, jl, cs_], ALU.mult, r=["g1", "szT"], w=["gT"])
                            dpc = tokq[:, 4, hc:hc + 1]
                            for kt in range(4):
                                pu, puk = (pb[0], "pb0") if kt % 2 == 0 else (pb[6], "pb6")
                                P.mm(pu[:, :], kwt[:, kt * 128:(kt + 1) * 128], vtok[:, :], True, True, r=["kwt", "vtok"], w=[puk])
                                P.stt(CT[:, kt, :], CT[:, kt, :], dpc, pu[:, :], ALU.mult, ALU.add, r=["CT", "tokq", puk], w=["CT"])
                                P.copy("act", CTb[:, kt, :], CT[:, kt, :], r=["CT"], w=["CTb"])
                                P.mm(pb[5][:, 260 + kt:261 + kt], kwt[:, kt * 128:(kt + 1) * 128], ones_b[:, 0:1], True, True, r=["kwt", "ones_b"], w=["pb5"])
                            P.stt(nst[:, :], nst[:, :], dpc, pb[5][:, 260:264], ALU.mult, ALU.add, r=["nst", "tokq", "pb5"], w=["nst"])
                            P.copy("dve", nstb[:, :], nst[:, :], r=["nst"], w=["nstb"])
                        for dt_ in range(8):
                            po, pk = pb[dt_ % 2], "pb%d" % (dt_ % 2)
                            for jl in range(4):
                                P.mm(po[:, :], wo[:, jl, dt_ * 128:(dt_ + 1) * 128], gT[:, jl, :], jl == 0, jl == 3, r=["wo", "gT"], w=[pk])
                            P.tt("dve", xres[:, dt_, t0:t0 + 512], xres[:, dt_, t0:t0 + 512], po[:, :], ALU.add, r=[pk, xk(dt_, ti)], w=[xk(dt_, ti)])
                    for vt in range(4):
                        for kt in range(4):
                            P.tr(pb[2][:, kt * 128:(kt + 1) * 128], CT[:, kt, vt * 128:(vt + 1) * 128], ident_f, r=["CT", "cst"], w=["pb2"])
                        P.copy("dve", Cout[:, :], pb[2][:, :], r=["pb2"], w=["Cout"])
                        P.dma("sp", o_pC[h, vt * 128:(vt + 1) * 128, :], Cout[:, :], r=["Cout"], dkey="o_Cout")
                    P.tr(pb[3][0:4, 0:128], nst[:, :], ident_f, r=["nst", "cst"], w=["pb3"])
                    P.copy("dve", ctout[:, :].bitcast(F32)[0:3, :], pb[3][0:3, 0:128], r=["pb3"], w=["ctout"]) if False else None
                    P.copy("dve", st6[0:4, 0:8].bitcast(F32), pb[3][0:4, 0:8], r=["pb3"], w=["st6"]) if False else None
                    P.copy("dve", Cout[0:4, 0:128], pb[3][0:4, 0:128], r=["pb3"], w=["Cout"])
                    P.dma("sp", o_pn[h].rearrange("(kt k) -> kt k", k=128), Cout[0:4, 0:128], r=["Cout"], dkey="o_pn")

                    if cfg.get("ml_nosample"):
                        continue
                    for jl in range(4):
                        jt = 4 * h + jl
                        P.copy("dve", sB[:, 0, :], xcs_all[:, jt, :], r=["xcs_all"], w=["sB"])
                        P.copy("dve", sB[:, 1, :], xms_all[:, jt, :], r=["xms_all"], w=["sB"])
                        js = slice(jl * 128, (jl + 1) * 128)
                        P.mm(pb[2][0:NS, js], sB[:, 0, :], Dm[:, 4 + jl, :], True, True, r=["sB", "Dm"], w=["pb2"])
                        P.mm(pb[3][0:NS, js], sB[:, 0, :], Dm[:, 0 + jl, :], True, True, r=["sB", "Dm"], w=["pb3"])
                        P.mm(pb[4][0:NS, js], sB[:, 1, :], Dm[:, 8 + jl, :], True, True, r=["sB", "Dm"], w=["pb4"])
                        P.mm(pb[5][:, jl * NS:(jl + 1) * NS], Dm[:, 12 + jl, :], sB[:, 1, :], True, True, r=["sB", "Dm"], w=["pb5"])
                        for kc in range(8):
                            P.mm(pb[5][:, 64 + jl * NS:64 + (jl + 1) * NS], wbuf[:, kc, 512 + jl * 128:512 + (jl + 1) * 128], xn[:, kc, SEQ:NT], kc == 0, kc == 7,
                                 r=["wBz", xnk(4)], w=["pb5"])
                        P.ts("dve", mrs[:, jl, :], pb[5][:, jl * NS:(jl + 1) * NS], bocol[:, jt:jt + 1], None, ALU.add, r=["pb5", "bocol"], w=["mrs"])
                        P.act(mrs[:, jl, :], mrs[:, jl, :], AF.Tanh, r=["mrs"], w=["mrs"], scale=0.5)
                        P.ts("dve", mrs[:, jl, :], mrs[:, jl, :], 0.5, 0.5, ALU.mult, ALU.add, r=["mrs"], w=["mrs"])
                        P.act(hs2[:, jl, :], pb[5][:, 64 + jl * NS:64 + (jl + 1) * NS], AF.Silu, r=["pb5"], w=["hs2"])
                    ktok, vwt, kmk, qmk = bobc[0:NS, :], otok[0:NS, :], vtok[0:NS, :], kwt[0:NS, :]
                    C0t, Cnt = [Asb, num], [Cout, Cnt1]
                    P.ts("dve", ktok, pb[2][0:NS, :], KS, None, ALU.mult, r=["pb2"], w=["bobc"])
                    P.copy("dve", qtok[:, :], pb[3][0:NS, :], r=["pb3"], w=["qtok"])
                    P.ts("dve", vwt, pb[4][0:NS, :], sg[:, 16 + h:17 + h], None, ALU.mult, r=["pb4", "sg"], w=["otok"])
                    P.dma("sp", n0t[:, :], st_n[:, h, :], w=["n0t"], dkey="n0t")
                    P.ts("dve", n0t[:, :], n0t[:, :], sg[:, 20 + h:21 + h], None, ALU.mult, r=["n0t", "sg"], w=["n0t"])
                    P.stt(n0t[:, :], ktok, sg[:, 16 + h:17 + h], n0t[:, :], ALU.mult, ALU.add, r=["bobc", "sg", "n0t"], w=["n0t"])
                    P.dma("sp", o_sn[:, h, :], n0t[:, :], r=["n0t"], dkey="o_sn")
                    P.S.add("dve", (lambda e, o=Cout[0:NS, :], a=n0t[:, :], b=qtok[:, :], ac=sg[:, 36:37]:
                                    e.scalar_tensor_tensor(out=o, in0=a, scalar=1.0, in1=b, op0=ALU.mult, op1=ALU.mult, accum_out=ac)),
                            ["n0t", "qtok", "Cout"], ["Cout", "sg36"])
                    P.stt(sg[:, 36:37], sg[:, 36:37], -1.0, sg[:, 36:37], ALU.mult, ALU.max, r=["sg36"], w=["sg36"])
                    P.tt("dve", sg[:, 36:37], sg[:, 36:37], sg[:, 24 + h:25 + h], ALU.max, r=["sg36", "sg"], w=["sg36"])
                    P.recip(sg[:, 36:37], sg[:, 36:37], r=["sg36"], w=["sg36"])
                    P.tt("dve", wid[:, :, 4:5], bc(sg[:, 36:37].rearrange("p (o h) -> p o h", o=1), [NS, NS, 1]),
                         bc(ident_f[0:NS, 0:NS].rearrange("p (b o) -> p b o", o=1), [NS, NS, 1]), ALU.mult, r=["sg36", "cst"], w=["wid"])
                    P.mm(pb[5][:, 128:256], ones_f[0:NS, :], wid[:, :, :].rearrange("p b x -> p (b x)"), True, True, r=["cst", "wid"], w=["pb5"])
                    P.copy("dve", wibc[:, :, :], pb[5][:, 128:256].rearrange("p (b x) -> p b x", x=8), r=["pb5"], w=["wibc"])
                    for b in range(NS):
                        P.ts("dve", kmk, ktok, ident_f[0:NS, b:b + 1], None, ALU.mult, r=["bobc", "cst"], w=["vtok"])
                        P.ts("dve", qmk, qtok[:, :], ident_f[0:NS, b:b + 1], None, ALU.mult, r=["qtok", "cst"], w=["kwt"])
                        P.mm(pb[7][:, :], ones_b[0:NS, :], qmk, True, True, r=["ones_b", "kwt"], w=["pb7"])
                        for vt in range(4):
                            i2 = (b * 4 + vt) % 2
                            c0, c0k = C0t[i2], ("Asb", "num")[i2]
                            cn, cnk = Cnt[i2], ("Cout", "Cnt1")[i2]
                            pu, puk = (pb[2], "pb2") if vt % 2 == 0 else (pb[3], "pb3")
                            P.dma("sp", c0[:, :], st_C[b, h, vt * 128:(vt + 1) * 128, :], w=[c0k], dkey=c0k)
                            P.mm(pu[:, :], vwt[:, vt * 128:(vt + 1) * 128], kmk, True, True, r=["otok", "vtok"], w=[puk])
                            P.stt(cn[:, :], c0[:, :], wibc[:, b, h:h + 1], pu[:, :], ALU.mult, ALU.add, r=[c0k, "wibc", puk], w=[cnk])
                            P.dma("pool", o_sC[b, h, vt * 128:(vt + 1) * 128, :], cn[:, :], r=[cnk], dkey="o_" + cnk)
                            P.S.add("dve", (lambda e, o=c0[:, :], a=cn[:, :], bb=pb[7][:, :], ac=numT[:, vt, b:b + 1]:
                                            e.scalar_tensor_tensor(out=o, in0=a, scalar=1.0, in1=bb, op0=ALU.mult, op1=ALU.mult, accum_out=ac)),
                                    [cnk, "pb7", c0k], [c0k, "numT"])
                    P.tt("dve", hsT[:, :, :], numT[:, :, :], bc(wibc[:, :, 4:5].rearrange("p b o -> p o b"), [128, 4, NS]), ALU.mult, r=["numT", "wibc"], w=["hsT"])
                    P.tt("dve", hsT[:, :, :], hsT[:, :, :], mrs[:, :, :], ALU.mult, r=["hsT", "mrs"], w=["hsT"])
                    P.tt("dve", numT[:, :, :], hsT[:, :, :], hsT[:, :, :], ALU.mult, r=["hsT"], w=["numT"])
                    for vt in range(4):
                        P.mm(pb[4][:, 0:NS], ones_f, hsT[:, vt, :], vt == 0, vt == 3, r=["cst", "hsT"], w=["pb4"])
                    for vt in range(4):
                        P.mm(pb[4][:, NS:2 * NS], ones_f, numT[:, vt, :], vt == 0, vt == 3, r=["cst", "numT"], w=["pb4"])
                    mean = sA[:, 0, :]
                    var = sA[:, 1, :]
                    P.ts("dve", mean, pb[4][:, 0:NS], 1.0 / 512, None, ALU.mult, r=["pb4"], w=["sA"])
                    P.ts("dve", var, pb[4][:, NS:2 * NS], 1.0 / 512, None, ALU.mult, r=["pb4"], w=["sA"])
                    P.tt("dve", sA[:, 3, :], mean, mean, ALU.mult, r=["sA"], w=["sA"])
                    P.tt("dve", var, var, sA[:, 3, :], ALU.subtract, r=["sA"], w=["sA"])
                    rstd_pool(var, var, 1.0, ["sA"], ["sA"])
                    P.tt("dve", hsT[:, :, :], hsT[:, :, :], bc(mean.rearrange("p (o b) -> p o b", o=1), [128, 4, NS]), ALU.subtract, r=["hsT", "sA"], w=["hsT"])
                    P.tt("dve", hsT[:, :, :], hsT[:, :, :], bc(var.rearrange("p (o b) -> p o b", o=1), [128, 4, NS]), ALU.mult, r=["hsT", "sA"], w=["hsT"])
                    for jl in range(4):
                        jt = 4 * h + jl
                        P.ts("dve", hsT[:, jl, :], hsT[:, jl, :], mhw[:, jt:jt + 1], None, ALU.mult, r=["hsT", "mhw"], w=["hsT"])
                        P.stt(hsT[:, jl, :], xcs_all[:, jt, :], skp[:, jt:jt + 1], hsT[:, jl, :], ALU.mult, ALU.add, r=["xcs_all", "skp", "hsT"], w=["hsT"])
                    P.tt("dve", gTs[:, :, :], hsT[:, :, :], hs2[:, :, :], ALU.mult, r=["hsT", "hs2"], w=["gTs"])
                    for dt_ in range(8):
                        po, pk = pb[dt_ % 2], "pb%d" % (dt_ % 2)
                        for jl in range(4):
                            P.mm(po[:, 0:NS], wo[:, jl, dt_ * 128:(dt_ + 1) * 128], gTs[:, jl, :], jl == 0, jl == 3, r=["wo", "gTs"], w=[pk])
                        P.tt("dve", xres[:, dt_, SEQ:NT], xres[:, dt_, SEQ:NT], po[:, 0:NS], ALU.add, r=[pk, xk(dt_, 4)], w=[xk(dt_, 4)])
                P.S.barrier()

        for li_ in (0, 1, 2, 3):
            if li_ in layers:
                if li_ in (0, 3):
                    layer_ssd(li_)
                elif li_ == 1:
                    layer_gmlp()
                else:
                    layer_mlstm()

        with ExitStack() as L:
            ystage = [P.sb(L, "ystage%d" % i, [128, D], F32) for i in range(2)]
            ysq = P.sb(L, "ysq", [128, D], F32)
            fnw_r = P.sb(L, "fnw_r", [128, D], F32)
            ss = P.sb(L, "ss", [128, 2], F32)
            if final_norm:
                P.dma("sp", fnw_r[:, :], bc(W["final_norm_w"].rearrange("(o n) -> o n", o=1), [128, D]), w=["fnw_r"], dkey="fnw_r")
            for c in range(17):
                if c < 16:
                    t0, M, ti = c * 128, 128, c // 4
                    dst = y_p[t0:t0 + 128, :]
                else:
                    t0, M, ti = SEQ, NS, 4
                    dst = y_s[:, :]
                yb = ystage[c % 2]
                yk = "ystage%d" % (c % 2)
                for h in range(2):
                    ps = pb[(2 * c + h) % 8]
                    pk = "pb%d" % ((2 * c + h) % 8)
                    for k4 in range(4):
                        kc = h * 4 + k4
                        P.tr(ps[:M, k4 * 128:(k4 + 1) * 128], xres[:, kc, t0:t0 + M], ident_f, r=[xk(kc, ti), "cst"], w=[pk])
                    P.copy("act" if h else "dve", yb[:M, h * 512:(h + 1) * 512], ps[:M, :], r=[pk], w=[yk + ".%d" % h])
                if final_norm:
                    P.act(ysq[:M, :], yb[:M, :], AF.Square, r=[yk + ".0", yk + ".1"], w=["ysq", "ss"], accum=ss[:M, 0:1])
                    P.act(ss[:M, 1:2], ss[:M, 0:1], AF.Sqrt, r=["ss", "epsc"], w=["ss"], bias=epsc[:M, 0:1], scale=1.0 / D)
                    P.recip(ss[:M, 1:2], ss[:M, 1:2], r=["ss"], w=["ss"])
                    P.stt(yb[:M, :], yb[:M, :], ss[:M, 1:2], fnw_r[:M, :], ALU.mult, ALU.mult,
                          r=[yk + ".0", yk + ".1", "ss", "fnw_r"], w=[yk + ".0", yk + ".1"])
                P.dma("sp", dst, yb[:M, :], r=[yk + ".0", yk + ".1"], dkey="o_" + yk)

        if cfg.get("resched", False):
            P.S.look = cfg.get("look", 24)
            P.S.reorder(sync_lat=cfg.get("sync_lat", 150))
        P.S.emit(root)
    return P


def make_consts():
    c = np.zeros((128, 7 * 128), np.float32)
    i = np.arange(128)
    c[:, 0:128] = np.eye(128, dtype=np.float32)
    c[:, 128:256] = 1.0
    c[:, 256:384] = (i[:, None] <= i[None, :]).astype(np.float32)
    c[:, 384:512] = (i[:, None] > i[None, :]).astype(np.float32)
    c[:, 512:640] = (i[None, :] < i[:, None]).astype(np.float32)
    c[:, 640:768] = (i[:, None] // 4 == i[None, :] // 4).astype(np.float32)
    c[:, 768:896] = (i[:, None] >= i[None, :]).astype(np.float32)
    return c


_CACHE = {}


def run(cfg, inputs, ncores=NCORES):
    key = repr(sorted(cfg.items()))
    if key not in _CACHE:
        _CACHE[key] = build(cfg)
    P = _CACHE[key]
    missing = [n for n in ALL_INPUT_NAMES if n not in inputs]
    assert not missing or len(cfg.get('layers', ())) < 4, missing
    consts = make_consts()
    in_maps = []
    for c in range(ncores):
        m = {}
        for name in P.dram:
            t = P.dram[name]
            if name in ("y_prompt", "y_sample") or name.startswith(("p0_", "s0_", "s1_", "p2_", "s2_", "p3_", "s3_")):
                continue
            if name == "consts":
                m[name] = consts
            elif name == "consts2":
                m[name] = (np.arange(32)[:, None] == (np.arange(2048)[None, :] // 64)).astype(np.float32)
            elif name == "x_prompt":
                m[name] = np.ascontiguousarray(inputs["x_prompt"][c])
            elif name == "x_sample":
                m[name] = np.ascontiguousarray(inputs["x_sample"][c * NS:(c + 1) * NS, 0, :])
            elif name.startswith("state_"):
                m[name] = np.ascontiguousarray(inputs[name][c * NS:(c + 1) * NS])
            else:
                m[name] = np.ascontiguousarray(inputs[name])
        in_maps.append(m)
    res = run_bass_kernel_spmd(P.nc, in_maps, core_ids=list(range(ncores)))
    return res.results


def kernel(**inputs):
    cfg = {"layers": (0, 1, 2, 3), "final_norm": True, "resched": True, "sync_lat": 1000}
    r = run(cfg, inputs)
    def pstack(name):
        return np.stack([r[c][name] for c in range(NCORES)], 0)
    def scat(name):
        return np.concatenate([r[c][name] for c in range(NCORES)], 0)
    outs = [pstack("y_prompt"), scat("y_sample")[:, None, :]]
    outs += [pstack("p0_ssm"), pstack("p0_conv"), scat("s0_ssm"), scat("s0_conv")]
    outs += [scat("s1_v")[:, None, :]]
    outs += [pstack("p2_C"), pstack("p2_n"), pstack("p2_m"), pstack("p2_conv"), scat("s2_C"), scat("s2_n"), scat("s2_m"), scat("s2_conv")]
    outs += [pstack("p3_ssm"), pstack("p3_conv"), scat("s3_ssm"), scat("s3_conv")]
    return tuple(np.ascontiguousarray(o, dtype=np.float32) for o in outs)
```

```python
import sys
import math
from contextlib import ExitStack
import numpy as np
import concourse.bass as bass
import concourse.mybir as mybir
from concourse.bass_utils import run_bass_kernel_spmd

F32 = mybir.dt.float32
BF16 = mybir.dt.bfloat16
AF = mybir.ActivationFunctionType
ALU = mybir.AluOpType
AX = mybir.AxisListType

ALL_INPUT_NAMES = (
    "x_prompt",
    "x_sample",
    "state_l0_ssm",
    "state_l0_conv",
    "state_l2_C",
    "state_l2_n",
    "state_l2_m",
    "state_l2_conv",
    "state_l3_ssm",
    "state_l3_conv",
    "l0_norm_w",
    "l0_in_proj",
    "l0_conv_w",
    "l0_conv_b",
    "l0_dt_bias",
    "l0_A_log",
    "l0_D_skip",
    "l0_gnorm_w",
    "l0_out_proj",
    "l1_norm_w",
    "l1_in_proj",
    "l1_v_ln_w",
    "l1_v_ln_b",
    "l1_spatial_w",
    "l1_spatial_b",
    "l1_out_proj",
    "l2_norm_w",
    "l2_in_proj",
    "l2_conv_w",
    "l2_conv_b",
    "l2_w_q",
    "l2_w_k",
    "l2_w_v",
    "l2_w_o",
    "l2_b_o",
    "l2_w_if",
    "l2_b_if",
    "l2_mh_norm_w",
    "l2_skip",
    "l2_out_proj",
    "l3_norm_w",
    "l3_in_proj",
    "l3_conv_w",
    "l3_conv_b",
    "l3_dt_bias",
    "l3_A_log",
    "l3_D_skip",
    "l3_gnorm_w",
    "l3_out_proj",
    "final_norm_w",
)

NCORES = 8
D = 1024
SEQ = 2048
NS = 16
NT = SEQ + NS
EPS = 1e-6
TT = [(0, 512), (512, 512), (1024, 512), (1536, 512), (2048, 16)]
NEG = -30000.0


class Op:
    __slots__ = ("eng", "fn", "deps", "dkey", "token", "hasdep", "idx", "where", "seg", "cost", "tab")


class Sched:
    ENGS = ("pe", "act", "dve", "pool", "sp")
    EPOCH = 12000

    def __init__(self, nc):
        self.nc = nc
        self.ops = []
        self.lw = {}
        self.rd = {}
        self.last_dma = {}
        self.last_on = {}
        self.psum_rd = {}
        self.seg = 0
        self.keep_all_readers = False
        self.look = 24
        self.use_blev = False
        self.verbose = False

    def add(self, eng, fn, reads=(), writes=(), dkey=None, cost=None, tab=None):
        op = Op()
        op.eng, op.fn, op.dkey, op.hasdep, op.token = eng, fn, dkey, False, None
        op.idx = len(self.ops)
        op.tab = tab
        op.cost = cost if cost is not None else {"pe": 200, "act": 550, "dve": 450, "pool": 650, "sp": 100}.get(eng, 300)
        op.seg = self.seg
        f = sys._getframe(1)
        wl = []
        while f is not None and len(wl) < 4:
            wl.append(f.f_lineno)
            f = f.f_back
        op.where = wl
        deps = {}
        for k in reads:
            w = self.lw.get(k)
            if w is not None:
                deps[w.idx] = w
            if k.startswith("pb") and eng in ("act", "dve"):
                bank = k.split(".")[0]
                lst_ = self.psum_rd.setdefault(bank, [])
                for r_ in lst_:
                    if r_.eng != eng:
                        deps[r_.idx] = r_
                lst_.append(op)
        for k in writes:
            w = self.lw.get(k)
            if w is not None:
                deps[w.idx] = w
            for r in self.rd.get(k, ()):
                deps[r.idx] = r
        op.deps = list(deps.values())
        for d in op.deps:
            if not (d.eng == "pe" and eng == "pe" and d.dkey is None and dkey is None):
                d.hasdep = True
        for k in reads:
            lst = self.rd.setdefault(k, [])
            if dkey is None and not self.keep_all_readers:
                lst[:] = [r for r in lst if r.dkey is not None or r.eng != eng]
            lst.append(op)
        for k in writes:
            self.lw[k] = op
            self.rd[k] = []
            if k.startswith("pb"):
                self.psum_rd[k.split(".")[0]] = []
        self.ops.append(op)
        if dkey is not None:
            self.last_dma[dkey] = op
        else:
            self.last_on[eng] = op
        return op

    def barrier(self):
        prev = list(self.last_on.values()) + list(self.last_dma.values())
        for e in self.ENGS:
            op = self.add(e, None)
            op.deps = list(prev)
            for d in prev:
                d.hasdep = True
        self.lw.clear()
        self.rd.clear()
        self.seg += 1

    def reorder(self, sync_lat=150, dma_lat=2500):
        import heapq
        out = []
        n = len(self.ops)
        i = 0
        while i < n:
            j = i
            seg = self.ops[i].seg
            while j < n and self.ops[j].seg == seg:
                j += 1
            ops = self.ops[i:j]
            body = [o for o in ops if o.fn is not None]
            bars = [o for o in ops if o.fn is None]
            inseg = {id(o) for o in body}
            succ = {id(o): [] for o in body}
            ndep = {}
            for o in body:
                ds = [d for d in o.deps if id(d) in inseg]
                ndep[id(o)] = len(ds)
                for d in ds:
                    succ[id(d)].append(o)
            blev = {}
            for o in reversed(body):
                m_ = 0
                for s_ in succ[id(o)]:
                    v_ = blev[id(s_)] + (0 if (s_.eng == o.eng == "pe") else sync_lat)
                    if v_ > m_:
                        m_ = v_
                blev[id(o)] = m_ + (dma_lat if o.dkey is not None else o.cost)
            use_blev = self.use_blev
            finish = {}
            eng_free = {}
            heaps = {}
            ready_t = {}

            def push(o):
                rt = 0
                for d in o.deps:
                    if id(d) in finish:
                        lat = 0 if (d.eng == o.eng == "pe" and d.dkey is None) else sync_lat
                        rt = max(rt, finish[id(d)] + lat)
                ready_t[id(o)] = rt
                heapq.heappush(heaps.setdefault(o.eng, []), (o.idx, id(o), o))

            for o in body:
                if ndep[id(o)] == 0:
                    push(o)
            order = []
            LOOK = self.look
            cur_tab = [None]
            TABSW = 1300

            def tpen(o):
                return TABSW if (o.eng == "act" and o.tab is not None and o.tab != cur_tab[0]) else 0
            while True:
                best = None
                for eng, h in heaps.items():
                    if not h:
                        continue
                    ef = eng_free.get(eng, 0)
                    cands = heapq.nsmallest(LOOK, h)
                    if use_blev:
                        c = min(cands, key=lambda t: (max(ef, ready_t[t[1]]) + tpen(t[2]), -blev[t[1]], t[0]))
                    else:
                        c = min(cands, key=lambda t: (max(ef, ready_t[t[1]]) + tpen(t[2]), t[0]))
                    st = max(ef, ready_t[c[1]]) + tpen(c[2])
                    if best is None or (st, c[0]) < (best[0], best[1][0]):
                        best = (st, c, eng)
                if best is None:
                    break
                st, c, eng = best
                heaps[eng].remove(c)
                heapq.heapify(heaps[eng])
                o = c[2]
                if o.eng == "act" and o.tab is not None:
                    cur_tab[0] = o.tab
                if o.dkey is not None:
                    eng_free[eng] = st + o.cost
                    finish[id(o)] = st + dma_lat
                else:
                    eng_free[eng] = st + o.cost
                    finish[id(o)] = st + o.cost
                order.append((st, o))
                for s_ in succ[id(o)]:
                    ndep[id(s_)] -= 1
                    if ndep[id(s_)] == 0:
                        push(s_)
            assert len(order) == len(body), (len(order), len(body))
            if self.verbose and order:
                busy = {}
                for st_, o_ in order:
                    busy[o_.eng] = busy.get(o_.eng, 0) + o_.cost
                print("[sched] seg %d ops %d makespan %.0f us busy %s" % (seg, len(body), max(finish.values()) / 1000.0,
                      {k_: round(v_ / 1000.0) for k_, v_ in busy.items()}))
            order.sort(key=lambda t: (t[0], t[1].idx))
            out.extend(o for _, o in order)
            out.extend(bars)
            i = j
        self.ops = out
        for k, o in enumerate(self.ops):
            o.idx = k

    def emit(self, stack):
        nc = self.nc
        cnt = {}
        sems = {}

        def sem_for(key):
            if key not in sems:
                sems[key] = stack.enter_context(nc.semaphore("s%d" % len(sems)))
            return sems[key]

        slot_of = {}
        cur_seg = -1
        for op in self.ops:
            if op.seg != cur_seg:
                cur_seg = op.seg
                slot_of = {}
            if op.dkey is not None:
                sk_ = (op.eng, op.dkey)
                if sk_ not in slot_of:
                    slot_of[sk_] = (op.eng, sum(1 for x in slot_of if x[0] == op.eng))
                k = ("dma", slot_of[sk_])
                cnt[k] = cnt.get(k, 0) + 16
                op.token = (k, cnt[k])
            elif op.hasdep:
                c = cnt.get(op.eng, 0) + 1
                cnt[op.eng] = c
                ep = (c - 1) // self.EPOCH
                op.token = ((op.eng, ep), c - ep * self.EPOCH)
        for op in self.ops:
            if op.token is not None:
                sem_for(op.token[0])
        final = [v.token for k, v in self.last_dma.items()]
        block = stack.enter_context(nc.Block())
        handles = {"pe": nc.tensor, "act": nc.scalar, "dve": nc.vector, "pool": nc.gpsimd, "sp": nc.sync}
        deco = {"pe": block.tensor, "act": block.scalar, "dve": block.vector, "pool": block.gpsimd, "sp": block.sync}
        for eng in self.ENGS:
            myops = [o for o in self.ops if o.eng == eng]

            def body(e, myops=myops, eng=eng):
                waited = {}
                for op in myops:
                    need = {}
                    for d in op.deps:
                        if d.token is None:
                            continue
                        if d.dkey is None and d.eng == eng == "pe":
                            continue
                        sk, val = d.token
                        if val > need.get(sk, 0):
                            need[sk] = val
                    for sk, val in need.items():
                        if waited.get(sk, 0) >= val:
                            continue
                        waited[sk] = val
                        e.wait_ge(sems[sk], val)
                    if op.fn is None:
                        if op.token is not None:
                            e.nop().then_inc(sems[op.token[0]], 1)
                        continue
                    try:
                        inst = op.fn(e)
                    except Exception:
                        print('EMIT FAILED for op added at lines', op.where, 'eng', op.eng)
                        raise
                    if op.token is not None:
                        inst.then_inc(sems[op.token[0]], 16 if op.dkey is not None else 1)
                if eng == "sp":
                    for sk, val in final:
                        e.wait_ge(sems[sk], val)

            deco[eng](body)
        self.nsems = len(sems)


class Prog:
    def __init__(self, cfg):
        self.cfg = cfg
        self.nc = bass.Bass("TRN2", target_bir_lowering=False)
        self.S = Sched(self.nc)
        self.dram = {}
        self.uid = 0

    def din(self, name, shape):
        t = self.nc.dram_tensor(name, list(shape), F32, kind="ExternalInput")
        self.dram[name] = t
        return t.ap()

    def dout(self, name, shape):
        t = self.nc.dram_tensor(name, list(shape), F32, kind="ExternalOutput")
        self.dram[name] = t
        return t.ap()

    def sb(self, stack, name, shape, dt):
        self.uid += 1
        return stack.enter_context(self.nc.sbuf_tensor("%s_%d" % (name, self.uid), list(shape), dt))

    def key(self, base):
        self.uid += 1
        return "%s#%d" % (base, self.uid)

    def op(self, eng, fn, r=(), w=()):
        return self.S.add(eng, fn, r, w)

    def dma(self, q, out, in_, r=(), w=(), dkey=None, nonc=False):
        if nonc:
            fn = lambda e: e.dma_start(out=out, in_=in_, allow_slow_non_contiguous=True)
        else:
            fn = lambda e: e.dma_start(out=out, in_=in_)
        return self.S.add(q, fn, r, w, dkey=dkey)

    def mm(self, out, lhsT, rhs, start, stop, r=(), w=()):
        nfree = int(np.prod(rhs.shape[1:]))
        cost = max(64, nfree) / 1.2 * (2 if rhs.dtype == F32 else 1) + 20
        return self.S.add("pe", lambda e: e.matmul(out, lhsT=lhsT, rhs=rhs, start=start, stop=stop), r, w, cost=cost)

    def tr(self, out, in_, ident, r=(), w=()):
        return self.S.add("pe", lambda e: e.transpose(out, in_, ident), r, w, cost=130)

    def act(self, out, in_, func, r=(), w=(), bias=None, scale=None, accum=None):
        kw = {}
        if bias is not None:
            kw["bias"] = bias
        if scale is not None:
            kw["scale"] = scale
        if accum is not None:
            kw["accum_out"] = accum
        tab = {AF.Silu: "silu", AF.Exp: "exp", AF.Tanh: "exp", AF.Ln: "ln", AF.Sqrt: "sqrt", AF.Sigmoid: "sig"}.get(func)
        return self.S.add("act", lambda e: e.activation(out=out, in_=in_, func=func, **kw), r, w,
                          cost=300 + 0.7 * int(np.prod(out.shape[1:])), tab=tab)

    def tt(self, eng, out, in0, in1, op, r=(), w=()):
        return self.S.add(eng, lambda e: e.tensor_tensor(out=out, in0=in0, in1=in1, op=op), r, w, cost=(220 if eng == "dve" else 400) + 0.9 * int(np.prod(out.shape[1:])))

    def ts(self, eng, out, in0, s1, s2, op0, op1=None, r=(), w=()):
        if op1 is None:
            return self.S.add(eng, lambda e: e.tensor_scalar(out=out, in0=in0, scalar1=s1, scalar2=None, op0=op0), r, w, cost=(220 if eng == "dve" else 400) + 0.6 * int(np.prod(out.shape[1:])))
        return self.S.add(eng, lambda e: e.tensor_scalar(out=out, in0=in0, scalar1=s1, scalar2=s2, op0=op0, op1=op1), r, w, cost=(220 if eng == "dve" else 400) + 0.6 * int(np.prod(out.shape[1:])))

    def stt(self, out, in0, scalar, in1, op0, op1, r=(), w=()):
        return self.S.add("dve", lambda e: e.scalar_tensor_tensor(out=out, in0=in0, scalar=scalar, in1=in1, op0=op0, op1=op1), r, w, cost=220 + 1.0 * int(np.prod(out.shape[1:])))

    def copy(self, eng, out, in_, r=(), w=()):
        if eng == "act":
            return self.S.add("act", lambda e: e.activation(out=out, in_=in_, func=AF.Copy), r, w)
        return self.S.add(eng, lambda e: e.tensor_copy(out=out, in_=in_), r, w)

    def recip(self, out, in_, r=(), w=()):
        return self.S.add("dve", lambda e: e.reciprocal(out=out, in_=in_), r, w)

    def memset(self, eng, ap, val, r=(), w=()):
        return self.S.add(eng, lambda e: e.memset(ap, val), r, w)


def bc(ap, shape):
    return ap.to_broadcast(list(shape))


def build(cfg):
    P = Prog(cfg)
    nc = P.nc
    P.S.keep_all_readers = bool(cfg.get("resched", False))
    layers = cfg.get("layers", [0, 1, 2, 3])
    final_norm = cfg.get("final_norm", True)

    x_p = P.din("x_prompt", [SEQ, D])
    x_s = P.din("x_sample", [NS, D])
    consts_d = P.din("consts", [128, 7 * 128])
    W = {}
    wshapes = {
        "final_norm_w": [D],
        "l1_norm_w": [D], "l1_in_proj": [D, 6144], "l1_v_ln_w": [2048], "l1_v_ln_b": [2048],
        "l1_spatial_w": [8, 128, 128], "l1_spatial_b": [8, 128], "l1_out_proj": [2048, D],
    }
    for li_ in (0, 3):
        p_ = "l%d_" % li_
        wshapes.update({p_ + "norm_w": [D], p_ + "in_proj": [D, 6176], p_ + "conv_w": [4, 4096], p_ + "conv_b": [4096],
                        p_ + "dt_bias": [32], p_ + "A_log": [32], p_ + "D_skip": [32], p_ + "gnorm_w": [2048], p_ + "out_proj": [2048, D]})
    for k, s in wshapes.items():
        if k == "final_norm_w" or int(k[1]) in layers:
            W[k] = P.din(k, s)
    DR = {}
    if 2 in layers:
        for k_, s_ in {"l2_norm_w": [D], "l2_in_proj": [D, 4096], "l2_conv_w": [4, 2048], "l2_conv_b": [2048], "l2_w_q": [512, 4, 4],
                       "l2_w_k": [512, 4, 4], "l2_w_v": [512, 4, 4], "l2_w_o": [512, 4, 4], "l2_b_o": [2048], "l2_w_if": [6144, 8],
                       "l2_b_if": [8], "l2_mh_norm_w": [2048], "l2_skip": [2048], "l2_out_proj": [2048, D]}.items():
            W[k_] = P.din(k_, s_)
        for k_, s_ in {"state_l2_C": [NS, 4, 512, 512], "state_l2_n": [NS, 4, 512], "state_l2_m": [NS, 4], "state_l2_conv": [NS, 3, 2048]}.items():
            DR[k_] = P.din(k_, s_)
        for k_, s_ in {"p2_C": [4, 512, 512], "p2_n": [4, 512], "p2_m": [4], "p2_conv": [3, 2048],
                       "s2_C": [NS, 4, 512, 512], "s2_n": [NS, 4, 512], "s2_m": [NS, 4], "s2_conv": [NS, 3, 2048]}.items():
            DR[k_] = P.dout(k_, s_)
    consts2_d = P.din("consts2", [32, 2048])
    for li_ in (0, 3):
        if li_ in layers:
            DR["state_l%d_ssm" % li_] = P.din("state_l%d_ssm" % li_, [NS, 32, 64, 128])
            DR["state_l%d_conv" % li_] = P.din("state_l%d_conv" % li_, [NS, 3, 4096])
            DR["p%d_ssm" % li_] = P.dout("p%d_ssm" % li_, [32, 64, 128])
            DR["p%d_conv" % li_] = P.dout("p%d_conv" % li_, [3, 4096])
            DR["s%d_ssm" % li_] = P.dout("s%d_ssm" % li_, [NS, 32, 64, 128])
            DR["s%d_conv" % li_] = P.dout("s%d_conv" % li_, [NS, 3, 4096])
    y_p = P.dout("y_prompt", [SEQ, D])
    y_s = P.dout("y_sample", [NS, D])
    s1_v = P.dout("s1_v", [NS, 2048]) if 1 in layers else None

    root = ExitStack()
    with root:
        G = root
        xres = P.sb(G, "xres", [128, 8, NT], F32)
        xn = P.sb(G, "xn", [128, 8, NT], BF16)
        cst = P.sb(G, "cst", [128, 7 * 128], F32)
        ident_b = P.sb(G, "ident_b", [128, 128], BF16)
        ones_b = P.sb(G, "ones_b", [128, 128], BF16)
        nw = P.sb(G, "nw", [128, 5, 8], F32)
        rs_s = P.sb(G, "rs_s", [128, 512], F32)
        rs_r = P.sb(G, "rs_r", [128, 512], F32)
        sq = [P.sb(G, "sq%d" % i, [128, 512], BF16) for i in range(2)]
        pb = [G.enter_context(nc.psum_tensor("pb%d" % i, [128, 512], F32)) for i in range(8)]
        ident_f = cst[:, 0:128]

        def xk(kc, ti):
            return "xres.%d.%d" % (kc, ti)

        def xnk(ti):
            return "xn.%d" % ti

        P.dma("sp", cst[:, :], consts_d[:, :], w=["cst"], dkey="cst")
        P.copy("dve", ident_b[:, :], cst[:, 0:128], r=["cst"], w=["ident_b"])
        P.copy("dve", ones_b[:, :], cst[:, 128:256], r=["cst"], w=["ones_b"])
        nwn = {0: "l0_norm_w", 1: "l1_norm_w", 2: "l2_norm_w", 3: "l3_norm_w", 4: "final_norm_w"}
        for li in list(layers) + [4]:
            if nwn[li] in W:
                P.dma("sp", nw[:, li, :], W[nwn[li]].rearrange("(k p) -> p k", p=128), w=["nw%d" % li],
                      dkey="nw%d" % li, nonc=True)

        with ExitStack() as L:
            xin = [P.sb(L, "xin%d" % i, [128, 4, D], F32) for i in range(2)]
            xsin = P.sb(L, "xsin", [NS, D], F32)
            for ti in range(4):
                buf = xin[ti % 2]
                bk = "xin%d" % (ti % 2)
                P.dma("sp", buf[:, :, :], x_p[ti * 512:(ti + 1) * 512, :].rearrange("(c p) d -> p c d", p=128),
                      w=[bk], dkey=bk)
                for kc in range(8):
                    ps = pb[kc % 8]
                    pk = "pb%d" % (kc % 8)
                    for c in range(4):
                        P.tr(ps[:, c * 128:(c + 1) * 128], buf[:, c, kc * 128:(kc + 1) * 128], ident_f,
                             r=[bk, "cst"], w=[pk])
                    P.copy("act" if kc % 2 else "dve", xres[:, kc, ti * 512:(ti + 1) * 512], ps[:, :], r=[pk], w=[xk(kc, ti)])
            P.dma("sp", xsin[:, :], x_s[:, :], w=["xsin"], dkey="xsin")
            for kc in range(8):
                P.tr(pb[0][:, kc * 16:(kc + 1) * 16], xsin[:, kc * 128:(kc + 1) * 128], ident_f[0:NS, 0:NS],
                     r=["xsin", "cst"], w=["pb0"])
            P.copy("dve", xres[:, :, SEQ:NT], pb[0][:, 0:128].rearrange("p (k t) -> p k t", k=8),
                   r=["pb0"], w=[xk(kc, 4) for kc in range(8)])
            P.S.barrier()

        def emit_norm(li):
            for ti, (t0, tn) in enumerate(TT):
                for kc in range(8):
                    P.act(sq[kc % 2][:, :tn], xres[:, kc, t0:t0 + tn], AF.Square, r=[xk(kc, ti)], w=["sq%d" % (kc % 2)])
                    P.mm(pb[7][:, :tn], ones_b[:, :], sq[kc % 2][:, :tn], kc == 0, kc == 7,
                         r=["sq%d" % (kc % 2), "ones_b"], w=["pb7"])
                P.act(rs_s[:, :tn], pb[7][:, :tn], AF.Sqrt, r=["pb7", "epsc"], w=["rs_s"], bias=epsc[:, 0:1], scale=1.0 / D)
                P.recip(rs_r[:, :tn], rs_s[:, :tn], r=["rs_s"], w=["rs_r"])
                for kc in range(8):
                    P.stt(xn[:, kc, t0:t0 + tn], xres[:, kc, t0:t0 + tn], nw[:, li, kc:kc + 1], rs_r[:, :tn],
                          ALU.mult, ALU.mult, r=[xk(kc, ti), "rs_r", "nw%d" % li], w=[xnk(ti)])

        epsc = P.sb(G, "epsc", [128, 2], F32)
        P.memset("dve", epsc[:, 0:1], EPS, w=["epsc"])
        P.memset("dve", epsc[:, 1:2], -0.5, w=["epsc"])

        def rstd_pool(out, in_, scale, r, w):
            P.ts("pool", out, in_, scale, EPS, ALU.mult, ALU.add, r=r, w=w)
            npart, nfree = out.shape[0], out.shape[1]
            P.tt("pool", out, out, bc(epsc[0:npart, 1:2], [npart, nfree]), ALU.pow, r=list(w) + ["epsc"], w=w)

        def layer_gmlp():
            li = 1
            emit_norm(li)
            Win = W["l1_in_proj"].rearrange("(k p) j -> p k j", p=128)
            Wout = W["l1_out_proj"].rearrange("(k p) j -> p k j", p=128)
            with ExitStack() as L:
                vtok = P.sb(L, "vtok", [128, 9, 2048], BF16)
                wv = [P.sb(L, "wv%d" % i, [128, 8, 512], BF16) for i in range(2)]
                wuz = wv
                wo = [P.sb(L, "wo%d" % i, [128, 2, D], BF16) for i in range(2)]
                stats = P.sb(L, "stats", [128, 9, 4, 6], F32)
                mv = P.sb(L, "mv", [128, 9, 2], F32)
                rstd = P.sb(L, "rstd", [128, 9], F32)
                nmr = P.sb(L, "nmr", [128, 9], F32)
                wsT = P.sb(L, "wsT", [128, 8, 128], BF16)
                lnw = P.sb(L, "lnw", [128, 16], F32)
                lnb = P.sb(L, "lnb", [128, 16], F32)
                Ec = P.sb(L, "Ec", [128, 2, 128], F32)
                rsb = P.sb(L, "rsb", [128, 8, 128], F32)
                sbb = P.sb(L, "sbb", [128, 8, 128], F32)
                lnr = P.sb(L, "lnr", [NS, 2048], F32)
                ws00 = P.sb(L, "ws00", [NS, 8], F32)
                sb00 = P.sb(L, "sb00", [NS, 8], F32)
                vns = P.sb(L, "vns", [NS, 2048], F32)
                mxsT = P.sb(L, "mxsT", [128, 16, NS], F32)
                sz = [P.sb(L, "sz%d" % i, [128, 512], BF16) for i in range(2)]
                mx = [P.sb(L, "mx%d" % i, [128, 512], F32) for i in range(2)]
                gt = [P.sb(L, "gt%d" % i, [128, 2, 512], BF16) for i in range(2)]
                onesw = P.sb(L, "onesw", [128, 128], BF16)

                P.dma("sp", lnw[:, :], W["l1_v_ln_w"].rearrange("(k p) -> p k", p=128), w=["lnw"], dkey="lnw", nonc=True)
                P.dma("sp", lnb[:, :], W["l1_v_ln_b"].rearrange("(k p) -> p k", p=128), w=["lnb"], dkey="lnb", nonc=True)
                P.dma("sp", ws00[:, :], bc(W["l1_spatial_w"][:, 0:1, 0:1].rearrange("g a b -> (a b) g"), [NS, 8]),
                      w=["ws00"], dkey="ws00", nonc=True)
                P.dma("sp", sb00[:, :], bc(W["l1_spatial_b"][:, 0:1].rearrange("g a -> a g"), [NS, 8]),
                      w=["sb00"], dkey="sb00", nonc=True)
                for h2 in range(2):
                    P.dma("sp", mx[h2][:, :].rearrange("p (g s) -> p g s", g=4),
                          W["l1_spatial_w"][h2 * 4:(h2 + 1) * 4].rearrange("g t s -> t g s"), w=["mx%d" % h2], dkey="wsl%d" % h2)
                P.dma("sp", sbb[:, :, :], bc(W["l1_spatial_b"].rearrange("(o g) t -> o g t", o=1), [128, 8, 128]),
                      w=["sbb"], dkey="sbb")
                causal = cst[:, 256:384]
                for g in range(8):
                    ps = pb[g % 2]
                    pk = "pb%d" % (g % 2)
                    P.tr(ps[:, 0:128], mx[g // 4][:, (g % 4) * 128:(g % 4 + 1) * 128], ident_f, r=["mx%d" % (g // 4), "cst"], w=[pk])
                    P.tt("dve", wsT[:, g, :], ps[:, 0:128], causal, ALU.mult, r=[pk, "cst"], w=["wsT"])
                P.copy("dve", onesw[:, :], cst[:, 128:256], r=["cst"], w=["onesw"])
                for g2 in range(2):
                    P.mm(pb[2 + g2][:, :], onesw[:, :], wsT[:, g2 * 4:(g2 + 1) * 4, :], True, True,
                         r=["onesw", "wsT"], w=["pb%d" % (2 + g2)])
                    P.copy("dve", rsb[:, g2 * 4:(g2 + 1) * 4, :], pb[2 + g2][:, :].rearrange("p (g t) -> p g t", g=4),
                           r=["pb%d" % (2 + g2)], w=["rsb"])

                wq = 0
                stg = cfg.get('gm_stage', 9)
                for half in range(cfg.get('gm_halves', 2)):
                    chunks = list(range(8 * half, 8 * half + 8))
                    ttiles = [2 * half, 2 * half + 1] + ([4] if half == 1 else [])
                    nslot = 8 + (1 if half == 1 else 0)
                    bank = 0
                    for ct in range(cfg.get('v_nct', 4) if stg >= 1 else 0):
                        wb = wv[ct % 2]
                        wk = "wv%d" % (ct % 2)
                        P.dma("pool", wb[:, :, :], Win[:, :, 2048 + ct * 512: 2048 + (ct + 1) * 512], w=[wk + "u", wk + "z"], dkey=wk + "u")
                        for sl in range(cfg.get("v_nsl", nslot)):
                            if sl < 8:
                                c = chunks[sl]
                                t0, M, ti = c * 128, 128, c // 4
                            else:
                                t0, M, ti = SEQ, NS, 4
                            ps = pb[bank % 7]
                            pk = "pb%d" % (bank % 7)
                            bank += 1
                            for kc in range(8):
                                P.mm(ps[:M, :], xn[:, kc, t0:t0 + M], wb[:, kc, :], kc == 0, kc == 7,
                                     r=[xnk(ti), wk + "u", wk + "z"], w=[pk])
                            if cfg.get("v_statcopy"):
                                P.copy("dve", mx[0][:M, :], ps[:M, :], r=[pk], w=["mx0"])
                            elif not cfg.get("v_nostats"):
                                P.S.add("dve", (lambda e, o=stats[:M, sl, ct, :], i=ps[:M, :]: e.bn_stats(out=o, in_=i)),
                                        [pk], ["stats.%d" % sl])
                            if not cfg.get("v_nocopy"):
                                P.copy(cfg.get("v_copyeng", "act"), vtok[:M, sl, ct * 512:(ct + 1) * 512], ps[:M, :], r=[pk] + (["stats.%d" % sl] if cfg.get("v_serial") else []), w=["vtok.%d" % sl])
                    for sl in range(nslot if stg >= 2 else 0):
                        M = 128 if sl < 8 else NS
                        P.S.add("dve", (lambda e, o=mv[:M, sl, :], i=stats[:M, sl, :, :]: e.bn_aggr(out=o, in_=i)),
                                ["stats.%d" % sl], ["mv.%d" % sl])
                        P.act(rstd[:M, sl:sl + 1], mv[:M, sl, 1:2], AF.Sqrt, r=["mv.%d" % sl, "epsc"], w=["rstd.%d" % sl],
                              bias=epsc[:M, 0:1], scale=1.0)
                        P.recip(rstd[:M, sl:sl + 1], rstd[:M, sl:sl + 1], r=["rstd.%d" % sl], w=["rstd.%d" % sl])
                        P.stt(nmr[:M, sl:sl + 1], mv[:M, sl, 0:1], -1.0, rstd[:M, sl:sl + 1], ALU.mult, ALU.mult,
                              r=["mv.%d" % sl, "rstd.%d" % sl], w=["nmr.%d" % sl])
                        if sl < 8:
                            P.ts("dve", vtok[:M, sl, :], vtok[:M, sl, :], rstd[:M, sl:sl + 1], nmr[:M, sl:sl + 1],
                                 ALU.mult, ALU.add, r=["vtok.%d" % sl, "rstd.%d" % sl, "nmr.%d" % sl], w=["vtok.%d" % sl])
                        else:
                            P.ts("dve", vns[:, :], vtok[:M, sl, :], rstd[:M, sl:sl + 1], nmr[:M, sl:sl + 1],
                                 ALU.mult, ALU.add, r=["vtok.%d" % sl, "rstd.%d" % sl, "nmr.%d" % sl], w=["vns"])
                            P.dma("sp", lnr[:, :], bc(W["l1_v_ln_w"].rearrange("(o n) -> o n", o=1), [NS, 2048]), w=["lnr"], dkey="lnr")
                            P.tt("dve", vns[:, :], vns[:, :], lnr[:, :], ALU.mult, r=["vns", "lnr"], w=["vns"])
                            P.dma("sp", lnr[:, :], bc(W["l1_v_ln_b"].rearrange("(o n) -> o n", o=1), [NS, 2048]), w=["lnr"], dkey="lnr")
                            P.tt("dve", vns[:, :], vns[:, :], lnr[:, :], ALU.add, r=["vns", "lnr"], w=["vns"])
                            P.dma("sp", s1_v[:, :], vns[:, :], r=["vns"], dkey="o_s1v")
                            mxs = lnr
                            P.tt("dve", mxs[:, :].rearrange("p (g d) -> p g d", g=8), vns[:, :].rearrange("p (g d) -> p g d", g=8),
                                 bc(ws00[:, :].rearrange("p (g o) -> p g o", o=1), [NS, 8, 256]), ALU.mult,
                                 r=["vns", "ws00"], w=["lnr"])
                            P.tt("dve", mxs[:, :].rearrange("p (g d) -> p g d", g=8), mxs[:, :].rearrange("p (g d) -> p g d", g=8),
                                 bc(sb00[:, :].rearrange("p (g o) -> p g o", o=1), [NS, 8, 256]), ALU.add,
                                 r=["lnr", "sb00"], w=["lnr"])
                            for jt in range(16):
                                P.tr(pb[7][:, jt * NS:(jt + 1) * NS], mxs[:, jt * 128:(jt + 1) * 128], ident_f[0:NS, 0:NS],
                                     r=["lnr", "cst"], w=["pb7"])
                            P.copy("dve", mxsT[:, :, :], pb[7][:, 0:16 * NS].rearrange("p (j t) -> p j t", j=16),
                                   r=["pb7"], w=["mxsT"])
                    def load_g(g_, slot_):
                        wb_, wk_ = wuz[slot_], "wv%d" % slot_
                        ob_, ok_ = wo[slot_], "wo%d" % slot_
                        P.dma("pool", wb_[:, :, 0:256], Win[:, :, g_ * 256:(g_ + 1) * 256], w=[wk_ + "u"], dkey=wk_ + "u")
                        P.dma("pool", wb_[:, :, 256:512], Win[:, :, 4096 + g_ * 256: 4096 + (g_ + 1) * 256], w=[wk_ + "z"], dkey=wk_ + "z")
                        P.dma("pool", ob_[:, :, :], Wout[:, 2 * g_:2 * g_ + 2, :], w=[ok_], dkey=ok_)
                    if stg >= 3:
                        load_g(0, wq % 2)
                    for g in range(8 if stg >= 3 else 0):
                        slot = wq % 2
                        wq += 1
                        wb, wk = wuz[slot], "wv%d" % slot
                        ob, ok = wo[slot], "wo%d" % slot
                        if g < 7:
                            load_g(g + 1, wq % 2)
                        for ti in ttiles:
                            t0, tn = TT[ti]
                            gb = gt[ti % 2]
                            gk = "gt%d" % (ti % 2)
                            for jl in range(2):
                                jt = 2 * g + jl
                                o = 3 * jl
                                pu, pz, pm = pb[o], pb[o + 1], pb[o + 2]
                                ku, kz, km = "pb%d" % o, "pb%d" % (o + 1), "pb%d" % (o + 2)
                                for kc in range(8):
                                    P.mm(pu[:, :tn], wb[:, kc, jl * 128:(jl + 1) * 128], xn[:, kc, t0:t0 + tn], kc == 0, kc == 7,
                                         r=[wk + "u", xnk(ti)], w=[ku])
                                for kc in range(8):
                                    P.mm(pz[:, :tn], wb[:, kc, 256 + jl * 128:256 + (jl + 1) * 128], xn[:, kc, t0:t0 + tn],
                                         kc == 0, kc == 7, r=[wk + "z", xnk(ti)], w=[kz])
                                s_ = sz[jl]
                                sk_ = "sz%d" % jl
                                m_ = mx[jl]
                                mk_ = "mx%d" % jl
                                P.act(s_[:, :tn], pz[:, :tn], AF.Silu, r=[kz], w=[sk_])
                                if ti < 4:
                                    for c4 in range(4):
                                        c = ti * 4 + c4
                                        sl = c - 8 * half
                                        P.mm(pm[:, c4 * 128:(c4 + 1) * 128], vtok[:, sl, jt * 128:(jt + 1) * 128], wsT[:, g, :],
                                             True, True, r=["vtok.%d" % sl, "wsT"], w=[km])
                                    P.stt(Ec[:, jl, :], rsb[:, g, :], lnb[:, jt:jt + 1], sbb[:, g, :], ALU.mult, ALU.add,
                                          r=["rsb", "lnb", "sbb"], w=["Ec%d" % jl])
                                    P.stt(m_[:, :].rearrange("p (c t) -> p c t", c=4), pm[:, :].rearrange("p (c t) -> p c t", c=4),
                                          lnw[:, jt:jt + 1], bc(Ec[:, jl:jl + 1, :], [128, 4, 128]), ALU.mult, ALU.add,
                                          r=[km, "lnw", "Ec%d" % jl], w=[mk_])
                                    min_ = m_[:, :tn]
                                    rk = [mk_]
                                else:
                                    min_ = mxsT[:, jt, :]
                                    rk = ["mxsT"]
                                P.tt("dve", m_[:, :tn], pu[:, :tn], min_, ALU.mult, r=[ku] + rk, w=[mk_])
                                P.tt("pool", gb[:, jl, :tn], m_[:, :tn], s_[:, :tn], ALU.mult, r=[mk_, sk_], w=[gk])
                            for dt_ in range(8):
                                po = pb[6 + dt_ % 2]
                                pk = "pb%d" % (6 + dt_ % 2)
                                for jl in range(2):
                                    P.mm(po[:, :tn], ob[:, jl, dt_ * 128:(dt_ + 1) * 128], gb[:, jl, :tn], jl == 0, jl == 1,
                                         r=[ok, gk], w=[pk])
                                P.tt("dve", xres[:, dt_, t0:t0 + tn], xres[:, dt_, t0:t0 + tn], po[:, :tn], ALU.add,
                                     r=[pk, xk(dt_, ti)], w=[xk(dt_, ti)])
                P.S.barrier()

        def layer_ssd(li):
            p_ = "l%d_" % li
            emit_norm(li)
            Win = W[p_ + "in_proj"].rearrange("(k p) j -> p k j", p=128)
            Wout = W[p_ + "out_proj"].rearrange("(k p) j -> p k j", p=128)
            st_ssm = DR["state_l%d_ssm" % li]
            st_conv = DR["state_l%d_conv" % li]
            o_pssm, o_pconv, o_sssm, o_sconv = DR["p%d_ssm" % li], DR["p%d_conv" % li], DR["s%d_ssm" % li], DR["s%d_conv" % li]
            triu_f, sgt_f, ltm_f = cst[:, 256:384], cst[:, 384:512], cst[:, 512:640]
            ones_f = cst[:, 128:256]
            with ExitStack() as L:
                cw = P.sb(L, "cw", [128, 32, 5], F32)
                dtb = P.sb(L, "dtb", [32, 4], F32)
                Dcol = P.sb(L, "Dcol", [128, 16], F32)
                gnw = P.sb(L, "gnw", [128, 16], F32)
                wdt = P.sb(L, "wdt", [128, 8, 32], BF16)
                ddt = P.sb(L, "ddt", [128, 17, 64], F32)
                ecum = P.sb(L, "ecum", [128, 16, 32], F32)
                wend = P.sb(L, "wend", [128, 16, 32], F32)
                eL = P.sb(L, "eL", [128, 16, 32], F32)
                dtmp = P.sb(L, "dtmp", [32, 2, 512], F32)
                dsT = P.sb(L, "dsT", [32, 2, NS], F32)
                csb = P.sb(L, "csb", [128, 32], F32)
                negI = P.sb(L, "negI", [128, 128], F32)
                wblk = [P.sb(L, "wblk%d" % i, [128, 8, 768], BF16) for i in range(2)]
                wo = [P.sb(L, "wo0", [128, 2, D], BF16)] * 2
                dg = P.sb(L, "dg", [128, 16, 128], BF16)
                dgD = P.sb(L, "dgD", [128, 2, 128], BF16)
                cwg = P.sb(L, "cwg", [128, 4, 5], F32)
                xr = [[P.sb(L, "xr%d_0" % j, [128, 515], BF16)] * 2 for j in range(4)]
                xbc = [P.sb(L, "xbc%d" % i, [128, 4, 512], BF16) for i in range(2)]
                ctail = P.sb(L, "ctail", [128, 4, 3], F32)
                xBtr = [P.sb(L, "xBt%d" % i, [128, 384], BF16) for i in range(2)]
                sztiles = [P.sb(L, "sztile%d" % i, [128, 4, 256], BF16) for i in range(2)]
                scrA = P.sb(L, "scrA", [128, 512], F32)
                scrB = P.sb(L, "scrB", [128, 512], F32)
                Ub = [P.sb(L, "Ub%d" % i, [128, 128], F32) for i in range(2)]
                dec = P.sb(L, "dec", [128, 512], BF16)
                mixT = P.sb(L, "mixT", [128, 4, 128], BF16)
                ssq = P.sb(L, "ssq", [128, 2], F32)
                yn = P.sb(L, "yn", [128, 256], BF16)
                gT = [P.sb(L, "gT%d" % i, [128, 2, 512], BF16) for i in range(2)]
                xw = P.sb(L, "xw", [128, 256], BF16)
                hT = P.sb(L, "hT", [128, 256], F32)
                hTb = P.sb(L, "hTb", [128, 256], BF16)
                htmp = P.sb(L, "htmp", [128, 256], F32)
                hout = scrA[:, 0:256].rearrange("p (j n) -> p j n", j=2)
                ctout = scrB[0:3, :]
                xrs = P.sb(L, "xrs", [128, 4, NS], F32)
                cstk = scrA[0:48, :]
                cs = P.sb(L, "cs", [128, 4, 48], F32)
                acc = P.sb(L, "acc", [128, 4, NS], F32)
                atmp = P.sb(L, "atmp", [128, 4, NS], F32)
                xbs = P.sb(L, "xbs", [128, 4, NS], F32)
                ncs = P.sb(L, "ncs", [128, 4, 48], F32)
                szs = P.sb(L, "szs", [128, 2, NS], F32)
                Eg = P.sb(L, "Eg", [32, 256], F32)
                dtc = P.sb(L, "dtc", [128, 2, 2, NS], F32)
                xdtc = P.sb(L, "xdtc", [128, 2, NS], F32)
                rhsB = P.sb(L, "rhsB", [128, 8, 128], BF16)
                bcsb = P.sb(L, "bcsb", [128, 8, 128], BF16)
                gTs_all = P.sb(L, "gTs_all", [128, 16, NS], BF16)
                h0t = [P.sb(L, "h0t%d" % i, [128, 4, 128], F32) for i in range(2)]
                hnt = [P.sb(L, "hnt%d" % i, [128, 4, 128], F32) for i in range(2)]
                s1 = P.sb(L, "s1", [128, 4, 128], F32)
                ys = P.sb(L, "ys", [128, 2, NS], F32)
                gsm = P.sb(L, "gsm", [128, 2, NS], F32)
                gsq = P.sb(L, "gsq", [128, 2, NS], BF16)
                gTs = P.sb(L, "gTs", [128, 2, NS], BF16)
                rsd = P.sb(L, "rsd", [128, NS], F32)
                pbT2 = pb[2].bitcast(BF16)
                pbT7 = pb[7].bitcast(BF16)
                pbT1 = pb[1].bitcast(BF16)

                for k in range(4):
                    P.dma("sp", cw[:, :, k], W[p_ + "conv_w"][k].rearrange("(j p) -> p j", p=128), w=["cw"], dkey="cw%d" % k, nonc=True)
                P.dma("sp", cw[:, :, 4], W[p_ + "conv_b"].rearrange("(j p) -> p j", p=128), w=["cw"], dkey="cw4", nonc=True)
                P.dma("sp", dtb[:, 0:1], W[p_ + "dt_bias"].rearrange("(h o) -> h o", o=1), w=["dtb"], dkey="dtb0", nonc=True)
                P.dma("sp", dtb[:, 1:2], W[p_ + "A_log"].rearrange("(h o) -> h o", o=1), w=["dtb"], dkey="dtb1", nonc=True)
                P.act(dtb[:, 2:3], dtb[:, 1:2], AF.Exp, r=["dtb"], w=["dtb"])
                P.ts("dve", dtb[:, 2:3], dtb[:, 2:3], -1.0, None, ALU.mult, r=["dtb"], w=["dtb"])
                P.memset("dve", dtb[:, 3:4], 1.0, w=["dtb"])
                Dsk = W[p_ + "D_skip"].rearrange("(j two o) -> two o j", two=2, o=1)
                P.dma("sp", Dcol[0:64, :], bc(Dsk[0], [64, 16]), w=["Dcol"], dkey="Dcol0", nonc=True)
                P.dma("sp", Dcol[64:128, :], bc(Dsk[1], [64, 16]), w=["Dcol"], dkey="Dcol1", nonc=True)
                P.dma("sp", gnw[:, :], W[p_ + "gnorm_w"].rearrange("(k p) -> p k", p=128), w=["gnw"], dkey="gnw", nonc=True)
                P.dma("pool", wdt[:, :, :], Win[:, :, 6144:6176], w=["wdt"], dkey="wdt")
                P.ts("dve", negI[:, :], ident_f, NEG, None, ALU.mult, r=["cst"], w=["negI"])

                for ti, (t0, tn) in enumerate(TT):
                    for kc in range(8):
                        P.mm(pb[0][:32, :tn], wdt[:, kc, :], xn[:, kc, t0:t0 + tn], kc == 0, kc == 7, r=["wdt", xnk(ti)], w=["pb0"])
                    P.act(dtmp[:, 0, :tn], pb[0][:32, :tn], AF.Exp, r=["pb0", "dtb"], w=["dtmp0"], bias=dtb[:, 0:1])
                    P.act(dtmp[:, 0, :tn], dtmp[:, 0, :tn], AF.Ln, r=["dtmp0", "dtb"], w=["dtmp0"], bias=dtb[:, 3:4])
                    P.ts("dve", dtmp[:, 1, :tn], dtmp[:, 0, :tn], dtb[:, 2:3], None, ALU.mult, r=["dtmp0", "dtb"], w=["dtmp1"])
                    if ti == 4:
                        P.copy("dve", dsT[:, :, :], dtmp[:, :, 0:NS], r=["dtmp0", "dtmp1"], w=["dsT"])
                        continue
                    for c4 in range(4):
                        c = ti * 4 + c4
                        P.tr(pb[1][:, 0:32], dtmp[:, 0, c4 * 128:(c4 + 1) * 128], ident_f[0:32, 0:32], r=["dtmp0", "cst"], w=["pb1"])
                        P.tr(pb[1][:, 32:64], dtmp[:, 1, c4 * 128:(c4 + 1) * 128], ident_f[0:32, 0:32], r=["dtmp1", "cst"], w=["pb1"])
                        P.copy("dve", ddt[:, c, :], pb[1][:, 0:64], r=["pb1"], w=["ddt.%d" % c])
                        P.mm(pb[2][:, 0:32], triu_f, ddt[:, c, 32:64], True, True, r=["cst", "ddt.%d" % c], w=["pb2"])
                        P.mm(pb[2][:, 32:64], ones_f, ddt[:, c, 32:64], True, True, r=["cst", "ddt.%d" % c], w=["pb2"])
                        P.act(ecum[:, c, :], pb[2][:, 0:32], AF.Exp, r=["pb2"], w=["ecum.%d" % c])
                        P.act(eL[:, c, :], pb[2][:, 32:64], AF.Exp, r=["pb2"], w=["eL.%d" % c])
                        P.copy("dve", csb[:, :], pb[2][:, 0:32], r=["pb2"], w=["csb"])
                        P.tt("dve", csb[:, :], pb[2][:, 32:64], csb[:, :], ALU.subtract, r=["pb2", "csb"], w=["csb"])
                        P.act(wend[:, c, :], csb[:, :], AF.Exp, r=["csb"], w=["wend.%d" % c])
                        P.tt("dve", wend[:, c, :], wend[:, c, :], ddt[:, c, 0:32], ALU.mult, r=["wend.%d" % c, "ddt.%d" % c], w=["wend.%d" % c])

                ngroups = cfg.get("ssd_groups", 8)

                def load_g(g, slot):
                    wb, wk = wblk[slot], "wblk%d" % slot
                    P.dma("pool", wb[:, :, 0:256], Win[:, :, 2048 + 256 * g: 2048 + 256 * (g + 1)], w=[wk + "x"], dkey=wk + "x")
                    P.dma("pool", wb[:, :, 256:384], Win[:, :, 4096 + 128 * g: 4096 + 128 * (g + 1)], w=[wk + "B"], dkey=wk + "B")
                    P.dma("pool", wb[:, :, 384:512], Win[:, :, 5120 + 128 * g: 5120 + 128 * (g + 1)], w=[wk + "C"], dkey=wk + "C")
                    P.dma("pool", wb[:, :, 512:768], Win[:, :, 256 * g: 256 * (g + 1)], w=[wk + "z"], dkey=wk + "z")

                load_g(0, 0)
                for g in range(ngroups):
                    slot = g % 2
                    wb, wk = wblk[slot], "wblk%d" % slot
                    wkeys = [wk + "x", wk + "x", wk + "B", wk + "C"]
                    ob, ok = wo[slot], "wo0"
                    P.dma("pool", ob[:, :, :], Wout[:, 2 * g:2 * g + 2, :], w=["wo0"], dkey="wo0")
                    if g + 1 < ngroups:
                        load_g(g + 1, 1 - slot)
                    jts = [2 * g, 2 * g + 1, 16 + g, 24 + g]
                    for j in range(4):
                        P.copy("dve", cwg[:, j, :], cw[:, jts[j], :], r=["cw"], w=["cwg"])
                        for k in range(4):
                            P.ts("dve", dg[:, j * 4 + k, :], ident_f, cw[:, jts[j], k:k + 1], None, ALU.mult, r=["cst", "cw"], w=["dg"])
                    for jl in range(2):
                        P.ts("dve", dgD[:, jl, :], ident_f, Dcol[:, 2 * g + jl:2 * g + jl + 1], None, ALU.mult, r=["cst", "Dcol"], w=["dgD"])
                    P.memset("dve", hT[:, :], 0.0, w=["hT"])
                    P.memset("dve", hTb[:, :], 0.0, w=["hTb"])
                    for j in range(4):
                        P.memset("dve", xr[j][0][:, 0:3], 0.0, w=["xr%d_0" % j])

                    fill = []

                    def pump(n=1):
                        for _ in range(n):
                            if fill:
                                fill.pop(0)()

                    def u_proj(ti, j):
                        t0 = ti * 512
                        xrb, xrk = xr[j][0], "xr%d_0" % j
                        for kc in range(8):
                            P.mm(pb[0][:, :], wb[:, kc, j * 128:(j + 1) * 128], xn[:, kc, t0:t0 + 512], kc == 0, kc == 7,
                                 r=[wkeys[j], xnk(ti)], w=["pb0"])
                        P.copy("act", xrb[:, 3:515], pb[0][:, :], r=["pb0"], w=[xrk])
                        if ti == 3:
                            P.copy("dve", ctail[:, j, :], pb[0][:, 509:512], r=["pb0"], w=["ctail"])

                    def u_conv(ti, j):
                        xb, xbk = xbc[ti % 2], "xbc%d" % (ti % 2)
                        xrb, xrk = xr[j][0], "xr%d_0" % j
                        for k in range(4):
                            P.mm(pb[6][:, :], dg[:, j * 4 + k, :], xrb[:, k:k + 512], k == 0, k == 3, r=["dg", xrk], w=["pb6"])
                        if ti < 3:
                            P.copy("pool", xrb[:, 0:3], xrb[:, 512:515], r=[xrk], w=[xrk])
                        P.act(xb[:, j, :], pb[6][:, :], AF.Silu, r=["pb6", "cw"], w=[xbk + ".%d" % j], bias=cw[:, jts[j], 4:5])

                    def u_z(ti, c4):
                        tok = slice((ti * 4 + c4) * 128, (ti * 4 + c4 + 1) * 128)
                        for kc in range(8):
                            P.mm(pb[6][:, 0:256], xn[:, kc, tok], wb[:, kc, 512:768], kc == 0, kc == 7, r=[xnk(ti), wk + "z"], w=["pb6"])
                        P.act(sztiles[ti % 2][:, c4, :], pb[6][:, 0:256], AF.Silu, r=["pb6"], w=["szt%d.%d" % (ti % 2, c4)])

                    def u_op(ti, dt_):
                        t0 = ti * 512
                        gb, gk = gT[ti % 2], "gT%d" % (ti % 2)
                        for jl in range(2):
                            P.mm(pb[0][:, :], ob[:, jl, dt_ * 128:(dt_ + 1) * 128], gb[:, jl, :], jl == 0, jl == 1, r=[ok, gk], w=["pb0"])
                        P.tt("dve", xres[:, dt_, t0:t0 + 512], xres[:, dt_, t0:t0 + 512], pb[0][:, :], ALU.add, r=["pb0", xk(dt_, ti)], w=[xk(dt_, ti)])

                    def front(c):
                        ti, c4 = c // 4, c % 4
                        xb, xbk = xbc[ti % 2], "xbc%d" % (ti % 2)
                        cs_ = slice(c4 * 128, (c4 + 1) * 128)
                        tok = slice(c * 128, (c + 1) * 128)
                        xBt, xBk = xBtr[c % 2], "xBt%d" % (c % 2)
                        ysb, ysk = scrA[:, (c % 2) * 256:(c % 2 + 1) * 256], "scrA.%d" % (c % 2)
                        for j in range(3):
                            P.tr(pbT2[:, j * 128:(j + 1) * 128], xb[:, j, cs_], ident_b[:, :], r=[xbk + ".%d" % j, "ident_b"], w=["pb2"])
                        P.copy("act", xBt[:, :], pbT2[:, 0:384], r=["pb2"], w=[xBk])
                        pump()
                        P.mm(pb[3][:, 0:128], xb[:, 2, cs_], xb[:, 3, cs_], True, True, r=[xbk + ".2", xbk + ".3"], w=["pb3"])
                        for hh in range(4):
                            h = 4 * g + hh
                            U, Uk = Ub[hh % 2], "Ub%d" % (hh % 2)
                            P.ts("dve", U[:, :], sgt_f, ddt[:, c, 32 + h:33 + h], None, ALU.mult, r=["cst", "ddt.%d" % c], w=[Uk])
                            P.mm(pb[4][:, hh * 128:(hh + 1) * 128], U[:, :], triu_f, True, False, r=[Uk, "cst"], w=["pb4"])
                            P.mm(pb[4][:, hh * 128:(hh + 1) * 128], negI[:, :], ltm_f, False, True, r=["negI", "cst"], w=["pb4"])
                        P.act(dec[:, :], pb[4][:, :], AF.Exp, r=["pb4"], w=["dec"])
                        pump()
                        for hh in range(4):
                            h = 4 * g + hh
                            P.stt(mixT[:, hh, :], dec[:, hh * 128:(hh + 1) * 128], ddt[:, c, h:h + 1], pb[3][:, 0:128],
                                  ALU.mult, ALU.mult, r=["dec", "ddt.%d" % c, "pb3"], w=["mixT"])
                        for jl in range(2):
                            P.mm(pb[5][:, jl * 128:(jl + 1) * 128], xb[:, jl, cs_], dgD[:, jl, :], True, False, r=[xbk + ".%d" % jl, "dgD"], w=["pb5"])
                            for hh in (2 * jl, 2 * jl + 1):
                                P.mm(pb[5][:, hh * 64:(hh + 1) * 64], mixT[:, hh, :], xBt[:, hh * 64:(hh + 1) * 64], False, hh == 2 * jl + 1,
                                     r=["mixT", xBk], w=["pb5"])
                        P.copy("act", ysb, pb[5][:, 0:256], r=["pb5"], w=[ysk])
                        pump()

                    def tail(c):
                        ti, c4 = c // 4, c % 4
                        xb, xbk = xbc[ti % 2], "xbc%d" % (ti % 2)
                        cs_ = slice(c4 * 128, (c4 + 1) * 128)
                        gb, gk = gT[ti % 2], "gT%d" % (ti % 2)
                        xBt, xBk = xBtr[c % 2], "xBt%d" % (c % 2)
                        szt, szk = sztiles[ti % 2][:, c4, :], "szt%d.%d" % (ti % 2, c4)
                        ysb, ysk = scrA[:, (c % 2) * 256:(c % 2 + 1) * 256], "scrA.%d" % (c % 2)
                        t1, t2 = scrB[:, 0:256], scrB[:, 256:512]
                        P.mm(pb[7][:, 0:256], xb[:, 3, cs_], hTb[:, :], True, True, r=[xbk + ".3", "hTb"], w=["pb7"])
                        pump()
                        P.tt("dve", t1.rearrange("p (h q) -> p h q", h=4), pb[7][:, 0:256].rearrange("p (h q) -> p h q", h=4),
                             bc(ecum[:, c, 4 * g:4 * g + 4].rearrange("p (h o) -> p h o", o=1), [128, 4, 64]), ALU.mult,
                             r=["pb7", "ecum.%d" % c], w=["scrB.0"])
                        P.tt("dve", t2, ysb, t1, ALU.add, r=[ysk, "scrB.0"], w=["scrB.1"])
                        P.tt("pool", t2, t2, szt, ALU.mult, r=["scrB.1", szk], w=["scrB.1"])
                        P.act(t1, t2, AF.Square, r=["scrB.1"], w=["scrB.0", "ssq"], accum=ssq[:, 0:1])
                        rstd_pool(ssq[:, 1:2], ssq[:, 0:1], 1.0 / 256, ["ssq"], ["ssq"])
                        P.ts("dve", yn[:, :], t2, ssq[:, 1:2], None, ALU.mult, r=["scrB.1", "ssq"], w=["yn"])
                        for jl in range(2):
                            P.tr(pbT1[:, jl * 128:(jl + 1) * 128], yn[:, jl * 128:(jl + 1) * 128], ident_b[:, :], r=["yn", "ident_b"], w=["pb1"])
                        pump()
                        for jl in range(2):
                            P.act(gb[:, jl, cs_], pbT1[:, jl * 128:(jl + 1) * 128], AF.Copy, r=["pb1", "gnw"], w=[gk],
                                  scale=gnw[:, 2 * g + jl:2 * g + jl + 1])
                        P.tt("pool", xw[:, :].rearrange("p (h q) -> p h q", h=4), xBt[:, 0:256].rearrange("p (h q) -> p h q", h=4),
                             bc(wend[:, c, 4 * g:4 * g + 4].rearrange("p (h o) -> p h o", o=1), [128, 4, 64]), ALU.mult,
                             r=[xBk, "wend.%d" % c], w=["xw"])
                        P.mm(pb[7][:, 256:512], xBt[:, 256:384], xw[:, :], True, True, r=[xBk, "xw"], w=["pb7"])
                        P.tt("pool", htmp[:, :].rearrange("p (h q) -> p h q", h=4), hT[:, :].rearrange("p (h q) -> p h q", h=4),
                             bc(eL[:, c, 4 * g:4 * g + 4].rearrange("p (h o) -> p h o", o=1), [128, 4, 64]), ALU.mult,
                             r=["hT", "eL.%d" % c], w=["htmp"])
                        P.tt("dve", hT[:, :], htmp[:, :], pb[7][:, 256:512], ALU.add, r=["htmp", "pb7"], w=["hT"])
                        P.copy("act", hTb[:, :], hT[:, :], r=["hT"], w=["hTb"])

                    def refill(ti):
                        prep = []
                        if ti < 3:
                            for j in range(4):
                                prep.append(lambda ti=ti, j=j: u_proj(ti + 1, j))
                                prep.append(lambda ti=ti, j=j: u_conv(ti + 1, j))
                        ops_ = [(lambda ti=ti, d=d: u_op(ti - 1, d)) for d in range(8)] if ti >= 1 else []
                        while prep or ops_:
                            if prep:
                                fill.append(prep.pop(0))
                            if ops_:
                                fill.append(ops_.pop(0))

                    for j in range(4):
                        u_proj(0, j)
                        u_conv(0, j)
                    for c4 in range(4):
                        u_z(0, c4)
                    front(0)
                    for c in range(16):
                        ti, c4 = c // 4, c % 4
                        if c4 == 0:
                            pump(len(fill))
                            refill(ti)
                        if c + 1 < 16:
                            if c4 == 3:
                                pump(len(fill))
                            front(c + 1)
                        tail(c)
                        if ti < 3:
                            u_z(ti + 1, c4)
                    pump(len(fill))
                    for d in range(8):
                        u_op(3, d)
                    for jl in range(2):
                        P.tr(pb[2][:, jl * 128:(jl + 1) * 128], hT[:, jl * 128:(jl + 1) * 128], ident_f, r=["hT", "cst"], w=["pb2"])
                    P.copy("dve", hout[:, :, :], pb[2][:, 0:256].rearrange("p (j n) -> p j n", j=2), r=["pb2"], w=["scrA.0"])
                    P.dma("sp", o_pssm[4 * g:4 * g + 4].rearrange("(j hh) q n -> (hh q) j n", j=2), hout[:, :, :], r=["scrA.0"], dkey="o_hout")
                    for j in range(4):
                        P.tr(pb[3][0:3, j * 128:(j + 1) * 128], ctail[:, j, :], ident_f, r=["ctail", "cst"], w=["pb3"])
                    P.copy("dve", ctout[:, :], pb[3][0:3, :], r=["pb3"], w=["scrB.0", "scrB.1"])
                    P.dma("sp", o_pconv[:, 256 * g:256 * (g + 1)], ctout[:, 0:256], r=["scrB.0", "scrB.1"], dkey="o_ctx")
                    P.dma("sp", o_pconv[:, 2048 + 128 * g:2048 + 128 * (g + 1)], ctout[:, 256:384], r=["scrB.0", "scrB.1"], dkey="o_ctB")
                    P.dma("sp", o_pconv[:, 3072 + 128 * g:3072 + 128 * (g + 1)], ctout[:, 384:512], r=["scrB.0", "scrB.1"], dkey="o_ctC")

                    if cfg.get('ssd_nosample'):
                        continue
                    ti = 4
                    for j in range(4):
                        for kc in range(8):
                            P.mm(pb[0][:, j * NS:(j + 1) * NS], wb[:, kc, j * 128:(j + 1) * 128], xn[:, kc, SEQ:NT], kc == 0, kc == 7,
                                 r=[wkeys[j], xnk(4)], w=["pb0"])
                    P.copy("dve", xrs[:, :, :], pb[0][:, 0:4 * NS].rearrange("p (j b) -> p j b", j=4), r=["pb0"], w=["xrs"])
                    cranges = [(256 * g, 256), (2048 + 128 * g, 128), (3072 + 128 * g, 128)]
                    off = 0
                    for (c0, cn) in cranges:
                        P.dma("sp", cstk[:, off:off + cn], st_conv[:, :, c0:c0 + cn].rearrange("b k c -> (b k) c"), w=["scrA.0", "scrA.1"], dkey="cstk%d" % off)
                        off += cn
                    for j in range(4):
                        P.tr(pb[1][:, j * 48:(j + 1) * 48], cstk[:, j * 128:(j + 1) * 128], ident_f[0:48, 0:48], r=["scrA.0", "scrA.1", "cst"], w=["pb1"])
                    P.copy("dve", cs[:, :, :], pb[1][:, 0:192].rearrange("p (j q) -> p j q", j=4), r=["pb1"], w=["cs"])
                    cs4 = cs[:, :, :].rearrange("p j (b k) -> p j b k", k=3)
                    ncs4 = ncs[:, :, :].rearrange("p j (b k) -> p j b k", k=3)
                    P.tt("dve", acc[:, :, :], xrs[:, :, :], bc(cwg[:, :, 3:4], [128, 4, NS]), ALU.mult, r=["xrs", "cwg"], w=["acc"])
                    P.tt("dve", acc[:, :, :], acc[:, :, :], bc(cwg[:, :, 4:5], [128, 4, NS]), ALU.add, r=["acc", "cwg"], w=["acc"])
                    for k in range(3):
                        P.tt("dve", atmp[:, :, :], cs4[:, :, :, k], bc(cwg[:, :, k:k + 1], [128, 4, NS]), ALU.mult, r=["cs", "cwg"], w=["atmp"])
                        P.tt("dve", acc[:, :, :], acc[:, :, :], atmp[:, :, :], ALU.add, r=["acc", "atmp"], w=["acc"])
                    P.act(xbs[:, :, :], acc[:, :, :], AF.Silu, r=["acc"], w=["xbs"])
                    P.copy("dve", ncs4[:, :, :, 0:2], cs4[:, :, :, 1:3], r=["cs"], w=["ncs"])
                    P.copy("dve", ncs4[:, :, :, 2], xrs[:, :, :], r=["xrs", "ncs"], w=["ncs"])
                    for j in range(4):
                        P.tr(pb[1][0:48, j * 128:(j + 1) * 128], ncs[:, j, :], ident_f, r=["ncs", "cst"], w=["pb1"])
                    P.copy("dve", cstk[:, :], pb[1][0:48, :], r=["pb1"], w=["scrA.0", "scrA.1"])
                    off = 0
                    for (c0, cn) in cranges:
                        P.dma("sp", o_sconv[:, :, c0:c0 + cn].rearrange("b k c -> (b k) c"), cstk[:, off:off + cn], r=["scrA.0", "scrA.1"], dkey="o_cstk%d" % off)
                        off += cn
                    for jl in range(2):
                        for kc in range(8):
                            P.mm(pb[2][:, jl * NS:(jl + 1) * NS], wb[:, kc, 512 + jl * 128:512 + (jl + 1) * 128], xn[:, kc, SEQ:NT], kc == 0, kc == 7,
                                 r=[wk + "z", xnk(4)], w=["pb2"])
                    P.act(szs[:, :, :], pb[2][:, 0:2 * NS].rearrange("p (j b) -> p j b", j=2), AF.Silu, r=["pb2"], w=["szs"])
                    P.dma("sp", Eg[:, :], consts2_d[:, 256 * g:256 * (g + 1)], w=["Eg"], dkey="Eg")
                    for jl in range(2):
                        for q in range(2):
                            P.mm(pb[2][:, 64 + (jl * 2 + q) * NS: 64 + (jl * 2 + q + 1) * NS], Eg[:, jl * 128:(jl + 1) * 128], dsT[:, q, :], True, True,
                                 r=["Eg", "dsT"], w=["pb2"])
                    dview = pb[2][:, 64:64 + 4 * NS].rearrange("p (j q b) -> p j q b", j=2, q=2)
                    P.copy("dve", dtc[:, :, 0, :], dview[:, :, 0, :], r=["pb2"], w=["dtc"])
                    P.act(dtc[:, :, 1, :], dview[:, :, 1, :], AF.Exp, r=["pb2"], w=["dtc"])
                    P.tt("dve", xdtc[:, :, :], xbs[:, 0:2, :], dtc[:, :, 0, :], ALU.mult, r=["xbs", "dtc"], w=["xdtc"])
                    def v4(ap_):
                        return ap_.bitcast(F32).rearrange("p (b n) -> p b n", b=4)
                    hring = [(h0t[0][:, :, :], ["h0t0"]), (h0t[1][:, :, :], ["h0t1"]),
                             (v4(xbc[0][:, 0:2, :].rearrange("p a c -> p (a c)")), ["xbc0.0", "xbc0.1"]),
                             (v4(xbc[1][:, 0:2, :].rearrange("p a c -> p (a c)")), ["xbc1.0", "xbc1.1"]),
                             (v4(sztiles[0][:, :, :].rearrange("p a c -> p (a c)")), ["szt0.0", "szt0.1", "szt0.2", "szt0.3"])]
                    nring = [(hnt[0][:, :, :], ["hnt0"]), (hnt[1][:, :, :], ["hnt1"]),
                             (v4(gT[0][:, :, :].rearrange("p a c -> p (a c)")), ["gT0"]),
                             (v4(gT[1][:, :, :].rearrange("p a c -> p (a c)")), ["gT1"]),
                             (v4(sztiles[1][:, :, :].rearrange("p a c -> p (a c)")), ["szt1.0", "szt1.1", "szt1.2", "szt1.3"])]
                    for q in range(4):
                        bs = slice(4 * q, 4 * q + 4)
                        for which, j, bank in (("B", 2, 2), ("C", 3, 3)):
                            rb = rhsB[:, (0 if which == "B" else 4):(4 if which == "B" else 8), :]
                            rbk = "rhsB" + which
                            P.tt("dve", rb, bc(ident_b[:, :].rearrange("p (o n) -> p o n", o=1), [128, 4, 128]),
                                 bc(xbs[:, j, bs].rearrange("p (b o) -> p b o", o=1), [128, 4, 128]), ALU.mult, r=["ident_b", "xbs"], w=[rbk])
                            P.mm(pb[bank][:, :], ones_b[:, :], rb, True, True, r=["ones_b", rbk], w=["pb%d" % bank])
                            P.copy("act", bcsb[:, (0 if which == "B" else 4):(4 if which == "B" else 8), :],
                                   pb[bank][:, :].rearrange("p (b n) -> p b n", b=4), r=["pb%d" % bank], w=["bcsb" + which])
                        for jl in range(2):
                            hb, hk = hring[(2 * q + jl) % len(hring)]
                            nb, nk = nring[(2 * q + jl) % len(nring)]
                            P.dma("sp", hb, st_ssm[4 * q:4 * q + 4, 4 * g + 2 * jl:4 * g + 2 * jl + 2].rearrange("b h q n -> (h q) b n"),
                                  w=hk, dkey=hk[0])
                            P.tt("dve", s1[:, :, :], bcsb[:, 0:4, :],
                                 bc(xdtc[:, jl, bs].rearrange("p (b o) -> p b o", o=1), [128, 4, 128]), ALU.mult, r=["bcsbB", "xdtc"], w=["s1"])
                            P.tt("pool", hb, hb,
                                 bc(dtc[:, jl, 1, bs].rearrange("p (b o) -> p b o", o=1), [128, 4, 128]), ALU.mult, r=hk + ["dtc"], w=hk)
                            P.tt("pool", nb, hb, s1[:, :, :], ALU.add, r=["s1"] + hk, w=nk)
                            P.tt("dve", s1[:, :, :], nb, bcsb[:, 4:8, :], ALU.mult,
                                 r=nk + ["bcsbC"], w=["s1"])
                            P.S.add("dve", (lambda e, o=ys[:, jl, bs], i=s1[:, :, :]: e.tensor_reduce(out=o, in_=i, axis=AX.X, op=ALU.add)),
                                    ["s1"], ["ys"])
                            P.dma("pool", o_sssm[4 * q:4 * q + 4, 4 * g + 2 * jl:4 * g + 2 * jl + 2].rearrange("b h q n -> (h q) b n"), nb,
                                  r=nk, dkey="o_" + nk[0])
                    P.tt("dve", gsm[:, :, :], xbs[:, 0:2, :], bc(Dcol[:, 2 * g:2 * g + 2].rearrange("p (j o) -> p j o", o=1), [128, 2, NS]), ALU.mult,
                         r=["xbs", "Dcol"], w=["gsm"])
                    P.tt("dve", gsm[:, :, :], gsm[:, :, :], ys[:, :, :], ALU.add, r=["gsm", "ys"], w=["gsm"])
                    P.tt("dve", gsm[:, :, :], gsm[:, :, :], szs[:, :, :], ALU.mult, r=["gsm", "szs"], w=["gsm"])
                    P.tt("dve", gsq[:, :, :], gsm[:, :, :], gsm[:, :, :], ALU.mult, r=["gsm"], w=["gsq"])
                    for jl in range(2):
                        P.mm(pb[0][:, 0:NS], ones_b[:, :], gsq[:, jl, :], jl == 0, jl == 1, r=["ones_b", "gsq"], w=["pb0"])
                    P.copy("dve", rsd[:, :], pb[0][:, 0:NS], r=["pb0"], w=["rsd"])
                    rstd_pool(rsd[:, :], rsd[:, :], 1.0 / 256, ["rsd"], ["rsd"])
                    for jl in range(2):
                        P.stt(gTs_all[:, 2 * g + jl, :], gsm[:, jl, :], gnw[:, 2 * g + jl:2 * g + jl + 1], rsd[:, :], ALU.mult, ALU.mult,
                              r=["gsm", "gnw", "rsd"], w=["gTs_all"])
                if not cfg.get('ssd_nosample'):
                    for g in range(ngroups):
                        P.dma("pool", wo[0][:, :, :], Wout[:, 2 * g:2 * g + 2, :], w=["wo0"], dkey="wo0")
                        for dt_ in range(8):
                            po, pk = pb[dt_ % 2], "pb%d" % (dt_ % 2)
                            for jl in range(2):
                                P.mm(po[:, 0:NS], wo[0][:, jl, dt_ * 128:(dt_ + 1) * 128], gTs_all[:, 2 * g + jl, :], jl == 0, jl == 1,
                                     r=["wo0", "gTs_all"], w=[pk])
                            P.tt("dve", xres[:, dt_, SEQ:NT], xres[:, dt_, SEQ:NT], po[:, 0:NS], ALU.add, r=[pk, xk(dt_, 4)], w=[xk(dt_, 4)])
                P.S.barrier()

        def layer_mlstm():
            li = 2
            emit_norm(li)
            Win = W["l2_in_proj"].rearrange("(k p) j -> p k j", p=128)
            Wout = W["l2_out_proj"].rearrange("(k p) j -> p k j", p=128)
            st_C, st_n, st_m, st_conv = DR["state_l2_C"], DR["state_l2_n"], DR["state_l2_m"], DR["state_l2_conv"]
            o_pC, o_pn, o_pm, o_pconv = DR["p2_C"], DR["p2_n"], DR["p2_m"], DR["p2_conv"]
            o_sC, o_sn, o_sm, o_sconv = DR["s2_C"], DR["s2_n"], DR["s2_m"], DR["s2_conv"]
            bmask = cst[:, 640:768]
            gemask = cst[:, 768:896]
            ones_f = cst[:, 128:256]
            KS = 512.0 ** -0.5
            with ExitStack() as L:
                cw = P.sb(L, "cw", [128, 16, 5], F32)
                wq4 = P.sb(L, "wq4", [128, 16, 4, 4], F32)
                wif = P.sb(L, "wif", [128, 48, 8], BF16)
                mhw = P.sb(L, "mhw", [128, 16], F32)
                skp = P.sb(L, "skp", [128, 16], F32)
                bocol = P.sb(L, "bocol", [128, 16], F32)
                gacc = P.sb(L, "gacc", [128, 17, 8], F32)
                xcs_all = P.sb(L, "xcs_all", [128, 16, NS], F32)
                xms_all = P.sb(L, "xms_all", [128, 16, NS], F32)
                dg = P.sb(L, "dg", [128, 16, 128], BF16)
                Dm = P.sb(L, "Dm", [128, 16, 128], BF16)
                xr = [P.sb(L, "xr%d" % j, [128, 515], BF16) for j in range(4)]
                xc = P.sb(L, "xc", [128, 4, 512], BF16)
                qT = P.sb(L, "qT", [128, 4, 512], BF16)
                khT = P.sb(L, "khT", [128, 4, 512], BF16)
                szT = P.sb(L, "szT", [128, 4, 512], BF16)
                gT = P.sb(L, "gT", [128, 4, 512], BF16)
                wbuf = P.sb(L, "wbuf", [128, 8, 1024], BF16)
                wo = P.sb(L, "wo", [128, 4, D], BF16)
                ctail = P.sb(L, "ctail", [128, 3], F32)
                ctout = P.sb(L, "ctout", [3, 128], F32)
                cstk = P.sb(L, "cstk", [48, 128], F32)
                cs = P.sb(L, "cs", [128, 48], F32)
                ncs = P.sb(L, "ncs", [128, 48], F32)
                sA = P.sb(L, "sA", [128, 4, NS], F32)
                sB = P.sb(L, "sB", [128, 4, NS], BF16)
                gcol = P.sb(L, "gcol", [128, 128], F32)
                ga = P.sb(L, "ga", [64, 128], F32)
                gs1 = P.sb(L, "gs1", [64, 8], F32)
                ends4 = P.sb(L, "ends4", [4, 16, 2], F32)
                m4 = P.sb(L, "m4", [4, 2, 16], F32)
                tokq = P.sb(L, "tokq", [128, 5, 64], F32)
                vtok = P.sb(L, "vtok", [128, 512], BF16)
                kwt = P.sb(L, "kwt", [128, 512], BF16)
                otok = P.sb(L, "otok", [128, 512], BF16)
                bobc = P.sb(L, "bobc", [128, 512], F32)
                Sm = P.sb(L, "Sm", [128, 128], F32)
                Eg_ = P.sb(L, "Eg_", [128, 128], F32)
                wls = P.sb(L, "wls", [128, 128], BF16)
                wT = P.sb(L, "wT", [128, 128], BF16)
                Asb = P.sb(L, "Asb", [128, 512], F32)
                num = P.sb(L, "num", [128, 512], F32)
                sm = P.sb(L, "sm", [128, 16], F32)
                st6 = P.sb(L, "st6", [128, 8], F32)
                hn = P.sb(L, "hn", [128, 512], BF16)
                g1 = P.sb(L, "g1", [128, 128], F32)
                sxc = P.sb(L, "sxc", [128, 128], F32)
                CT = P.sb(L, "CT", [128, 4, 512], F32)
                CTb = P.sb(L, "CTb", [128, 4, 512], BF16)
                nst = P.sb(L, "nst", [128, 4], F32)
                nstb = P.sb(L, "nstb", [128, 4], BF16)
                Cout = P.sb(L, "Cout", [128, 512], F32)
                sg = P.sb(L, "sg", [NS, 64], F32)
                n0t = P.sb(L, "n0t", [NS, 512], F32)
                qtok = P.sb(L, "qtok", [NS, 512], F32)
                wid = P.sb(L, "wid", [NS, NS, 8], F32)
                wibc = P.sb(L, "wibc", [128, NS, 8], F32)
                Cnt1 = P.sb(L, "Cnt1", [128, 512], F32)
                numT = P.sb(L, "numT", [128, 4, NS], F32)
                hsT = P.sb(L, "hsT", [128, 4, NS], F32)
                hs2 = P.sb(L, "hs2", [128, 4, NS], F32)
                mrs = P.sb(L, "mrs", [128, 4, NS], F32)
                gTs = P.sb(L, "gTs", [128, 4, NS], BF16)
                pbT1 = pb[1].bitcast(BF16)
                scr = P.nc.dram_tensor("ml_scr", [64, 4], F32, kind="Internal").ap()

                for k in range(4):
                    P.dma("sp", cw[:, :, k], W["l2_conv_w"][k].rearrange("(j p) -> p j", p=128), w=["cw"], dkey="cw%d" % k, nonc=True)
                P.dma("sp", cw[:, :, 4], W["l2_conv_b"].rearrange("(j p) -> p j", p=128), w=["cw"], dkey="cw4", nonc=True)
                for pi, nm in enumerate(("w_q", "w_k", "w_v", "w_o")):
                    P.dma("sp", wq4[:, :, pi, :], W["l2_" + nm].rearrange("n j i -> (n j) i").rearrange("(t p) i -> p t i", p=128),
                          w=["wq4"], dkey="wq4%d" % pi, nonc=True)
                P.dma("pool", wif[:, :, :], W["l2_w_if"].rearrange("(t p) g -> p t g", p=128), w=["wif"], dkey="wif")
                P.dma("sp", mhw[:, :], W["l2_mh_norm_w"].rearrange("(k p) -> p k", p=128), w=["mhw"], dkey="mhw", nonc=True)
                P.dma("sp", skp[:, :], W["l2_skip"].rearrange("(k p) -> p k", p=128), w=["skp"], dkey="skp", nonc=True)
                P.dma("sp", bocol[:, :], W["l2_b_o"].rearrange("(k p) -> p k", p=128), w=["bocol"], dkey="bocol", nonc=True)
                bif2 = W["l2_b_if"].rearrange("(g o) -> g o", o=1)
                for h in range(4):
                    P.dma("sp", gs1[h * 16:(h + 1) * 16, 0:1], bc(bif2[h:h + 1, :], [16, 1]), w=["gs1"], dkey="gs1a%d" % h, nonc=True)
                    P.dma("sp", gs1[h * 16:(h + 1) * 16, 1:2], bc(bif2[4 + h:5 + h, :], [16, 1]), w=["gs1"], dkey="gs1b%d" % h, nonc=True)
                P.ts("dve", gs1[:, 2:3], gs1[:, 1:2], -1.0, None, ALU.mult, r=["gs1"], w=["gs1"])
                P.memset("dve", gs1[:, 3:4], 1.0, w=["gs1"])
                P.memset("dve", wid[:, :, :], 0.0, w=["wid"])

                def build_D(dst_idx, jt, pi):
                    P.tt("dve", Dm[:, dst_idx, :].rearrange("p (n i) -> p n i", i=4),
                         bc(wq4[:, jt, pi:pi + 1, :], [128, 32, 4]), bmask.rearrange("p (n i) -> p n i", i=4), ALU.mult,
                         r=["wq4", "cst"], w=["Dm"])

                def build_dg(dst_base, jt):
                    for k in range(4):
                        P.ts("dve", dg[:, dst_base + k, :], ident_f, cw[:, jt, k:k + 1], None, ALU.mult, r=["cst", "cw"], w=["dg"])

                def proj_conv(jt, jl, ti, wcols, wkey):
                    t0 = ti * 512
                    for kc in range(8):
                        P.mm(pb[0][:, :], wbuf[:, kc, wcols:wcols + 128], xn[:, kc, t0:t0 + 512], kc == 0, kc == 7, r=[wkey, xnk(ti)], w=["pb0"])
                    P.copy("act", xr[jl][:, 3:515], pb[0][:, :], r=["pb0"], w=["xr%d" % jl])
                    for k in range(4):
                        P.mm(pb[1][:, :], dg[:, jl * 4 + k, :], xr[jl][:, k:k + 512], k == 0, k == 3, r=["dg", "xr%d" % jl], w=["pb1"])
                    P.act(xc[:, jl, :], pb[1][:, :], AF.Silu, r=["pb1", "cw"], w=["xc.%d" % jl], bias=cw[:, jt, 4:5])

                stgA = cfg.get("ml_stage", 9)
                for jt in range(16 if stgA >= 1 else 0):
                    slot = jt % 4
                    wk = "wA%d" % slot
                    P.dma("pool", wbuf[:, :, slot * 128:(slot + 1) * 128], Win[:, :, jt * 128:(jt + 1) * 128], w=[wk], dkey=wk)
                    build_dg(0, jt)
                    for pi in range(3):
                        build_D(pi * 4, jt, pi)
                    P.memset("dve", xr[0][:, 0:3], 0.0, w=["xr0"])
                    for ti in range(4):
                        proj_conv(jt, 0, ti, slot * 128, wk)
                        if ti == 3:
                            P.copy("dve", ctail[:, :], pb[0][:, 509:512], r=["pb0"], w=["ctail"])
                        else:
                            P.copy("pool", xr[0][:, 0:3], xr[0][:, 512:515], r=["xr0"], w=["xr0"])
                        P.mm(pb[2][:, :], Dm[:, 0, :], xc[:, 0, :], True, True, r=["Dm", "xc.0"], w=["pb2"])
                        P.mm(pb[3][:, :], Dm[:, 4, :], xc[:, 0, :], True, True, r=["Dm", "xc.0"], w=["pb3"])
                        P.mm(pb[4][:, :], Dm[:, 8, :], xr[0][:, 3:515], True, True, r=["Dm", "xr0"], w=["pb4"])
                        P.copy("act", qT[:, 0, :], pb[2][:, :], r=["pb2"], w=["qT"])
                        P.copy("dve", khT[:, 0, :], pb[3][:, :], r=["pb3"], w=["khT"])
                        P.copy("act", szT[:, 0, :], pb[4][:, :], r=["pb4"], w=["szT"])
                        for c4 in range(4):
                            cs_ = slice(c4 * 128, (c4 + 1) * 128)
                            o_ = pb[5][:, c4 * 8:(c4 + 1) * 8]
                            P.mm(o_, qT[:, 0, cs_], wif[:, jt, :], True, False, r=["qT", "wif"], w=["pb5"])
                            P.mm(o_, khT[:, 0, cs_], wif[:, 16 + jt, :], False, False, r=["khT", "wif"], w=["pb5"])
                            P.mm(o_, szT[:, 0, cs_], wif[:, 32 + jt, :], False, True, r=["szT", "wif"], w=["pb5"])
                        gv = gacc[:, 4 * ti:4 * ti + 4, :]
                        pv = pb[5][:, 0:32].rearrange("p (c g) -> p c g", g=8)
                        if jt == 0:
                            P.copy("dve", gv, pv, r=["pb5"], w=["gacc.%d" % ti])
                        else:
                            P.tt("dve", gv, gv, pv, ALU.add, r=["pb5", "gacc.%d" % ti], w=["gacc.%d" % ti])
                    P.tr(pb[6][0:3, 0:128], ctail[:, :], ident_f, r=["ctail", "cst"], w=["pb6"])
                    P.copy("dve", ctout[:, :], pb[6][0:3, 0:128], r=["pb6"], w=["ctout"])
                    P.dma("sp", o_pconv[:, jt * 128:(jt + 1) * 128], ctout[:, :], r=["ctout"], dkey="o_ctout")
                    for kc in range(8):
                        P.mm(pb[6][:, 128:128 + NS], wbuf[:, kc, slot * 128:(slot + 1) * 128], xn[:, kc, SEQ:NT], kc == 0, kc == 7, r=[wk, xnk(4)], w=["pb6"])
                    P.copy("dve", xms_all[:, jt, :], pb[6][:, 128:128 + NS], r=["pb6"], w=["xms_all"])
                    P.dma("sp", cstk[:, :], st_conv[:, :, jt * 128:(jt + 1) * 128].rearrange("b k c -> (b k) c"), w=["cstk"], dkey="cstk")
                    P.tr(pb[6][:, 256:304], cstk[:, :], ident_f[0:48, 0:48], r=["cstk", "cst"], w=["pb6"])
                    P.copy("dve", cs[:, :], pb[6][:, 256:304], r=["pb6"], w=["cs"])
                    cs3 = cs[:, :].rearrange("p (b k) -> p b k", k=3)
                    ncs3 = ncs[:, :].rearrange("p (b k) -> p b k", k=3)
                    a0 = sA[:, 0, :]
                    a1 = sA[:, 1, :]
                    P.ts("dve", a0, xms_all[:, jt, :], cw[:, jt, 3:4], cw[:, jt, 4:5], ALU.mult, ALU.add, r=["xms_all", "cw"], w=["sA"])
                    for k in range(3):
                        P.stt(a0, cs3[:, :, k], cw[:, jt, k:k + 1], a0, ALU.mult, ALU.add, r=["cs", "cw", "sA"], w=["sA"])
                    P.act(xcs_all[:, jt, :], a0, AF.Silu, r=["sA"], w=["xcs_all"])
                    P.copy("dve", ncs3[:, :, 0:2], cs3[:, :, 1:3], r=["cs"], w=["ncs"])
                    P.copy("dve", ncs3[:, :, 2], xms_all[:, jt, :], r=["xms_all", "ncs"], w=["ncs"])
                    P.tr(pb[6][0:48, 320:448], ncs[:, :], ident_f, r=["ncs", "cst"], w=["pb6"])
                    P.copy("dve", cstk[:, :], pb[6][0:48, 320:448], r=["pb6"], w=["cstk"])
                    P.dma("sp", o_sconv[:, :, jt * 128:(jt + 1) * 128].rearrange("b k c -> (b k) c"), cstk[:, :], r=["cstk"], dkey="o_cstk")
                    P.copy("dve", sB[:, 0, :], xcs_all[:, jt, :], r=["xcs_all"], w=["sB"])
                    P.copy("dve", sB[:, 1, :], xms_all[:, jt, :], r=["xms_all"], w=["sB"])
                    for pi in range(3):
                        P.mm(pb[7][:, pi * NS:(pi + 1) * NS], Dm[:, pi * 4, :], sB[:, 0 if pi < 2 else 1, :], True, True, r=["Dm", "sB"], w=["pb7"])
                    P.copy("dve", sB[:, 2:4, :].rearrange("p a b -> p (a b)")[:, 0:2 * NS], pb[7][:, 0:2 * NS], r=["pb7"], w=["sB2"])
                    P.copy("dve", sA[:, 2, :].bitcast(BF16)[:, 0:NS], pb[7][:, 2 * NS:3 * NS], r=["pb7"], w=["sA2"])
                    sv = sA[:, 2, :].bitcast(BF16)[:, 0:NS]
                    P.mm(pb[7][0:NS, 64:72], sB[:, 2, :], wif[:, jt, :], True, False, r=["sB2", "wif"], w=["pb7"])
                    P.mm(pb[7][0:NS, 64:72], sB[:, 3, :], wif[:, 16 + jt, :], False, False, r=["sB2", "wif"], w=["pb7"])
                    P.mm(pb[7][0:NS, 64:72], sv, wif[:, 32 + jt, :], False, True, r=["sA2", "wif"], w=["pb7"])
                    if jt == 0:
                        P.copy("dve", gacc[0:NS, 16, :], pb[7][0:NS, 64:72], r=["pb7"], w=["gacc.4"])
                    else:
                        P.tt("dve", gacc[0:NS, 16, :], gacc[0:NS, 16, :], pb[7][0:NS, 64:72], ALU.add, r=["pb7", "gacc.4"], w=["gacc.4"])

                if stgA >= 2:
                    gk = ["gacc.%d" % i for i in range(4)]
                    P.copy("dve", gcol[:, :].rearrange("p (g c) -> p c g", c=16), gacc[:, 0:16, :], r=gk, w=["gcol"])
                    P.memset("dve", num[0:64, 256:384], 1.0, w=["gq7"])
                    P.tr(pb[0][0:64, 0:128], gcol[:, 0:64], ident_f, r=["gcol", "cst"], w=["pb0"])
                    P.tr(pb[0][0:64, 128:256], gcol[:, 64:128], ident_f, r=["gcol", "cst"], w=["pb0"])
                    ig = ga[:, :]
                    lf, cm, wi_, em_ = [Asb[0:64, i * 128:(i + 1) * 128] for i in range(4)]
                    we_, dpb, one_ = [num[0:64, i * 128:(i + 1) * 128] for i in range(3)]
                    P.ts("dve", ig, pb[0][0:64, 0:128], gs1[:, 0:1], None, ALU.add, r=["pb0", "gs1"], w=["gq0"])
                    P.act(lf, pb[0][0:64, 128:256], AF.Exp, r=["pb0", "gs1"], w=["gq1"], bias=gs1[:, 2:3], scale=-1.0)
                    P.act(lf, lf, AF.Ln, r=["gq1", "gs1"], w=["gq1"], bias=gs1[:, 3:4])
                    P.ts("dve", lf, lf, -1.0, None, ALU.mult, r=["gq1"], w=["gq1"])
                    P.S.add("dve", (lambda e: e.tensor_tensor_scan(out=lf, data0=one_, data1=lf, initial=0.0, op0=ALU.mult, op1=ALU.add)),
                            ["gq1", "gq7"], ["gq1"])
                    P.tt("dve", ig, ig, lf, ALU.subtract, r=["gq0", "gq1"], w=["gq0"])
                    P.S.add("dve", (lambda e: e.tensor_tensor_scan(out=cm, data0=ig, data1=ig, initial=-1e30, op0=ALU.max, op1=ALU.max)),
                            ["gq0"], ["gq2"])
                    P.copy("dve", gs1[:, 4:5], cm[:, 127:128], r=["gq2"], w=["gs1e"])
                    P.copy("dve", gs1[:, 5:6], lf[:, 127:128], r=["gq1"], w=["gs1e"])
                    P.dma("sp", scr[:, 0:2], gs1[:, 4:6], r=["gs1e"], w=["scr01"], dkey="scr_a", nonc=True)
                    P.dma("sp", ends4[:, :, :], scr[:, 0:2].rearrange("(h c) t -> h c t", c=16), r=["scr01"], w=["ends4"], dkey="scr_b", nonc=True)
                    P.S.add("dve", (lambda e: e.tensor_tensor_scan(out=m4[:, 0, :], data0=ends4[:, :, 0], data1=ends4[:, :, 1], initial=0.0,
                                                                   op0=ALU.max, op1=ALU.add)), ["ends4"], ["m4"])
                    P.memset("dve", m4[:, 1, 0:1], 0.0, w=["m4b"])
                    P.copy("dve", m4[:, 1, 1:16], m4[:, 0, 0:15], r=["m4"], w=["m4b"])
                    P.dma("sp", o_pm.rearrange("(h o) -> h o", o=1), m4[:, 0, 15:16], r=["m4"], dkey="o_pm", nonc=True)
                    P.dma("sp", scr[:, 2:3].rearrange("(h c) t -> h c t", c=16), m4[:, 1, :].rearrange("h (c o) -> h c o", o=1), r=["m4b"], w=["scr2"],
                          dkey="scr_c", nonc=True)
                    P.dma("sp", gs1[:, 6:7], scr[:, 2:3], r=["scr2"], w=["gs1m"], dkey="scr_d", nonc=True)
                    mprev = gs1[:, 6:7]
                    P.ts("dve", cm, cm, mprev, None, ALU.max, r=["gq2", "gs1m"], w=["gq2"])
                    P.act(wi_, cm, AF.Exp, r=["gq2", "gs1m"], w=["gq3"], bias=mprev, scale=-1.0)
                    P.tt("dve", em_, lf, cm, ALU.add, r=["gq1", "gq2"], w=["gq4"])
                    P.act(em_, em_, AF.Exp, r=["gq4"], w=["gq4"], scale=-1.0)
                    P.ts("dve", gs1[:, 7:8], cm[:, 127:128], -1.0, math.log(KS), ALU.mult, ALU.add, r=["gq2"], w=["gs1n"])
                    P.act(we_, ig, AF.Exp, r=["gq0", "gs1n"], w=["gq5"], bias=gs1[:, 7:8])
                    P.tt("dve", gs1[:, 4:5], mprev, cm[:, 127:128], ALU.subtract, r=["gs1m", "gq2", "gs1e"], w=["gs1e"])
                    P.act(gs1[:, 4:5], gs1[:, 4:5], AF.Exp, r=["gs1e"], w=["gs1e"])
                    P.ts("dve", dpb, one_, gs1[:, 4:5], None, ALU.mult, r=["gq7", "gs1e"], w=["gq6"])
                    for qi, src in enumerate((cm, wi_, em_, we_, dpb)):
                        P.tr(pb[2][:, qi * 64:(qi + 1) * 64], src, ident_f[0:64, 0:64], r=["gq%d" % (2 + qi), "cst"], w=["pb2"])
                    P.copy("dve", tokq[:, :, :], pb[2][:, 0:320].rearrange("p (q x) -> p q x", q=5), r=["pb2"], w=["tokq"])

                    P.dma("sp", sg[:, 28:36], bc(W["l2_b_if"].rearrange("(o g) -> o g", o=1), [NS, 8]), w=["sg"], dkey="sg_b")
                    P.dma("sp", sg[:, 8:12], st_m[:, :], w=["sg"], dkey="sg_m")
                    P.tt("dve", sg[:, 0:8], gacc[0:NS, 16, :], sg[:, 28:36], ALU.add, r=["gacc.4", "sg"], w=["sg"])
                    P.act(sg[:, 4:8], sg[:, 4:8], AF.Exp, r=["sg"], w=["sg"], scale=-1.0)
                    P.act(sg[:, 4:8], sg[:, 4:8], AF.Ln, r=["sg", "gs1"], w=["sg"], bias=gs1[0:NS, 3:4])
                    P.ts("dve", sg[:, 4:8], sg[:, 4:8], -1.0, None, ALU.mult, r=["sg"], w=["sg"])
                    P.tt("dve", sg[:, 8:12], sg[:, 8:12], sg[:, 4:8], ALU.add, r=["sg"], w=["sg"])
                    P.tt("dve", sg[:, 12:16], sg[:, 8:12], sg[:, 0:4], ALU.max, r=["sg"], w=["sg"])
                    P.dma("sp", o_sm[:, :], sg[:, 12:16], r=["sg"], dkey="o_sm")
                    P.tt("dve", sg[:, 16:20], sg[:, 0:4], sg[:, 12:16], ALU.subtract, r=["sg"], w=["sg"])
                    P.act(sg[:, 16:20], sg[:, 16:20], AF.Exp, r=["sg"], w=["sg"])
                    P.tt("dve", sg[:, 20:24], sg[:, 8:12], sg[:, 12:16], ALU.subtract, r=["sg"], w=["sg"])
                    P.act(sg[:, 20:24], sg[:, 20:24], AF.Exp, r=["sg"], w=["sg"])
                    P.act(sg[:, 24:28], sg[:, 12:16], AF.Exp, r=["sg"], w=["sg"], scale=-1.0)
                    P.tt("dve", wid[:, :, 0:4], bc(sg[:, 20:24].rearrange("p (o h) -> p o h", o=1), [NS, NS, 4]),
                         bc(ident_f[0:NS, 0:NS].rearrange("p (b o) -> p b o", o=1), [NS, NS, 4]), ALU.mult, r=["sg", "cst"], w=["wid"])

                nheads = cfg.get("ml_heads", 4) if stgA >= 3 else 0
                for h in range(nheads):
                    P.dma("pool", wbuf[:, :, 0:512], Win[:, :, 512 * h:512 * (h + 1)], w=["wA0", "wA1", "wA2", "wA3"], dkey="wBx")
                    P.dma("pool", wbuf[:, :, 512:1024], Win[:, :, 2048 + 512 * h:2048 + 512 * (h + 1)], w=["wBz"], dkey="wBz")
                    P.dma("pool", wo[:, :, :], Wout[:, 4 * h:4 * h + 4, :], w=["wo"], dkey="wo")
                    P.dma("sp", bobc[:, :], bc(W["l2_b_o"][512 * h:512 * (h + 1)].rearrange("(o n) -> o n", o=1), [128, 512]), w=["bobc"], dkey="bobc")
                    wxk = ["wA0", "wA1", "wA2", "wA3"]
                    for jl in range(4):
                        build_dg(jl * 4, 4 * h + jl)
                        for pi in range(4):
                            build_D(pi * 4 + jl, 4 * h + jl, pi)
                        P.memset("dve", xr[jl][:, 0:3], 0.0, w=["xr%d" % jl])
                    P.memset("dve", CT[:, :, :], 0.0, w=["CT"])
                    P.memset("pool", CTb[:, :, :], 0.0, w=["CTb"])
                    P.memset("dve", nst[:, :], 0.0, w=["nst"])
                    P.memset("dve", nstb[:, :], 0.0, w=["nstb"])
                    for ti in range(4):
                        t0 = ti * 512
                        for jl in range(4):
                            jt = 4 * h + jl
                            proj_conv(jt, jl, ti, jl * 128, wxk[jl])
                            if ti < 3:
                                P.copy("pool", xr[jl][:, 0:3], xr[jl][:, 512:515], r=["xr%d" % jl], w=["xr%d" % jl])
                            P.mm(pb[2][:, :], Dm[:, 0 + jl, :], xc[:, jl, :], True, True, r=["Dm", "xc.%d" % jl], w=["pb2"])
                            P.mm(pb[3][:, :], Dm[:, 4 + jl, :], xc[:, jl, :], True, True, r=["Dm", "xc.%d" % jl], w=["pb3"])
                            P.copy("act", qT[:, jl, :], pb[2][:, :], r=["pb2"], w=["qT"])
                            P.act(khT[:, jl, :], pb[3][:, :], AF.Copy, r=["pb3"], w=["khT"], scale=KS)
                            for kc in range(8):
                                P.mm(pb[4][:, :], wbuf[:, kc, 512 + jl * 128:512 + (jl + 1) * 128], xn[:, kc, t0:t0 + 512], kc == 0, kc == 7,
                                     r=["wBz", xnk(ti)], w=["pb4"])
                            P.act(szT[:, jl, :], pb[4][:, :], AF.Silu, r=["pb4"], w=["szT"])
                        for c4 in range(4):
                            c = 4 * ti + c4
                            hc = h * 16 + c
                            cs_ = slice(c4 * 128, (c4 + 1) * 128)
                            xs_ = slice(3 + c4 * 128, 3 + (c4 + 1) * 128)
                            for jl in range(4):
                                js = slice(jl * 128, (jl + 1) * 128)
                                P.mm(pb[2][:, js], xr[jl][:, xs_], Dm[:, 8 + jl, :], True, True, r=["xr%d" % jl, "Dm"], w=["pb2"])
                                P.mm(pb[3][:, js], xc[:, jl, cs_], Dm[:, 4 + jl, :], True, True, r=["xc.%d" % jl, "Dm"], w=["pb3"])
                                P.mm(pb[4][:, js], xr[jl][:, xs_], Dm[:, 12 + jl, :], True, True, r=["xr%d" % jl, "Dm"], w=["pb4"])
                            P.copy("act", vtok[:, :], pb[2][:, :], r=["pb2"], w=["vtok"])
                            P.act(kwt[:, :], pb[3][:, :], AF.Copy, r=["pb3", "tokq"], w=["kwt"], scale=tokq[:, 3, hc:hc + 1])
                            P.tt("dve", num[:, :], pb[4][:, :], bobc[:, :], ALU.add, r=["pb4", "bobc"], w=["num"])
                            P.act(otok[:, :], num[:, :], AF.Tanh, r=["num"], w=["otok"], scale=0.5)
                            P.ts("pool", otok[:, :], otok[:, :], 0.5, 0.5, ALU.mult, ALU.add, r=["otok"], w=["otok"])
                            for jl in range(4):
                                P.mm(pb[5][:, 0:128], qT[:, jl, cs_], khT[:, jl, cs_], jl == 0, jl == 3, r=["qT", "khT"], w=["pb5"])
                            P.mm(pb[5][:, 128:256], bc(ident_f[0:64, hc:hc + 1], [64, 128]), ga[:, :], True, True, r=["cst", "gq0"], w=["pb5"])
                            for jl in range(4):
                                P.mm(pb[5][:, 256:257], qT[:, jl, cs_], nstb[:, jl:jl + 1], jl == 0, jl == 3, r=["qT", "nstb"], w=["pb5"])
                            P.tt("dve", Sm[:, :], pb[5][:, 0:128], gemask, ALU.mult, r=["pb5", "cst"], w=["Sm"])
                            P.ts("dve", Eg_[:, :], pb[5][:, 128:256], tokq[:, 0, hc:hc + 1], 0.0, ALU.subtract, ALU.min, r=["pb5", "tokq"], w=["Eg_"])
                            P.act(Eg_[:, :], Eg_[:, :], AF.Exp, r=["Eg_"], w=["Eg_"])
                            P.S.add("dve", (lambda e, o=wls[:, :], a=Eg_[:, :], b=Sm[:, :], ac=sm[:, 0:1]:
                                            e.scalar_tensor_tensor(out=o, in0=a, scalar=1.0, in1=b, op0=ALU.mult, op1=ALU.mult, accum_out=ac)),
                                    ["Eg_", "Sm"], ["wls", "sm0"])
                            P.tr(pbT1[:, 0:128], wls[:, :], ident_b[:, :], r=["wls", "ident_b"], w=["pb1"])
                            P.copy("dve", wT[:, :], pbT1[:, 0:128], r=["pb1"], w=["wT"])
                            P.mm(pb[6][:, :], wT[:, :], vtok[:, :], True, True, r=["wT", "vtok"], w=["pb6"])
                            for jl in range(4):
                                P.mm(pb[7][:, :], qT[:, jl, cs_], CTb[:, jl, :], jl == 0, jl == 3, r=["qT", "CTb"], w=["pb7"])
                            P.copy("act", Asb[:, :], pb[6][:, :], r=["pb6"], w=["Asb"])
                            wic = tokq[:, 1, hc:hc + 1]
                            P.stt(num[:, :], pb[7][:, :], wic, Asb[:, :], ALU.mult, ALU.add, r=["pb7", "tokq", "Asb"], w=["num"])
                            P.stt(sm[:, 1:2], pb[5][:, 256:257], wic, sm[:, 0:1], ALU.mult, ALU.add, r=["pb5", "tokq", "sm0"], w=["sm1"])
                            P.stt(sm[:, 1:2], sm[:, 1:2], -1.0, sm[:, 1:2], ALU.mult, ALU.max, r=["sm1"], w=["sm1"])
                            P.tt("dve", sm[:, 1:2], sm[:, 1:2], tokq[:, 2, hc:hc + 1], ALU.max, r=["sm1", "tokq"], w=["sm1"])
                            P.recip(sm[:, 1:2], sm[:, 1:2], r=["sm1"], w=["sm1"])
                            P.stt(num[:, :], num[:, :], sm[:, 1:2], otok[:, :], ALU.mult, ALU.mult, r=["num", "sm1", "otok"], w=["num"])
                            P.S.add("dve", (lambda e, o=st6[:, 0:6], i=num[:, :]: e.bn_stats(out=o, in_=i)), ["num"], ["st6"])
                            P.S.add("dve", (lambda e, o=sm[:, 2:4], i=st6[:, 0:6]: e.bn_aggr(out=o, in_=i)), ["st6"], ["sm2"])
                            rstd_pool(sm[:, 4:5], sm[:, 3:4], 1.0, ["sm2"], ["sm4"])
                            P.stt(sm[:, 5:6], sm[:, 2:3], -1.0, sm[:, 4:5], ALU.mult, ALU.mult, r=["sm2", "sm4"], w=["sm5"])
                            P.ts("dve", hn[:, :], num[:, :], sm[:, 4:5], sm[:, 5:6], ALU.mult, ALU.add, r=["num", "sm4", "sm5"], w=["hn"])
                            for jl in range(4):
                                P.tr(pbT1[:, 256 + jl * 128:256 + (jl + 1) * 128], hn[:, jl * 128:(jl + 1) * 128], ident_b[:, :], r=["hn", "ident_b"], w=["pb1"])
                            for jl in range(4):
                                jt = 4 * h + jl
                                P.ts("dve", sxc[:, :], xc[:, jl, cs_], skp[:, jt:jt + 1], None, ALU.mult, r=["xc.%d" % jl, "skp"], w=["sxc"])
                                P.stt(g1[:, :], pbT1[:, 256 + jl * 128:256 + (jl + 1) * 128], mhw[:, jt:jt + 1], sxc[:, :], ALU.mult, ALU.add,
                                      r=["pb1", "mhw", "sxc"], w=["g1"])
                                P.tt("pool", gT[:, jl, cs_], g1[:, :], szT[:, jl, cs_], ALU.mult, r=["g1", "szT"], w=["gT"])
                            dpc = tokq[:, 4, hc:hc + 1]
                            for kt in range(4):
                                pu, puk = (pb[0], "pb0") if kt % 2 == 0 else (pb[6], "pb6")
                                P.mm(pu[:, :], kwt[:, kt * 128:(kt + 1) * 128], vtok[:, :], True, True, r=["kwt", "vtok"], w=[puk])
                                P.stt(CT[:, kt, :], CT[:, kt, :], dpc, pu[:, :], ALU.mult, ALU.add, r=["CT", "tokq", puk], w=["CT"])
                                P.copy("act", CTb[:, kt, :], CT[:, kt, :], r=["CT"], w=["CTb"])
                                P.mm(pb[5][:, 260 + kt:261 + kt], kwt[:, kt * 128:(kt + 1) * 128], ones_b[:, 0:1], True, True, r=["kwt", "ones_b"], w=["pb5"])
                            P.stt(nst[:, :], nst[:, :], dpc, pb[5][:, 260:264], ALU.mult, ALU.add, r=["nst", "tokq", "pb5"], w=["nst"])
                            P.copy("dve", nstb[:, :], nst[:, :], r=["nst"], w=["nstb"])
                        for dt_ in range(8):
                            po, pk = pb[dt_ % 2], "pb%d" % (dt_ % 2)
                            for jl in range(4):
                                P.mm(po[:, :], wo[:, jl, dt_ * 128:(dt_ + 1) * 128], gT[:, jl, :], jl == 0, jl == 3, r=["wo", "gT"], w=[pk])
                            P.tt("dve", xres[:, dt_, t0:t0 + 512], xres[:, dt_, t0:t0 + 512], po[:, :], ALU.add, r=[pk, xk(dt_, ti)], w=[xk(dt_, ti)])
                    for vt in range(4):
                        for kt in range(4):
                            P.tr(pb[2][:, kt * 128:(kt + 1) * 128], CT[:, kt, vt * 128:(vt + 1) * 128], ident_f, r=["CT", "cst"], w=["pb2"])
                        P.copy("dve", Cout[:, :], pb[2][:, :], r=["pb2"], w=["Cout"])
                        P.dma("sp", o_pC[h, vt * 128:(vt + 1) * 128, :], Cout[:, :], r=["Cout"], dkey="o_Cout")
                    P.tr(pb[3][0:4, 0:128], nst[:, :], ident_f, r=["nst", "cst"], w=["pb3"])
                    P.copy("dve", ctout[:, :].bitcast(F32)[0:3, :], pb[3][0:3, 0:128], r=["pb3"], w=["ctout"]) if False else None
                    P.copy("dve", st6[0:4, 0:8].bitcast(F32), pb[3][0:4, 0:8], r=["pb3"], w=["st6"]) if False else None
                    P.copy("dve", Cout[0:4, 0:128], pb[3][0:4, 0:128], r=["pb3"], w=["Cout"])
                    P.dma("sp", o_pn[h].rearrange("(kt k) -> kt k", k=128), Cout[0:4, 0:128], r=["Cout"], dkey="o_pn")

                    if cfg.get("ml_nosample"):
                        continue
                    for jl in range(4):
                        jt = 4 * h + jl
                        P.copy("dve", sB[:, 0, :], xcs_all[:, jt, :], r=["xcs_all"], w=["sB"])
                        P.copy("dve", sB[:, 1, :], xms_all[:, jt, :], r=["xms_all"], w=["sB"])
                        js = slice(jl * 128, (jl + 1) * 128)
                        P.mm(pb[2][0:NS, js], sB[:, 0, :], Dm[:, 4 + jl, :], True, True, r=["sB", "Dm"], w=["pb2"])
                        P.mm(pb[3][0:NS, js], sB[:, 0, :], Dm[:, 0 + jl, :], True, True, r=["sB", "Dm"], w=["pb3"])
                        P.mm(pb[4][0:NS, js], sB[:, 1, :], Dm[:, 8 + jl, :], True, True, r=["sB", "Dm"], w=["pb4"])
                        P.mm(pb[5][:, jl * NS:(jl + 1) * NS], Dm[:, 12 + jl, :], sB[:, 1, :], True, True, r=["sB", "Dm"], w=["pb5"])
                        for kc in range(8):
                            P.mm(pb[5][:, 64 + jl * NS:64 + (jl + 1) * NS], wbuf[:, kc, 512 + jl * 128:512 + (jl + 1) * 128], xn[:, kc, SEQ:NT], kc == 0, kc == 7,
                                 r=["wBz", xnk(4)], w=["pb5"])
                        P.ts("dve", mrs[:, jl, :], pb[5][:, jl * NS:(jl + 1) * NS], bocol[:, jt:jt + 1], None, ALU.add, r=["pb5", "bocol"], w=["mrs"])
                        P.act(mrs[:, jl, :], mrs[:, jl, :], AF.Tanh, r=["mrs"], w=["mrs"], scale=0.5)
                        P.ts("dve", mrs[:, jl, :], mrs[:, jl, :], 0.5, 0.5, ALU.mult, ALU.add, r=["mrs"], w=["mrs"])
                        P.act(hs2[:, jl, :], pb[5][:, 64 + jl * NS:64 + (jl + 1) * NS], AF.Silu, r=["pb5"], w=["hs2"])
                    ktok, vwt, kmk, qmk = bobc[0:NS, :], otok[0:NS, :], vtok[0:NS, :], kwt[0:NS, :]
                    def f32view(buf_):
                        return buf_[:, 0:2, :].rearrange("p a c -> p (a c)").bitcast(F32)
                    C0t = [(Asb[:, :], ["Asb"]), (num[:, :], ["num"]), (f32view(qT), ["qT"]), (f32view(khT), ["khT"]),
                           (f32view(xc), ["xc.0", "xc.1", "xc.2", "xc.3"]), (f32view(CTb), ["CTb"])]
                    Cnt = [(Cout[:, :], ["Cout"]), (Cnt1[:, :], ["Cnt1"]), (f32view(szT), ["szT"]), (f32view(gT), ["gT"]), (CT[:, 0, :], ["CT"])]
                    P.ts("dve", ktok, pb[2][0:NS, :], KS, None, ALU.mult, r=["pb2"], w=["bobc"])
                    P.copy("dve", qtok[:, :], pb[3][0:NS, :], r=["pb3"], w=["qtok"])
                    P.ts("dve", vwt, pb[4][0:NS, :], sg[:, 16 + h:17 + h], None, ALU.mult, r=["pb4", "sg"], w=["otok"])
                    P.dma("sp", n0t[:, :], st_n[:, h, :], w=["n0t"], dkey="n0t")
                    P.ts("dve", n0t[:, :], n0t[:, :], sg[:, 20 + h:21 + h], None, ALU.mult, r=["n0t", "sg"], w=["n0t"])
                    P.stt(n0t[:, :], ktok, sg[:, 16 + h:17 + h], n0t[:, :], ALU.mult, ALU.add, r=["bobc", "sg", "n0t"], w=["n0t"])
                    P.dma("sp", o_sn[:, h, :], n0t[:, :], r=["n0t"], dkey="o_sn")
                    P.S.add("dve", (lambda e, o=Cout[0:NS, :], a=n0t[:, :], b=qtok[:, :], ac=sg[:, 36:37]:
                                    e.scalar_tensor_tensor(out=o, in0=a, scalar=1.0, in1=b, op0=ALU.mult, op1=ALU.mult, accum_out=ac)),
                            ["n0t", "qtok", "Cout"], ["Cout", "sg36"])
                    P.stt(sg[:, 36:37], sg[:, 36:37], -1.0, sg[:, 36:37], ALU.mult, ALU.max, r=["sg36"], w=["sg36"])
                    P.tt("dve", sg[:, 36:37], sg[:, 36:37], sg[:, 24 + h:25 + h], ALU.max, r=["sg36", "sg"], w=["sg36"])
                    P.recip(sg[:, 36:37], sg[:, 36:37], r=["sg36"], w=["sg36"])
                    P.tt("dve", wid[:, :, 4:5], bc(sg[:, 36:37].rearrange("p (o h) -> p o h", o=1), [NS, NS, 1]),
                         bc(ident_f[0:NS, 0:NS].rearrange("p (b o) -> p b o", o=1), [NS, NS, 1]), ALU.mult, r=["sg36", "cst"], w=["wid"])
                    P.mm(pb[5][:, 128:256], ones_f[0:NS, :], wid[:, :, :].rearrange("p b x -> p (b x)"), True, True, r=["cst", "wid"], w=["pb5"])
                    P.copy("dve", wibc[:, :, :], pb[5][:, 128:256].rearrange("p (b x) -> p b x", x=8), r=["pb5"], w=["wibc"])
                    for b in range(NS):
                        P.ts("dve", kmk, ktok, ident_f[0:NS, b:b + 1], None, ALU.mult, r=["bobc", "cst"], w=["vtok"])
                        P.ts("dve", qmk, qtok[:, :], ident_f[0:NS, b:b + 1], None, ALU.mult, r=["qtok", "cst"], w=["kwt"])
                        P.mm(pb[7][:, :], ones_b[0:NS, :], qmk, True, True, r=["ones_b", "kwt"], w=["pb7"])
                        for vt in range(4):
                            c0, c0k = C0t[(b * 4 + vt) % len(C0t)]
                            cn, cnk = Cnt[(b * 4 + vt) % len(Cnt)]
                            pu, puk = (pb[2], "pb2") if vt % 2 == 0 else (pb[3], "pb3")
                            P.dma("sp", c0, st_C[b, h, vt * 128:(vt + 1) * 128, :], w=c0k, dkey=c0k[0])
                            P.mm(pu[:, :], vwt[:, vt * 128:(vt + 1) * 128], kmk, True, True, r=["otok", "vtok"], w=[puk])
                            P.stt(cn, c0, wibc[:, b, h:h + 1], pu[:, :], ALU.mult, ALU.add, r=c0k + ["wibc", puk], w=cnk)
                            P.dma("pool", o_sC[b, h, vt * 128:(vt + 1) * 128, :], cn, r=cnk, dkey="o_" + cnk[0])
                            P.S.add("dve", (lambda e, o=c0, a=cn, bb=pb[7][:, :], ac=numT[:, vt, b:b + 1]:
                                            e.scalar_tensor_tensor(out=o, in0=a, scalar=1.0, in1=bb, op0=ALU.mult, op1=ALU.mult, accum_out=ac)),
                                    cnk + ["pb7"] + c0k, c0k + ["numT"])
                    P.tt("dve", hsT[:, :, :], numT[:, :, :], bc(wibc[:, :, 4:5].rearrange("p b o -> p o b"), [128, 4, NS]), ALU.mult, r=["numT", "wibc"], w=["hsT"])
                    P.tt("dve", hsT[:, :, :], hsT[:, :, :], mrs[:, :, :], ALU.mult, r=["hsT", "mrs"], w=["hsT"])
                    P.tt("dve", numT[:, :, :], hsT[:, :, :], hsT[:, :, :], ALU.mult, r=["hsT"], w=["numT"])
                    for vt in range(4):
                        P.mm(pb[4][:, 0:NS], ones_f, hsT[:, vt, :], vt == 0, vt == 3, r=["cst", "hsT"], w=["pb4"])
                    for vt in range(4):
                        P.mm(pb[4][:, NS:2 * NS], ones_f, numT[:, vt, :], vt == 0, vt == 3, r=["cst", "numT"], w=["pb4"])
                    mean = sA[:, 0, :]
                    var = sA[:, 1, :]
                    P.ts("dve", mean, pb[4][:, 0:NS], 1.0 / 512, None, ALU.mult, r=["pb4"], w=["sA"])
                    P.ts("dve", var, pb[4][:, NS:2 * NS], 1.0 / 512, None, ALU.mult, r=["pb4"], w=["sA"])
                    P.tt("dve", sA[:, 3, :], mean, mean, ALU.mult, r=["sA"], w=["sA"])
                    P.tt("dve", var, var, sA[:, 3, :], ALU.subtract, r=["sA"], w=["sA"])
                    rstd_pool(var, var, 1.0, ["sA"], ["sA"])
                    P.tt("dve", hsT[:, :, :], hsT[:, :, :], bc(mean.rearrange("p (o b) -> p o b", o=1), [128, 4, NS]), ALU.subtract, r=["hsT", "sA"], w=["hsT"])
                    P.tt("dve", hsT[:, :, :], hsT[:, :, :], bc(var.rearrange("p (o b) -> p o b", o=1), [128, 4, NS]), ALU.mult, r=["hsT", "sA"], w=["hsT"])
                    for jl in range(4):
                        jt = 4 * h + jl
                        P.ts("dve", hsT[:, jl, :], hsT[:, jl, :], mhw[:, jt:jt + 1], None, ALU.mult, r=["hsT", "mhw"], w=["hsT"])
                        P.stt(hsT[:, jl, :], xcs_all[:, jt, :], skp[:, jt:jt + 1], hsT[:, jl, :], ALU.mult, ALU.add, r=["xcs_all", "skp", "hsT"], w=["hsT"])
                    P.tt("dve", gTs[:, :, :], hsT[:, :, :], hs2[:, :, :], ALU.mult, r=["hsT", "hs2"], w=["gTs"])
                    for dt_ in range(8):
                        po, pk = pb[dt_ % 2], "pb%d" % (dt_ % 2)
                        for jl in range(4):
                            P.mm(po[:, 0:NS], wo[:, jl, dt_ * 128:(dt_ + 1) * 128], gTs[:, jl, :], jl == 0, jl == 3, r=["wo", "gTs"], w=[pk])
                        P.tt("dve", xres[:, dt_, SEQ:NT], xres[:, dt_, SEQ:NT], po[:, 0:NS], ALU.add, r=[pk, xk(dt_, 4)], w=[xk(dt_, 4)])
                P.S.barrier()

        for li_ in (0, 1, 2, 3):
            if li_ in layers:
                if li_ in (0, 3):
                    layer_ssd(li_)
                elif li_ == 1:
                    layer_gmlp()
                else:
                    layer_mlstm()

        with ExitStack() as L:
            ystage = [P.sb(L, "ystage%d" % i, [128, D], F32) for i in range(2)]
            ysq = P.sb(L, "ysq", [128, D], F32)
            fnw_r = P.sb(L, "fnw_r", [128, D], F32)
            ss = P.sb(L, "ss", [128, 2], F32)
            if final_norm:
                P.dma("sp", fnw_r[:, :], bc(W["final_norm_w"].rearrange("(o n) -> o n", o=1), [128, D]), w=["fnw_r"], dkey="fnw_r")
            for c in range(17):
                if c < 16:
                    t0, M, ti = c * 128, 128, c // 4
                    dst = y_p[t0:t0 + 128, :]
                else:
                    t0, M, ti = SEQ, NS, 4
                    dst = y_s[:, :]
                yb = ystage[c % 2]
                yk = "ystage%d" % (c % 2)
                for h in range(2):
                    ps = pb[(2 * c + h) % 8]
                    pk = "pb%d" % ((2 * c + h) % 8)
                    for k4 in range(4):
                        kc = h * 4 + k4
                        P.tr(ps[:M, k4 * 128:(k4 + 1) * 128], xres[:, kc, t0:t0 + M], ident_f, r=[xk(kc, ti), "cst"], w=[pk])
                    P.copy("act" if h else "dve", yb[:M, h * 512:(h + 1) * 512], ps[:M, :], r=[pk], w=[yk + ".%d" % h])
                if final_norm:
                    P.act(ysq[:M, :], yb[:M, :], AF.Square, r=[yk + ".0", yk + ".1"], w=["ysq", "ss"], accum=ss[:M, 0:1])
                    P.act(ss[:M, 1:2], ss[:M, 0:1], AF.Sqrt, r=["ss", "epsc"], w=["ss"], bias=epsc[:M, 0:1], scale=1.0 / D)
                    P.recip(ss[:M, 1:2], ss[:M, 1:2], r=["ss"], w=["ss"])
                    P.stt(yb[:M, :], yb[:M, :], ss[:M, 1:2], fnw_r[:M, :], ALU.mult, ALU.mult,
                          r=[yk + ".0", yk + ".1", "ss", "fnw_r"], w=[yk + ".0", yk + ".1"])
                P.dma("sp", dst, yb[:M, :], r=[yk + ".0", yk + ".1"], dkey="o_" + yk)

        if cfg.get("resched", False):
            P.S.look = cfg.get("look", 24)
            P.S.use_blev = bool(cfg.get("blev", False))
            P.S.verbose = bool(cfg.get("sched_verbose", False))
            P.S.reorder(sync_lat=cfg.get("sync_lat", 150))
        P.S.emit(root)
    return P


def make_consts():
    c = np.zeros((128, 7 * 128), np.float32)
    i = np.arange(128)
    c[:, 0:128] = np.eye(128, dtype=np.float32)
    c[:, 128:256] = 1.0
    c[:, 256:384] = (i[:, None] <= i[None, :]).astype(np.float32)
    c[:, 384:512] = (i[:, None] > i[None, :]).astype(np.float32)
    c[:, 512:640] = (i[None, :] < i[:, None]).astype(np.float32)
    c[:, 640:768] = (i[:, None] // 4 == i[None, :] // 4).astype(np.float32)
    c[:, 768:896] = (i[:, None] >= i[None, :]).astype(np.float32)
    return c


_CACHE = {}


def run(cfg, inputs, ncores=NCORES):
    key = repr(sorted(cfg.items()))
    if key not in _CACHE:
        _CACHE[key] = build(cfg)
    P = _CACHE[key]
    missing = [n for n in ALL_INPUT_NAMES if n not in inputs]
    assert not missing or len(cfg.get('layers', ())) < 4, missing
    consts = make_consts()
    in_maps = []
    for c in range(ncores):
        m = {}
        for name in P.dram:
            t = P.dram[name]
            if name in ("y_prompt", "y_sample") or name.startswith(("p0_", "s0_", "s1_", "p2_", "s2_", "p3_", "s3_")):
                continue
            if name == "consts":
                m[name] = consts
            elif name == "consts2":
                m[name] = (np.arange(32)[:, None] == (np.arange(2048)[None, :] // 64)).astype(np.float32)
            elif name == "x_prompt":
                m[name] = np.ascontiguousarray(inputs["x_prompt"][c])
            elif name == "x_sample":
                m[name] = np.ascontiguousarray(inputs["x_sample"][c * NS:(c + 1) * NS, 0, :])
            elif name.startswith("state_"):
                m[name] = np.ascontiguousarray(inputs[name][c * NS:(c + 1) * NS])
            else:
                m[name] = np.ascontiguousarray(inputs[name])
        in_maps.append(m)
    res = run_bass_kernel_spmd(P.nc, in_maps, core_ids=list(range(ncores)))
    return res.results


def kernel(**inputs):
    cfg = {"layers": (0, 1, 2, 3), "final_norm": True, "resched": True, "sync_lat": 1000}
    r = run(cfg, inputs)
    def pstack(name):
        return np.stack([r[c][name] for c in range(NCORES)], 0)
    def scat(name):
        return np.concatenate([r[c][name] for c in range(NCORES)], 0)
    outs = [pstack("y_prompt"), scat("y_sample")[:, None, :]]
    outs += [pstack("p0_ssm"), pstack("p0_conv"), scat("s0_ssm"), scat("s0_conv")]
    outs += [scat("s1_v")[:, None, :]]
    outs += [pstack("p2_C"), pstack("p2_n"), pstack("p2_m"), pstack("p2_conv"), scat("s2_C"), scat("s2_n"), scat("s2_m"), scat("s2_conv")]
    outs += [pstack("p3_ssm"), pstack("p3_conv"), scat("s3_ssm"), scat("s3_conv")]
    return tuple(np.ascontiguousarray(o, dtype=np.float32) for o in outs)
```
